# Optimizing a Trainium2 kernel written in Bass

```python
import math
import jax, jax.numpy as jnp
from jax import lax
import numpy as np


D_MODEL = 1024
BATCH = 8
SEQ = 4096
DEPTH = 2

DN_ALPHA = (2.0 * DEPTH) ** 0.25
DN_BETA = (8.0 * DEPTH) ** -0.25
LN_EPS = 1e-5

DIFF_HEAD_DIM = 64
DIFF_V_DIM = 2 * DIFF_HEAD_DIM
DIFF_HEADS = (D_MODEL // 2) // DIFF_V_DIM
DIFF_QK_WIDTH = DIFF_HEADS * 2 * DIFF_HEAD_DIM
DIFF_V_WIDTH = DIFF_HEADS * DIFF_V_DIM
Q_BLOCK = 128

RET_QK_DIM = 64
RET_V_DIM = 128
RET_HEADS = (D_MODEL // 2) // RET_V_DIM
RET_QK_WIDTH = RET_HEADS * RET_QK_DIM
RET_V_WIDTH = RET_HEADS * RET_V_DIM
RET_CHUNK = 128

EVEN_SPLIT_WIDTHS = (DIFF_QK_WIDTH, DIFF_QK_WIDTH, DIFF_V_WIDTH, RET_QK_WIDTH, RET_QK_WIDTH, RET_V_WIDTH, RET_V_WIDTH)
EVEN_IN_WIDTH = sum(EVEN_SPLIT_WIDTHS)

RWKV_HEAD_DIM = 64
RWKV_HEADS = D_MODEL // RWKV_HEAD_DIM
DECAY_LORA = max(32, int(round(D_MODEL ** 0.5 * 1.8 / 32)) * 32)
ICLR_LORA = max(32, int(round(D_MODEL ** 0.5 * 1.8 / 32)) * 32)
GATE_LORA = max(32, int(round(D_MODEL ** 0.8 * 0.6 / 32)) * 32)
RWKV_GN_EPS = 64e-5

FFN_HIDDEN = -(-(8 * D_MODEL // 3) // 256) * 256
CONV_WIDTH = 3

kernel_name = 'hybrid_diffattn_retention_rwkv7_convglu'


def layer_norm(x, g, b):
    xf = x.astype(jnp.float32)
    mu = jnp.mean(xf, axis=-1, keepdims=True)
    var = jnp.mean(jnp.square(xf - mu), axis=-1, keepdims=True)
    return ((xf - mu) * lax.rsqrt(var + LN_EPS) * g.astype(jnp.float32) + b.astype(jnp.float32)).astype(x.dtype)


def group_norm(y, eps):
    mu = jnp.mean(y, axis=-1, keepdims=True)
    var = jnp.mean(jnp.square(y - mu), axis=-1, keepdims=True)
    return (y - mu) * lax.rsqrt(var + eps)


def alibi_slopes(n_heads):
    return 2.0 ** (-8.0 * jnp.arange(1, n_heads + 1, dtype=jnp.float32) / n_heads)


def diff_attention(q, k, v, lam, subln_g, lambda_init):
    B, S = q.shape[0], q.shape[1]
    n_blk = S // Q_BLOCK
    scale = DIFF_HEAD_DIM ** -0.5
    kt = jnp.transpose(k, (0, 2, 3, 1, 4))
    vt = jnp.transpose(v, (0, 2, 1, 3))
    qb = q.reshape(B, n_blk, Q_BLOCK, DIFF_HEADS, 2, DIFF_HEAD_DIM).transpose(1, 0, 3, 4, 2, 5)
    slopes = alibi_slopes(DIFF_HEADS)[None, :, None, None, None]
    key_pos = jnp.arange(S)

    def one_block(args):
        q_blk, blk = args
        q_pos = blk * Q_BLOCK + jnp.arange(Q_BLOCK)
        dist = (q_pos[:, None] - key_pos[None, :]).astype(jnp.float32)
        s = jnp.einsum('bhmqd,bhmkd->bhmqk', q_blk, kt).astype(jnp.float32) * scale
        s = jnp.where(dist >= 0, s - slopes * dist, -jnp.inf)
        p = jax.nn.softmax(s, axis=-1)
        a = p[:, :, 0] - lam * p[:, :, 1]
        return jnp.einsum('bhqk,bhkv->bhqv', a.astype(v.dtype), vt)

    o = lax.map(one_block, (qb, jnp.arange(n_blk)))
    o = o.transpose(1, 0, 3, 2, 4).reshape(B, S, DIFF_HEADS, DIFF_V_DIM).astype(jnp.float32)
    o = o * lax.rsqrt(jnp.mean(jnp.square(o), axis=-1, keepdims=True) + LN_EPS)
    o = o * subln_g.astype(jnp.float32) * (1.0 - lambda_init)
    return o.reshape(B, S, DIFF_V_WIDTH).astype(v.dtype)


def retention(q, k, v, g):
    B, S = q.shape[0], q.shape[1]
    C = RET_CHUNK
    n = S // C
    f32 = jnp.float32
    log_gamma = jnp.log1p(-(2.0 ** (-5.0 - jnp.arange(RET_HEADS, dtype=f32))))

    def chunks(t):
        return t.astype(f32).reshape(B, n, C, RET_HEADS, t.shape[-1]).transpose(1, 0, 3, 2, 4)

    qc = chunks(q)
    kc = chunks(k) * (RET_QK_DIM ** -0.5)
    vc = chunks(v)
    idx = jnp.arange(C, dtype=f32)
    rel = idx[:, None] - idx[None, :]
    decay = jnp.where(rel >= 0, jnp.exp(log_gamma[:, None, None] * jnp.maximum(rel, 0.0)), 0.0)
    scores = jnp.einsum('nbhid,nbhjd->nbhij', qc, kc) * decay[None, None]
    inner = jnp.einsum('nbhij,nbhjv->nbhiv', scores, vc)
    q_decay = jnp.exp(log_gamma[:, None] * (idx + 1.0))[None, :, :, None]
    k_decay = jnp.exp(log_gamma[:, None] * (C - 1.0 - idx))[None, :, :, None]
    chunk_decay = jnp.exp(log_gamma * C)[None, :, None, None]

    def step(R, xs):
        q_n, k_n, v_n = xs
        cross = jnp.einsum('bhid,bhdv->bhiv', q_n * q_decay, R)
        R = R * chunk_decay + jnp.einsum('bhjd,bhjv->bhdv', k_n * k_decay, v_n)
        return R, cross

    R0 = jnp.zeros((B, RET_HEADS, RET_QK_DIM, RET_V_DIM), f32)
    _, cross = lax.scan(step, R0, (qc, kc, vc))
    y = (inner + cross).transpose(1, 0, 3, 2, 4).reshape(B, S, RET_HEADS, RET_V_DIM)
    y = group_norm(y, LN_EPS).reshape(B, S, RET_V_WIDTH)
    return (jax.nn.silu(g.astype(f32)) * y).astype(v.dtype)


def wkv7(r, w, k, v, a, b):
    def step(state, xs):
        r_t, w_t, k_t, v_t, a_t, b_t = xs
        sa = jnp.einsum('bhij,bhj->bhi', state, a_t)
        state = state * w_t[:, :, None, :] + sa[..., :, None] * b_t[..., None, :] + v_t[..., :, None] * k_t[..., None, :]
        y = jnp.einsum('bhij,bhj->bhi', state, r_t)
        return state, y

    B, _, H, N = r.shape
    xs = tuple(jnp.moveaxis(t.astype(jnp.float32), 1, 0) for t in (r, w, k, v, a, b))
    s0 = jnp.zeros((B, H, N, N), jnp.float32)
    _, y = lax.scan(step, s0, xs)
    return jnp.moveaxis(y, 0, 1)


def rwkv7_time_mix(x, mu, w_rkv, w0, w1, w2, a0, a1, a2, g1, g2, k_k, k_a, r_k, lnx_g, lnx_b, w_out):
    B, S, D = x.shape
    H, N = RWKV_HEADS, RWKV_HEAD_DIM
    f32 = jnp.float32
    x_prev = jnp.pad(x, ((0, 0), (1, 0), (0, 0)))[:, :-1]
    xx = x_prev - x
    mix = x[None] + xx[None] * mu[:, None, None, :]
    rkv = jnp.einsum('nbsd,nde->nbse', mix[:3], w_rkv)
    r, k, v = rkv[0], rkv[1], rkv[2]
    xw, xa, xg = mix[3], mix[4], mix[5]
    w = -jax.nn.softplus(-(w0 + jnp.tanh(xw @ w1) @ w2).astype(f32)) - 0.5
    decay = jnp.exp(-jnp.exp(w))
    a = jax.nn.sigmoid((a0 + (xa @ a1) @ a2).astype(f32))
    g = jax.nn.sigmoid(xg @ g1) @ g2
    kk = (k * k_k).astype(f32).reshape(B, S, H, N)
    kk = kk / jnp.maximum(jnp.sqrt(jnp.sum(jnp.square(kk), axis=-1, keepdims=True)), 1e-12)
    k = k.astype(f32) * (1.0 + (a - 1.0) * k_a.astype(f32))
    rh = r.astype(f32).reshape(B, S, H, N)
    kh = k.reshape(B, S, H, N)
    vh = v.astype(f32).reshape(B, S, H, N)
    ah = a.reshape(B, S, H, N)
    y = wkv7(rh, decay.reshape(B, S, H, N), kh, vh, -kk, kk * ah)
    y = group_norm(y, RWKV_GN_EPS).reshape(B, S, D) * lnx_g.astype(f32) + lnx_b.astype(f32)
    bonus = jnp.sum(rh * kh * r_k.astype(f32).reshape(H, N), axis=-1, keepdims=True) * vh
    y = (y + bonus.reshape(B, S, D)).astype(x.dtype)
    return (y * g) @ w_out


def conv_glu_ffn(x, w_up, conv_w, conv_b, w_down):
    S = x.shape[1]
    u, v = jnp.split(x @ w_up, 2, axis=-1)
    up = jnp.pad(u, ((0, 0), (CONV_WIDTH - 1, 0), (0, 0)))
    c = conv_b + sum(up[:, j:j + S] * conv_w[j] for j in range(CONV_WIDTH))
    return (jax.nn.gelu(c, approximate=False) * v) @ w_down


def setup_inputs(seed: int = 0) -> dict:
    key = jax.random.key(seed)
    ks = iter(jax.random.split(key, 40))
    f32 = jnp.float32
    D, F = D_MODEL, FFN_HIDDEN
    NE, NO = (DEPTH + 1) // 2, DEPTH // 2

    def nrm(shape, scale):
        return jax.random.normal(next(ks), shape, f32) * scale

    return {
        'x': nrm((BATCH, SEQ, D), 1.0),
        'ev_w_in': nrm((NE, D, EVEN_IN_WIDTH), D ** -0.5),
        'ev_lambda': nrm((NE, 4, DIFF_HEAD_DIM), 0.1),
        'ev_subln_g': 1.0 + nrm((NE, DIFF_V_DIM), 0.02),
        'ev_w_out': nrm((NE, D, D), DN_BETA * D ** -0.5),
        'od_mu': jax.random.uniform(next(ks), (NO, 6, D), f32),
        'od_w_rkv': nrm((NO, 3, D, D), D ** -0.5),
        'od_w0': jax.random.uniform(next(ks), (NO, D), f32, -4.0, 2.0),
        'od_w1': nrm((NO, D, DECAY_LORA), D ** -0.5),
        'od_w2': nrm((NO, DECAY_LORA, D), 0.1 * DECAY_LORA ** -0.5),
        'od_a0': nrm((NO, D), 0.1),
        'od_a1': nrm((NO, D, ICLR_LORA), D ** -0.5),
        'od_a2': nrm((NO, ICLR_LORA, D), 0.5 * ICLR_LORA ** -0.5),
        'od_g1': nrm((NO, D, GATE_LORA), D ** -0.5),
        'od_g2': nrm((NO, GATE_LORA, D), GATE_LORA ** -0.5),
        'od_k_k': 0.85 + nrm((NO, D), 0.02),
        'od_k_a': 1.0 + nrm((NO, D), 0.02),
        'od_r_k': nrm((NO, D), 0.1),
        'od_lnx_g': 1.0 + nrm((NO, D), 0.02),
        'od_lnx_b': nrm((NO, D), 0.02),
        'od_w_out': nrm((NO, D, D), DN_BETA * D ** -0.5),
        'ln_mix_g': 1.0 + nrm((DEPTH, D), 0.02),
        'ln_mix_b': nrm((DEPTH, D), 0.02),
        'ffn_w_up': nrm((DEPTH, D, 2 * F), D ** -0.5),
        'ffn_conv_w': nrm((DEPTH, CONV_WIDTH, F), CONV_WIDTH ** -0.5),
        'ffn_conv_b': nrm((DEPTH, F), 0.02),
        'ffn_w_down': nrm((DEPTH, F, D), DN_BETA * F ** -0.5),
        'ln_ffn_g': 1.0 + nrm((DEPTH, D), 0.02),
        'ln_ffn_b': nrm((DEPTH, D), 0.02),
    }


def reference(x, ev_w_in, ev_lambda, ev_subln_g, ev_w_out, od_mu, od_w_rkv, od_w0, od_w1, od_w2, od_a0, od_a1, od_a2, od_g1, od_g2, od_k_k, od_k_a, od_r_k, od_lnx_g, od_lnx_b, od_w_out, ln_mix_g, ln_mix_b, ffn_w_up, ffn_conv_w, ffn_conv_b, ffn_w_down, ln_ffn_g, ln_ffn_b):
    B, S, _ = x.shape
    offsets = np.cumsum(EVEN_SPLIT_WIDTHS)[:-1].tolist()
    for i in range(DEPTH):
        j = i // 2
        if i % 2 == 0:
            h = x @ ev_w_in[j]
            dq, dk, dv, rq, rk, rv, rg = jnp.split(h, offsets, axis=-1)
            lambda_init = 0.8 - 0.6 * math.exp(-0.3 * i)
            lp = ev_lambda[j].astype(jnp.float32)
            lam = jnp.exp(jnp.sum(lp[0] * lp[1])) - jnp.exp(jnp.sum(lp[2] * lp[3])) + lambda_init
            a_out = diff_attention(dq.reshape(B, S, DIFF_HEADS, 2, DIFF_HEAD_DIM),
                                   dk.reshape(B, S, DIFF_HEADS, 2, DIFF_HEAD_DIM),
                                   dv.reshape(B, S, DIFF_HEADS, DIFF_V_DIM),
                                   lam, ev_subln_g[j], lambda_init)
            b_out = retention(rq.reshape(B, S, RET_HEADS, RET_QK_DIM),
                              rk.reshape(B, S, RET_HEADS, RET_QK_DIM),
                              rv.reshape(B, S, RET_HEADS, RET_V_DIM), rg)
            mix = jnp.concatenate([a_out, b_out], axis=-1) @ ev_w_out[j]
        else:
            mix = rwkv7_time_mix(x, od_mu[j], od_w_rkv[j], od_w0[j], od_w1[j], od_w2[j],
                                 od_a0[j], od_a1[j], od_a2[j], od_g1[j], od_g2[j],
                                 od_k_k[j], od_k_a[j], od_r_k[j], od_lnx_g[j], od_lnx_b[j], od_w_out[j])
        x = layer_norm(DN_ALPHA * x + mix, ln_mix_g[i], ln_mix_b[i])
        ffn = conv_glu_ffn(x, ffn_w_up[i], ffn_conv_w[i], ffn_conv_b[i], ffn_w_down[i])
        x = layer_norm(DN_ALPHA * x + ffn, ln_ffn_g[i], ln_ffn_b[i])
    return x
```

```python
import math
from contextlib import ExitStack

import numpy as np
import ml_dtypes

import concourse.bass as bass
import concourse.mybir as mybir
from concourse.bass_utils import run_bass_kernel_spmd

F32 = mybir.dt.float32
BF16 = mybir.dt.bfloat16
AF = mybir.ActivationFunctionType
ALU = mybir.AluOpType
AX = mybir.AxisListType

S = 4096
D = 1024
NB = S // 128
FF = 2816
NFC = FF // 128
DN_ALPHA = (2.0 * 2) ** 0.25
LN_EPS = 1e-5
LAMBDA_INIT0 = 0.8 - 0.6 * math.exp(-0.3 * 0)
SLOPES = [2.0 ** (-8.0 * (i + 1) / 4) for i in range(4)]
GAMMAS = [1.0 - 2.0 ** (-5.0 - h) for h in range(4)]


MIX_STOP = None
LOOPV = 9
RB_STOP = 0


class StopBuild(Exception):
    pass


def dump_and_stop(kb, dbg, tile_ap, reg):
    kb.barrier()
    kb.dma("sp", out=dbg, in_=tile_ap, reads=[reg], pool="st")
    kb.barrier()
    return True


class Reg:
    __slots__ = ("w", "r", "nowaw", "psum")

    def __init__(self, nowaw=False):
        self.w = {}
        self.r = {}
        self.nowaw = nowaw
        self.psum = False


class T:
    def __init__(self, t):
        self.t = t
        self.g = Reg()


class KB:
    def __init__(self, nc, es):
        self.nc = nc
        self.es = es
        self.E = {"pe": nc.tensor, "dve": nc.vector, "act": nc.scalar, "pool": nc.gpsimd, "sp": nc.sync}
        self.sems = {}
        self.cnt = {}
        for e in self.E:
            self.sems[e] = es.enter_context(nc.semaphore("s_" + e))
            self.cnt[e] = 0
        self.seen = {e: {} for e in self.E}
        self.dpool = {}
        self.dnext = {}

    def dma_pool(self, name, n):
        keys = []
        for i in range(n):
            k = "%s%d" % (name, i)
            self.sems[k] = self.es.enter_context(self.nc.semaphore("d_" + k))
            self.cnt[k] = 0
            keys.append(k)
        self.dpool[name] = keys
        self.dnext[name] = 0

    def _deps(self, e, reads, writes):
        need = {}

        def add(d, same_ok):
            for k, v in d.items():
                if k == e and same_ok:
                    continue
                if need.get(k, 0) < v:
                    need[k] = v

        for r in reads:
            add(r.w, e == "pe")
            if r.psum:
                add(r.r, True)
        for w in writes:
            if not w.nowaw:
                add(w.w, e == "pe")
            add(w.r, e == "pe")
        return need

    def _wait(self, e, need):
        sn = self.seen[e]
        for k, v in need.items():
            if sn.get(k, 0) < v:
                self.E[e].wait_ge(self.sems[k], v)
                sn[k] = v

    def op(self, e, fn, reads=(), writes=(), rg=(0, 128)):
        need = self._deps(e, reads, writes)
        if e == "pe":
            last = getattr(self, "_last_rg", (0, 128))
            if (rg[0] + rg[1] <= last[0] or last[0] + last[1] <= rg[0]) and self.cnt["pe"] > 0:
                need["pe"] = self.cnt["pe"]
            self._last_rg = rg
        self._wait(e, need)
        ins = fn(self.E[e])
        self.cnt[e] += 1
        ins.then_inc(self.sems[e], 1)
        tok = self.cnt[e]
        for r in reads:
            r.r[e] = tok
        for w in writes:
            w.w[e] = tok
            if not w.nowaw:
                w.r = {}
        return ins

    def dma(self, q, out, in_, reads=(), writes=(), pool="ld", **kw):
        pool = q + "_" + pool
        keys = self.dpool[pool]
        k = keys[self.dnext[pool] % len(keys)]
        self.dnext[pool] += 1
        need = self._deps(q, reads, writes)
        if self.cnt[k] > 0:
            need[k] = max(need.get(k, 0), self.cnt[k])
        self._wait(q, need)
        ins = self.E[q].dma_start(out=out, in_=in_, **kw)
        self.cnt[k] += 16
        ins.then_inc(self.sems[k], 16)
        tok = self.cnt[k]
        for r in reads:
            r.r[k] = tok
        for w in writes:
            w.w[k] = tok
            if not w.nowaw:
                w.r = {}
        return ins

    def barrier(self):
        for e in self.E:
            need = {k: v for k, v in self.cnt.items() if k != e and v > 0}
            self._wait(e, need)


_UNIQ = [0]


def _uniq(name):
    _UNIQ[0] += 1
    return "%s_%d" % (name, _UNIQ[0])


class _RecEng:
    def __init__(self):
        self.call = None

    def __getattr__(self, name):
        def f(**kw):
            self.call = (name, kw)
            return self
        return f


class Rec:
    def __init__(self):
        self.items = []

    def op(self, e, fn, reads=(), writes=(), rg=(0, 128)):
        pe = _RecEng()
        fn(pe)
        name, kw = pe.call
        self.items.append(("op", e, name, kw, list(reads), list(writes), rg))

    def dma(self, q, out, in_, reads=(), writes=(), pool="ld", **kw):
        self.items.append(("dma", q, out, in_, list(reads), list(writes), pool, kw))


def replay(kb, item):
    if item[0] == "op":
        _, e, name, kw, reads, writes, rg = item
        kb.op(e, lambda eng: getattr(eng, name)(**kw), reads=reads, writes=writes, rg=rg)
    else:
        _, q, out, in_, reads, writes, pool, kw = item
        kb.dma(q, out=out, in_=in_, reads=reads, writes=writes, pool=pool, **kw)


def sbt(nc, es, name, shape, dt):
    return T(es.enter_context(nc.sbuf_tensor(_uniq(name), list(shape), dt)))


def pst(nc, es, name, shape, dt):
    t = T(es.enter_context(nc.psum_tensor(_uniq(name), list(shape), dt)))
    t.g.psum = True
    return t


def ln_block(kb, z, gam, bet, outp, stats, mv, rstd, eps):
    kb.op("dve", lambda e: e.bn_stats(out=stats.t[:, 0:6], in_=z.t[:, 0:512]), reads=[z.g], writes=[stats.g])
    kb.op("dve", lambda e: e.bn_stats(out=stats.t[:, 6:12], in_=z.t[:, 512:1024]), reads=[z.g], writes=[stats.g])
    kb.op("dve", lambda e: e.bn_aggr(out=mv.t[:, 0:2], in_=stats.t[:, 0:12]), reads=[stats.g], writes=[mv.g])
    kb.op("dve", lambda e: e.tensor_scalar(out=rstd.t[:, 0:1], in0=mv.t[:, 1:2], scalar1=eps, scalar2=None, op0=ALU.add),
          reads=[mv.g], writes=[rstd.g])
    kb.op("act", lambda e: e.activation(out=rstd.t[:, 0:1], in_=rstd.t[:, 0:1], func=AF.Sqrt), reads=[rstd.g], writes=[rstd.g])
    kb.op("dve", lambda e: e.reciprocal(out=rstd.t[:, 0:1], in_=rstd.t[:, 0:1]), reads=[rstd.g], writes=[rstd.g])
    kb.op("dve", lambda e: e.tensor_scalar(out=z.t[:, :], in0=z.t[:, :], scalar1=mv.t[:, 0:1], scalar2=rstd.t[:, 0:1],
                                           op0=ALU.subtract, op1=ALU.mult), reads=[z.g, mv.g, rstd.g], writes=[z.g])
    kb.op("dve", lambda e: e.tensor_tensor(out=z.t[:, :], in0=z.t[:, :], in1=gam.t[:, :], op=ALU.mult),
          reads=[z.g, gam.g], writes=[z.g])
    kb.op("dve", lambda e: e.tensor_tensor(out=outp.t[:, :], in0=z.t[:, :], in1=bet.t[:, :], op=ALU.add),
          reads=[z.g, bet.g], writes=[outp.g])


def transpose_to_fm(kb, src32, srcbf, dstT, b, ident, ptr):
    kb.op("act", lambda e: e.activation(out=srcbf.t[:, :], in_=src32.t[:, :], func=AF.Copy), reads=[src32.g], writes=[srcbf.g])
    for kc in range(8):
        kb.op("pe", lambda e: e.transpose(out=ptr.t[:, kc * 128:(kc + 1) * 128], in_=srcbf.t[:, kc * 128:(kc + 1) * 128],
                                          identity=ident.t[:, :]), reads=[srcbf.g, ident.g], writes=[ptr.g])
    kb.op("dve", lambda e: e.tensor_copy(out=dstT.t[:, :, b * 128:(b + 1) * 128],
                                         in_=ptr.t[:, :].rearrange("p (k t) -> p k t", k=8)), reads=[ptr.g], writes=[dstT.g])


def load_bcast(kb, dst, vec_ap):
    kb.dma("sp", out=dst.t[:, :], in_=vec_ap.partition_broadcast(128), writes=[dst.g])


def phase_prologue(kb, nc, io, xT, ident):
    with ExitStack() as es:
        xin = [sbt(nc, es, "pr_x%d" % i, [128, 1024], F32) for i in range(2)]
        xbf = [sbt(nc, es, "pr_xb%d" % i, [128, 1024], BF16) for i in range(2)]
        ptr = [pst(nc, es, "pr_pt%d" % i, [128, 1024], BF16) for i in range(2)]
        for b in range(NB):
            xi = xin[b % 2]
            kb.dma("sp", out=xi.t[:, :], in_=io["x"][b * 128:(b + 1) * 128, :], writes=[xi.g])
            transpose_to_fm(kb, xi, xbf[b % 2], xT, b, ident, ptr[b % 2])
        kb.barrier()


def phase_l0_mixer(kb, nc, io, xT, OT, ident, ones, tri, dbg=None):
    W_in = io["ev_w_in"].rearrange("(kc p) n -> p kc n", p=128)
    with ExitStack() as es:
        wq = sbt(nc, es, "m_wq", [128, 8, 384], BF16)
        qT = [sbt(nc, es, "m_qT%d" % m, [128, S], BF16) for m in range(2)]
        kT = [sbt(nc, es, "m_kT%d" % m, [128, S], BF16) for m in range(2)]
        vtok = sbt(nc, es, "m_vtok", [128, NB, 128], BF16)
        abias = sbt(nc, es, "m_abias", [128, 32], F32)
        lamt = sbt(nc, es, "m_lam", [128, 256], F32)
        lprod = sbt(nc, es, "m_lprod", [128, 128], F32)
        lsum = sbt(nc, es, "m_lsum", [128, 2], F32)
        lexp = sbt(nc, es, "m_lexp", [128, 2], F32)
        neglam = sbt(nc, es, "m_neglam", [128, 1], F32)
        gsc = sbt(nc, es, "m_gsc", [128, 1], F32)
        pT = [sbt(nc, es, "m_pT%d" % i, [128, 512], BF16) for i in range(4)]
        r1 = sbt(nc, es, "m_r1", [128, 512], F32)
        r2 = sbt(nc, es, "m_r2", [128, 512], F32)
        t1 = sbt(nc, es, "m_t1", [128, 512], F32)
        t2 = sbt(nc, es, "m_t2", [128, 512], F32)
        sqb = sbt(nc, es, "m_sqb", [128, 512], BF16)
        ybf = sbt(nc, es, "m_ybf", [128, 512], BF16)
        ktd = sbt(nc, es, "m_ktd", [128, NB, 64], BF16)
        DT = sbt(nc, es, "m_DT", [128, 512], F32)
        qdec = sbt(nc, es, "m_qdec", [64, 512], F32)
        kdec = sbt(nc, es, "m_kdec", [128, 1], F32)
        Rst = sbt(nc, es, "m_Rst", [64, 2, 128], F32)
        Rbf = sbt(nc, es, "m_Rbf", [64, NB, 128], BF16)
        bank = [pst(nc, es, "m_b%d" % i, [128, 512], F32) for i in range(8)]

        kb.dma("sp", out=lamt.t[:, :], in_=io["ev_lambda"].rearrange("a b c -> (a b c)").partition_broadcast(128), writes=[lamt.g])
        kb.op("dve", lambda e: e.tensor_tensor(out=lprod.t[:, 0:64], in0=lamt.t[:, 0:64], in1=lamt.t[:, 64:128], op=ALU.mult),
              reads=[lamt.g], writes=[lprod.g])
        kb.op("dve", lambda e: e.tensor_tensor(out=lprod.t[:, 64:128], in0=lamt.t[:, 128:192], in1=lamt.t[:, 192:256], op=ALU.mult),
              reads=[lamt.g], writes=[lprod.g])
        kb.op("dve", lambda e: e.reduce_sum(out=lsum.t[:, 0:1], in_=lprod.t[:, 0:64], axis=AX.X), reads=[lprod.g], writes=[lsum.g])
        kb.op("dve", lambda e: e.reduce_sum(out=lsum.t[:, 1:2], in_=lprod.t[:, 64:128], axis=AX.X), reads=[lprod.g], writes=[lsum.g])
        kb.op("act", lambda e: e.activation(out=lexp.t[:, 0:2], in_=lsum.t[:, 0:2], func=AF.Exp), reads=[lsum.g], writes=[lexp.g])
        kb.op("dve", lambda e: e.tensor_tensor(out=neglam.t[:, 0:1], in0=lexp.t[:, 1:2], in1=lexp.t[:, 0:1], op=ALU.subtract),
              reads=[lexp.g], writes=[neglam.g])
        kb.op("dve", lambda e: e.tensor_scalar(out=neglam.t[:, 0:1], in0=neglam.t[:, 0:1], scalar1=-LAMBDA_INIT0, scalar2=None, op0=ALU.add),
              reads=[neglam.g], writes=[neglam.g])
        kb.dma("sp", out=gsc.t[:, :], in_=io["ev_subln_g"].rearrange("o v -> v o"), writes=[gsc.g], allow_slow_non_contiguous=True)
        kb.op("dve", lambda e: e.tensor_scalar(out=gsc.t[:, 0:1], in0=gsc.t[:, 0:1], scalar1=1.0 - LAMBDA_INIT0, scalar2=None, op0=ALU.mult),
              reads=[gsc.g], writes=[gsc.g])
        for m in range(2):
            kb.dma("sp", out=kT[m].t[64:66, :], in_=io["c_ones2"][:, :], writes=[kT[m].g])

        def proj_fm(dst, prow, co, ncol, evac_eng_i):
            for tt in range(8):
                bk = bank[tt % 2]
                for kc in range(8):
                    kb.op("pe", lambda e: e.matmul(out=bk.t[0:ncol, :], lhsT=wq.t[:, kc, co:co + ncol],
                                                   rhs=xT.t[:, kc, tt * 512:(tt + 1) * 512], start=(kc == 0), stop=(kc == 7)),
                          reads=[wq.g, xT.g], writes=[bk.g])
                if (tt + evac_eng_i) % 2 == 0:
                    kb.op("act", lambda e: e.activation(out=dst.t[prow:prow + ncol, tt * 512:(tt + 1) * 512], in_=bk.t[0:ncol, :], func=AF.Copy),
                          reads=[bk.g], writes=[dst.g])
                else:
                    kb.op("dve", lambda e: e.tensor_copy(out=dst.t[prow:prow + ncol, tt * 512:(tt + 1) * 512], in_=bk.t[0:ncol, :]),
                          reads=[bk.g], writes=[dst.g])

        for h in range(4):
            for i, co in enumerate((h * 128, 512 + h * 128, 1024 + h * 128)):
                kb.dma("pool", out=wq.t[:, :, i * 128:(i + 1) * 128], in_=W_in[:, :, co:co + 128], reads=[], writes=[wq.g])
            kb.dma("sp", out=abias.t[:, :], in_=io["c_abias"][h, :, :], writes=[abias.g])
            for m in range(2):
                kb.dma("sp", out=qT[m].t[64:66, :], in_=io["c_alibiq"][h, :, :], writes=[qT[m].g])
            for m in range(2):
                proj_fm(qT[m], 0, m * 64, 64, 0)
                proj_fm(kT[m], 0, 128 + m * 64, 64, 1)
            for g4 in range(8):
                bk = bank[2 + g4 % 2]
                for bb in range(4):
                    b = g4 * 4 + bb
                    for kc in range(8):
                        kb.op("pe", lambda e: e.matmul(out=bk.t[:, bb * 128:(bb + 1) * 128], lhsT=xT.t[:, kc, b * 128:(b + 1) * 128],
                                                       rhs=wq.t[:, kc, 256:384], start=(kc == 0), stop=(kc == 7)),
                              reads=[wq.g, xT.g], writes=[bk.g])
                kb.op("dve", lambda e: e.tensor_copy(out=vtok.t[:, g4 * 4:(g4 + 1) * 4, :],
                                                     in_=bk.t[:, :].rearrange("p (b v) -> p b v", b=4)), reads=[bk.g], writes=[vtok.g])
            if MIX_STOP == "h0proj":
                kb.barrier()
                kb.dma("sp", out=dbg[0:64, 0, :], in_=qT[0].t[0:64, :], reads=[qT[0].g], pool="st")
                kb.dma("sp", out=dbg[0:64, 1, :], in_=qT[1].t[0:64, :], reads=[qT[1].g], pool="st")
                kb.dma("sp", out=dbg[0:64, 2, :], in_=kT[0].t[0:64, :], reads=[kT[0].g], pool="st")
                kb.dma("sp", out=dbg[0:64, 3, :], in_=kT[1].t[0:64, :], reads=[kT[1].g], pool="st")
                kb.dma("sp", out=dbg[:, 4, :].rearrange("p (b v) -> p b v", b=NB), in_=vtok.t[:, :, :], reads=[vtok.g], pool="st")
                kb.barrier()
                return True
            O = [bank[4], bank[6]]
            Sm = [bank[5], bank[7]]
            for c in range(8):
                steps = [(kbi, m) for kbi in range(4 * c + 4) for m in range(2)]

                def geom(i):
                    kbi, m = steps[i]
                    j = kbi - 4 * c
                    lo = 128 * j if j > 0 else 0
                    return kbi, m, j, lo

                def emit_qk(i):
                    kbi, m, j, lo = geom(i)
                    sb = bank[i % 4]
                    KK = 64 if LOOPV == -1 else 66
                    kb.op("pe", lambda e: e.matmul(out=sb.t[:, lo:512], lhsT=kT[m].t[0:KK, kbi * 128:(kbi + 1) * 128],
                                                   rhs=qT[m].t[0:KK, c * 512 + lo:(c + 1) * 512], start=True, stop=True),
                          reads=[kT[m].g, qT[m].g], writes=[sb.g])

                def emit_pv(i):
                    kbi, m, j, lo = geom(i)
                    sb = bank[i % 4]
                    pt = pT[i % 4]
                    oi = (kbi - 4 * c) + 28
                    if LOOPV == -4:
                        return
                    kb.op("act", lambda e: e.activation(out=pt.t[:, lo:512], in_=sb.t[:, lo:512], func=(AF.Copy if LOOPV == -3 else AF.Exp),
                                                        bias=(0.0 if LOOPV in (-2, -3) else abias.t[:, oi:oi + 1]), scale=0.125),
                          reads=[sb.g, abias.g], writes=[pt.g])
                    if LOOPV < 1:
                        return
                    if j >= 0:
                        kb.op("dve", lambda e: e.tensor_tensor(out=pt.t[:, lo:lo + 128], in0=pt.t[:, lo:lo + 128], in1=tri.t[:, :], op=ALU.mult),
                              reads=[pt.g, tri.g], writes=[pt.g])
                    if LOOPV < 2:
                        return
                    first = kbi == 0
                    last = kbi == 4 * c + 3
                    kb.op("pe", lambda e: e.matmul(out=O[m].t[:, lo:512], lhsT=vtok.t[:, kbi, :], rhs=pt.t[:, lo:512], start=first, stop=last),
                          reads=[vtok.g, pt.g], writes=[O[m].g])
                    kb.op("pe", lambda e: e.matmul(out=Sm[m].t[:, lo:512], lhsT=ones.t[:, :], rhs=pt.t[:, lo:512], start=first, stop=last),
                          reads=[ones.g, pt.g], writes=[Sm[m].g])

                n = len(steps)
                emit_qk(0)
                emit_qk(1)
                for i in range(n):
                    if i + 2 < n:
                        emit_qk(i + 2)
                    emit_pv(i)
                if MIX_STOP == "c0loop":
                    kb.barrier()
                    kb.dma("sp", out=dbg[:, 4, :].rearrange("p (b v) -> p b v", b=NB), in_=vtok.t[:, :, :], reads=[vtok.g], pool="st")
                    kb.barrier()
                    return True
                kb.op("dve", lambda e: e.reciprocal(out=r1.t[:, :], in_=Sm[0].t[:, :]), reads=[Sm[0].g], writes=[r1.g])
                kb.op("dve", lambda e: e.reciprocal(out=r2.t[:, :], in_=Sm[1].t[:, :]), reads=[Sm[1].g], writes=[r2.g])
                kb.op("dve", lambda e: e.tensor_tensor(out=t1.t[:, :], in0=O[0].t[:, :], in1=r1.t[:, :], op=ALU.mult), reads=[O[0].g, r1.g], writes=[t1.g])
                kb.op("dve", lambda e: e.tensor_tensor(out=t2.t[:, :], in0=O[1].t[:, :], in1=r2.t[:, :], op=ALU.mult), reads=[O[1].g, r2.g], writes=[t2.g])
                kb.op("dve", lambda e: e.scalar_tensor_tensor(out=t1.t[:, :], in0=t2.t[:, :], scalar=neglam.t[:, 0:1], in1=t1.t[:, :],
                                                              op0=ALU.mult, op1=ALU.add), reads=[t1.g, t2.g, neglam.g], writes=[t1.g])
                kb.op("act", lambda e: e.activation(out=sqb.t[:, :], in_=t1.t[:, :], func=AF.Square), reads=[t1.g], writes=[sqb.g])
                ssb = bank[0]
                kb.op("pe", lambda e: e.matmul(out=ssb.t[:, :], lhsT=ones.t[:, :], rhs=sqb.t[:, :], start=True, stop=True),
                      reads=[ones.g, sqb.g], writes=[ssb.g])
                kb.op("dve", lambda e: e.tensor_scalar(out=r1.t[:, :], in0=ssb.t[:, :], scalar1=1.0 / 128, scalar2=LN_EPS, op0=ALU.mult, op1=ALU.add),
                      reads=[ssb.g], writes=[r1.g])
                kb.op("act", lambda e: e.activation(out=r1.t[:, :], in_=r1.t[:, :], func=AF.Sqrt), reads=[r1.g], writes=[r1.g])
                kb.op("dve", lambda e: e.reciprocal(out=r1.t[:, :], in_=r1.t[:, :]), reads=[r1.g], writes=[r1.g])
                kb.op("dve", lambda e: e.scalar_tensor_tensor(out=OT.t[:, h, c * 512:(c + 1) * 512], in0=t1.t[:, :], scalar=gsc.t[:, 0:1], in1=r1.t[:, :],
                                                              op0=ALU.mult, op1=ALU.mult), reads=[t1.g, gsc.g, r1.g], writes=[OT.g])

        if MIX_STOP == "diff":
            return dump_and_stop(kb, dbg[:, :, :], OT.t[:, :, :], OT.g)
        for h in range(4):
            gam = GAMMAS[h]
            kb.dma("pool", out=wq.t[:, :, 0:64], in_=W_in[:, :, 1536 + h * 64:1536 + (h + 1) * 64], writes=[wq.g])
            kb.dma("pool", out=wq.t[:, :, 64:128], in_=W_in[:, :, 1792 + h * 64:1792 + (h + 1) * 64], writes=[wq.g])
            kb.dma("pool", out=wq.t[:, :, 128:256], in_=W_in[:, :, 2048 + h * 128:2048 + (h + 1) * 128], writes=[wq.g])
            kb.dma("pool", out=wq.t[:, :, 256:384], in_=W_in[:, :, 2560 + h * 128:2560 + (h + 1) * 128], writes=[wq.g])
            kb.dma("sp", out=DT.t[:, :], in_=io["c_retDT"][h, :, :], writes=[DT.g])
            kb.dma("sp", out=qdec.t[:, :], in_=io["c_retqdec"][h, :, :], writes=[qdec.g])
            kb.dma("sp", out=kdec.t[:, :], in_=io["c_retkdec"][h, :, :], writes=[kdec.g])
            if MIX_STOP == "r_load":
                kb.barrier()
                kb.dma("sp", out=dbg[:, 4, :].rearrange("p (b v) -> p b v", b=NB), in_=vtok.t[:, :, :], reads=[vtok.g], pool="st")
                kb.barrier()
                return True
            for tt in range(8):
                bk = bank[tt % 2]
                for kc in range(8):
                    kb.op("pe", lambda e: e.matmul(out=bk.t[0:64, :], lhsT=wq.t[:, kc, 0:64], rhs=xT.t[:, kc, tt * 512:(tt + 1) * 512],
                                                   start=(kc == 0), stop=(kc == 7)), reads=[wq.g, xT.g], writes=[bk.g])
                kb.op("act", lambda e: e.activation(out=qT[0].t[0:64, tt * 512:(tt + 1) * 512], in_=bk.t[0:64, :], func=AF.Copy),
                      reads=[bk.g], writes=[qT[0].g])
                if LOOPV >= 1:
                    kb.op("dve", lambda e: e.tensor_tensor(out=qT[1].t[0:64, tt * 512:(tt + 1) * 512], in0=bk.t[0:64, :], in1=qdec.t[:, :], op=ALU.mult),
                          reads=[bk.g, qdec.g], writes=[qT[1].g])
            if LOOPV >= 2:
                proj_fm(kT[0], 0, 64, 64, 0)
            for b in range(NB if LOOPV >= 3 else 0):
                bk = bank[2 + b % 2]
                for kc in range(8):
                    kb.op("pe", lambda e: e.matmul(out=bk.t[:, 0:192], lhsT=xT.t[:, kc, b * 128:(b + 1) * 128], rhs=wq.t[:, kc, 64:256],
                                                   start=(kc == 0), stop=(kc == 7)), reads=[wq.g, xT.g], writes=[bk.g])
                kb.op("dve", lambda e: e.tensor_scalar(out=ktd.t[:, b, :], in0=bk.t[:, 0:64], scalar1=kdec.t[:, 0:1], scalar2=None, op0=ALU.mult),
                      reads=[bk.g, kdec.g], writes=[ktd.g])
                kb.op("act", lambda e: e.activation(out=vtok.t[:, b, :], in_=bk.t[:, 64:192], func=AF.Copy), reads=[bk.g], writes=[vtok.g])
            if MIX_STOP == "r_proj":
                kb.barrier()
                kb.dma("sp", out=dbg[:, 4, :].rearrange("p (b v) -> p b v", b=NB), in_=vtok.t[:, :, :], reads=[vtok.g], pool="st")
                kb.barrier()
                return True
            kb.op("dve", lambda e: e.memset(Rst.t[:, 0, :], 0.0), writes=[Rst.g])
            kb.op("dve", lambda e: e.memset(Rbf.t[:, 0, :], 0.0), writes=[Rbf.g])
            for g4 in range(8):
                bk = bank[4 + g4 % 2]
                for bb in range(4):
                    b = g4 * 4 + bb
                    kb.op("pe", lambda e: e.matmul(out=bk.t[0:64, bb * 128:(bb + 1) * 128], lhsT=ktd.t[:, b, :], rhs=vtok.t[:, b, :],
                                                   start=True, stop=True), reads=[ktd.g, vtok.g], writes=[bk.g])
                for bb in range(4):
                    b = g4 * 4 + bb
                    if b == NB - 1:
                        continue
                    kb.op("dve", lambda e: e.scalar_tensor_tensor(out=Rst.t[:, (b + 1) % 2, :], in0=Rst.t[:, b % 2, :], scalar=float(gam ** 128),
                                                                  in1=bk.t[0:64, bb * 128:(bb + 1) * 128], op0=ALU.mult, op1=ALU.add),
                          reads=[Rst.g, bk.g], writes=[Rst.g])
                    kb.op("act", lambda e: e.activation(out=Rbf.t[:, b + 1, :], in_=Rst.t[:, (b + 1) % 2, :], func=AF.Copy), reads=[Rst.g], writes=[Rbf.g])
            if MIX_STOP == "r_state":
                kb.barrier()
                kb.dma("sp", out=dbg[:, 4, :].rearrange("p (b v) -> p b v", b=NB), in_=vtok.t[:, :, :], reads=[vtok.g], pool="st")
                kb.barrier()
                return True
            for g4 in range(8):
                sc = bank[g4 % 2]
                yb = bank[2 + g4 % 2]
                gb = bank[6 + g4 % 2]
                pt = pT[g4 % 2]
                for bb in range(4):
                    b = g4 * 4 + bb
                    kb.op("pe", lambda e: e.matmul(out=sc.t[:, bb * 128:(bb + 1) * 128], lhsT=kT[0].t[0:64, b * 128:(b + 1) * 128],
                                                   rhs=qT[0].t[0:64, b * 128:(b + 1) * 128], start=True, stop=True),
                          reads=[kT[0].g, qT[0].g], writes=[sc.g])
                for kc in range(8):
                    kb.op("pe", lambda e: e.matmul(out=gb.t[:, :], lhsT=wq.t[:, kc, 256:384], rhs=xT.t[:, kc, g4 * 512:(g4 + 1) * 512],
                                                   start=(kc == 0), stop=(kc == 7)), reads=[wq.g, xT.g], writes=[gb.g])
                kb.op("dve", lambda e: e.tensor_tensor(out=pt.t[:, :], in0=sc.t[:, :], in1=DT.t[:, :], op=ALU.mult), reads=[sc.g, DT.g], writes=[pt.g])
                for bb in range(4):
                    b = g4 * 4 + bb
                    kb.op("pe", lambda e: e.matmul(out=yb.t[:, bb * 128:(bb + 1) * 128], lhsT=vtok.t[:, b, :], rhs=pt.t[:, bb * 128:(bb + 1) * 128],
                                                   start=True, stop=False), reads=[vtok.g, pt.g], writes=[yb.g])
                    kb.op("pe", lambda e: e.matmul(out=yb.t[:, bb * 128:(bb + 1) * 128], lhsT=Rbf.t[:, b, :], rhs=qT[1].t[0:64, b * 128:(b + 1) * 128],
                                                   start=False, stop=True), reads=[Rbf.g, qT[1].g], writes=[yb.g])
                kb.op("act", lambda e: e.activation(out=ybf.t[:, :], in_=yb.t[:, :], func=AF.Copy), reads=[yb.g], writes=[ybf.g])
                kb.op("act", lambda e: e.activation(out=sqb.t[:, :], in_=yb.t[:, :], func=AF.Square), reads=[yb.g], writes=[sqb.g])
                m1 = bank[4]
                m2 = bank[5]
                kb.op("pe", lambda e: e.matmul(out=m1.t[:, :], lhsT=ones.t[:, :], rhs=ybf.t[:, :], start=True, stop=True), reads=[ones.g, ybf.g], writes=[m1.g])
                kb.op("pe", lambda e: e.matmul(out=m2.t[:, :], lhsT=ones.t[:, :], rhs=sqb.t[:, :], start=True, stop=True), reads=[ones.g, sqb.g], writes=[m2.g])
                kb.op("dve", lambda e: e.tensor_scalar(out=r1.t[:, :], in0=m1.t[:, :], scalar1=1.0 / 128, scalar2=None, op0=ALU.mult), reads=[m1.g], writes=[r1.g])
                kb.op("dve", lambda e: e.tensor_tensor(out=t2.t[:, :], in0=r1.t[:, :], in1=r1.t[:, :], op=ALU.mult), reads=[r1.g], writes=[t2.g])
                kb.op("dve", lambda e: e.scalar_tensor_tensor(out=r2.t[:, :], in0=m2.t[:, :], scalar=1.0 / 128, in1=t2.t[:, :], op0=ALU.mult, op1=ALU.subtract),
                      reads=[m2.g, t2.g], writes=[r2.g])
                kb.op("dve", lambda e: e.tensor_scalar(out=r2.t[:, :], in0=r2.t[:, :], scalar1=LN_EPS, scalar2=None, op0=ALU.add),
                      reads=[r2.g], writes=[r2.g])
                kb.op("act", lambda e: e.activation(out=r2.t[:, :], in_=r2.t[:, :], func=AF.Sqrt), reads=[r2.g], writes=[r2.g])
                kb.op("dve", lambda e: e.reciprocal(out=r2.t[:, :], in_=r2.t[:, :]), reads=[r2.g], writes=[r2.g])
                sg = t2
                kb.op("act", lambda e: e.activation(out=sg.t[:, :], in_=gb.t[:, :], func=AF.Silu), reads=[gb.g], writes=[sg.g])
                kb.op("dve", lambda e: e.tensor_tensor(out=t1.t[:, :], in0=yb.t[:, :], in1=r1.t[:, :], op=ALU.subtract), reads=[yb.g, r1.g], writes=[t1.g])
                kb.op("dve", lambda e: e.tensor_tensor(out=t1.t[:, :], in0=t1.t[:, :], in1=r2.t[:, :], op=ALU.mult), reads=[t1.g, r2.g], writes=[t1.g])
                kb.op("dve", lambda e: e.tensor_tensor(out=OT.t[:, 4 + h, g4 * 512:(g4 + 1) * 512], in0=t1.t[:, :], in1=sg.t[:, :], op=ALU.mult),
                      reads=[t1.g, sg.g], writes=[OT.g])
        kb.barrier()
        if dbg is not None:
            kb.dma("sp", out=dbg[:, :, :], in_=OT.t[:, :, :], reads=[OT.g], pool="st")
            kb.barrier()


def phase_out_ln(kb, nc, io, OT, xT, ident, w_ap, xres_ap, gam_ap, bet_ap, xout_ap, xout_reg):
    with ExitStack() as es:
        wo = sbt(nc, es, "o_w", [128, 8, 1024], BF16)
        gam = sbt(nc, es, "o_gam", [128, 1024], F32)
        bet = sbt(nc, es, "o_bet", [128, 1024], F32)
        xin = [sbt(nc, es, "o_x%d" % i, [128, 1024], F32) for i in range(2)]
        z = [sbt(nc, es, "o_z%d" % i, [128, 1024], F32) for i in range(2)]
        xo = [sbt(nc, es, "o_xo%d" % i, [128, 1024], F32) for i in range(2)]
        xbf = [sbt(nc, es, "o_xb%d" % i, [128, 1024], BF16) for i in range(2)]
        stats = sbt(nc, es, "o_stats", [128, 12], F32)
        mv = sbt(nc, es, "o_mv", [128, 2], F32)
        rstd = sbt(nc, es, "o_rstd", [128, 1], F32)
        mm = [[pst(nc, es, "o_mm%d%d" % (i, j), [128, 512], F32) for j in range(2)] for i in range(2)]
        ptr = [pst(nc, es, "o_pt%d" % i, [128, 1024], BF16) for i in range(2)]
        kb.dma("pool", out=wo.t[:, :, :], in_=w_ap.rearrange("(kc p) n -> p kc n", p=128), writes=[wo.g])
        load_bcast(kb, gam, gam_ap)
        load_bcast(kb, bet, bet_ap)
        for b in range(NB):
            xi = xin[b % 2]
            kb.dma("sp", out=xi.t[:, :], in_=xres_ap[b * 128:(b + 1) * 128, :], writes=[xi.g])
            for hf in range(2):
                for fc in range(8):
                    kb.op("pe", lambda e: e.matmul(out=mm[b % 2][hf].t[:, :], lhsT=OT.t[:, fc, b * 128:(b + 1) * 128],
                                                   rhs=wo.t[:, fc, hf * 512:(hf + 1) * 512], start=(fc == 0), stop=(fc == 7)),
                          reads=[OT.g, wo.g], writes=[mm[b % 2][hf].g])
            zz = z[b % 2]
            for hf in range(2):
                kb.op("dve", lambda e: e.scalar_tensor_tensor(out=zz.t[:, hf * 512:(hf + 1) * 512], in0=xi.t[:, hf * 512:(hf + 1) * 512], scalar=DN_ALPHA,
                                                              in1=mm[b % 2][hf].t[:, :], op0=ALU.mult, op1=ALU.add),
                      reads=[xi.g, mm[b % 2][hf].g], writes=[zz.g])
            ln_block(kb, zz, gam, bet, xo[b % 2], stats, mv, rstd, LN_EPS)
            kb.dma("sp", out=xout_ap[b * 128:(b + 1) * 128, :], in_=xo[b % 2].t[:, :], reads=[xo[b % 2].g], writes=[xout_reg], pool="st")
            transpose_to_fm(kb, xo[b % 2], xbf[b % 2], xT, b, ident, ptr[b % 2])
        kb.barrier()


def phase_ffn(kb, nc, io, layer, xT, ident, xres_ap, xres_reg, xout_ap, xout_reg, G_ap, G_reg, want_T):
    Wup = io["ffn_w_up"][layer].rearrange("(kc p) n -> p kc n", p=128)
    Wdn = io["ffn_w_down"][layer].rearrange("(fc p) n -> p fc n", p=128)
    with ExitStack() as es:
        wu = [sbt(nc, es, "f_wu%d" % i, [128, 8, 256], BF16) for i in range(2)]
        cw = sbt(nc, es, "f_cw", [128, NFC, 3], F32)
        cb = sbt(nc, es, "f_cb", [128, NFC], F32)
        ubuf = [sbt(nc, es, "f_ub%d" % i, [128, 514], F32) for i in range(2)]
        cbuf = [sbt(nc, es, "f_c%d" % i, [128, 512], F32) for i in range(2)]
        gl = [sbt(nc, es, "f_gl%d" % i, [128, 512], F32) for i in range(2)]
        gt = [sbt(nc, es, "f_gt%d" % i, [128, 512], BF16) for i in range(3)]
        pu = [pst(nc, es, "f_pu%d" % i, [128, 512], F32) for i in range(3)]
        pv = [pst(nc, es, "f_pv%d" % i, [128, 512], F32) for i in range(3)]
        for j in range(3):
            kb.dma("sp", out=cw.t[:, :, j], in_=io["ffn_conv_w"][layer][j].rearrange("(fc p) -> p fc", p=128), writes=[cw.g],
                   allow_slow_non_contiguous=True)
        kb.dma("sp", out=cb.t[:, :], in_=io["ffn_conv_b"][layer].rearrange("(fc p) -> p fc", p=128), writes=[cb.g],
               allow_slow_non_contiguous=True)
        it = 0
        for fc in range(NFC):
            w = wu[fc % 2]
            kb.dma("pool", out=w.t[:, :, 0:128], in_=Wup[:, :, fc * 128:(fc + 1) * 128], writes=[w.g])
            kb.dma("pool", out=w.t[:, :, 128:256], in_=Wup[:, :, FF + fc * 128:FF + (fc + 1) * 128], writes=[w.g])
            for tt in range(8):
                u_ps = pu[it % 3]
                v_ps = pv[it % 3]
                ub = ubuf[it % 2]
                ubn = ubuf[(it + 1) % 2]
                c = cbuf[it % 2]
                g_ = gl[it % 2]
                go = gt[it % 3]
                for kc in range(8):
                    kb.op("pe", lambda e: e.matmul(out=u_ps.t[:, :], lhsT=w.t[:, kc, 0:128], rhs=xT.t[:, kc, tt * 512:(tt + 1) * 512],
                                                   start=(kc == 0), stop=(kc == 7)), reads=[w.g, xT.g], writes=[u_ps.g])
                for kc in range(8):
                    kb.op("pe", lambda e: e.matmul(out=v_ps.t[:, :], lhsT=w.t[:, kc, 128:256], rhs=xT.t[:, kc, tt * 512:(tt + 1) * 512],
                                                   start=(kc == 0), stop=(kc == 7)), reads=[w.g, xT.g], writes=[v_ps.g])
                if tt == 0:
                    kb.op("dve", lambda e: e.memset(ub.t[:, 0:2], 0.0), writes=[ub.g])
                kb.op("act", lambda e: e.activation(out=ub.t[:, 2:514], in_=u_ps.t[:, :], func=AF.Copy), reads=[u_ps.g], writes=[ub.g])
                if tt < 7:
                    kb.op("dve", lambda e: e.tensor_copy(out=ubn.t[:, 0:2], in_=ub.t[:, 512:514]), reads=[ub.g], writes=[ubn.g])
                kb.op("dve", lambda e: e.tensor_scalar(out=c.t[:, :], in0=ub.t[:, 2:514], scalar1=cw.t[:, fc, 2:3], scalar2=cb.t[:, fc:fc + 1],
                                                       op0=ALU.mult, op1=ALU.add), reads=[ub.g, cw.g, cb.g], writes=[c.g])
                kb.op("dve", lambda e: e.scalar_tensor_tensor(out=c.t[:, :], in0=ub.t[:, 1:513], scalar=cw.t[:, fc, 1:2], in1=c.t[:, :],
                                                              op0=ALU.mult, op1=ALU.add), reads=[ub.g, cw.g, c.g], writes=[c.g])
                kb.op("dve", lambda e: e.scalar_tensor_tensor(out=c.t[:, :], in0=ub.t[:, 0:512], scalar=cw.t[:, fc, 0:1], in1=c.t[:, :],
                                                              op0=ALU.mult, op1=ALU.add), reads=[ub.g, cw.g, c.g], writes=[c.g])
                kb.op("act", lambda e: e.activation(out=g_.t[:, :], in_=c.t[:, :], func=AF.Gelu), reads=[c.g], writes=[g_.g])
                kb.op("dve", lambda e: e.tensor_tensor(out=go.t[:, :], in0=v_ps.t[:, :], in1=g_.t[:, :], op=ALU.mult), reads=[v_ps.g, g_.g], writes=[go.g])
                kb.dma("sp", out=G_ap[fc * 128:(fc + 1) * 128, tt * 512:(tt + 1) * 512], in_=go.t[:, :], reads=[go.g], writes=[G_reg], pool="st")
                it += 1
        kb.barrier()
    Gv = G_ap.rearrange("(fc p) t -> p fc t", p=128)
    with ExitStack() as es:
        wd = sbt(nc, es, "g_wd", [128, NFC, 1024], BF16)
        gin = [sbt(nc, es, "g_gin%d" % i, [128, NFC, 512], BF16) for i in range(2)]
        gam = sbt(nc, es, "g_gam", [128, 1024], F32)
        bet = sbt(nc, es, "g_bet", [128, 1024], F32)
        xin = [sbt(nc, es, "g_x%d" % i, [128, 1024], F32) for i in range(2)]
        z = [sbt(nc, es, "g_z%d" % i, [128, 1024], F32) for i in range(2)]
        xo = [sbt(nc, es, "g_xo%d" % i, [128, 1024], F32) for i in range(2)]
        xbf = [sbt(nc, es, "g_xb%d" % i, [128, 1024], BF16) for i in range(2)]
        stats = sbt(nc, es, "g_stats", [128, 12], F32)
        mv = sbt(nc, es, "g_mv", [128, 2], F32)
        rstd = sbt(nc, es, "g_rstd", [128, 1], F32)
        mm = [[pst(nc, es, "g_mm%d%d" % (i, j), [128, 512], F32) for j in range(2)] for i in range(2)]
        ptr = [pst(nc, es, "g_pt%d" % i, [128, 1024], BF16) for i in range(2)]
        for q4 in range(2):
            kb.dma("pool", out=wd.t[:, q4 * 11:(q4 + 1) * 11, :], in_=Wdn[:, q4 * 11:(q4 + 1) * 11, :], writes=[wd.g])
        load_bcast(kb, gam, io["ln_ffn_g"][layer])
        load_bcast(kb, bet, io["ln_ffn_b"][layer])
        for tt in range(8):
            gi = gin[tt % 2]
            for q4 in range(2):
                kb.dma("sp", out=gi.t[:, q4 * 11:(q4 + 1) * 11, :], in_=Gv[:, q4 * 11:(q4 + 1) * 11, tt * 512:(tt + 1) * 512],
                       reads=[G_reg], writes=[gi.g])
            for bb in range(4):
                b = tt * 4 + bb
                xi = xin[b % 2]
                kb.dma("sp", out=xi.t[:, :], in_=xres_ap[b * 128:(b + 1) * 128, :], reads=[xres_reg], writes=[xi.g])
                for hf in range(2):
                    for fc in range(NFC):
                        kb.op("pe", lambda e: e.matmul(out=mm[b % 2][hf].t[:, :], lhsT=gi.t[:, fc, bb * 128:(bb + 1) * 128],
                                                       rhs=wd.t[:, fc, hf * 512:(hf + 1) * 512], start=(fc == 0), stop=(fc == NFC - 1)),
                              reads=[gi.g, wd.g], writes=[mm[b % 2][hf].g])
                zz = z[b % 2]
                for hf in range(2):
                    kb.op("dve", lambda e: e.scalar_tensor_tensor(out=zz.t[:, hf * 512:(hf + 1) * 512], in0=xi.t[:, hf * 512:(hf + 1) * 512], scalar=DN_ALPHA,
                                                                  in1=mm[b % 2][hf].t[:, :], op0=ALU.mult, op1=ALU.add),
                          reads=[xi.g, mm[b % 2][hf].g], writes=[zz.g])
                ln_block(kb, zz, gam, bet, xo[b % 2], stats, mv, rstd, LN_EPS)
                kb.dma("sp", out=xout_ap[b * 128:(b + 1) * 128, :], in_=xo[b % 2].t[:, :], reads=[xo[b % 2].g], writes=[xout_reg], pool="st")
                if want_T:
                    transpose_to_fm(kb, xo[b % 2], xbf[b % 2], xT, b, ident, ptr[b % 2])
        kb.barrier()


def phase_rwkv_a(kb, nc, io, xT, scr, scr_reg):
    Wrkv = io["od_w_rkv"][0].rearrange("n (kc p) e -> p n kc e", p=128)
    with ExitStack() as es:
        wr = sbt(nc, es, "ra_w", [128, 3, 8, 1024], BF16)
        l1 = sbt(nc, es, "ra_l1", [128, 8, 288], BF16)
        w2 = sbt(nc, es, "ra_w2", [64, 1024], BF16)
        a2 = sbt(nc, es, "ra_a2", [64, 1024], BF16)
        g2a = sbt(nc, es, "ra_g2a", [128, 1024], BF16)
        g2b = sbt(nc, es, "ra_g2b", [32, 1024], BF16)
        w0b = sbt(nc, es, "ra_w0b", [128, 1024], F32)
        a0b = sbt(nc, es, "ra_a0b", [128, 1024], F32)
        mu = sbt(nc, es, "ra_mu", [128, 6, 8], F32)
        xx = [sbt(nc, es, "ra_xx%d" % i, [128, 8, 128], F32) for i in range(2)]
        mixT = [[sbt(nc, es, "ra_mix%d_%d" % (n, i), [128, 8, 128], BF16) for i in range(2)] for n in range(6)]
        lo1 = [sbt(nc, es, "ra_lo%d" % i, [128, 128], BF16) for i in range(4)]
        lo2 = sbt(nc, es, "ra_l32", [32, 128], BF16)
        outF = [sbt(nc, es, "ra_o%d" % i, [128, 1024], F32) for i in range(4)]
        P = [pst(nc, es, "ra_p%d" % i, [128, 1024], F32) for i in range(3)]
        Q = [pst(nc, es, "ra_q%d" % i, [128, 512], F32) for i in range(2)]
        for n in range(3):
            kb.dma("pool", out=wr.t[:, n, :, :], in_=Wrkv[:, n, :, :], writes=[wr.g])
        kb.dma("pool", out=l1.t[:, :, 0:64], in_=io["od_w1"].rearrange("(kc p) e -> p kc e", p=128), writes=[l1.g])
        kb.dma("pool", out=l1.t[:, :, 64:128], in_=io["od_a1"].rearrange("(kc p) e -> p kc e", p=128), writes=[l1.g])
        kb.dma("pool", out=l1.t[:, :, 128:288], in_=io["od_g1"].rearrange("(kc p) e -> p kc e", p=128), writes=[l1.g])
        kb.dma("pool", out=w2.t[:, :], in_=io["od_w2"][:, :], writes=[w2.g])
        kb.dma("pool", out=a2.t[:, :], in_=io["od_a2"][:, :], writes=[a2.g])
        kb.dma("pool", out=g2a.t[:, :], in_=io["od_g2"][0:128, :], writes=[g2a.g])
        kb.dma("pool", out=g2b.t[:, :], in_=io["od_g2"][128:160, :], writes=[g2b.g])
        load_bcast(kb, w0b, io["od_w0"][0])
        load_bcast(kb, a0b, io["od_a0"][0])
        for n in range(6):
            kb.dma("sp", out=mu.t[:, n, :], in_=io["od_mu"][0, n].rearrange("(kc p) -> p kc", p=128), writes=[mu.g],
                   allow_slow_non_contiguous=True)
        oi = 0
        for b in range(NB):
            t0 = b * 128
            x_ = xx[b % 2]
            if b == 0:
                kb.op("dve", lambda e: e.tensor_tensor(out=x_.t[:, :, 1:128], in0=xT.t[:, :, 0:127], in1=xT.t[:, :, 1:128], op=ALU.subtract),
                      reads=[xT.g], writes=[x_.g])
                kb.op("dve", lambda e: e.tensor_scalar(out=x_.t[:, :, 0:1], in0=xT.t[:, :, 0:1], scalar1=-1.0, scalar2=None, op0=ALU.mult),
                      reads=[xT.g], writes=[x_.g])
            else:
                kb.op("dve", lambda e: e.tensor_tensor(out=x_.t[:, :, :], in0=xT.t[:, :, t0 - 1:t0 + 127], in1=xT.t[:, :, t0:t0 + 128], op=ALU.subtract),
                      reads=[xT.g], writes=[x_.g])
            mx = [mixT[n][b % 2] for n in range(6)]
            for n in range(6):
                for kc in range(8):
                    eng = "dve"
                    kb.op(eng, lambda e: e.scalar_tensor_tensor(out=mx[n].t[:, kc, :], in0=x_.t[:, kc, :], scalar=mu.t[:, n, kc:kc + 1],
                                                                in1=xT.t[:, kc, t0:t0 + 128], op0=ALU.mult, op1=ALU.add),
                          reads=[x_.g, mu.g, xT.g], writes=[mx[n].g])

            def store(idx, ps, pre=None, func=None):
                nonlocal oi
                o = outF[oi % 4]
                oi += 1
                if pre is not None:
                    kb.op("dve", lambda e: e.tensor_tensor(out=o.t[:, :], in0=ps.t[:, :], in1=pre.t[:, :], op=ALU.add), reads=[ps.g, pre.g], writes=[o.g])
                    kb.op("act", lambda e: e.activation(out=o.t[:, :], in_=o.t[:, :], func=func), reads=[o.g], writes=[o.g])
                else:
                    kb.op("act", lambda e: e.activation(out=o.t[:, :], in_=ps.t[:, :], func=AF.Copy), reads=[ps.g], writes=[o.g])
                kb.dma("sp", out=scr[idx][t0:t0 + 128, :], in_=o.t[:, :], reads=[o.g], writes=[scr_reg], pool="st")

            for n in range(3):
                ps = P[n]
                for hf in range(2):
                    for kc in range(8):
                        kb.op("pe", lambda e: e.matmul(out=ps.t[:, hf * 512:(hf + 1) * 512], lhsT=mx[n].t[:, kc, :], rhs=wr.t[:, n, kc, hf * 512:(hf + 1) * 512],
                                                       start=(kc == 0), stop=(kc == 7)), reads=[mx[n].g, wr.g], writes=[ps.g])
                store(n, ps)
            q = Q[0]
            for kc in range(8):
                kb.op("pe", lambda e: e.matmul(out=q.t[0:64, 0:128], lhsT=l1.t[:, kc, 0:64], rhs=mx[3].t[:, kc, :], start=(kc == 0), stop=(kc == 7)),
                      reads=[l1.g, mx[3].g], writes=[q.g])
            for kc in range(8):
                kb.op("pe", lambda e: e.matmul(out=q.t[0:64, 128:256], lhsT=l1.t[:, kc, 64:128], rhs=mx[4].t[:, kc, :], start=(kc == 0), stop=(kc == 7)),
                      reads=[l1.g, mx[4].g], writes=[q.g])
            for kc in range(8):
                kb.op("pe", lambda e: e.matmul(out=q.t[:, 256:384], lhsT=l1.t[:, kc, 128:256], rhs=mx[5].t[:, kc, :], start=(kc == 0), stop=(kc == 7)),
                      reads=[l1.g, mx[5].g], writes=[q.g])
            for kc in range(8):
                kb.op("pe", lambda e: e.matmul(out=q.t[0:32, 384:512], lhsT=l1.t[:, kc, 256:288], rhs=mx[5].t[:, kc, :], start=(kc == 0), stop=(kc == 7)),
                      reads=[l1.g, mx[5].g], writes=[q.g])
            tw, al, sg1 = lo1[0], lo1[1], lo1[2]
            kb.op("act", lambda e: e.activation(out=tw.t[0:64, :], in_=q.t[0:64, 0:128], func=AF.Tanh), reads=[q.g], writes=[tw.g])
            kb.op("act", lambda e: e.activation(out=al.t[0:64, :], in_=q.t[0:64, 128:256], func=AF.Copy), reads=[q.g], writes=[al.g])
            kb.op("act", lambda e: e.activation(out=sg1.t[:, :], in_=q.t[:, 256:384], func=AF.Sigmoid), reads=[q.g], writes=[sg1.g])
            kb.op("act", lambda e: e.activation(out=lo2.t[:, :], in_=q.t[0:32, 384:512], func=AF.Sigmoid), reads=[q.g], writes=[lo2.g])
            ps = P[0]
            for hf in range(2):
                kb.op("pe", lambda e: e.matmul(out=ps.t[:, hf * 512:(hf + 1) * 512], lhsT=tw.t[0:64, :], rhs=w2.t[:, hf * 512:(hf + 1) * 512], start=True, stop=True),
                      reads=[tw.g, w2.g], writes=[ps.g])
            store(3, ps, pre=w0b, func=AF.Sigmoid)
            ps = P[1]
            for hf in range(2):
                kb.op("pe", lambda e: e.matmul(out=ps.t[:, hf * 512:(hf + 1) * 512], lhsT=al.t[0:64, :], rhs=a2.t[:, hf * 512:(hf + 1) * 512], start=True, stop=True),
                      reads=[al.g, a2.g], writes=[ps.g])
            store(4, ps, pre=a0b, func=AF.Sigmoid)
            ps = P[2]
            for hf in range(2):
                kb.op("pe", lambda e: e.matmul(out=ps.t[:, hf * 512:(hf + 1) * 512], lhsT=sg1.t[:, :], rhs=g2a.t[:, hf * 512:(hf + 1) * 512], start=True, stop=False),
                      reads=[sg1.g, g2a.g], writes=[ps.g])
                kb.op("pe", lambda e: e.matmul(out=ps.t[:, hf * 512:(hf + 1) * 512], lhsT=lo2.t[:, :], rhs=g2b.t[:, hf * 512:(hf + 1) * 512], start=False, stop=True),
                      reads=[lo2.g, g2b.g], writes=[ps.g])
            store(5, ps)
        kb.barrier()


def phase_rwkv_b(kb, nc, io, XT3, xt3_reg, ident, ones, tri, scr, scr_reg, xres_ap, xres_reg, xout_ap, xout_reg):
    H3 = lambda ap: ap.rearrange("p (h d) -> p h d", h=16)
    with ExitStack() as es:
        def F(name):
            return sbt(nc, es, "rb_" + name, [128, 1024], F32)

        def B(name):
            return sbt(nc, es, "rb_" + name, [128, 1024], BF16)

        wo = sbt(nc, es, "rb_wo", [128, 8, 1024], BF16)
        vec = {}
        for nm, ap in (("k_k", io["od_k_k"][0]), ("k_a", io["od_k_a"][0]), ("r_k", io["od_r_k"][0]), ("lnx_g", io["od_lnx_g"][0]),
                       ("lnx_b", io["od_lnx_b"][0]), ("lng", io["ln_mix_g"][1]), ("lnb", io["ln_mix_b"][1])):
            vec[nm] = F("v_" + nm)
            load_bcast(kb, vec[nm], ap)
        kb.dma("pool", out=wo.t[:, :, :], in_=io["od_w_out"].rearrange("(kc p) n -> p kc n", p=128), writes=[wo.g])
        msk = {}
        for nm in ("c_su4", "c_sl4", "c_iu4", "c_id4"):
            msk[nm] = sbt(nc, es, "rb_" + nm, [128, 512], BF16)
            kb.dma("sp", out=msk[nm].t[:, :], in_=io[nm][:, :], writes=[msk[nm].g])
        inb = [[F("in%d_%d" % (i, j)) for i in range(6)] for j in range(2)]
        _x3 = B("x3t0")
        x3ts = [_x3, _x3]
        fprep = (F("f1"), F("f2"), F("f3"))
        _g1, _g2 = F("g1"), F("g2")
        frest = (_g1, _g2, _g1)
        bsets = [[B("ps%d_%d" % (j, i)) for i in range(13)] for j in range(2)]
        Arb, Ark = [B("Arb0"), B("Arb1")], [B("Ark0"), B("Ark1")]
        Xall = [B("X0"), B("X1")]
        AhT = B("AhT")
        tmpg2 = [[sbt(nc, es, "rb_tg%d_%d" % (g, i), [128, 512], BF16) for i in range(8)] for g in range(2)]
        tmpg = [tmpg2[0], tmpg2[1], tmpg2[0], tmpg2[1]]
        tmpb = tmpg[0]
        smalls = (sbt(nc, es, "rb_small", [128, 96], F32), sbt(nc, es, "rb_small2", [128, 96], F32))
        eLCs = (sbt(nc, es, "rb_eLC", [128, 8], F32), sbt(nc, es, "rb_eLC2", [128, 8], F32))
        Hs = sbt(nc, es, "rb_H", [128, 8, 64], F32)
        Hb = sbt(nc, es, "rb_Hb", [128, 8, 64], BF16)
        stats = sbt(nc, es, "rb_stats", [128, 12], F32)
        mv = sbt(nc, es, "rb_mv", [128, 2], F32)
        rstd = sbt(nc, es, "rb_rstd", [128, 1], F32)
        Prest = [pst(nc, es, "rb_p%d" % i, [128, 1024], F32) for i in range(2)]
        Pp = pst(nc, es, "rb_pp", [128, 1024], F32)
        QTrest = pst(nc, es, "rb_qt", [128, 1024], BF16)
        QTp = pst(nc, es, "rb_qtp", [128, 1024], BF16)
        kb.op("dve", lambda e: e.memset(Hs.t[:, :, :], 0.0), writes=[Hs.g])
        kb.op("dve", lambda e: e.memset(Hb.t[:, :, :], 0.0), writes=[Hb.g])

        def bc16(t, c0):
            return t.t[:, c0:c0 + 16].unsqueeze(2).to_broadcast([128, 16, 64])

        real_kb = kb
        recs = []

        for b in range(NB):
            t0 = b * 128
            recA, recB = Rec(), Rec()
            recs.append((recA, recB))
            kb = recA
            f1, f2, f3 = fprep
            small = smalls[0]
            eLC = eLCs[b % 2]
            vB, lhi, llo, rtB, atB, btB, ktB, bpB, kpB, rT, aT, bT, kTt = bsets[b % 2]
            W1b, Ub, ygB, xbf = rtB, btB, ktB, lhi
            if b == 0:
                for idx, dst in enumerate(inb[0]):
                    kb.dma("sp", out=dst.t[:, :], in_=scr[idx][0:128, :], reads=[scr_reg], writes=[dst.g])
            rF, kF, vF, wF, aF, gF = inb[b % 2]
            xin = rF
            kb.op("act", lambda e: e.activation(out=vB.t[:, :], in_=vF.t[:, :], func=AF.Copy), reads=[vF.g], writes=[vB.g])
            kb.op("dve", lambda e: e.tensor_scalar(out=wF.t[:, :], in0=wF.t[:, :], scalar1=-math.exp(-0.5), scalar2=None, op0=ALU.mult), reads=[wF.g], writes=[wF.g])
            kb.op("act", lambda e: e.activation(out=lhi.t[:, :], in_=wF.t[:, :], func=AF.Copy), reads=[wF.g], writes=[lhi.g])
            kb.op("dve", lambda e: e.tensor_tensor(out=llo.t[:, :], in0=wF.t[:, :], in1=lhi.t[:, :], op=ALU.subtract), reads=[wF.g, lhi.g], writes=[llo.g])
            kb.op("dve", lambda e: e.tensor_tensor(out=f1.t[:, :], in0=kF.t[:, :], in1=vec["k_k"].t[:, :], op=ALU.mult), reads=[kF.g, vec["k_k"].g], writes=[f1.g])
            kb.op("act", lambda e: e.activation(out=f2.t[:, :], in_=f1.t[:, :], func=AF.Square), reads=[f1.g], writes=[f2.g])
            kb.op("dve", lambda e: e.tensor_reduce(out=small.t[:, 0:16], in_=H3(f2.t[:, :]), axis=AX.X, op=ALU.add), reads=[f2.g], writes=[small.g])
            kb.op("act", lambda e: e.activation(out=small.t[:, 0:16], in_=small.t[:, 0:16], func=AF.Sqrt), reads=[small.g], writes=[small.g])
            kb.op("dve", lambda e: e.tensor_scalar(out=small.t[:, 0:16], in0=small.t[:, 0:16], scalar1=1e-12, scalar2=None, op0=ALU.max), reads=[small.g], writes=[small.g])
            kb.op("dve", lambda e: e.reciprocal(out=small.t[:, 0:16], in_=small.t[:, 0:16]), reads=[small.g], writes=[small.g])
            kb.op("dve", lambda e: e.tensor_tensor(out=H3(f1.t[:, :]), in0=H3(f1.t[:, :]), in1=bc16(small, 0), op=ALU.mult), reads=[f1.g, small.g], writes=[f1.g])
            kb.op("dve", lambda e: e.scalar_tensor_tensor(out=f2.t[:, :], in0=aF.t[:, :], scalar=-1.0, in1=vec["k_a"].t[:, :], op0=ALU.add, op1=ALU.mult),
                  reads=[aF.g, vec["k_a"].g], writes=[f2.g])
            kb.op("dve", lambda e: e.scalar_tensor_tensor(out=f2.t[:, :], in0=f2.t[:, :], scalar=1.0, in1=kF.t[:, :], op0=ALU.add, op1=ALU.mult),
                  reads=[f2.g, kF.g], writes=[f2.g])
            kb.op("dve", lambda e: e.tensor_tensor(out=kF.t[:, :], in0=f1.t[:, :], in1=aF.t[:, :], op=ALU.mult), reads=[f1.g, aF.g], writes=[kF.g])
            for hf in range(2):
                sl = slice(hf * 512, (hf + 1) * 512)
                kb.op("pe", lambda e: e.matmul(out=Pp.t[:, sl], lhsT=tri.t[:, :], rhs=lhi.t[:, sl], start=True, stop=False), reads=[tri.g, lhi.g], writes=[Pp.g])
                kb.op("pe", lambda e: e.matmul(out=Pp.t[:, sl], lhsT=tri.t[:, :], rhs=llo.t[:, sl], start=False, stop=True), reads=[tri.g, llo.g], writes=[Pp.g])
            kb.op("act", lambda e: e.activation(out=aF.t[:, :], in_=Pp.t[:, :], func=AF.Copy), reads=[Pp.g], writes=[aF.g])
            kb.op("act", lambda e: e.activation(out=f3.t[:, :], in_=Pp.t[:, :], func=AF.Exp), reads=[Pp.g], writes=[f3.g])
            kb.op("dve", lambda e: e.tensor_tensor(out=rtB.t[:, :], in0=rF.t[:, :], in1=f3.t[:, :], op=ALU.mult), reads=[rF.g, f3.g], writes=[rtB.g])
            kb.op("dve", lambda e: e.tensor_tensor(out=wF.t[:, :], in0=aF.t[:, :], in1=wF.t[:, :], op=ALU.subtract), reads=[aF.g, wF.g], writes=[wF.g])
            kb.op("act", lambda e: e.activation(out=wF.t[:, :], in_=wF.t[:, :], func=AF.Exp), reads=[wF.g], writes=[wF.g])
            kb.op("dve", lambda e: e.scalar_tensor_tensor(out=atB.t[:, :], in0=f1.t[:, :], scalar=-1.0, in1=wF.t[:, :], op0=ALU.mult, op1=ALU.mult),
                  reads=[f1.g, wF.g], writes=[atB.g])
            kb.op("act", lambda e: e.activation(out=f3.t[:, :], in_=aF.t[:, :], func=AF.Exp, scale=-1.0), reads=[aF.g], writes=[f3.g])
            kb.op("dve", lambda e: e.tensor_tensor(out=btB.t[:, :], in0=kF.t[:, :], in1=f3.t[:, :], op=ALU.mult), reads=[kF.g, f3.g], writes=[btB.g])
            kb.op("dve", lambda e: e.tensor_tensor(out=ktB.t[:, :], in0=f2.t[:, :], in1=f3.t[:, :], op=ALU.mult), reads=[f2.g, f3.g], writes=[ktB.g])
            for hf in range(2):
                sl = slice(hf * 512, (hf + 1) * 512)
                kb.op("pe", lambda e: e.matmul(out=Pp.t[:, sl], lhsT=ones.t[:, :], rhs=lhi.t[:, sl], start=True, stop=False), reads=[ones.g, lhi.g], writes=[Pp.g])
                kb.op("pe", lambda e: e.matmul(out=Pp.t[:, sl], lhsT=ones.t[:, :], rhs=llo.t[:, sl], start=False, stop=True), reads=[ones.g, llo.g], writes=[Pp.g])
            kb.op("dve", lambda e: e.tensor_tensor(out=f3.t[:, :], in0=Pp.t[:, :], in1=aF.t[:, :], op=ALU.subtract), reads=[Pp.g, aF.g], writes=[f3.g])
            for hp in range(8):
                kb.op("pe", lambda e: e.matmul(out=Pp.t[:, hp:hp + 1], lhsT=lhi.t[:, hp * 128:(hp + 1) * 128], rhs=ones.t[:, 0:1], start=True, stop=False),
                      reads=[lhi.g, ones.g], writes=[Pp.g])
                kb.op("pe", lambda e: e.matmul(out=Pp.t[:, hp:hp + 1], lhsT=llo.t[:, hp * 128:(hp + 1) * 128], rhs=ones.t[:, 0:1], start=False, stop=True),
                      reads=[llo.g, ones.g], writes=[Pp.g])
            kb.op("act", lambda e: e.activation(out=eLC.t[:, :], in_=Pp.t[:, 0:8], func=AF.Exp), reads=[Pp.g], writes=[eLC.g])
            kb.op("act", lambda e: e.activation(out=f3.t[:, :], in_=f3.t[:, :], func=AF.Exp), reads=[f3.g], writes=[f3.g])
            kb.op("dve", lambda e: e.tensor_tensor(out=bpB.t[:, :], in0=kF.t[:, :], in1=f3.t[:, :], op=ALU.mult), reads=[kF.g, f3.g], writes=[bpB.g])
            kb.op("dve", lambda e: e.tensor_tensor(out=kpB.t[:, :], in0=f2.t[:, :], in1=f3.t[:, :], op=ALU.mult), reads=[f2.g, f3.g], writes=[kpB.g])
            kb.op("dve", lambda e: e.tensor_tensor(out=f3.t[:, :], in0=rF.t[:, :], in1=f2.t[:, :], op=ALU.mult), reads=[rF.g, f2.g], writes=[f3.g])
            kb.op("dve", lambda e: e.tensor_tensor(out=f3.t[:, :], in0=f3.t[:, :], in1=vec["r_k"].t[:, :], op=ALU.mult), reads=[f3.g, vec["r_k"].g], writes=[f3.g])
            kb.op("dve", lambda e: e.tensor_reduce(out=small.t[:, 16:32], in_=H3(f3.t[:, :]), axis=AX.X, op=ALU.add), reads=[f3.g], writes=[small.g])
            kb.op("dve", lambda e: e.tensor_tensor(out=H3(vF.t[:, :]), in0=H3(vF.t[:, :]), in1=bc16(small, 16), op=ALU.mult), reads=[vF.g, small.g], writes=[vF.g])
            for src, dst in ((rtB, rT), (atB, aT), (btB, bT), (ktB, kTt)):
                for hp in range(8):
                    kb.op("pe", lambda e: e.transpose(out=QTp.t[:, hp * 128:(hp + 1) * 128], in_=src.t[:, hp * 128:(hp + 1) * 128], identity=ident.t[:, :]),
                          reads=[src.g, ident.g], writes=[QTp.g])
                kb.op("act", lambda e: e.activation(out=dst.t[:, :], in_=QTp.t[:, :], func=AF.Copy), reads=[QTp.g], writes=[dst.g])

            kb = recB
            if b + 1 < NB:
                for idx, dst in enumerate(inb[(b + 1) % 2]):
                    kb.dma("sp", out=dst.t[:, :], in_=scr[idx][t0 + 128:t0 + 256, :], reads=[scr_reg], writes=[dst.g])
            f1, f2, f3 = frest
            xo = f3
            small = smalls[1]
            def fm(t, h):
                r0 = 64 * (h % 2)
                return t.t[r0:r0 + 64, (h // 2) * 128:(h // 2 + 1) * 128]

            gst = {}
            for pair_ in ((0, 1), (2, 3)):
                for g4 in pair_:
                    heads = [g4 * 4 + i for i in range(4)]
                    order = [(0, heads[0]), (2, heads[2]), (1, heads[1]), (3, heads[3])]
                    tb = tmpg[g4]
                    Nb, NTb = tb[0], tb[1]
                    specs = ((bT, aT, Nb, "c_su4"), (aT, bT, NTb, "c_sl4"))
                    ps = Prest[g4 % 2]
                    for si, (la, rb_, dst, mk) in enumerate(specs):
                        off = si * 512
                        for i, h in order:
                            kb.op("pe", lambda e: e.matmul(out=ps.t[:, off + i * 128:off + (i + 1) * 128], lhsT=fm(la, h), rhs=fm(rb_, h), start=True, stop=True),
                                  reads=[la.g, rb_.g], writes=[ps.g], rg=(64 * (h % 2), 64))
                        kb.op("dve", lambda e: e.tensor_tensor(out=dst.t[:, :], in0=ps.t[:, off:off + 512], in1=msk[mk].t[:, :], op=ALU.mult),
                              reads=[ps.g, msk[mk].g], writes=[dst.g])
                    hi_ = g4 // 2
                    co = (g4 % 2) * 512
                    ps = Prest[(g4 + 1) % 2]
                    for si, (la, rb_, dst) in enumerate(((bT, rT, Arb[hi_]), (kTt, rT, Ark[hi_]))):
                        off = si * 512
                        for i, h in order:
                            kb.op("pe", lambda e: e.matmul(out=ps.t[:, off + i * 128:off + (i + 1) * 128], lhsT=fm(la, h), rhs=fm(rb_, h), start=True, stop=True),
                                  reads=[la.g, rb_.g], writes=[ps.g], rg=(64 * (h % 2), 64))
                        kb.op("dve", lambda e: e.tensor_tensor(out=dst.t[:, co:co + 512], in0=ps.t[:, off:off + 512], in1=msk["c_iu4"].t[:, :], op=ALU.mult),
                              reads=[ps.g, msk["c_iu4"].g], writes=[dst.g])
                    X, XT = tb[2], tb[3]
                    kb.op("dve", lambda e: e.tensor_tensor(out=X.t[:, :], in0=Nb.t[:, :], in1=msk["c_id4"].t[:, :], op=ALU.add), reads=[Nb.g, msk["c_id4"].g], writes=[X.g])
                    kb.op("dve", lambda e: e.tensor_tensor(out=XT.t[:, :], in0=NTb.t[:, :], in1=msk["c_id4"].t[:, :], op=ALU.add), reads=[NTb.g, msk["c_id4"].g], writes=[XT.g])
                    gst[g4] = {"X": X, "XT": XT, "P": Nb, "PT": NTb, "pp": 0}
                for it in range(6):
                    last = it == 5
                    for g4 in pair_:
                        st = gst[g4]
                        tb = tmpg[g4]
                        hi_ = g4 // 2
                        co = (g4 % 2) * 512
                        X, XT, Pm, PTm = st["X"], st["XT"], st["P"], st["PT"]
                        P2, P2T = tb[4 + st["pp"]], tb[6 + st["pp"]]
                        st["pp"] ^= 1
                        psa = Prest[g4 % 2]
                        psb = Prest[(g4 + 1) % 2]
                        for i in range(4):
                            sl = slice(i * 128, (i + 1) * 128)
                            kb.op("pe", lambda e: e.matmul(out=psa.t[:, sl], lhsT=PTm.t[:, sl], rhs=Pm.t[:, sl], start=True, stop=True), reads=[PTm.g, Pm.g], writes=[psa.g])
                        if not last:
                            for i in range(4):
                                sl = slice(i * 128, (i + 1) * 128)
                                sl2 = slice(512 + i * 128, 512 + (i + 1) * 128)
                                kb.op("pe", lambda e: e.matmul(out=psa.t[:, sl2], lhsT=Pm.t[:, sl], rhs=PTm.t[:, sl], start=True, stop=True), reads=[PTm.g, Pm.g], writes=[psa.g])
                        kb.op("act", lambda e: e.activation(out=P2.t[:, :], in_=psa.t[:, 0:512], func=AF.Copy), reads=[psa.g], writes=[P2.g])
                        if not last:
                            kb.op("act", lambda e: e.activation(out=P2T.t[:, :], in_=psa.t[:, 512:1024], func=AF.Copy), reads=[psa.g], writes=[P2T.g])
                        for i in range(4):
                            sl = slice(i * 128, (i + 1) * 128)
                            kb.op("pe", lambda e: e.matmul(out=psb.t[:, sl], lhsT=XT.t[:, sl], rhs=P2.t[:, sl], start=True, stop=True), reads=[XT.g, P2.g], writes=[psb.g])
                        if not last:
                            for i in range(4):
                                sl = slice(i * 128, (i + 1) * 128)
                                sl2 = slice(512 + i * 128, 512 + (i + 1) * 128)
                                kb.op("pe", lambda e: e.matmul(out=psb.t[:, sl2], lhsT=P2.t[:, sl], rhs=XT.t[:, sl], start=True, stop=True), reads=[XT.g, P2.g], writes=[psb.g])
                        if last:
                            kb.op("dve", lambda e: e.tensor_tensor(out=Xall[hi_].t[:, co:co + 512], in0=psb.t[:, 0:512], in1=X.t[:, :], op=ALU.add),
                                  reads=[psb.g, X.g], writes=[Xall[hi_].g])
                        else:
                            Xn, XTn = (tb[0], tb[1]) if (it % 2 == 0) else (tb[2], tb[3])
                            kb.op("dve", lambda e: e.tensor_tensor(out=Xn.t[:, :], in0=psb.t[:, 0:512], in1=X.t[:, :], op=ALU.add), reads=[psb.g, X.g], writes=[Xn.g])
                            kb.op("dve", lambda e: e.tensor_tensor(out=XTn.t[:, :], in0=psb.t[:, 512:1024], in1=XT.t[:, :], op=ALU.add), reads=[psb.g, XT.g], writes=[XTn.g])
                            st["X"], st["XT"], st["P"], st["PT"] = Xn, XTn, P2, P2T
            for g4 in range(4):
                heads = [g4 * 4 + i for i in range(4)]
                order = [(0, heads[0]), (2, heads[2]), (1, heads[1]), (3, heads[3])]
                Aak = tmpb[2]
                ps = Prest[0]
                for i, h in enumerate(heads):
                    kb.op("pe", lambda e: e.matmul(out=ps.t[:, i * 128:(i + 1) * 128], lhsT=fm(kTt, h), rhs=fm(aT, h), start=True, stop=True),
                          reads=[kTt.g, aT.g], writes=[ps.g], rg=(64 * (h % 2), 64))
                kb.op("dve", lambda e: e.tensor_tensor(out=Aak.t[:, :], in0=ps.t[:, 0:512], in1=msk["c_su4"].t[:, :], op=ALU.mult),
                      reads=[ps.g, msk["c_su4"].g], writes=[Aak.g])
                for i, h in enumerate(heads):
                    kb.op("pe", lambda e: e.matmul(out=Prest[1].t[:, h * 64:(h + 1) * 64], lhsT=Aak.t[:, i * 128:(i + 1) * 128], rhs=vB.t[:, h * 64:(h + 1) * 64],
                                                   start=True, stop=True), reads=[Aak.g, vB.g], writes=[Prest[1].g])
                hi_ = g4 // 2
                co = (g4 % 2) * 512
                for i, h in enumerate(heads):
                    hp = h // 2
                    kb.op("pe", lambda e: e.matmul(out=ps.t[:, 512 + i * 128:512 + (i + 1) * 128], lhsT=atB.t[:, hp * 128:(hp + 1) * 128],
                                                   rhs=Xall[hi_].t[:, co + i * 128:co + (i + 1) * 128], start=True, stop=True),
                          reads=[atB.g, Xall[hi_].g], writes=[ps.g])
                v4 = ps.t[:, 512:1024].rearrange("p (j two t) -> p j two t", j=2, two=2)
                o4 = AhT.t[:, g4 * 256:(g4 + 1) * 256].rearrange("p (j t) -> p j t", j=2)
                kb.op("act", lambda e: e.activation(out=o4[0:64, :, :], in_=v4[0:64, :, 0, :], func=AF.Copy), reads=[ps.g], writes=[AhT.g])
                kb.op("act", lambda e: e.activation(out=o4[64:128, :, :], in_=v4[64:128, :, 1, :], func=AF.Copy), reads=[ps.g], writes=[AhT.g])
            kb.op("act", lambda e: e.activation(out=W1b.t[:, :], in_=Prest[1].t[:, :], func=AF.Copy), reads=[Prest[1].g], writes=[W1b.g])
            Hb3 = Hb.t
            for h in range(16):
                hp, r0 = h // 2, 64 * (h % 2)
                hi_, co = h // 8, (h % 8) * 128
                kb.op("pe", lambda e: e.matmul(out=Prest[0].t[:, h * 64:(h + 1) * 64], lhsT=AhT.t[r0:r0 + 64, hp * 128:(hp + 1) * 128], rhs=Hb3[r0:r0 + 64, hp, :],
                                               start=True, stop=False), reads=[AhT.g, Hb.g], writes=[Prest[0].g])
                kb.op("pe", lambda e: e.matmul(out=Prest[0].t[:, h * 64:(h + 1) * 64], lhsT=Xall[hi_].t[:, co:co + 128], rhs=W1b.t[:, h * 64:(h + 1) * 64],
                                               start=False, stop=True), reads=[Xall[hi_].g, W1b.g], writes=[Prest[0].g])
            kb.op("act", lambda e: e.activation(out=Ub.t[:, :], in_=Prest[0].t[:, :], func=AF.Copy), reads=[Prest[0].g], writes=[Ub.g])
            for h in range(16):
                hp, r0 = h // 2, 64 * (h % 2)
                hi_, co = h // 8, (h % 8) * 128
                o = Prest[0].t[:, h * 64:(h + 1) * 64]
                kb.op("pe", lambda e: e.matmul(out=o, lhsT=fm(rT, h), rhs=Hb3[r0:r0 + 64, hp, :], start=True, stop=False), reads=[rT.g, Hb.g], writes=[Prest[0].g])
                kb.op("pe", lambda e: e.matmul(out=o, lhsT=Arb[hi_].t[:, co:co + 128], rhs=Ub.t[:, h * 64:(h + 1) * 64], start=False, stop=False),
                      reads=[Arb[hi_].g, Ub.g], writes=[Prest[0].g])
                kb.op("pe", lambda e: e.matmul(out=o, lhsT=Ark[hi_].t[:, co:co + 128], rhs=vB.t[:, h * 64:(h + 1) * 64], start=False, stop=True),
                      reads=[Ark[hi_].g, vB.g], writes=[Prest[0].g])
            for hp in range(8):
                sl = slice(hp * 128, (hp + 1) * 128)
                kb.op("pe", lambda e: e.matmul(out=Prest[1].t[:, sl], lhsT=bpB.t[:, sl], rhs=Ub.t[:, sl], start=True, stop=False), reads=[bpB.g, Ub.g], writes=[Prest[1].g])
                kb.op("pe", lambda e: e.matmul(out=Prest[1].t[:, sl], lhsT=kpB.t[:, sl], rhs=vB.t[:, sl], start=False, stop=True), reads=[kpB.g, vB.g], writes=[Prest[1].g])
            kb.op("dve", lambda e: e.tensor_tensor(out=Hs.t[:, :, :], in0=Hs.t[:, :, :], in1=eLC.t[:, 0:8].unsqueeze(2).to_broadcast([128, 8, 64]), op=ALU.mult),
                  reads=[Hs.g, eLC.g], writes=[Hs.g])
            hv = Prest[1].t[:, :].rearrange("p (hp two d) -> p hp two d", hp=8, two=2)
            kb.op("dve", lambda e: e.tensor_tensor(out=Hs.t[0:64, :, :], in0=Hs.t[0:64, :, :], in1=hv[0:64, :, 0, :], op=ALU.add), reads=[Hs.g, Prest[1].g], writes=[Hs.g])
            kb.op("dve", lambda e: e.tensor_tensor(out=Hs.t[64:128, :, :], in0=Hs.t[64:128, :, :], in1=hv[64:128, :, 1, :], op=ALU.add), reads=[Hs.g, Prest[1].g], writes=[Hs.g])
            kb.op("act", lambda e: e.activation(out=Hb.t[:, :, :], in_=Hs.t[:, :, :], func=AF.Copy), reads=[Hs.g], writes=[Hb.g])
            Y = Prest[0]
            kb.op("act", lambda e: e.activation(out=f1.t[:, :], in_=Y.t[:, :], func=AF.Copy), reads=[Y.g], writes=[f1.g])
            kb.op("act", lambda e: e.activation(out=f2.t[:, :], in_=Y.t[:, :], func=AF.Square), reads=[Y.g], writes=[f2.g])
            kb.op("dve", lambda e: e.tensor_reduce(out=small.t[:, 32:48], in_=H3(f1.t[:, :]), axis=AX.X, op=ALU.add), reads=[f1.g], writes=[small.g])
            kb.op("dve", lambda e: e.tensor_reduce(out=small.t[:, 48:64], in_=H3(f2.t[:, :]), axis=AX.X, op=ALU.add), reads=[f2.g], writes=[small.g])
            kb.op("dve", lambda e: e.tensor_scalar(out=small.t[:, 32:64], in0=small.t[:, 32:64], scalar1=1.0 / 64, scalar2=None, op0=ALU.mult), reads=[small.g], writes=[small.g])
            kb.op("dve", lambda e: e.tensor_tensor(out=small.t[:, 64:80], in0=small.t[:, 32:48], in1=small.t[:, 32:48], op=ALU.mult), reads=[small.g], writes=[small.g])
            kb.op("dve", lambda e: e.tensor_tensor(out=small.t[:, 64:80], in0=small.t[:, 48:64], in1=small.t[:, 64:80], op=ALU.subtract), reads=[small.g], writes=[small.g])
            kb.op("dve", lambda e: e.tensor_scalar(out=small.t[:, 64:80], in0=small.t[:, 64:80], scalar1=64e-5, scalar2=None, op0=ALU.add), reads=[small.g], writes=[small.g])
            kb.op("act", lambda e: e.activation(out=small.t[:, 64:80], in_=small.t[:, 64:80], func=AF.Sqrt), reads=[small.g], writes=[small.g])
            kb.op("dve", lambda e: e.reciprocal(out=small.t[:, 64:80], in_=small.t[:, 64:80]), reads=[small.g], writes=[small.g])
            kb.op("dve", lambda e: e.tensor_tensor(out=H3(f1.t[:, :]), in0=H3(f1.t[:, :]), in1=bc16(small, 32), op=ALU.subtract), reads=[f1.g, small.g], writes=[f1.g])
            kb.op("dve", lambda e: e.tensor_tensor(out=H3(f1.t[:, :]), in0=H3(f1.t[:, :]), in1=bc16(small, 64), op=ALU.mult), reads=[f1.g, small.g], writes=[f1.g])
            kb.op("dve", lambda e: e.tensor_tensor(out=f1.t[:, :], in0=f1.t[:, :], in1=vec["lnx_g"].t[:, :], op=ALU.mult), reads=[f1.g, vec["lnx_g"].g], writes=[f1.g])
            kb.op("dve", lambda e: e.tensor_tensor(out=f1.t[:, :], in0=f1.t[:, :], in1=vec["lnx_b"].t[:, :], op=ALU.add), reads=[f1.g, vec["lnx_b"].g], writes=[f1.g])
            kb.op("dve", lambda e: e.tensor_tensor(out=f1.t[:, :], in0=f1.t[:, :], in1=vF.t[:, :], op=ALU.add), reads=[f1.g, vF.g], writes=[f1.g])
            kb.op("dve", lambda e: e.tensor_tensor(out=ygB.t[:, :], in0=f1.t[:, :], in1=gF.t[:, :], op=ALU.mult), reads=[f1.g, gF.g], writes=[ygB.g])
            for kc in range(8):
                kb.op("pe", lambda e: e.transpose(out=QTrest.t[:, kc * 128:(kc + 1) * 128], in_=ygB.t[:, kc * 128:(kc + 1) * 128], identity=ident.t[:, :]),
                      reads=[ygB.g, ident.g], writes=[QTrest.g])
            ygT = aT
            kb.op("act", lambda e: e.activation(out=ygT.t[:, :], in_=QTrest.t[:, :], func=AF.Copy), reads=[QTrest.g], writes=[ygT.g])
            kb.dma("sp", out=xin.t[:, :], in_=xres_ap[t0:t0 + 128, :], reads=[xres_reg], writes=[xin.g])
            for hf in range(2):
                for fc in range(8):
                    kb.op("pe", lambda e: e.matmul(out=Prest[0].t[:, hf * 512:(hf + 1) * 512], lhsT=ygT.t[:, fc * 128:(fc + 1) * 128], rhs=wo.t[:, fc, hf * 512:(hf + 1) * 512],
                                                   start=(fc == 0), stop=(fc == 7)), reads=[ygT.g, wo.g], writes=[Prest[0].g])
            kb.op("dve", lambda e: e.scalar_tensor_tensor(out=f2.t[:, :], in0=xin.t[:, :], scalar=DN_ALPHA, in1=Prest[0].t[:, :], op0=ALU.mult, op1=ALU.add),
                  reads=[xin.g, Prest[0].g], writes=[f2.g])
            ln_block(kb, f2, vec["lng"], vec["lnb"], xo, stats, mv, rstd, LN_EPS)
            kb.dma("sp", out=xout_ap[t0:t0 + 128, :], in_=xo.t[:, :], reads=[xo.g], writes=[xout_reg], pool="st")
            kb.op("act", lambda e: e.activation(out=xbf.t[:, :], in_=xo.t[:, :], func=AF.Copy), reads=[xo.g], writes=[xbf.g])
            for kc in range(8):
                kb.op("pe", lambda e: e.transpose(out=QTrest.t[:, kc * 128:(kc + 1) * 128], in_=xbf.t[:, kc * 128:(kc + 1) * 128], identity=ident.t[:, :]),
                      reads=[xbf.g, ident.g], writes=[QTrest.g])
            x3t = x3ts[b % 2]
            kb.op("act", lambda e: e.activation(out=x3t.t[:, :], in_=QTrest.t[:, :], func=AF.Copy), reads=[QTrest.g], writes=[x3t.g])
            kb.dma("sp", out=XT3[:, :, t0:t0 + 128], in_=x3t.t[:, :].rearrange("p (k t) -> p k t", k=8), reads=[x3t.g], writes=[xt3_reg], pool="st")
        kb = real_kb
        for it_ in recs[0][0].items:
            replay(kb, it_)
        for b in range(NB):
            rest = recs[b][1].items
            prep = recs[b + 1][0].items if b + 1 < NB else []
            na, nb_ = len(prep), len(rest)
            ia = 0
            for ib in range(nb_):
                replay(kb, rest[ib])
                tgt = ((ib + 1) * na) // nb_
                while ia < tgt:
                    replay(kb, prep[ia])
                    ia += 1
            while ia < na:
                replay(kb, prep[ia])
                ia += 1
        kb.barrier()


CONST_SPECS = {
    "c_ident": ([128, 128], BF16),
    "c_ones": ([128, 128], BF16),
    "c_tri": ([128, 128], BF16),
    "c_ones2": ([2, S], BF16),
    "c_alibiq": ([4, 2, S], BF16),
    "c_abias": ([4, 128, 32], F32),
    "c_retDT": ([4, 128, 512], F32),
    "c_retqdec": ([4, 64, 512], F32),
    "c_retkdec": ([4, 128, 1], F32),
    "c_su4": ([128, 512], BF16),
    "c_sl4": ([128, 512], BF16),
    "c_iu4": ([128, 512], BF16),
    "c_id4": ([128, 512], BF16),
}


def make_consts():
    bf = ml_dtypes.bfloat16
    c = {}
    c["c_ident"] = np.eye(128, dtype=np.float32).astype(bf)
    c["c_ones"] = np.ones((128, 128), np.float32).astype(bf)
    p = np.arange(128)
    c["c_tri"] = (p[None, :] >= p[:, None]).astype(np.float32).astype(bf)
    c["c_ones2"] = np.ones((2, S), np.float32).astype(bf)
    t = np.arange(S) % 512
    hi = (t // 16) * 16
    lo = t % 16
    aq = np.zeros((4, 2, S), np.float64)
    ab = np.zeros((4, 128, 32), np.float64)
    for h in range(4):
        aq[h, 0] = -8.0 * SLOPES[h] * hi
        aq[h, 1] = -8.0 * SLOPES[h] * lo
        for oi in range(32):
            ab[h, :, oi] = SLOPES[h] * (p + 128.0 * (oi - 28))
    c["c_alibiq"] = aq.astype(np.float32).astype(bf)
    c["c_abias"] = ab.astype(np.float32)
    DTm = np.zeros((4, 128, 512), np.float64)
    qd = np.zeros((4, 64, 512), np.float64)
    kd = np.zeros((4, 128, 1), np.float64)
    i = np.arange(128)
    for h in range(4):
        g = GAMMAS[h]
        rel = i[None, :] - i[:, None]
        m = np.where(rel >= 0, 0.125 * g ** np.maximum(rel, 0), 0.0)
        DTm[h] = np.tile(m, (1, 4))
        qd[h] = np.tile(g ** (i + 1.0), (64, 4))
        kd[h, :, 0] = 0.125 * g ** (127.0 - i)
    c["c_retDT"] = DTm.astype(np.float32)
    c["c_retqdec"] = qd.astype(np.float32)
    c["c_retkdec"] = kd.astype(np.float32)
    c["c_su4"] = np.tile((p[None, :] > p[:, None]).astype(np.float32), (1, 4)).astype(bf)
    c["c_sl4"] = np.tile((p[None, :] < p[:, None]).astype(np.float32), (1, 4)).astype(bf)
    c["c_iu4"] = np.tile((p[None, :] >= p[:, None]).astype(np.float32), (1, 4)).astype(bf)
    c["c_id4"] = np.tile(np.eye(128, dtype=np.float32), (1, 4)).astype(bf)
    return c


INPUT_SHAPES = {
    "ev_w_in": [1, 1024, 3072], "ev_lambda": [1, 4, 64], "ev_subln_g": [1, 128], "ev_w_out": [1, 1024, 1024],
    "od_mu": [1, 6, 1024], "od_w_rkv": [1, 3, 1024, 1024], "od_w0": [1, 1024], "od_w1": [1, 1024, 64], "od_w2": [1, 64, 1024],
    "od_a0": [1, 1024], "od_a1": [1, 1024, 64], "od_a2": [1, 64, 1024], "od_g1": [1, 1024, 160], "od_g2": [1, 160, 1024],
    "od_k_k": [1, 1024], "od_k_a": [1, 1024], "od_r_k": [1, 1024], "od_lnx_g": [1, 1024], "od_lnx_b": [1, 1024],
    "od_w_out": [1, 1024, 1024], "ln_mix_g": [2, 1024], "ln_mix_b": [2, 1024], "ffn_w_up": [2, 1024, 5632],
    "ffn_conv_w": [2, 3, 2816], "ffn_conv_b": [2, 2816], "ffn_w_down": [2, 2816, 1024], "ln_ffn_g": [2, 1024], "ln_ffn_b": [2, 1024],
}


def build(stop_after=None, debug=False):
    nc = bass.Bass("TRN2", target_bir_lowering=False)
    io = {}
    io["x"] = nc.dram_tensor("x", [S, D], F32, kind="ExternalInput").ap()
    for k, shp in INPUT_SHAPES.items():
        io[k] = nc.dram_tensor(k, shp, F32, kind="ExternalInput").ap()
    for k, (shp, dt) in CONST_SPECS.items():
        io[k] = nc.dram_tensor(k, shp, dt, kind="ExternalInput").ap()
    y = nc.dram_tensor("y", [S, D], F32, kind="ExternalOutput").ap()
    XA = nc.dram_tensor("scr_xa", [S, D], F32, kind="Internal").ap()
    XB = nc.dram_tensor("scr_xb", [S, D], F32, kind="Internal").ap()
    G = nc.dram_tensor("scr_g", [FF, S], BF16, kind="Internal").ap()
    scr = [nc.dram_tensor("scr_r%d" % i, [S, D], F32, kind="Internal").ap() for i in range(6)]
    scr_reg = Reg(True)
    dbg = None
    if debug:
        dbg = nc.dram_tensor("dbg_ot", [128, 8, S], BF16, kind="ExternalOutput").ap()
    for k in ("ev_w_in", "ev_w_out", "od_w_out", "od_w1", "od_w2", "od_a1", "od_a2", "od_g1", "od_g2"):
        io[k] = io[k][0]
    xa_reg, xb_reg, g_reg, y_reg = Reg(True), Reg(True), Reg(True), Reg(True)
    with ExitStack() as es:
        kb = KB(nc, es)
        kb.dma_pool("sp_ld", 12)
        kb.dma_pool("sp_st", 8)
        kb.dma_pool("pool_ld", 6)
        ident = sbt(nc, es, "ident", [128, 128], BF16)
        ones = sbt(nc, es, "ones", [128, 128], BF16)
        kb.dma("sp", out=ident.t[:, :], in_=io["c_ident"][:, :], writes=[ident.g])
        kb.dma("sp", out=ones.t[:, :], in_=io["c_ones"][:, :], writes=[ones.g])
        tri = sbt(nc, es, "tri", [128, 128], BF16)
        kb.dma("sp", out=tri.t[:, :], in_=io["c_tri"][:, :], writes=[tri.g])
        XT3 = nc.dram_tensor("scr_xt3", [128, 8, S], BF16, kind="Internal").ap()
        xt3_reg = Reg(True)
        with ExitStack() as esx:
            xT = sbt(nc, esx, "xT", [128, 8, S], BF16)
            phase_prologue(kb, nc, io, xT, ident)
            if stop_after == "prologue":
                dump_and_stop(kb, dbg[:, :, :], xT.t[:, :, :], xT.g)
                return nc
            if stop_after in ("rwkvonly", "rwkvonly_a"):
                phase_rwkv_a(kb, nc, io, xT, scr, scr_reg)
                if stop_after == "rwkvonly_a":
                    return nc
            else:
                with ExitStack() as es2:
                    OT = sbt(nc, es2, "OT", [128, 8, S], BF16)
                    if phase_l0_mixer(kb, nc, io, xT, OT, ident, ones, tri, dbg):
                        return nc
                    if stop_after == "mixer":
                        kb.barrier()
                        return nc
                    phase_out_ln(kb, nc, io, OT, xT, ident, io["ev_w_out"], io["x"], io["ln_mix_g"][0], io["ln_mix_b"][0], XA, xa_reg)
                if stop_after == "outln":
                    return nc
                phase_ffn(kb, nc, io, 0, xT, ident, XA, xa_reg, y if stop_after == "ffn0" else XB, y_reg if stop_after == "ffn0" else xb_reg,
                          G, g_reg, want_T=True)
                if stop_after == "ffn0":
                    return nc
                phase_rwkv_a(kb, nc, io, xT, scr, scr_reg)
        if stop_after == "rwkvonly":
            phase_rwkv_b(kb, nc, io, XT3, xt3_reg, ident, ones, tri, scr, scr_reg, io["x"], Reg(True), y, y_reg)
            return nc
        last = stop_after == "rwkv"
        phase_rwkv_b(kb, nc, io, XT3, xt3_reg, ident, ones, tri, scr, scr_reg, XB, xb_reg, y if last else XA, y_reg if last else xa_reg)
        if last:
            return nc
        with ExitStack() as esx:
            xT = sbt(nc, esx, "xT", [128, 8, S], BF16)
            for kc in range(8):
                kb.dma("sp", out=xT.t[:, kc, :], in_=XT3[:, kc, :], reads=[xt3_reg], writes=[xT.g])
            phase_ffn(kb, nc, io, 1, xT, ident, XA, xa_reg, y, y_reg, G, g_reg, want_T=False)
    return nc


_NC_CACHE = {}


def kernel(**inputs):
    if "nc" not in _NC_CACHE:
        _NC_CACHE["nc"] = build()
        _NC_CACHE["consts"] = make_consts()
    nc = _NC_CACHE["nc"]
    consts = _NC_CACHE["consts"]
    x = np.ascontiguousarray(np.asarray(inputs["x"], dtype=np.float32))
    shared = {k: np.ascontiguousarray(np.asarray(inputs[k], dtype=np.float32)) for k in INPUT_SHAPES}
    in_maps = []
    for c in range(8):
        m = {"x": x[c]}
        m.update(shared)
        m.update(consts)
        in_maps.append(m)
    res = run_bass_kernel_spmd(nc, in_maps, core_ids=list(range(8)))
    return np.stack([np.asarray(res.results[c]["y"], dtype=np.float32) for c in range(8)], axis=0)
```

```python
import math
from contextlib import ExitStack

import numpy as np
import ml_dtypes

import concourse.bass as bass
import concourse.mybir as mybir
from concourse.bass_utils import run_bass_kernel_spmd

F32 = mybir.dt.float32
BF16 = mybir.dt.bfloat16
AF = mybir.ActivationFunctionType
ALU = mybir.AluOpType
AX = mybir.AxisListType

S = 4096
D = 1024
NB = S // 128
FF = 2816
NFC = FF // 128
DN_ALPHA = (2.0 * 2) ** 0.25
LN_EPS = 1e-5
LAMBDA_INIT0 = 0.8 - 0.6 * math.exp(-0.3 * 0)
SLOPES = [2.0 ** (-8.0 * (i + 1) / 4) for i in range(4)]
GAMMAS = [1.0 - 2.0 ** (-5.0 - h) for h in range(4)]


MIX_STOP = None
LOOPV = 9
RB_STOP = 0


class StopBuild(Exception):
    pass


def dump_and_stop(kb, dbg, tile_ap, reg):
    kb.barrier()
    kb.dma("sp", out=dbg, in_=tile_ap, reads=[reg], pool="st")
    kb.barrier()
    return True


class Reg:
    __slots__ = ("w", "r", "nowaw", "psum")

    def __init__(self, nowaw=False):
        self.w = {}
        self.r = {}
        self.nowaw = nowaw
        self.psum = False


class T:
    def __init__(self, t):
        self.t = t
        self.g = Reg()


class KB:
    def __init__(self, nc, es):
        self.nc = nc
        self.es = es
        self.E = {"pe": nc.tensor, "dve": nc.vector, "act": nc.scalar, "pool": nc.gpsimd, "sp": nc.sync}
        self.sems = {}
        self.cnt = {}
        for e in self.E:
            self.sems[e] = es.enter_context(nc.semaphore("s_" + e))
            self.cnt[e] = 0
        self.seen = {e: {} for e in self.E}
        self.dpool = {}
        self.dnext = {}

    def dma_pool(self, name, n):
        keys = []
        for i in range(n):
            k = "%s%d" % (name, i)
            self.sems[k] = self.es.enter_context(self.nc.semaphore("d_" + k))
            self.cnt[k] = 0
            keys.append(k)
        self.dpool[name] = keys
        self.dnext[name] = 0

    def _deps(self, e, reads, writes):
        need = {}

        def add(d, same_ok):
            for k, v in d.items():
                if k == e and same_ok:
                    continue
                if need.get(k, 0) < v:
                    need[k] = v

        for r in reads:
            add(r.w, e == "pe")
            if r.psum:
                add(r.r, True)
        for w in writes:
            if not w.nowaw:
                add(w.w, e == "pe")
            add(w.r, e == "pe")
        return need

    def _wait(self, e, need):
        sn = self.seen[e]
        for k, v in need.items():
            if sn.get(k, 0) < v:
                self.E[e].wait_ge(self.sems[k], v)
                sn[k] = v

    def op(self, e, fn, reads=(), writes=(), rg=(0, 128)):
        need = self._deps(e, reads, writes)
        if e == "pe":
            last = getattr(self, "_last_rg", (0, 128))
            if (rg[0] + rg[1] <= last[0] or last[0] + last[1] <= rg[0]) and self.cnt["pe"] > 0:
                need["pe"] = self.cnt["pe"]
            self._last_rg = rg
        self._wait(e, need)
        ins = fn(self.E[e])
        self.cnt[e] += 1
        ins.then_inc(self.sems[e], 1)
        tok = self.cnt[e]
        for r in reads:
            r.r[e] = tok
        for w in writes:
            w.w[e] = tok
            if not w.nowaw:
                w.r = {}
        return ins

    def dma(self, q, out, in_, reads=(), writes=(), pool="ld", **kw):
        pool = q + "_" + pool
        keys = self.dpool[pool]
        k = keys[self.dnext[pool] % len(keys)]
        self.dnext[pool] += 1
        need = self._deps(q, reads, writes)
        if self.cnt[k] > 0:
            need[k] = max(need.get(k, 0), self.cnt[k])
        self._wait(q, need)
        ins = self.E[q].dma_start(out=out, in_=in_, **kw)
        self.cnt[k] += 16
        ins.then_inc(self.sems[k], 16)
        tok = self.cnt[k]
        for r in reads:
            r.r[k] = tok
        for w in writes:
            w.w[k] = tok
            if not w.nowaw:
                w.r = {}
        return ins

    def barrier(self):
        for e in self.E:
            need = {k: v for k, v in self.cnt.items() if k != e and v > 0}
            self._wait(e, need)


_UNIQ = [0]


def _uniq(name):
    _UNIQ[0] += 1
    return "%s_%d" % (name, _UNIQ[0])


def sbt(nc, es, name, shape, dt):
    return T(es.enter_context(nc.sbuf_tensor(_uniq(name), list(shape), dt)))


def pst(nc, es, name, shape, dt):
    t = T(es.enter_context(nc.psum_tensor(_uniq(name), list(shape), dt)))
    t.g.psum = True
    return t


def ln_block(kb, z, gam, bet, outp, stats, mv, rstd, eps):
    kb.op("dve", lambda e: e.bn_stats(out=stats.t[:, 0:6], in_=z.t[:, 0:512]), reads=[z.g], writes=[stats.g])
    kb.op("dve", lambda e: e.bn_stats(out=stats.t[:, 6:12], in_=z.t[:, 512:1024]), reads=[z.g], writes=[stats.g])
    kb.op("dve", lambda e: e.bn_aggr(out=mv.t[:, 0:2], in_=stats.t[:, 0:12]), reads=[stats.g], writes=[mv.g])
    kb.op("dve", lambda e: e.tensor_scalar(out=rstd.t[:, 0:1], in0=mv.t[:, 1:2], scalar1=eps, scalar2=None, op0=ALU.add),
          reads=[mv.g], writes=[rstd.g])
    kb.op("act", lambda e: e.activation(out=rstd.t[:, 0:1], in_=rstd.t[:, 0:1], func=AF.Sqrt), reads=[rstd.g], writes=[rstd.g])
    kb.op("dve", lambda e: e.reciprocal(out=rstd.t[:, 0:1], in_=rstd.t[:, 0:1]), reads=[rstd.g], writes=[rstd.g])
    kb.op("dve", lambda e: e.tensor_scalar(out=z.t[:, :], in0=z.t[:, :], scalar1=mv.t[:, 0:1], scalar2=rstd.t[:, 0:1],
                                           op0=ALU.subtract, op1=ALU.mult), reads=[z.g, mv.g, rstd.g], writes=[z.g])
    kb.op("dve", lambda e: e.tensor_tensor(out=z.t[:, :], in0=z.t[:, :], in1=gam.t[:, :], op=ALU.mult),
          reads=[z.g, gam.g], writes=[z.g])
    kb.op("dve", lambda e: e.tensor_tensor(out=outp.t[:, :], in0=z.t[:, :], in1=bet.t[:, :], op=ALU.add),
          reads=[z.g, bet.g], writes=[outp.g])


def transpose_to_fm(kb, src32, srcbf, dstT, b, ident, ptr):
    kb.op("act", lambda e: e.activation(out=srcbf.t[:, :], in_=src32.t[:, :], func=AF.Copy), reads=[src32.g], writes=[srcbf.g])
    for kc in range(8):
        kb.op("pe", lambda e: e.transpose(out=ptr.t[:, kc * 128:(kc + 1) * 128], in_=srcbf.t[:, kc * 128:(kc + 1) * 128],
                                          identity=ident.t[:, :]), reads=[srcbf.g, ident.g], writes=[ptr.g])
    kb.op("dve", lambda e: e.tensor_copy(out=dstT.t[:, :, b * 128:(b + 1) * 128],
                                         in_=ptr.t[:, :].rearrange("p (k t) -> p k t", k=8)), reads=[ptr.g], writes=[dstT.g])


def load_bcast(kb, dst, vec_ap):
    kb.dma("sp", out=dst.t[:, :], in_=vec_ap.partition_broadcast(128), writes=[dst.g])


def phase_prologue(kb, nc, io, xT, ident):
    with ExitStack() as es:
        xin = [sbt(nc, es, "pr_x%d" % i, [128, 1024], F32) for i in range(2)]
        xbf = [sbt(nc, es, "pr_xb%d" % i, [128, 1024], BF16) for i in range(2)]
        ptr = [pst(nc, es, "pr_pt%d" % i, [128, 1024], BF16) for i in range(2)]
        for b in range(NB):
            xi = xin[b % 2]
            kb.dma("sp", out=xi.t[:, :], in_=io["x"][b * 128:(b + 1) * 128, :], writes=[xi.g])
            transpose_to_fm(kb, xi, xbf[b % 2], xT, b, ident, ptr[b % 2])
        kb.barrier()


def phase_l0_mixer(kb, nc, io, xT, OT, ident, ones, tri, dbg=None):
    W_in = io["ev_w_in"].rearrange("(kc p) n -> p kc n", p=128)
    with ExitStack() as es:
        wq = sbt(nc, es, "m_wq", [128, 8, 384], BF16)
        qT = [sbt(nc, es, "m_qT%d" % m, [128, S], BF16) for m in range(2)]
        kT = [sbt(nc, es, "m_kT%d" % m, [128, S], BF16) for m in range(2)]
        vtok = sbt(nc, es, "m_vtok", [128, NB, 128], BF16)
        abias = sbt(nc, es, "m_abias", [128, 32], F32)
        lamt = sbt(nc, es, "m_lam", [128, 256], F32)
        lprod = sbt(nc, es, "m_lprod", [128, 128], F32)
        lsum = sbt(nc, es, "m_lsum", [128, 2], F32)
        lexp = sbt(nc, es, "m_lexp", [128, 2], F32)
        neglam = sbt(nc, es, "m_neglam", [128, 1], F32)
        gsc = sbt(nc, es, "m_gsc", [128, 1], F32)
        pT = [sbt(nc, es, "m_pT%d" % i, [128, 512], BF16) for i in range(4)]
        r1 = sbt(nc, es, "m_r1", [128, 512], F32)
        r2 = sbt(nc, es, "m_r2", [128, 512], F32)
        t1 = sbt(nc, es, "m_t1", [128, 512], F32)
        t2 = sbt(nc, es, "m_t2", [128, 512], F32)
        sqb = sbt(nc, es, "m_sqb", [128, 512], BF16)
        ybf = sbt(nc, es, "m_ybf", [128, 512], BF16)
        ktd = sbt(nc, es, "m_ktd", [128, NB, 64], BF16)
        DT = sbt(nc, es, "m_DT", [128, 512], F32)
        qdec = sbt(nc, es, "m_qdec", [64, 512], F32)
        kdec = sbt(nc, es, "m_kdec", [128, 1], F32)
        Rst = sbt(nc, es, "m_Rst", [64, 2, 128], F32)
        Rbf = sbt(nc, es, "m_Rbf", [64, NB, 128], BF16)
        bank = [pst(nc, es, "m_b%d" % i, [128, 512], F32) for i in range(8)]

        kb.dma("sp", out=lamt.t[:, :], in_=io["ev_lambda"].rearrange("a b c -> (a b c)").partition_broadcast(128), writes=[lamt.g])
        kb.op("dve", lambda e: e.tensor_tensor(out=lprod.t[:, 0:64], in0=lamt.t[:, 0:64], in1=lamt.t[:, 64:128], op=ALU.mult),
              reads=[lamt.g], writes=[lprod.g])
        kb.op("dve", lambda e: e.tensor_tensor(out=lprod.t[:, 64:128], in0=lamt.t[:, 128:192], in1=lamt.t[:, 192:256], op=ALU.mult),
              reads=[lamt.g], writes=[lprod.g])
        kb.op("dve", lambda e: e.reduce_sum(out=lsum.t[:, 0:1], in_=lprod.t[:, 0:64], axis=AX.X), reads=[lprod.g], writes=[lsum.g])
        kb.op("dve", lambda e: e.reduce_sum(out=lsum.t[:, 1:2], in_=lprod.t[:, 64:128], axis=AX.X), reads=[lprod.g], writes=[lsum.g])
        kb.op("act", lambda e: e.activation(out=lexp.t[:, 0:2], in_=lsum.t[:, 0:2], func=AF.Exp), reads=[lsum.g], writes=[lexp.g])
        kb.op("dve", lambda e: e.tensor_tensor(out=neglam.t[:, 0:1], in0=lexp.t[:, 1:2], in1=lexp.t[:, 0:1], op=ALU.subtract),
              reads=[lexp.g], writes=[neglam.g])
        kb.op("dve", lambda e: e.tensor_scalar(out=neglam.t[:, 0:1], in0=neglam.t[:, 0:1], scalar1=-LAMBDA_INIT0, scalar2=None, op0=ALU.add),
              reads=[neglam.g], writes=[neglam.g])
        kb.dma("sp", out=gsc.t[:, :], in_=io["ev_subln_g"].rearrange("o v -> v o"), writes=[gsc.g], allow_slow_non_contiguous=True)
        kb.op("dve", lambda e: e.tensor_scalar(out=gsc.t[:, 0:1], in0=gsc.t[:, 0:1], scalar1=1.0 - LAMBDA_INIT0, scalar2=None, op0=ALU.mult),
              reads=[gsc.g], writes=[gsc.g])
        for m in range(2):
            kb.dma("sp", out=kT[m].t[64:66, :], in_=io["c_ones2"][:, :], writes=[kT[m].g])

        def proj_fm(dst, prow, co, ncol, evac_eng_i):
            for tt in range(8):
                bk = bank[tt % 2]
                for kc in range(8):
                    kb.op("pe", lambda e: e.matmul(out=bk.t[0:ncol, :], lhsT=wq.t[:, kc, co:co + ncol],
                                                   rhs=xT.t[:, kc, tt * 512:(tt + 1) * 512], start=(kc == 0), stop=(kc == 7)),
                          reads=[wq.g, xT.g], writes=[bk.g])
                if (tt + evac_eng_i) % 2 == 0:
                    kb.op("act", lambda e: e.activation(out=dst.t[prow:prow + ncol, tt * 512:(tt + 1) * 512], in_=bk.t[0:ncol, :], func=AF.Copy),
                          reads=[bk.g], writes=[dst.g])
                else:
                    kb.op("dve", lambda e: e.tensor_copy(out=dst.t[prow:prow + ncol, tt * 512:(tt + 1) * 512], in_=bk.t[0:ncol, :]),
                          reads=[bk.g], writes=[dst.g])

        for h in range(4):
            for i, co in enumerate((h * 128, 512 + h * 128, 1024 + h * 128)):
                kb.dma("pool", out=wq.t[:, :, i * 128:(i + 1) * 128], in_=W_in[:, :, co:co + 128], reads=[], writes=[wq.g])
            kb.dma("sp", out=abias.t[:, :], in_=io["c_abias"][h, :, :], writes=[abias.g])
            for m in range(2):
                kb.dma("sp", out=qT[m].t[64:66, :], in_=io["c_alibiq"][h, :, :], writes=[qT[m].g])
            for m in range(2):
                proj_fm(qT[m], 0, m * 64, 64, 0)
                proj_fm(kT[m], 0, 128 + m * 64, 64, 1)
            for g4 in range(8):
                bk = bank[2 + g4 % 2]
                for bb in range(4):
                    b = g4 * 4 + bb
                    for kc in range(8):
                        kb.op("pe", lambda e: e.matmul(out=bk.t[:, bb * 128:(bb + 1) * 128], lhsT=xT.t[:, kc, b * 128:(b + 1) * 128],
                                                       rhs=wq.t[:, kc, 256:384], start=(kc == 0), stop=(kc == 7)),
                              reads=[wq.g, xT.g], writes=[bk.g])
                kb.op("dve", lambda e: e.tensor_copy(out=vtok.t[:, g4 * 4:(g4 + 1) * 4, :],
                                                     in_=bk.t[:, :].rearrange("p (b v) -> p b v", b=4)), reads=[bk.g], writes=[vtok.g])
            if MIX_STOP == "h0proj":
                kb.barrier()
                kb.dma("sp", out=dbg[0:64, 0, :], in_=qT[0].t[0:64, :], reads=[qT[0].g], pool="st")
                kb.dma("sp", out=dbg[0:64, 1, :], in_=qT[1].t[0:64, :], reads=[qT[1].g], pool="st")
                kb.dma("sp", out=dbg[0:64, 2, :], in_=kT[0].t[0:64, :], reads=[kT[0].g], pool="st")
                kb.dma("sp", out=dbg[0:64, 3, :], in_=kT[1].t[0:64, :], reads=[kT[1].g], pool="st")
                kb.dma("sp", out=dbg[:, 4, :].rearrange("p (b v) -> p b v", b=NB), in_=vtok.t[:, :, :], reads=[vtok.g], pool="st")
                kb.barrier()
                return True
            O = [bank[4], bank[6]]
            Sm = [bank[5], bank[7]]
            for c in range(8):
                steps = [(kbi, m) for kbi in range(4 * c + 4) for m in range(2)]

                def geom(i):
                    kbi, m = steps[i]
                    j = kbi - 4 * c
                    lo = 128 * j if j > 0 else 0
                    return kbi, m, j, lo

                def emit_qk(i):
                    kbi, m, j, lo = geom(i)
                    sb = bank[i % 4]
                    KK = 64 if LOOPV == -1 else 66
                    kb.op("pe", lambda e: e.matmul(out=sb.t[:, lo:512], lhsT=kT[m].t[0:KK, kbi * 128:(kbi + 1) * 128],
                                                   rhs=qT[m].t[0:KK, c * 512 + lo:(c + 1) * 512], start=True, stop=True),
                          reads=[kT[m].g, qT[m].g], writes=[sb.g])

                def emit_pv(i):
                    kbi, m, j, lo = geom(i)
                    sb = bank[i % 4]
                    pt = pT[i % 4]
                    oi = (kbi - 4 * c) + 28
                    if LOOPV == -4:
                        return
                    kb.op("act", lambda e: e.activation(out=pt.t[:, lo:512], in_=sb.t[:, lo:512], func=(AF.Copy if LOOPV == -3 else AF.Exp),
                                                        bias=(0.0 if LOOPV in (-2, -3) else abias.t[:, oi:oi + 1]), scale=0.125),
                          reads=[sb.g, abias.g], writes=[pt.g])
                    if LOOPV < 1:
                        return
                    if j >= 0:
                        kb.op("dve", lambda e: e.tensor_tensor(out=pt.t[:, lo:lo + 128], in0=pt.t[:, lo:lo + 128], in1=tri.t[:, :], op=ALU.mult),
                              reads=[pt.g, tri.g], writes=[pt.g])
                    if LOOPV < 2:
                        return
                    first = kbi == 0
                    last = kbi == 4 * c + 3
                    kb.op("pe", lambda e: e.matmul(out=O[m].t[:, lo:512], lhsT=vtok.t[:, kbi, :], rhs=pt.t[:, lo:512], start=first, stop=last),
                          reads=[vtok.g, pt.g], writes=[O[m].g])
                    kb.op("pe", lambda e: e.matmul(out=Sm[m].t[:, lo:512], lhsT=ones.t[:, :], rhs=pt.t[:, lo:512], start=first, stop=last),
                          reads=[ones.g, pt.g], writes=[Sm[m].g])

                n = len(steps)
                emit_qk(0)
                emit_qk(1)
                for i in range(n):
                    if i + 2 < n:
                        emit_qk(i + 2)
                    emit_pv(i)
                if MIX_STOP == "c0loop":
                    kb.barrier()
                    kb.dma("sp", out=dbg[:, 4, :].rearrange("p (b v) -> p b v", b=NB), in_=vtok.t[:, :, :], reads=[vtok.g], pool="st")
                    kb.barrier()
                    return True
                kb.op("dve", lambda e: e.reciprocal(out=r1.t[:, :], in_=Sm[0].t[:, :]), reads=[Sm[0].g], writes=[r1.g])
                kb.op("dve", lambda e: e.reciprocal(out=r2.t[:, :], in_=Sm[1].t[:, :]), reads=[Sm[1].g], writes=[r2.g])
                kb.op("dve", lambda e: e.tensor_tensor(out=t1.t[:, :], in0=O[0].t[:, :], in1=r1.t[:, :], op=ALU.mult), reads=[O[0].g, r1.g], writes=[t1.g])
                kb.op("dve", lambda e: e.tensor_tensor(out=t2.t[:, :], in0=O[1].t[:, :], in1=r2.t[:, :], op=ALU.mult), reads=[O[1].g, r2.g], writes=[t2.g])
                kb.op("dve", lambda e: e.scalar_tensor_tensor(out=t1.t[:, :], in0=t2.t[:, :], scalar=neglam.t[:, 0:1], in1=t1.t[:, :],
                                                              op0=ALU.mult, op1=ALU.add), reads=[t1.g, t2.g, neglam.g], writes=[t1.g])
                kb.op("act", lambda e: e.activation(out=sqb.t[:, :], in_=t1.t[:, :], func=AF.Square), reads=[t1.g], writes=[sqb.g])
                ssb = bank[0]
                kb.op("pe", lambda e: e.matmul(out=ssb.t[:, :], lhsT=ones.t[:, :], rhs=sqb.t[:, :], start=True, stop=True),
                      reads=[ones.g, sqb.g], writes=[ssb.g])
                kb.op("dve", lambda e: e.tensor_scalar(out=r1.t[:, :], in0=ssb.t[:, :], scalar1=1.0 / 128, scalar2=LN_EPS, op0=ALU.mult, op1=ALU.add),
                      reads=[ssb.g], writes=[r1.g])
                kb.op("act", lambda e: e.activation(out=r1.t[:, :], in_=r1.t[:, :], func=AF.Sqrt), reads=[r1.g], writes=[r1.g])
                kb.op("dve", lambda e: e.reciprocal(out=r1.t[:, :], in_=r1.t[:, :]), reads=[r1.g], writes=[r1.g])
                kb.op("dve", lambda e: e.scalar_tensor_tensor(out=OT.t[:, h, c * 512:(c + 1) * 512], in0=t1.t[:, :], scalar=gsc.t[:, 0:1], in1=r1.t[:, :],
                                                              op0=ALU.mult, op1=ALU.mult), reads=[t1.g, gsc.g, r1.g], writes=[OT.g])

        if MIX_STOP == "diff":
            return dump_and_stop(kb, dbg[:, :, :], OT.t[:, :, :], OT.g)
        for h in range(4):
            gam = GAMMAS[h]
            kb.dma("pool", out=wq.t[:, :, 0:64], in_=W_in[:, :, 1536 + h * 64:1536 + (h + 1) * 64], writes=[wq.g])
            kb.dma("pool", out=wq.t[:, :, 64:128], in_=W_in[:, :, 1792 + h * 64:1792 + (h + 1) * 64], writes=[wq.g])
            kb.dma("pool", out=wq.t[:, :, 128:256], in_=W_in[:, :, 2048 + h * 128:2048 + (h + 1) * 128], writes=[wq.g])
            kb.dma("pool", out=wq.t[:, :, 256:384], in_=W_in[:, :, 2560 + h * 128:2560 + (h + 1) * 128], writes=[wq.g])
            kb.dma("sp", out=DT.t[:, :], in_=io["c_retDT"][h, :, :], writes=[DT.g])
            kb.dma("sp", out=qdec.t[:, :], in_=io["c_retqdec"][h, :, :], writes=[qdec.g])
            kb.dma("sp", out=kdec.t[:, :], in_=io["c_retkdec"][h, :, :], writes=[kdec.g])
            if MIX_STOP == "r_load":
                kb.barrier()
                kb.dma("sp", out=dbg[:, 4, :].rearrange("p (b v) -> p b v", b=NB), in_=vtok.t[:, :, :], reads=[vtok.g], pool="st")
                kb.barrier()
                return True
            for tt in range(8):
                bk = bank[tt % 2]
                for kc in range(8):
                    kb.op("pe", lambda e: e.matmul(out=bk.t[0:64, :], lhsT=wq.t[:, kc, 0:64], rhs=xT.t[:, kc, tt * 512:(tt + 1) * 512],
                                                   start=(kc == 0), stop=(kc == 7)), reads=[wq.g, xT.g], writes=[bk.g])
                kb.op("act", lambda e: e.activation(out=qT[0].t[0:64, tt * 512:(tt + 1) * 512], in_=bk.t[0:64, :], func=AF.Copy),
                      reads=[bk.g], writes=[qT[0].g])
                if LOOPV >= 1:
                    kb.op("dve", lambda e: e.tensor_tensor(out=qT[1].t[0:64, tt * 512:(tt + 1) * 512], in0=bk.t[0:64, :], in1=qdec.t[:, :], op=ALU.mult),
                          reads=[bk.g, qdec.g], writes=[qT[1].g])
            if LOOPV >= 2:
                proj_fm(kT[0], 0, 64, 64, 0)
            for b in range(NB if LOOPV >= 3 else 0):
                bk = bank[2 + b % 2]
                for kc in range(8):
                    kb.op("pe", lambda e: e.matmul(out=bk.t[:, 0:192], lhsT=xT.t[:, kc, b * 128:(b + 1) * 128], rhs=wq.t[:, kc, 64:256],
                                                   start=(kc == 0), stop=(kc == 7)), reads=[wq.g, xT.g], writes=[bk.g])
                kb.op("dve", lambda e: e.tensor_scalar(out=ktd.t[:, b, :], in0=bk.t[:, 0:64], scalar1=kdec.t[:, 0:1], scalar2=None, op0=ALU.mult),
                      reads=[bk.g, kdec.g], writes=[ktd.g])
                kb.op("act", lambda e: e.activation(out=vtok.t[:, b, :], in_=bk.t[:, 64:192], func=AF.Copy), reads=[bk.g], writes=[vtok.g])
            if MIX_STOP == "r_proj":
                kb.barrier()
                kb.dma("sp", out=dbg[:, 4, :].rearrange("p (b v) -> p b v", b=NB), in_=vtok.t[:, :, :], reads=[vtok.g], pool="st")
                kb.barrier()
                return True
            kb.op("dve", lambda e: e.memset(Rst.t[:, 0, :], 0.0), writes=[Rst.g])
            kb.op("dve", lambda e: e.memset(Rbf.t[:, 0, :], 0.0), writes=[Rbf.g])
            for g4 in range(8):
                bk = bank[4 + g4 % 2]
                for bb in range(4):
                    b = g4 * 4 + bb
                    kb.op("pe", lambda e: e.matmul(out=bk.t[0:64, bb * 128:(bb + 1) * 128], lhsT=ktd.t[:, b, :], rhs=vtok.t[:, b, :],
                                                   start=True, stop=True), reads=[ktd.g, vtok.g], writes=[bk.g])
                for bb in range(4):
                    b = g4 * 4 + bb
                    if b == NB - 1:
                        continue
                    kb.op("dve", lambda e: e.scalar_tensor_tensor(out=Rst.t[:, (b + 1) % 2, :], in0=Rst.t[:, b % 2, :], scalar=float(gam ** 128),
                                                                  in1=bk.t[0:64, bb * 128:(bb + 1) * 128], op0=ALU.mult, op1=ALU.add),
                          reads=[Rst.g, bk.g], writes=[Rst.g])
                    kb.op("act", lambda e: e.activation(out=Rbf.t[:, b + 1, :], in_=Rst.t[:, (b + 1) % 2, :], func=AF.Copy), reads=[Rst.g], writes=[Rbf.g])
            if MIX_STOP == "r_state":
                kb.barrier()
                kb.dma("sp", out=dbg[:, 4, :].rearrange("p (b v) -> p b v", b=NB), in_=vtok.t[:, :, :], reads=[vtok.g], pool="st")
                kb.barrier()
                return True
            for g4 in range(8):
                sc = bank[g4 % 2]
                yb = bank[2 + g4 % 2]
                gb = bank[6 + g4 % 2]
                pt = pT[g4 % 2]
                for bb in range(4):
                    b = g4 * 4 + bb
                    kb.op("pe", lambda e: e.matmul(out=sc.t[:, bb * 128:(bb + 1) * 128], lhsT=kT[0].t[0:64, b * 128:(b + 1) * 128],
                                                   rhs=qT[0].t[0:64, b * 128:(b + 1) * 128], start=True, stop=True),
                          reads=[kT[0].g, qT[0].g], writes=[sc.g])
                for kc in range(8):
                    kb.op("pe", lambda e: e.matmul(out=gb.t[:, :], lhsT=wq.t[:, kc, 256:384], rhs=xT.t[:, kc, g4 * 512:(g4 + 1) * 512],
                                                   start=(kc == 0), stop=(kc == 7)), reads=[wq.g, xT.g], writes=[gb.g])
                kb.op("dve", lambda e: e.tensor_tensor(out=pt.t[:, :], in0=sc.t[:, :], in1=DT.t[:, :], op=ALU.mult), reads=[sc.g, DT.g], writes=[pt.g])
                for bb in range(4):
                    b = g4 * 4 + bb
                    kb.op("pe", lambda e: e.matmul(out=yb.t[:, bb * 128:(bb + 1) * 128], lhsT=vtok.t[:, b, :], rhs=pt.t[:, bb * 128:(bb + 1) * 128],
                                                   start=True, stop=False), reads=[vtok.g, pt.g], writes=[yb.g])
                    kb.op("pe", lambda e: e.matmul(out=yb.t[:, bb * 128:(bb + 1) * 128], lhsT=Rbf.t[:, b, :], rhs=qT[1].t[0:64, b * 128:(b + 1) * 128],
                                                   start=False, stop=True), reads=[Rbf.g, qT[1].g], writes=[yb.g])
                kb.op("act", lambda e: e.activation(out=ybf.t[:, :], in_=yb.t[:, :], func=AF.Copy), reads=[yb.g], writes=[ybf.g])
                kb.op("act", lambda e: e.activation(out=sqb.t[:, :], in_=yb.t[:, :], func=AF.Square), reads=[yb.g], writes=[sqb.g])
                m1 = bank[4]
                m2 = bank[5]
                kb.op("pe", lambda e: e.matmul(out=m1.t[:, :], lhsT=ones.t[:, :], rhs=ybf.t[:, :], start=True, stop=True), reads=[ones.g, ybf.g], writes=[m1.g])
                kb.op("pe", lambda e: e.matmul(out=m2.t[:, :], lhsT=ones.t[:, :], rhs=sqb.t[:, :], start=True, stop=True), reads=[ones.g, sqb.g], writes=[m2.g])
                kb.op("dve", lambda e: e.tensor_scalar(out=r1.t[:, :], in0=m1.t[:, :], scalar1=1.0 / 128, scalar2=None, op0=ALU.mult), reads=[m1.g], writes=[r1.g])
                kb.op("dve", lambda e: e.tensor_tensor(out=t2.t[:, :], in0=r1.t[:, :], in1=r1.t[:, :], op=ALU.mult), reads=[r1.g], writes=[t2.g])
                kb.op("dve", lambda e: e.scalar_tensor_tensor(out=r2.t[:, :], in0=m2.t[:, :], scalar=1.0 / 128, in1=t2.t[:, :], op0=ALU.mult, op1=ALU.subtract),
                      reads=[m2.g, t2.g], writes=[r2.g])
                kb.op("dve", lambda e: e.tensor_scalar(out=r2.t[:, :], in0=r2.t[:, :], scalar1=LN_EPS, scalar2=None, op0=ALU.add),
                      reads=[r2.g], writes=[r2.g])
                kb.op("act", lambda e: e.activation(out=r2.t[:, :], in_=r2.t[:, :], func=AF.Sqrt), reads=[r2.g], writes=[r2.g])
                kb.op("dve", lambda e: e.reciprocal(out=r2.t[:, :], in_=r2.t[:, :]), reads=[r2.g], writes=[r2.g])
                sg = t2
                kb.op("act", lambda e: e.activation(out=sg.t[:, :], in_=gb.t[:, :], func=AF.Silu), reads=[gb.g], writes=[sg.g])
                kb.op("dve", lambda e: e.tensor_tensor(out=t1.t[:, :], in0=yb.t[:, :], in1=r1.t[:, :], op=ALU.subtract), reads=[yb.g, r1.g], writes=[t1.g])
                kb.op("dve", lambda e: e.tensor_tensor(out=t1.t[:, :], in0=t1.t[:, :], in1=r2.t[:, :], op=ALU.mult), reads=[t1.g, r2.g], writes=[t1.g])
                kb.op("dve", lambda e: e.tensor_tensor(out=OT.t[:, 4 + h, g4 * 512:(g4 + 1) * 512], in0=t1.t[:, :], in1=sg.t[:, :], op=ALU.mult),
                      reads=[t1.g, sg.g], writes=[OT.g])
        kb.barrier()
        if dbg is not None:
            kb.dma("sp", out=dbg[:, :, :], in_=OT.t[:, :, :], reads=[OT.g], pool="st")
            kb.barrier()


def phase_out_ln(kb, nc, io, OT, xT, ident, w_ap, xres_ap, gam_ap, bet_ap, xout_ap, xout_reg):
    with ExitStack() as es:
        wo = sbt(nc, es, "o_w", [128, 8, 1024], BF16)
        gam = sbt(nc, es, "o_gam", [128, 1024], F32)
        bet = sbt(nc, es, "o_bet", [128, 1024], F32)
        xin = [sbt(nc, es, "o_x%d" % i, [128, 1024], F32) for i in range(2)]
        z = [sbt(nc, es, "o_z%d" % i, [128, 1024], F32) for i in range(2)]
        xo = [sbt(nc, es, "o_xo%d" % i, [128, 1024], F32) for i in range(2)]
        xbf = [sbt(nc, es, "o_xb%d" % i, [128, 1024], BF16) for i in range(2)]
        stats = sbt(nc, es, "o_stats", [128, 12], F32)
        mv = sbt(nc, es, "o_mv", [128, 2], F32)
        rstd = sbt(nc, es, "o_rstd", [128, 1], F32)
        mm = [[pst(nc, es, "o_mm%d%d" % (i, j), [128, 512], F32) for j in range(2)] for i in range(2)]
        ptr = [pst(nc, es, "o_pt%d" % i, [128, 1024], BF16) for i in range(2)]
        kb.dma("pool", out=wo.t[:, :, :], in_=w_ap.rearrange("(kc p) n -> p kc n", p=128), writes=[wo.g])
        load_bcast(kb, gam, gam_ap)
        load_bcast(kb, bet, bet_ap)
        for b in range(NB):
            xi = xin[b % 2]
            kb.dma("sp", out=xi.t[:, :], in_=xres_ap[b * 128:(b + 1) * 128, :], writes=[xi.g])
            for hf in range(2):
                for fc in range(8):
                    kb.op("pe", lambda e: e.matmul(out=mm[b % 2][hf].t[:, :], lhsT=OT.t[:, fc, b * 128:(b + 1) * 128],
                                                   rhs=wo.t[:, fc, hf * 512:(hf + 1) * 512], start=(fc == 0), stop=(fc == 7)),
                          reads=[OT.g, wo.g], writes=[mm[b % 2][hf].g])
            zz = z[b % 2]
            for hf in range(2):
                kb.op("dve", lambda e: e.scalar_tensor_tensor(out=zz.t[:, hf * 512:(hf + 1) * 512], in0=xi.t[:, hf * 512:(hf + 1) * 512], scalar=DN_ALPHA,
                                                              in1=mm[b % 2][hf].t[:, :], op0=ALU.mult, op1=ALU.add),
                      reads=[xi.g, mm[b % 2][hf].g], writes=[zz.g])
            ln_block(kb, zz, gam, bet, xo[b % 2], stats, mv, rstd, LN_EPS)
            kb.dma("sp", out=xout_ap[b * 128:(b + 1) * 128, :], in_=xo[b % 2].t[:, :], reads=[xo[b % 2].g], writes=[xout_reg], pool="st")
            transpose_to_fm(kb, xo[b % 2], xbf[b % 2], xT, b, ident, ptr[b % 2])
        kb.barrier()


def phase_ffn(kb, nc, io, layer, xT, ident, xres_ap, xres_reg, xout_ap, xout_reg, G_ap, G_reg, want_T):
    Wup = io["ffn_w_up"][layer].rearrange("(kc p) n -> p kc n", p=128)
    Wdn = io["ffn_w_down"][layer].rearrange("(fc p) n -> p fc n", p=128)
    with ExitStack() as es:
        wu = [sbt(nc, es, "f_wu%d" % i, [128, 8, 256], BF16) for i in range(2)]
        cw = sbt(nc, es, "f_cw", [128, NFC, 3], F32)
        cb = sbt(nc, es, "f_cb", [128, NFC], F32)
        ubuf = [sbt(nc, es, "f_ub%d" % i, [128, 514], F32) for i in range(2)]
        cbuf = [sbt(nc, es, "f_c%d" % i, [128, 512], F32) for i in range(2)]
        gl = [sbt(nc, es, "f_gl%d" % i, [128, 512], F32) for i in range(2)]
        gt = [sbt(nc, es, "f_gt%d" % i, [128, 512], BF16) for i in range(3)]
        pu = [pst(nc, es, "f_pu%d" % i, [128, 512], F32) for i in range(3)]
        pv = [pst(nc, es, "f_pv%d" % i, [128, 512], F32) for i in range(3)]
        for j in range(3):
            kb.dma("sp", out=cw.t[:, :, j], in_=io["ffn_conv_w"][layer][j].rearrange("(fc p) -> p fc", p=128), writes=[cw.g],
                   allow_slow_non_contiguous=True)
        kb.dma("sp", out=cb.t[:, :], in_=io["ffn_conv_b"][layer].rearrange("(fc p) -> p fc", p=128), writes=[cb.g],
               allow_slow_non_contiguous=True)
        it = 0
        for fc in range(NFC):
            w = wu[fc % 2]
            kb.dma("pool", out=w.t[:, :, 0:128], in_=Wup[:, :, fc * 128:(fc + 1) * 128], writes=[w.g])
            kb.dma("pool", out=w.t[:, :, 128:256], in_=Wup[:, :, FF + fc * 128:FF + (fc + 1) * 128], writes=[w.g])
            for tt in range(8):
                u_ps = pu[it % 3]
                v_ps = pv[it % 3]
                ub = ubuf[it % 2]
                ubn = ubuf[(it + 1) % 2]
                c = cbuf[it % 2]
                g_ = gl[it % 2]
                go = gt[it % 3]
                for kc in range(8):
                    kb.op("pe", lambda e: e.matmul(out=u_ps.t[:, :], lhsT=w.t[:, kc, 0:128], rhs=xT.t[:, kc, tt * 512:(tt + 1) * 512],
                                                   start=(kc == 0), stop=(kc == 7)), reads=[w.g, xT.g], writes=[u_ps.g])
                for kc in range(8):
                    kb.op("pe", lambda e: e.matmul(out=v_ps.t[:, :], lhsT=w.t[:, kc, 128:256], rhs=xT.t[:, kc, tt * 512:(tt + 1) * 512],
                                                   start=(kc == 0), stop=(kc == 7)), reads=[w.g, xT.g], writes=[v_ps.g])
                if tt == 0:
                    kb.op("dve", lambda e: e.memset(ub.t[:, 0:2], 0.0), writes=[ub.g])
                kb.op("act", lambda e: e.activation(out=ub.t[:, 2:514], in_=u_ps.t[:, :], func=AF.Copy), reads=[u_ps.g], writes=[ub.g])
                if tt < 7:
                    kb.op("dve", lambda e: e.tensor_copy(out=ubn.t[:, 0:2], in_=ub.t[:, 512:514]), reads=[ub.g], writes=[ubn.g])
                kb.op("act", lambda e: e.activation(out=c.t[:, :], in_=u_ps.t[:, :], func=AF.Identity, scale=cw.t[:, fc, 2:3], bias=cb.t[:, fc:fc + 1]),
                      reads=[u_ps.g, cw.g, cb.g], writes=[c.g])
                kb.op("dve", lambda e: e.scalar_tensor_tensor(out=c.t[:, :], in0=ub.t[:, 1:513], scalar=cw.t[:, fc, 1:2], in1=c.t[:, :],
                                                              op0=ALU.mult, op1=ALU.add), reads=[ub.g, cw.g, c.g], writes=[c.g])
                kb.op("dve", lambda e: e.scalar_tensor_tensor(out=c.t[:, :], in0=ub.t[:, 0:512], scalar=cw.t[:, fc, 0:1], in1=c.t[:, :],
                                                              op0=ALU.mult, op1=ALU.add), reads=[ub.g, cw.g, c.g], writes=[c.g])
                kb.op("act", lambda e: e.activation(out=g_.t[:, :], in_=c.t[:, :], func=AF.Gelu), reads=[c.g], writes=[g_.g])
                kb.op("dve", lambda e: e.tensor_tensor(out=go.t[:, :], in0=v_ps.t[:, :], in1=g_.t[:, :], op=ALU.mult), reads=[v_ps.g, g_.g], writes=[go.g])
                kb.dma("sp", out=G_ap[fc * 128:(fc + 1) * 128, tt * 512:(tt + 1) * 512], in_=go.t[:, :], reads=[go.g], writes=[G_reg], pool="st")
                it += 1
        kb.barrier()
    Gv = G_ap.rearrange("(fc p) t -> p fc t", p=128)
    with ExitStack() as es:
        wd = sbt(nc, es, "g_wd", [128, NFC, 1024], BF16)
        gin = [sbt(nc, es, "g_gin%d" % i, [128, NFC, 512], BF16) for i in range(2)]
        gam = sbt(nc, es, "g_gam", [128, 1024], F32)
        bet = sbt(nc, es, "g_bet", [128, 1024], F32)
        xin = [sbt(nc, es, "g_x%d" % i, [128, 1024], F32) for i in range(2)]
        z = [sbt(nc, es, "g_z%d" % i, [128, 1024], F32) for i in range(2)]
        xo = [sbt(nc, es, "g_xo%d" % i, [128, 1024], F32) for i in range(2)]
        xbf = [sbt(nc, es, "g_xb%d" % i, [128, 1024], BF16) for i in range(2)]
        stats = sbt(nc, es, "g_stats", [128, 12], F32)
        mv = sbt(nc, es, "g_mv", [128, 2], F32)
        rstd = sbt(nc, es, "g_rstd", [128, 1], F32)
        mm = [[pst(nc, es, "g_mm%d%d" % (i, j), [128, 512], F32) for j in range(2)] for i in range(2)]
        ptr = [pst(nc, es, "g_pt%d" % i, [128, 1024], BF16) for i in range(2)]
        for q4 in range(2):
            kb.dma("pool", out=wd.t[:, q4 * 11:(q4 + 1) * 11, :], in_=Wdn[:, q4 * 11:(q4 + 1) * 11, :], writes=[wd.g])
        load_bcast(kb, gam, io["ln_ffn_g"][layer])
        load_bcast(kb, bet, io["ln_ffn_b"][layer])
        for tt in range(8):
            gi = gin[tt % 2]
            for q4 in range(2):
                kb.dma("sp", out=gi.t[:, q4 * 11:(q4 + 1) * 11, :], in_=Gv[:, q4 * 11:(q4 + 1) * 11, tt * 512:(tt + 1) * 512],
                       reads=[G_reg], writes=[gi.g])
            for bb in range(4):
                b = tt * 4 + bb
                xi = xin[b % 2]
                kb.dma("sp", out=xi.t[:, :], in_=xres_ap[b * 128:(b + 1) * 128, :], reads=[xres_reg], writes=[xi.g])
                for hf in range(2):
                    for fc in range(NFC):
                        kb.op("pe", lambda e: e.matmul(out=mm[b % 2][hf].t[:, :], lhsT=gi.t[:, fc, bb * 128:(bb + 1) * 128],
                                                       rhs=wd.t[:, fc, hf * 512:(hf + 1) * 512], start=(fc == 0), stop=(fc == NFC - 1)),
                              reads=[gi.g, wd.g], writes=[mm[b % 2][hf].g])
                zz = z[b % 2]
                for hf in range(2):
                    kb.op("dve", lambda e: e.scalar_tensor_tensor(out=zz.t[:, hf * 512:(hf + 1) * 512], in0=xi.t[:, hf * 512:(hf + 1) * 512], scalar=DN_ALPHA,
                                                                  in1=mm[b % 2][hf].t[:, :], op0=ALU.mult, op1=ALU.add),
                          reads=[xi.g, mm[b % 2][hf].g], writes=[zz.g])
                ln_block(kb, zz, gam, bet, xo[b % 2], stats, mv, rstd, LN_EPS)
                kb.dma("sp", out=xout_ap[b * 128:(b + 1) * 128, :], in_=xo[b % 2].t[:, :], reads=[xo[b % 2].g], writes=[xout_reg], pool="st")
                if want_T:
                    transpose_to_fm(kb, xo[b % 2], xbf[b % 2], xT, b, ident, ptr[b % 2])
        kb.barrier()


def phase_rwkv_a(kb, nc, io, xT, scr, scr_reg):
    Wrkv = io["od_w_rkv"][0].rearrange("n (kc p) e -> p n kc e", p=128)
    with ExitStack() as es:
        wr = sbt(nc, es, "ra_w", [128, 3, 8, 1024], BF16)
        l1 = sbt(nc, es, "ra_l1", [128, 8, 288], BF16)
        w2 = sbt(nc, es, "ra_w2", [64, 1024], BF16)
        a2 = sbt(nc, es, "ra_a2", [64, 1024], BF16)
        g2a = sbt(nc, es, "ra_g2a", [128, 1024], BF16)
        g2b = sbt(nc, es, "ra_g2b", [32, 1024], BF16)
        w0b = sbt(nc, es, "ra_w0b", [128, 1024], F32)
        a0b = sbt(nc, es, "ra_a0b", [128, 1024], F32)
        mu = sbt(nc, es, "ra_mu", [128, 6, 8], F32)
        xx = [sbt(nc, es, "ra_xx%d" % i, [128, 8, 128], F32) for i in range(2)]
        mixT = [[sbt(nc, es, "ra_mix%d_%d" % (n, i), [128, 8, 128], BF16) for i in range(2)] for n in range(6)]
        lo1 = [sbt(nc, es, "ra_lo%d" % i, [128, 128], BF16) for i in range(4)]
        lo2 = sbt(nc, es, "ra_l32", [32, 128], BF16)
        outF = [sbt(nc, es, "ra_o%d" % i, [128, 1024], F32) for i in range(4)]
        P = [pst(nc, es, "ra_p%d" % i, [128, 1024], F32) for i in range(3)]
        Q = [pst(nc, es, "ra_q%d" % i, [128, 512], F32) for i in range(2)]
        for n in range(3):
            kb.dma("pool", out=wr.t[:, n, :, :], in_=Wrkv[:, n, :, :], writes=[wr.g])
        kb.dma("pool", out=l1.t[:, :, 0:64], in_=io["od_w1"].rearrange("(kc p) e -> p kc e", p=128), writes=[l1.g])
        kb.dma("pool", out=l1.t[:, :, 64:128], in_=io["od_a1"].rearrange("(kc p) e -> p kc e", p=128), writes=[l1.g])
        kb.dma("pool", out=l1.t[:, :, 128:288], in_=io["od_g1"].rearrange("(kc p) e -> p kc e", p=128), writes=[l1.g])
        kb.dma("pool", out=w2.t[:, :], in_=io["od_w2"][:, :], writes=[w2.g])
        kb.dma("pool", out=a2.t[:, :], in_=io["od_a2"][:, :], writes=[a2.g])
        kb.dma("pool", out=g2a.t[:, :], in_=io["od_g2"][0:128, :], writes=[g2a.g])
        kb.dma("pool", out=g2b.t[:, :], in_=io["od_g2"][128:160, :], writes=[g2b.g])
        load_bcast(kb, w0b, io["od_w0"][0])
        load_bcast(kb, a0b, io["od_a0"][0])
        for n in range(6):
            kb.dma("sp", out=mu.t[:, n, :], in_=io["od_mu"][0, n].rearrange("(kc p) -> p kc", p=128), writes=[mu.g],
                   allow_slow_non_contiguous=True)
        oi = 0
        for b in range(NB):
            t0 = b * 128
            x_ = xx[b % 2]
            if b == 0:
                kb.op("dve", lambda e: e.tensor_tensor(out=x_.t[:, :, 1:128], in0=xT.t[:, :, 0:127], in1=xT.t[:, :, 1:128], op=ALU.subtract),
                      reads=[xT.g], writes=[x_.g])
                kb.op("dve", lambda e: e.tensor_scalar(out=x_.t[:, :, 0:1], in0=xT.t[:, :, 0:1], scalar1=-1.0, scalar2=None, op0=ALU.mult),
                      reads=[xT.g], writes=[x_.g])
            else:
                kb.op("dve", lambda e: e.tensor_tensor(out=x_.t[:, :, :], in0=xT.t[:, :, t0 - 1:t0 + 127], in1=xT.t[:, :, t0:t0 + 128], op=ALU.subtract),
                      reads=[xT.g], writes=[x_.g])
            mx = [mixT[n][b % 2] for n in range(6)]
            for n in range(6):
                for kc in range(8):
                    eng = "dve"
                    kb.op(eng, lambda e: e.scalar_tensor_tensor(out=mx[n].t[:, kc, :], in0=x_.t[:, kc, :], scalar=mu.t[:, n, kc:kc + 1],
                                                                in1=xT.t[:, kc, t0:t0 + 128], op0=ALU.mult, op1=ALU.add),
                          reads=[x_.g, mu.g, xT.g], writes=[mx[n].g])

            def store(idx, ps, pre=None, func=None):
                nonlocal oi
                o = outF[oi % 4]
                oi += 1
                if pre is not None:
                    kb.op("dve", lambda e: e.tensor_tensor(out=o.t[:, :], in0=ps.t[:, :], in1=pre.t[:, :], op=ALU.add), reads=[ps.g, pre.g], writes=[o.g])
                    kb.op("act", lambda e: e.activation(out=o.t[:, :], in_=o.t[:, :], func=func), reads=[o.g], writes=[o.g])
                else:
                    kb.op("act", lambda e: e.activation(out=o.t[:, :], in_=ps.t[:, :], func=AF.Copy), reads=[ps.g], writes=[o.g])
                kb.dma("sp", out=scr[idx][t0:t0 + 128, :], in_=o.t[:, :], reads=[o.g], writes=[scr_reg], pool="st")

            for n in range(3):
                ps = P[n]
                for hf in range(2):
                    for kc in range(8):
                        kb.op("pe", lambda e: e.matmul(out=ps.t[:, hf * 512:(hf + 1) * 512], lhsT=mx[n].t[:, kc, :], rhs=wr.t[:, n, kc, hf * 512:(hf + 1) * 512],
                                                       start=(kc == 0), stop=(kc == 7)), reads=[mx[n].g, wr.g], writes=[ps.g])
                store(n, ps)
            q = Q[0]
            for kc in range(8):
                kb.op("pe", lambda e: e.matmul(out=q.t[0:64, 0:128], lhsT=l1.t[:, kc, 0:64], rhs=mx[3].t[:, kc, :], start=(kc == 0), stop=(kc == 7)),
                      reads=[l1.g, mx[3].g], writes=[q.g])
            for kc in range(8):
                kb.op("pe", lambda e: e.matmul(out=q.t[0:64, 128:256], lhsT=l1.t[:, kc, 64:128], rhs=mx[4].t[:, kc, :], start=(kc == 0), stop=(kc == 7)),
                      reads=[l1.g, mx[4].g], writes=[q.g])
            for kc in range(8):
                kb.op("pe", lambda e: e.matmul(out=q.t[:, 256:384], lhsT=l1.t[:, kc, 128:256], rhs=mx[5].t[:, kc, :], start=(kc == 0), stop=(kc == 7)),
                      reads=[l1.g, mx[5].g], writes=[q.g])
            for kc in range(8):
                kb.op("pe", lambda e: e.matmul(out=q.t[0:32, 384:512], lhsT=l1.t[:, kc, 256:288], rhs=mx[5].t[:, kc, :], start=(kc == 0), stop=(kc == 7)),
                      reads=[l1.g, mx[5].g], writes=[q.g])
            tw, al, sg1 = lo1[0], lo1[1], lo1[2]
            kb.op("act", lambda e: e.activation(out=tw.t[0:64, :], in_=q.t[0:64, 0:128], func=AF.Tanh), reads=[q.g], writes=[tw.g])
            kb.op("act", lambda e: e.activation(out=al.t[0:64, :], in_=q.t[0:64, 128:256], func=AF.Copy), reads=[q.g], writes=[al.g])
            kb.op("act", lambda e: e.activation(out=sg1.t[:, :], in_=q.t[:, 256:384], func=AF.Sigmoid), reads=[q.g], writes=[sg1.g])
            kb.op("act", lambda e: e.activation(out=lo2.t[:, :], in_=q.t[0:32, 384:512], func=AF.Sigmoid), reads=[q.g], writes=[lo2.g])
            ps = P[0]
            for hf in range(2):
                kb.op("pe", lambda e: e.matmul(out=ps.t[:, hf * 512:(hf + 1) * 512], lhsT=tw.t[0:64, :], rhs=w2.t[:, hf * 512:(hf + 1) * 512], start=True, stop=True),
                      reads=[tw.g, w2.g], writes=[ps.g])
            store(3, ps, pre=w0b, func=AF.Sigmoid)
            ps = P[1]
            for hf in range(2):
                kb.op("pe", lambda e: e.matmul(out=ps.t[:, hf * 512:(hf + 1) * 512], lhsT=al.t[0:64, :], rhs=a2.t[:, hf * 512:(hf + 1) * 512], start=True, stop=True),
                      reads=[al.g, a2.g], writes=[ps.g])
            store(4, ps, pre=a0b, func=AF.Sigmoid)
            ps = P[2]
            for hf in range(2):
                kb.op("pe", lambda e: e.matmul(out=ps.t[:, hf * 512:(hf + 1) * 512], lhsT=sg1.t[:, :], rhs=g2a.t[:, hf * 512:(hf + 1) * 512], start=True, stop=False),
                      reads=[sg1.g, g2a.g], writes=[ps.g])
                kb.op("pe", lambda e: e.matmul(out=ps.t[:, hf * 512:(hf + 1) * 512], lhsT=lo2.t[:, :], rhs=g2b.t[:, hf * 512:(hf + 1) * 512], start=False, stop=True),
                      reads=[lo2.g, g2b.g], writes=[ps.g])
            store(5, ps)
        kb.barrier()


def phase_rwkv_b(kb, nc, io, XT3, xt3_reg, ident, ones, tri, scr, scr_reg, xres_ap, xres_reg, xout_ap, xout_reg):
    H3 = lambda ap: ap.rearrange("p (h d) -> p h d", h=16)
    with ExitStack() as es:
        def F(name):
            return sbt(nc, es, "rb_" + name, [128, 1024], F32)

        def B(name):
            return sbt(nc, es, "rb_" + name, [128, 1024], BF16)

        wo = sbt(nc, es, "rb_wo", [128, 8, 1024], BF16)
        vec = {}
        for nm, ap in (("k_k", io["od_k_k"][0]), ("k_a", io["od_k_a"][0]), ("r_k", io["od_r_k"][0]), ("lnx_g", io["od_lnx_g"][0]),
                       ("lnx_b", io["od_lnx_b"][0]), ("lng", io["ln_mix_g"][1]), ("lnb", io["ln_mix_b"][1])):
            vec[nm] = F("v_" + nm)
            load_bcast(kb, vec[nm], ap)
        kb.dma("pool", out=wo.t[:, :, :], in_=io["od_w_out"].rearrange("(kc p) n -> p kc n", p=128), writes=[wo.g])
        msk = {}
        for nm in ("c_su4", "c_sl4", "c_iu4", "c_id4"):
            msk[nm] = sbt(nc, es, "rb_" + nm, [128, 512], BF16)
            kb.dma("sp", out=msk[nm].t[:, :], in_=io[nm][:, :], writes=[msk[nm].g])
        inb = [[F("in%d_%d" % (i, j)) for i in range(6)] for j in range(2)]
        x3ts = [B("x3t0"), B("x3t1")]
        f1, f2, f3 = F("f1"), F("f2"), F("f3")
        vB, lhi, llo = B("vB"), B("lhi"), B("llo")
        rtB, atB, btB, ktB, bpB, kpB = B("rt"), B("at"), B("bt"), B("kt"), B("bp"), B("kp")
        rT, aT, bT, kTt = B("rT"), B("aT"), B("bT"), B("kT")
        Arb, Ark = [B("Arb0"), B("Arb1")], [B("Ark0"), B("Ark1")]
        Xall = [B("X0"), B("X1")]
        AhT = B("AhT")
        W1b, Ub, ygB = rtB, btB, ktB
        tmpg = [[sbt(nc, es, "rb_tg%d_%d" % (g, i), [128, 512], BF16) for i in range(10)] for g in range(4)]
        tmpb = tmpg[0]
        small = sbt(nc, es, "rb_small", [128, 96], F32)
        eLC = sbt(nc, es, "rb_eLC", [128, 8], F32)
        Hs = sbt(nc, es, "rb_H", [128, 8, 64], F32)
        Hb = sbt(nc, es, "rb_Hb", [128, 8, 64], BF16)
        xo = f3
        xbf = lhi
        stats = sbt(nc, es, "rb_stats", [128, 12], F32)
        mv = sbt(nc, es, "rb_mv", [128, 2], F32)
        rstd = sbt(nc, es, "rb_rstd", [128, 1], F32)
        P = [pst(nc, es, "rb_p%d" % i, [128, 1024], F32) for i in range(3)]
        QT = pst(nc, es, "rb_qt", [128, 1024], BF16)
        Q1 = pst(nc, es, "rb_q1", [128, 512], F32)
        kb.op("dve", lambda e: e.memset(Hs.t[:, :, :], 0.0), writes=[Hs.g])
        kb.op("dve", lambda e: e.memset(Hb.t[:, :, :], 0.0), writes=[Hb.g])

        def bc16(t, c0):
            return small.t[:, c0:c0 + 16].unsqueeze(2).to_broadcast([128, 16, 64])

        for b in range(NB):
            t0 = b * 128
            if b == 0:
                for idx, dst in enumerate(inb[0]):
                    kb.dma("sp", out=dst.t[:, :], in_=scr[idx][0:128, :], reads=[scr_reg], writes=[dst.g])
            rF, kF, vF, wF, aF, gF = inb[b % 2]
            xin = rF
            if b + 1 < NB:
                for idx, dst in enumerate(inb[(b + 1) % 2]):
                    kb.dma("sp", out=dst.t[:, :], in_=scr[idx][t0 + 128:t0 + 256, :], reads=[scr_reg], writes=[dst.g])
            kb.op("act", lambda e: e.activation(out=vB.t[:, :], in_=vF.t[:, :], func=AF.Copy), reads=[vF.g], writes=[vB.g])
            kb.op("dve", lambda e: e.tensor_scalar(out=wF.t[:, :], in0=wF.t[:, :], scalar1=-math.exp(-0.5), scalar2=None, op0=ALU.mult), reads=[wF.g], writes=[wF.g])
            kb.op("act", lambda e: e.activation(out=lhi.t[:, :], in_=wF.t[:, :], func=AF.Copy), reads=[wF.g], writes=[lhi.g])
            kb.op("dve", lambda e: e.tensor_tensor(out=llo.t[:, :], in0=wF.t[:, :], in1=lhi.t[:, :], op=ALU.subtract), reads=[wF.g, lhi.g], writes=[llo.g])
            kb.op("dve", lambda e: e.tensor_tensor(out=f1.t[:, :], in0=kF.t[:, :], in1=vec["k_k"].t[:, :], op=ALU.mult), reads=[kF.g, vec["k_k"].g], writes=[f1.g])
            kb.op("act", lambda e: e.activation(out=f2.t[:, :], in_=f1.t[:, :], func=AF.Square), reads=[f1.g], writes=[f2.g])
            kb.op("dve", lambda e: e.tensor_reduce(out=small.t[:, 0:16], in_=H3(f2.t[:, :]), axis=AX.X, op=ALU.add), reads=[f2.g], writes=[small.g])
            kb.op("act", lambda e: e.activation(out=small.t[:, 0:16], in_=small.t[:, 0:16], func=AF.Sqrt), reads=[small.g], writes=[small.g])
            kb.op("dve", lambda e: e.tensor_scalar(out=small.t[:, 0:16], in0=small.t[:, 0:16], scalar1=1e-12, scalar2=None, op0=ALU.max), reads=[small.g], writes=[small.g])
            kb.op("dve", lambda e: e.reciprocal(out=small.t[:, 0:16], in_=small.t[:, 0:16]), reads=[small.g], writes=[small.g])
            kb.op("dve", lambda e: e.tensor_tensor(out=H3(f1.t[:, :]), in0=H3(f1.t[:, :]), in1=bc16(small, 0), op=ALU.mult), reads=[f1.g, small.g], writes=[f1.g])
            kb.op("dve", lambda e: e.scalar_tensor_tensor(out=f2.t[:, :], in0=aF.t[:, :], scalar=-1.0, in1=vec["k_a"].t[:, :], op0=ALU.add, op1=ALU.mult),
                  reads=[aF.g, vec["k_a"].g], writes=[f2.g])
            kb.op("dve", lambda e: e.scalar_tensor_tensor(out=f2.t[:, :], in0=f2.t[:, :], scalar=1.0, in1=kF.t[:, :], op0=ALU.add, op1=ALU.mult),
                  reads=[f2.g, kF.g], writes=[f2.g])
            kb.op("dve", lambda e: e.tensor_tensor(out=kF.t[:, :], in0=f1.t[:, :], in1=aF.t[:, :], op=ALU.mult), reads=[f1.g, aF.g], writes=[kF.g])
            if RB_STOP == 1:
                kb.barrier()
                return
            for hf in range(2):
                sl = slice(hf * 512, (hf + 1) * 512)
                kb.op("pe", lambda e: e.matmul(out=P[0].t[:, sl], lhsT=tri.t[:, :], rhs=lhi.t[:, sl], start=True, stop=False), reads=[tri.g, lhi.g], writes=[P[0].g])
                kb.op("pe", lambda e: e.matmul(out=P[0].t[:, sl], lhsT=tri.t[:, :], rhs=llo.t[:, sl], start=False, stop=True), reads=[tri.g, llo.g], writes=[P[0].g])
                kb.op("pe", lambda e: e.matmul(out=P[1].t[:, sl], lhsT=ones.t[:, :], rhs=lhi.t[:, sl], start=True, stop=False), reads=[ones.g, lhi.g], writes=[P[1].g])
                kb.op("pe", lambda e: e.matmul(out=P[1].t[:, sl], lhsT=ones.t[:, :], rhs=llo.t[:, sl], start=False, stop=True), reads=[ones.g, llo.g], writes=[P[1].g])
            for hp in range(8):
                kb.op("pe", lambda e: e.matmul(out=Q1.t[:, hp:hp + 1], lhsT=lhi.t[:, hp * 128:(hp + 1) * 128], rhs=ones.t[:, 0:1], start=True, stop=False),
                      reads=[lhi.g, ones.g], writes=[Q1.g])
                kb.op("pe", lambda e: e.matmul(out=Q1.t[:, hp:hp + 1], lhsT=llo.t[:, hp * 128:(hp + 1) * 128], rhs=ones.t[:, 0:1], start=False, stop=True),
                      reads=[llo.g, ones.g], writes=[Q1.g])
            kb.op("act", lambda e: e.activation(out=eLC.t[:, :], in_=Q1.t[:, 0:8], func=AF.Exp), reads=[Q1.g], writes=[eLC.g])
            if RB_STOP == 2:
                kb.barrier()
                return
            kb.op("act", lambda e: e.activation(out=aF.t[:, :], in_=P[0].t[:, :], func=AF.Copy), reads=[P[0].g], writes=[aF.g])
            kb.op("act", lambda e: e.activation(out=f3.t[:, :], in_=P[0].t[:, :], func=AF.Exp), reads=[P[0].g], writes=[f3.g])
            kb.op("dve", lambda e: e.tensor_tensor(out=rtB.t[:, :], in0=rF.t[:, :], in1=f3.t[:, :], op=ALU.mult), reads=[rF.g, f3.g], writes=[rtB.g])
            kb.op("dve", lambda e: e.tensor_tensor(out=wF.t[:, :], in0=aF.t[:, :], in1=wF.t[:, :], op=ALU.subtract), reads=[aF.g, wF.g], writes=[wF.g])
            kb.op("act", lambda e: e.activation(out=wF.t[:, :], in_=wF.t[:, :], func=AF.Exp), reads=[wF.g], writes=[wF.g])
            kb.op("dve", lambda e: e.scalar_tensor_tensor(out=atB.t[:, :], in0=f1.t[:, :], scalar=-1.0, in1=wF.t[:, :], op0=ALU.mult, op1=ALU.mult),
                  reads=[f1.g, wF.g], writes=[atB.g])
            kb.op("act", lambda e: e.activation(out=f3.t[:, :], in_=aF.t[:, :], func=AF.Exp, scale=-1.0), reads=[aF.g], writes=[f3.g])
            kb.op("dve", lambda e: e.tensor_tensor(out=btB.t[:, :], in0=kF.t[:, :], in1=f3.t[:, :], op=ALU.mult), reads=[kF.g, f3.g], writes=[btB.g])
            kb.op("dve", lambda e: e.tensor_tensor(out=ktB.t[:, :], in0=f2.t[:, :], in1=f3.t[:, :], op=ALU.mult), reads=[f2.g, f3.g], writes=[ktB.g])
            kb.op("dve", lambda e: e.tensor_tensor(out=f3.t[:, :], in0=P[1].t[:, :], in1=aF.t[:, :], op=ALU.subtract), reads=[P[1].g, aF.g], writes=[f3.g])
            kb.op("act", lambda e: e.activation(out=f3.t[:, :], in_=f3.t[:, :], func=AF.Exp), reads=[f3.g], writes=[f3.g])
            kb.op("dve", lambda e: e.tensor_tensor(out=bpB.t[:, :], in0=kF.t[:, :], in1=f3.t[:, :], op=ALU.mult), reads=[kF.g, f3.g], writes=[bpB.g])
            kb.op("dve", lambda e: e.tensor_tensor(out=kpB.t[:, :], in0=f2.t[:, :], in1=f3.t[:, :], op=ALU.mult), reads=[f2.g, f3.g], writes=[kpB.g])
            kb.op("dve", lambda e: e.tensor_tensor(out=f3.t[:, :], in0=rF.t[:, :], in1=f2.t[:, :], op=ALU.mult), reads=[rF.g, f2.g], writes=[f3.g])
            kb.op("dve", lambda e: e.tensor_tensor(out=f3.t[:, :], in0=f3.t[:, :], in1=vec["r_k"].t[:, :], op=ALU.mult), reads=[f3.g, vec["r_k"].g], writes=[f3.g])
            kb.op("dve", lambda e: e.tensor_reduce(out=small.t[:, 16:32], in_=H3(f3.t[:, :]), axis=AX.X, op=ALU.add), reads=[f3.g], writes=[small.g])
            kb.op("dve", lambda e: e.tensor_tensor(out=H3(vF.t[:, :]), in0=H3(vF.t[:, :]), in1=bc16(small, 16), op=ALU.mult), reads=[vF.g, small.g], writes=[vF.g])
            if RB_STOP == 3:
                kb.barrier()
                return
            for src, dst in ((rtB, rT), (atB, aT), (btB, bT), (ktB, kTt)):
                for hp in range(8):
                    kb.op("pe", lambda e: e.transpose(out=QT.t[:, hp * 128:(hp + 1) * 128], in_=src.t[:, hp * 128:(hp + 1) * 128], identity=ident.t[:, :]),
                          reads=[src.g, ident.g], writes=[QT.g])
                kb.op("act", lambda e: e.activation(out=dst.t[:, :], in_=QT.t[:, :], func=AF.Copy), reads=[QT.g], writes=[dst.g])

            if RB_STOP == 4:
                kb.barrier()
                return
            def fm(t, h):
                r0 = 64 * (h % 2)
                return t.t[r0:r0 + 64, (h // 2) * 128:(h // 2 + 1) * 128]

            gst = []
            for g4 in range(4):
                heads = [g4 * 4 + i for i in range(4)]
                order = [(0, heads[0]), (2, heads[2]), (1, heads[1]), (3, heads[3])]
                tb = tmpg[g4]
                Nb, NTb = tb[0], tb[1]
                specs = ((bT, aT, Nb, "c_su4"), (aT, bT, NTb, "c_sl4"))
                ps = P[(2 * g4) % 3]
                for si, (la, rb_, dst, mk) in enumerate(specs):
                    off = si * 512
                    for i, h in order:
                        kb.op("pe", lambda e: e.matmul(out=ps.t[:, off + i * 128:off + (i + 1) * 128], lhsT=fm(la, h), rhs=fm(rb_, h), start=True, stop=True),
                              reads=[la.g, rb_.g], writes=[ps.g], rg=(64 * (h % 2), 64))
                    kb.op("dve", lambda e: e.tensor_tensor(out=dst.t[:, :], in0=ps.t[:, off:off + 512], in1=msk[mk].t[:, :], op=ALU.mult),
                          reads=[ps.g, msk[mk].g], writes=[dst.g])
                hi_ = g4 // 2
                co = (g4 % 2) * 512
                ps = P[(2 * g4 + 1) % 3]
                for si, (la, rb_, dst) in enumerate(((bT, rT, Arb[hi_]), (kTt, rT, Ark[hi_]))):
                    off = si * 512
                    for i, h in order:
                        kb.op("pe", lambda e: e.matmul(out=ps.t[:, off + i * 128:off + (i + 1) * 128], lhsT=fm(la, h), rhs=fm(rb_, h), start=True, stop=True),
                              reads=[la.g, rb_.g], writes=[ps.g], rg=(64 * (h % 2), 64))
                    kb.op("dve", lambda e: e.tensor_tensor(out=dst.t[:, co:co + 512], in0=ps.t[:, off:off + 512], in1=msk["c_iu4"].t[:, :], op=ALU.mult),
                          reads=[ps.g, msk["c_iu4"].g], writes=[dst.g])
                X, XT = tb[2], tb[3]
                kb.op("dve", lambda e: e.tensor_tensor(out=X.t[:, :], in0=Nb.t[:, :], in1=msk["c_id4"].t[:, :], op=ALU.add), reads=[Nb.g, msk["c_id4"].g], writes=[X.g])
                kb.op("dve", lambda e: e.tensor_tensor(out=XT.t[:, :], in0=NTb.t[:, :], in1=msk["c_id4"].t[:, :], op=ALU.add), reads=[NTb.g, msk["c_id4"].g], writes=[XT.g])
                gst.append({"X": X, "XT": XT, "P": Nb, "PT": NTb, "pp": 0})
            for it in range(6):
                last = it == 5
                for g4 in range(4):
                    st = gst[g4]
                    tb = tmpg[g4]
                    hi_ = g4 // 2
                    co = (g4 % 2) * 512
                    X, XT, Pm, PTm = st["X"], st["XT"], st["P"], st["PT"]
                    P2, P2T = tb[4 + st["pp"]], tb[6 + st["pp"]]
                    st["pp"] ^= 1
                    psa = P[(2 * g4 + 2 * it) % 3]
                    psb = P[(2 * g4 + 2 * it + 1) % 3]
                    for i in range(4):
                        sl = slice(i * 128, (i + 1) * 128)
                        kb.op("pe", lambda e: e.matmul(out=psa.t[:, sl], lhsT=PTm.t[:, sl], rhs=Pm.t[:, sl], start=True, stop=True), reads=[PTm.g, Pm.g], writes=[psa.g])
                    if not last:
                        for i in range(4):
                            sl = slice(i * 128, (i + 1) * 128)
                            sl2 = slice(512 + i * 128, 512 + (i + 1) * 128)
                            kb.op("pe", lambda e: e.matmul(out=psa.t[:, sl2], lhsT=Pm.t[:, sl], rhs=PTm.t[:, sl], start=True, stop=True), reads=[PTm.g, Pm.g], writes=[psa.g])
                    kb.op("act", lambda e: e.activation(out=P2.t[:, :], in_=psa.t[:, 0:512], func=AF.Copy), reads=[psa.g], writes=[P2.g])
                    if not last:
                        kb.op("act", lambda e: e.activation(out=P2T.t[:, :], in_=psa.t[:, 512:1024], func=AF.Copy), reads=[psa.g], writes=[P2T.g])
                    for i in range(4):
                        sl = slice(i * 128, (i + 1) * 128)
                        kb.op("pe", lambda e: e.matmul(out=psb.t[:, sl], lhsT=XT.t[:, sl], rhs=P2.t[:, sl], start=True, stop=True), reads=[XT.g, P2.g], writes=[psb.g])
                    if not last:
                        for i in range(4):
                            sl = slice(i * 128, (i + 1) * 128)
                            sl2 = slice(512 + i * 128, 512 + (i + 1) * 128)
                            kb.op("pe", lambda e: e.matmul(out=psb.t[:, sl2], lhsT=P2.t[:, sl], rhs=XT.t[:, sl], start=True, stop=True), reads=[XT.g, P2.g], writes=[psb.g])
                    if last:
                        kb.op("dve", lambda e: e.tensor_tensor(out=Xall[hi_].t[:, co:co + 512], in0=psb.t[:, 0:512], in1=X.t[:, :], op=ALU.add),
                              reads=[psb.g, X.g], writes=[Xall[hi_].g])
                    else:
                        Xn, XTn = (tb[8], tb[9]) if (it % 2 == 0) else (tb[2], tb[3])
                        kb.op("dve", lambda e: e.tensor_tensor(out=Xn.t[:, :], in0=psb.t[:, 0:512], in1=X.t[:, :], op=ALU.add), reads=[psb.g, X.g], writes=[Xn.g])
                        kb.op("dve", lambda e: e.tensor_tensor(out=XTn.t[:, :], in0=psb.t[:, 512:1024], in1=XT.t[:, :], op=ALU.add), reads=[psb.g, XT.g], writes=[XTn.g])
                        st["X"], st["XT"], st["P"], st["PT"] = Xn, XTn, P2, P2T
            if RB_STOP == 5:
                kb.barrier()
                return
            for g4 in range(4):
                heads = [g4 * 4 + i for i in range(4)]
                order = [(0, heads[0]), (2, heads[2]), (1, heads[1]), (3, heads[3])]
                Aak = tmpb[2]
                ps = P[2]
                for i, h in ((0, heads[0]), (2, heads[2]), (1, heads[1]), (3, heads[3])):
                    kb.op("pe", lambda e: e.matmul(out=ps.t[:, i * 128:(i + 1) * 128], lhsT=fm(kTt, h), rhs=fm(aT, h), start=True, stop=True),
                          reads=[kTt.g, aT.g], writes=[ps.g], rg=(64 * (h % 2), 64))
                kb.op("dve", lambda e: e.tensor_tensor(out=Aak.t[:, :], in0=ps.t[:, 0:512], in1=msk["c_su4"].t[:, :], op=ALU.mult),
                      reads=[ps.g, msk["c_su4"].g], writes=[Aak.g])
                for i, h in enumerate(heads):
                    kb.op("pe", lambda e: e.matmul(out=P[1].t[:, h * 64:(h + 1) * 64], lhsT=Aak.t[:, i * 128:(i + 1) * 128], rhs=vB.t[:, h * 64:(h + 1) * 64],
                                                   start=True, stop=True), reads=[Aak.g, vB.g], writes=[P[1].g])
                hi_ = g4 // 2
                co = (g4 % 2) * 512
                for i, h in enumerate(heads):
                    hp = h // 2
                    kb.op("pe", lambda e: e.matmul(out=ps.t[:, 512 + i * 128:512 + (i + 1) * 128], lhsT=atB.t[:, hp * 128:(hp + 1) * 128],
                                                   rhs=Xall[hi_].t[:, co + i * 128:co + (i + 1) * 128], start=True, stop=True),
                          reads=[atB.g, Xall[hi_].g], writes=[ps.g])
                v4 = ps.t[:, 512:1024].rearrange("p (j two t) -> p j two t", j=2, two=2)
                o4 = AhT.t[:, g4 * 256:(g4 + 1) * 256].rearrange("p (j t) -> p j t", j=2)
                kb.op("act", lambda e: e.activation(out=o4[0:64, :, :], in_=v4[0:64, :, 0, :], func=AF.Copy), reads=[ps.g], writes=[AhT.g])
                kb.op("act", lambda e: e.activation(out=o4[64:128, :, :], in_=v4[64:128, :, 1, :], func=AF.Copy), reads=[ps.g], writes=[AhT.g])
            kb.op("act", lambda e: e.activation(out=W1b.t[:, :], in_=P[1].t[:, :], func=AF.Copy), reads=[P[1].g], writes=[W1b.g])
            if RB_STOP == 6:
                kb.barrier()
                return
            Hb3 = Hb.t
            for h in range(16):
                hp, r0 = h // 2, 64 * (h % 2)
                hi_, co = h // 8, (h % 8) * 128
                kb.op("pe", lambda e: e.matmul(out=P[0].t[:, h * 64:(h + 1) * 64], lhsT=AhT.t[r0:r0 + 64, hp * 128:(hp + 1) * 128], rhs=Hb3[r0:r0 + 64, hp, :],
                                               start=True, stop=False), reads=[AhT.g, Hb.g], writes=[P[0].g])
                kb.op("pe", lambda e: e.matmul(out=P[0].t[:, h * 64:(h + 1) * 64], lhsT=Xall[hi_].t[:, co:co + 128], rhs=W1b.t[:, h * 64:(h + 1) * 64],
                                               start=False, stop=True), reads=[Xall[hi_].g, W1b.g], writes=[P[0].g])
            kb.op("act", lambda e: e.activation(out=Ub.t[:, :], in_=P[0].t[:, :], func=AF.Copy), reads=[P[0].g], writes=[Ub.g])
            for h in range(16):
                hp, r0 = h // 2, 64 * (h % 2)
                hi_, co = h // 8, (h % 8) * 128
                o = P[2].t[:, h * 64:(h + 1) * 64]
                kb.op("pe", lambda e: e.matmul(out=o, lhsT=fm(rT, h), rhs=Hb3[r0:r0 + 64, hp, :], start=True, stop=False), reads=[rT.g, Hb.g], writes=[P[2].g])
                kb.op("pe", lambda e: e.matmul(out=o, lhsT=Arb[hi_].t[:, co:co + 128], rhs=Ub.t[:, h * 64:(h + 1) * 64], start=False, stop=False),
                      reads=[Arb[hi_].g, Ub.g], writes=[P[2].g])
                kb.op("pe", lambda e: e.matmul(out=o, lhsT=Ark[hi_].t[:, co:co + 128], rhs=vB.t[:, h * 64:(h + 1) * 64], start=False, stop=True),
                      reads=[Ark[hi_].g, vB.g], writes=[P[2].g])
            for hp in range(8):
                sl = slice(hp * 128, (hp + 1) * 128)
                kb.op("pe", lambda e: e.matmul(out=P[1].t[:, sl], lhsT=bpB.t[:, sl], rhs=Ub.t[:, sl], start=True, stop=False), reads=[bpB.g, Ub.g], writes=[P[1].g])
                kb.op("pe", lambda e: e.matmul(out=P[1].t[:, sl], lhsT=kpB.t[:, sl], rhs=vB.t[:, sl], start=False, stop=True), reads=[kpB.g, vB.g], writes=[P[1].g])
            kb.op("dve", lambda e: e.tensor_tensor(out=Hs.t[:, :, :], in0=Hs.t[:, :, :], in1=eLC.t[:, 0:8].unsqueeze(2).to_broadcast([128, 8, 64]), op=ALU.mult),
                  reads=[Hs.g, eLC.g], writes=[Hs.g])
            hv = P[1].t[:, :].rearrange("p (hp two d) -> p hp two d", hp=8, two=2)
            kb.op("dve", lambda e: e.tensor_tensor(out=Hs.t[0:64, :, :], in0=Hs.t[0:64, :, :], in1=hv[0:64, :, 0, :], op=ALU.add), reads=[Hs.g, P[1].g], writes=[Hs.g])
            kb.op("dve", lambda e: e.tensor_tensor(out=Hs.t[64:128, :, :], in0=Hs.t[64:128, :, :], in1=hv[64:128, :, 1, :], op=ALU.add), reads=[Hs.g, P[1].g], writes=[Hs.g])
            kb.op("act", lambda e: e.activation(out=Hb.t[:, :, :], in_=Hs.t[:, :, :], func=AF.Copy), reads=[Hs.g], writes=[Hb.g])
            if RB_STOP == 7:
                kb.barrier()
                return
            Y = P[2]
            kb.op("act", lambda e: e.activation(out=f1.t[:, :], in_=Y.t[:, :], func=AF.Copy), reads=[Y.g], writes=[f1.g])
            kb.op("act", lambda e: e.activation(out=f2.t[:, :], in_=Y.t[:, :], func=AF.Square), reads=[Y.g], writes=[f2.g])
            kb.op("dve", lambda e: e.tensor_reduce(out=small.t[:, 32:48], in_=H3(f1.t[:, :]), axis=AX.X, op=ALU.add), reads=[f1.g], writes=[small.g])
            kb.op("dve", lambda e: e.tensor_reduce(out=small.t[:, 48:64], in_=H3(f2.t[:, :]), axis=AX.X, op=ALU.add), reads=[f2.g], writes=[small.g])
            kb.op("dve", lambda e: e.tensor_scalar(out=small.t[:, 32:64], in0=small.t[:, 32:64], scalar1=1.0 / 64, scalar2=None, op0=ALU.mult), reads=[small.g], writes=[small.g])
            kb.op("dve", lambda e: e.tensor_tensor(out=small.t[:, 64:80], in0=small.t[:, 32:48], in1=small.t[:, 32:48], op=ALU.mult), reads=[small.g], writes=[small.g])
            kb.op("dve", lambda e: e.tensor_tensor(out=small.t[:, 64:80], in0=small.t[:, 48:64], in1=small.t[:, 64:80], op=ALU.subtract), reads=[small.g], writes=[small.g])
            kb.op("dve", lambda e: e.tensor_scalar(out=small.t[:, 64:80], in0=small.t[:, 64:80], scalar1=64e-5, scalar2=None, op0=ALU.add), reads=[small.g], writes=[small.g])
            kb.op("act", lambda e: e.activation(out=small.t[:, 64:80], in_=small.t[:, 64:80], func=AF.Sqrt), reads=[small.g], writes=[small.g])
            kb.op("dve", lambda e: e.reciprocal(out=small.t[:, 64:80], in_=small.t[:, 64:80]), reads=[small.g], writes=[small.g])
            kb.op("dve", lambda e: e.tensor_tensor(out=H3(f1.t[:, :]), in0=H3(f1.t[:, :]), in1=bc16(small, 32), op=ALU.subtract), reads=[f1.g, small.g], writes=[f1.g])
            kb.op("dve", lambda e: e.tensor_tensor(out=H3(f1.t[:, :]), in0=H3(f1.t[:, :]), in1=bc16(small, 64), op=ALU.mult), reads=[f1.g, small.g], writes=[f1.g])
            kb.op("dve", lambda e: e.tensor_tensor(out=f1.t[:, :], in0=f1.t[:, :], in1=vec["lnx_g"].t[:, :], op=ALU.mult), reads=[f1.g, vec["lnx_g"].g], writes=[f1.g])
            kb.op("dve", lambda e: e.tensor_tensor(out=f1.t[:, :], in0=f1.t[:, :], in1=vec["lnx_b"].t[:, :], op=ALU.add), reads=[f1.g, vec["lnx_b"].g], writes=[f1.g])
            kb.op("dve", lambda e: e.tensor_tensor(out=f1.t[:, :], in0=f1.t[:, :], in1=vF.t[:, :], op=ALU.add), reads=[f1.g, vF.g], writes=[f1.g])
            kb.op("dve", lambda e: e.tensor_tensor(out=ygB.t[:, :], in0=f1.t[:, :], in1=gF.t[:, :], op=ALU.mult), reads=[f1.g, gF.g], writes=[ygB.g])
            if RB_STOP == 8:
                kb.barrier()
                return
            for kc in range(8):
                kb.op("pe", lambda e: e.transpose(out=QT.t[:, kc * 128:(kc + 1) * 128], in_=ygB.t[:, kc * 128:(kc + 1) * 128], identity=ident.t[:, :]),
                      reads=[ygB.g, ident.g], writes=[QT.g])
            ygT = aT
            kb.op("act", lambda e: e.activation(out=ygT.t[:, :], in_=QT.t[:, :], func=AF.Copy), reads=[QT.g], writes=[ygT.g])
            kb.dma("sp", out=xin.t[:, :], in_=xres_ap[t0:t0 + 128, :], reads=[xres_reg], writes=[xin.g])
            for hf in range(2):
                for fc in range(8):
                    kb.op("pe", lambda e: e.matmul(out=P[0].t[:, hf * 512:(hf + 1) * 512], lhsT=ygT.t[:, fc * 128:(fc + 1) * 128], rhs=wo.t[:, fc, hf * 512:(hf + 1) * 512],
                                                   start=(fc == 0), stop=(fc == 7)), reads=[ygT.g, wo.g], writes=[P[0].g])
            kb.op("dve", lambda e: e.scalar_tensor_tensor(out=f2.t[:, :], in0=xin.t[:, :], scalar=DN_ALPHA, in1=P[0].t[:, :], op0=ALU.mult, op1=ALU.add),
                  reads=[xin.g, P[0].g], writes=[f2.g])
            ln_block(kb, f2, vec["lng"], vec["lnb"], xo, stats, mv, rstd, LN_EPS)
            kb.dma("sp", out=xout_ap[t0:t0 + 128, :], in_=xo.t[:, :], reads=[xo.g], writes=[xout_reg], pool="st")
            kb.op("act", lambda e: e.activation(out=xbf.t[:, :], in_=xo.t[:, :], func=AF.Copy), reads=[xo.g], writes=[xbf.g])
            for kc in range(8):
                kb.op("pe", lambda e: e.transpose(out=QT.t[:, kc * 128:(kc + 1) * 128], in_=xbf.t[:, kc * 128:(kc + 1) * 128], identity=ident.t[:, :]),
                      reads=[xbf.g, ident.g], writes=[QT.g])
            x3t = x3ts[b % 2]
            kb.op("act", lambda e: e.activation(out=x3t.t[:, :], in_=QT.t[:, :], func=AF.Copy), reads=[QT.g], writes=[x3t.g])
            kb.dma("sp", out=XT3[:, :, t0:t0 + 128], in_=x3t.t[:, :].rearrange("p (k t) -> p k t", k=8), reads=[x3t.g], writes=[xt3_reg], pool="st")
            if RB_STOP >= 10 and b == RB_STOP - 10:
                kb.barrier()
                return
        kb.barrier()


CONST_SPECS = {
    "c_ident": ([128, 128], BF16),
    "c_ones": ([128, 128], BF16),
    "c_tri": ([128, 128], BF16),
    "c_ones2": ([2, S], BF16),
    "c_alibiq": ([4, 2, S], BF16),
    "c_abias": ([4, 128, 32], F32),
    "c_retDT": ([4, 128, 512], F32),
    "c_retqdec": ([4, 64, 512], F32),
    "c_retkdec": ([4, 128, 1], F32),
    "c_su4": ([128, 512], BF16),
    "c_sl4": ([128, 512], BF16),
    "c_iu4": ([128, 512], BF16),
    "c_id4": ([128, 512], BF16),
}


def make_consts():
    bf = ml_dtypes.bfloat16
    c = {}
    c["c_ident"] = np.eye(128, dtype=np.float32).astype(bf)
    c["c_ones"] = np.ones((128, 128), np.float32).astype(bf)
    p = np.arange(128)
    c["c_tri"] = (p[None, :] >= p[:, None]).astype(np.float32).astype(bf)
    c["c_ones2"] = np.ones((2, S), np.float32).astype(bf)
    t = np.arange(S) % 512
    hi = (t // 16) * 16
    lo = t % 16
    aq = np.zeros((4, 2, S), np.float64)
    ab = np.zeros((4, 128, 32), np.float64)
    for h in range(4):
        aq[h, 0] = -8.0 * SLOPES[h] * hi
        aq[h, 1] = -8.0 * SLOPES[h] * lo
        for oi in range(32):
            ab[h, :, oi] = SLOPES[h] * (p + 128.0 * (oi - 28))
    c["c_alibiq"] = aq.astype(np.float32).astype(bf)
    c["c_abias"] = ab.astype(np.float32)
    DTm = np.zeros((4, 128, 512), np.float64)
    qd = np.zeros((4, 64, 512), np.float64)
    kd = np.zeros((4, 128, 1), np.float64)
    i = np.arange(128)
    for h in range(4):
        g = GAMMAS[h]
        rel = i[None, :] - i[:, None]
        m = np.where(rel >= 0, 0.125 * g ** np.maximum(rel, 0), 0.0)
        DTm[h] = np.tile(m, (1, 4))
        qd[h] = np.tile(g ** (i + 1.0), (64, 4))
        kd[h, :, 0] = 0.125 * g ** (127.0 - i)
    c["c_retDT"] = DTm.astype(np.float32)
    c["c_retqdec"] = qd.astype(np.float32)
    c["c_retkdec"] = kd.astype(np.float32)
    c["c_su4"] = np.tile((p[None, :] > p[:, None]).astype(np.float32), (1, 4)).astype(bf)
    c["c_sl4"] = np.tile((p[None, :] < p[:, None]).astype(np.float32), (1, 4)).astype(bf)
    c["c_iu4"] = np.tile((p[None, :] >= p[:, None]).astype(np.float32), (1, 4)).astype(bf)
    c["c_id4"] = np.tile(np.eye(128, dtype=np.float32), (1, 4)).astype(bf)
    return c


INPUT_SHAPES = {
    "ev_w_in": [1, 1024, 3072], "ev_lambda": [1, 4, 64], "ev_subln_g": [1, 128], "ev_w_out": [1, 1024, 1024],
    "od_mu": [1, 6, 1024], "od_w_rkv": [1, 3, 1024, 1024], "od_w0": [1, 1024], "od_w1": [1, 1024, 64], "od_w2": [1, 64, 1024],
    "od_a0": [1, 1024], "od_a1": [1, 1024, 64], "od_a2": [1, 64, 1024], "od_g1": [1, 1024, 160], "od_g2": [1, 160, 1024],
    "od_k_k": [1, 1024], "od_k_a": [1, 1024], "od_r_k": [1, 1024], "od_lnx_g": [1, 1024], "od_lnx_b": [1, 1024],
    "od_w_out": [1, 1024, 1024], "ln_mix_g": [2, 1024], "ln_mix_b": [2, 1024], "ffn_w_up": [2, 1024, 5632],
    "ffn_conv_w": [2, 3, 2816], "ffn_conv_b": [2, 2816], "ffn_w_down": [2, 2816, 1024], "ln_ffn_g": [2, 1024], "ln_ffn_b": [2, 1024],
}


def build(stop_after=None, debug=False):
    nc = bass.Bass("TRN2", target_bir_lowering=False)
    io = {}
    io["x"] = nc.dram_tensor("x", [S, D], F32, kind="ExternalInput").ap()
    for k, shp in INPUT_SHAPES.items():
        io[k] = nc.dram_tensor(k, shp, F32, kind="ExternalInput").ap()
    for k, (shp, dt) in CONST_SPECS.items():
        io[k] = nc.dram_tensor(k, shp, dt, kind="ExternalInput").ap()
    y = nc.dram_tensor("y", [S, D], F32, kind="ExternalOutput").ap()
    XA = nc.dram_tensor("scr_xa", [S, D], F32, kind="Internal").ap()
    XB = nc.dram_tensor("scr_xb", [S, D], F32, kind="Internal").ap()
    G = nc.dram_tensor("scr_g", [FF, S], BF16, kind="Internal").ap()
    scr = [nc.dram_tensor("scr_r%d" % i, [S, D], F32, kind="Internal").ap() for i in range(6)]
    scr_reg = Reg(True)
    dbg = None
    if debug:
        dbg = nc.dram_tensor("dbg_ot", [128, 8, S], BF16, kind="ExternalOutput").ap()
    for k in ("ev_w_in", "ev_w_out", "od_w_out", "od_w1", "od_w2", "od_a1", "od_a2", "od_g1", "od_g2"):
        io[k] = io[k][0]
    xa_reg, xb_reg, g_reg, y_reg = Reg(True), Reg(True), Reg(True), Reg(True)
    with ExitStack() as es:
        kb = KB(nc, es)
        kb.dma_pool("sp_ld", 12)
        kb.dma_pool("sp_st", 8)
        kb.dma_pool("pool_ld", 6)
        ident = sbt(nc, es, "ident", [128, 128], BF16)
        ones = sbt(nc, es, "ones", [128, 128], BF16)
        kb.dma("sp", out=ident.t[:, :], in_=io["c_ident"][:, :], writes=[ident.g])
        kb.dma("sp", out=ones.t[:, :], in_=io["c_ones"][:, :], writes=[ones.g])
        tri = sbt(nc, es, "tri", [128, 128], BF16)
        kb.dma("sp", out=tri.t[:, :], in_=io["c_tri"][:, :], writes=[tri.g])
        XT3 = nc.dram_tensor("scr_xt3", [128, 8, S], BF16, kind="Internal").ap()
        xt3_reg = Reg(True)
        with ExitStack() as esx:
            xT = sbt(nc, esx, "xT", [128, 8, S], BF16)
            phase_prologue(kb, nc, io, xT, ident)
            if stop_after == "prologue":
                dump_and_stop(kb, dbg[:, :, :], xT.t[:, :, :], xT.g)
                return nc
            if stop_after in ("rwkvonly", "rwkvonly_a"):
                phase_rwkv_a(kb, nc, io, xT, scr, scr_reg)
                if stop_after == "rwkvonly_a":
                    return nc
            else:
                with ExitStack() as es2:
                    OT = sbt(nc, es2, "OT", [128, 8, S], BF16)
                    if phase_l0_mixer(kb, nc, io, xT, OT, ident, ones, tri, dbg):
                        return nc
                    if stop_after == "mixer":
                        kb.barrier()
                        return nc
                    phase_out_ln(kb, nc, io, OT, xT, ident, io["ev_w_out"], io["x"], io["ln_mix_g"][0], io["ln_mix_b"][0], XA, xa_reg)
                if stop_after == "outln":
                    return nc
                phase_ffn(kb, nc, io, 0, xT, ident, XA, xa_reg, y if stop_after == "ffn0" else XB, y_reg if stop_after == "ffn0" else xb_reg,
                          G, g_reg, want_T=True)
                if stop_after == "ffn0":
                    return nc
                phase_rwkv_a(kb, nc, io, xT, scr, scr_reg)
        if stop_after == "rwkvonly":
            phase_rwkv_b(kb, nc, io, XT3, xt3_reg, ident, ones, tri, scr, scr_reg, io["x"], Reg(True), y, y_reg)
            return nc
        last = stop_after == "rwkv"
        phase_rwkv_b(kb, nc, io, XT3, xt3_reg, ident, ones, tri, scr, scr_reg, XB, xb_reg, y if last else XA, y_reg if last else xa_reg)
        if last:
            return nc
        with ExitStack() as esx:
            xT = sbt(nc, esx, "xT", [128, 8, S], BF16)
            for kc in range(8):
                kb.dma("sp", out=xT.t[:, kc, :], in_=XT3[:, kc, :], reads=[xt3_reg], writes=[xT.g])
            phase_ffn(kb, nc, io, 1, xT, ident, XA, xa_reg, y, y_reg, G, g_reg, want_T=False)
    return nc


_NC_CACHE = {}


def kernel(**inputs):
    if "nc" not in _NC_CACHE:
        _NC_CACHE["nc"] = build()
        _NC_CACHE["consts"] = make_consts()
    nc = _NC_CACHE["nc"]
    consts = _NC_CACHE["consts"]
    x = np.ascontiguousarray(np.asarray(inputs["x"], dtype=np.float32))
    shared = {k: np.ascontiguousarray(np.asarray(inputs[k], dtype=np.float32)) for k in INPUT_SHAPES}
    in_maps = []
    for c in range(8):
        m = {"x": x[c]}
        m.update(shared)
        m.update(consts)
        in_maps.append(m)
    res = run_bass_kernel_spmd(nc, in_maps, core_ids=list(range(8)))
    return np.stack([np.asarray(res.results[c]["y"], dtype=np.float32) for c in range(8)], axis=0)
```

```python
import math
from contextlib import ExitStack

import numpy as np
import ml_dtypes

import concourse.bass as bass
import concourse.mybir as mybir
from concourse.bass_utils import run_bass_kernel_spmd

F32 = mybir.dt.float32
BF16 = mybir.dt.bfloat16
AF = mybir.ActivationFunctionType
ALU = mybir.AluOpType
AX = mybir.AxisListType

S = 4096
D = 1024
NB = S // 128
FF = 2816
NFC = FF // 128
DN_ALPHA = (2.0 * 2) ** 0.25
LN_EPS = 1e-5
LAMBDA_INIT0 = 0.8 - 0.6 * math.exp(-0.3 * 0)
SLOPES = [2.0 ** (-8.0 * (i + 1) / 4) for i in range(4)]
GAMMAS = [1.0 - 2.0 ** (-5.0 - h) for h in range(4)]


MIX_STOP = None
LOOPV = 9
RB_STOP = 0


class StopBuild(Exception):
    pass


def dump_and_stop(kb, dbg, tile_ap, reg):
    kb.barrier()
    kb.dma("sp", out=dbg, in_=tile_ap, reads=[reg], pool="st")
    kb.barrier()
    return True


class Reg:
    __slots__ = ("w", "r", "nowaw", "psum")

    def __init__(self, nowaw=False):
        self.w = {}
        self.r = {}
        self.nowaw = nowaw
        self.psum = False


class T:
    def __init__(self, t):
        self.t = t
        self.g = Reg()


class KB:
    def __init__(self, nc, es):
        self.nc = nc
        self.es = es
        self.E = {"pe": nc.tensor, "dve": nc.vector, "act": nc.scalar, "pool": nc.gpsimd, "sp": nc.sync}
        self.sems = {}
        self.cnt = {}
        for e in self.E:
            self.sems[e] = es.enter_context(nc.semaphore("s_" + e))
            self.cnt[e] = 0
        self.seen = {e: {} for e in self.E}
        self.dpool = {}
        self.dnext = {}

    def dma_pool(self, name, n):
        keys = []
        for i in range(n):
            k = "%s%d" % (name, i)
            self.sems[k] = self.es.enter_context(self.nc.semaphore("d_" + k))
            self.cnt[k] = 0
            keys.append(k)
        self.dpool[name] = keys
        self.dnext[name] = 0

    def _deps(self, e, reads, writes):
        need = {}

        def add(d, same_ok):
            for k, v in d.items():
                if k == e and same_ok:
                    continue
                if need.get(k, 0) < v:
                    need[k] = v

        for r in reads:
            add(r.w, e == "pe")
            if r.psum:
                add(r.r, True)
        for w in writes:
            if not w.nowaw:
                add(w.w, e == "pe")
            add(w.r, e == "pe")
        return need

    def _wait(self, e, need):
        sn = self.seen[e]
        for k, v in need.items():
            if sn.get(k, 0) < v:
                self.E[e].wait_ge(self.sems[k], v)
                sn[k] = v

    def op(self, e, fn, reads=(), writes=(), rg=(0, 128)):
        need = self._deps(e, reads, writes)
        if e == "pe":
            last = getattr(self, "_last_rg", (0, 128))
            if (rg[0] + rg[1] <= last[0] or last[0] + last[1] <= rg[0]) and self.cnt["pe"] > 0:
                need["pe"] = self.cnt["pe"]
            self._last_rg = rg
        self._wait(e, need)
        ins = fn(self.E[e])
        self.cnt[e] += 1
        ins.then_inc(self.sems[e], 1)
        tok = self.cnt[e]
        for r in reads:
            r.r[e] = tok
        for w in writes:
            w.w[e] = tok
            if not w.nowaw:
                w.r = {}
        return ins

    def dma(self, q, out, in_, reads=(), writes=(), pool="ld", **kw):
        pool = q + "_" + pool
        keys = self.dpool[pool]
        k = keys[self.dnext[pool] % len(keys)]
        self.dnext[pool] += 1
        need = self._deps(q, reads, writes)
        if self.cnt[k] > 0:
            need[k] = max(need.get(k, 0), self.cnt[k])
        self._wait(q, need)
        ins = self.E[q].dma_start(out=out, in_=in_, **kw)
        self.cnt[k] += 16
        ins.then_inc(self.sems[k], 16)
        tok = self.cnt[k]
        for r in reads:
            r.r[k] = tok
        for w in writes:
            w.w[k] = tok
            if not w.nowaw:
                w.r = {}
        return ins

    def barrier(self):
        for e in self.E:
            need = {k: v for k, v in self.cnt.items() if k != e and v > 0}
            self._wait(e, need)


_UNIQ = [0]


def _uniq(name):
    _UNIQ[0] += 1
    return "%s_%d" % (name, _UNIQ[0])


def sbt(nc, es, name, shape, dt):
    return T(es.enter_context(nc.sbuf_tensor(_uniq(name), list(shape), dt)))


def pst(nc, es, name, shape, dt):
    t = T(es.enter_context(nc.psum_tensor(_uniq(name), list(shape), dt)))
    t.g.psum = True
    return t


def ln_block(kb, z, gam, bet, outp, stats, mv, rstd, eps):
    kb.op("dve", lambda e: e.bn_stats(out=stats.t[:, 0:6], in_=z.t[:, 0:512]), reads=[z.g], writes=[stats.g])
    kb.op("dve", lambda e: e.bn_stats(out=stats.t[:, 6:12], in_=z.t[:, 512:1024]), reads=[z.g], writes=[stats.g])
    kb.op("dve", lambda e: e.bn_aggr(out=mv.t[:, 0:2], in_=stats.t[:, 0:12]), reads=[stats.g], writes=[mv.g])
    kb.op("dve", lambda e: e.tensor_scalar(out=rstd.t[:, 0:1], in0=mv.t[:, 1:2], scalar1=eps, scalar2=None, op0=ALU.add),
          reads=[mv.g], writes=[rstd.g])
    kb.op("act", lambda e: e.activation(out=rstd.t[:, 0:1], in_=rstd.t[:, 0:1], func=AF.Sqrt), reads=[rstd.g], writes=[rstd.g])
    kb.op("dve", lambda e: e.reciprocal(out=rstd.t[:, 0:1], in_=rstd.t[:, 0:1]), reads=[rstd.g], writes=[rstd.g])
    kb.op("dve", lambda e: e.tensor_scalar(out=z.t[:, :], in0=z.t[:, :], scalar1=mv.t[:, 0:1], scalar2=rstd.t[:, 0:1],
                                           op0=ALU.subtract, op1=ALU.mult), reads=[z.g, mv.g, rstd.g], writes=[z.g])
    kb.op("dve", lambda e: e.tensor_tensor(out=z.t[:, :], in0=z.t[:, :], in1=gam.t[:, :], op=ALU.mult),
          reads=[z.g, gam.g], writes=[z.g])
    kb.op("dve", lambda e: e.tensor_tensor(out=outp.t[:, :], in0=z.t[:, :], in1=bet.t[:, :], op=ALU.add),
          reads=[z.g, bet.g], writes=[outp.g])


def transpose_to_fm(kb, src32, srcbf, dstT, b, ident, ptr):
    kb.op("act", lambda e: e.activation(out=srcbf.t[:, :], in_=src32.t[:, :], func=AF.Copy), reads=[src32.g], writes=[srcbf.g])
    for kc in range(8):
        kb.op("pe", lambda e: e.transpose(out=ptr.t[:, kc * 128:(kc + 1) * 128], in_=srcbf.t[:, kc * 128:(kc + 1) * 128],
                                          identity=ident.t[:, :]), reads=[srcbf.g, ident.g], writes=[ptr.g])
    kb.op("dve", lambda e: e.tensor_copy(out=dstT.t[:, :, b * 128:(b + 1) * 128],
                                         in_=ptr.t[:, :].rearrange("p (k t) -> p k t", k=8)), reads=[ptr.g], writes=[dstT.g])


def load_bcast(kb, dst, vec_ap):
    kb.dma("sp", out=dst.t[:, :], in_=vec_ap.partition_broadcast(128), writes=[dst.g])


def phase_prologue(kb, nc, io, xT, ident):
    with ExitStack() as es:
        xin = [sbt(nc, es, "pr_x%d" % i, [128, 1024], F32) for i in range(2)]
        xbf = [sbt(nc, es, "pr_xb%d" % i, [128, 1024], BF16) for i in range(2)]
        ptr = [pst(nc, es, "pr_pt%d" % i, [128, 1024], BF16) for i in range(2)]
        for b in range(NB):
            xi = xin[b % 2]
            kb.dma("sp", out=xi.t[:, :], in_=io["x"][b * 128:(b + 1) * 128, :], writes=[xi.g])
            transpose_to_fm(kb, xi, xbf[b % 2], xT, b, ident, ptr[b % 2])
        kb.barrier()


def phase_l0_mixer(kb, nc, io, xT, OT, ident, ones, tri, dbg=None):
    W_in = io["ev_w_in"].rearrange("(kc p) n -> p kc n", p=128)
    with ExitStack() as es:
        wq = sbt(nc, es, "m_wq", [128, 8, 384], BF16)
        qT = [sbt(nc, es, "m_qT%d" % m, [128, S], BF16) for m in range(2)]
        kT = [sbt(nc, es, "m_kT%d" % m, [128, S], BF16) for m in range(2)]
        vtok = sbt(nc, es, "m_vtok", [128, NB, 128], BF16)
        abias = sbt(nc, es, "m_abias", [128, 32], F32)
        lamt = sbt(nc, es, "m_lam", [128, 256], F32)
        lprod = sbt(nc, es, "m_lprod", [128, 128], F32)
        lsum = sbt(nc, es, "m_lsum", [128, 2], F32)
        lexp = sbt(nc, es, "m_lexp", [128, 2], F32)
        neglam = sbt(nc, es, "m_neglam", [128, 1], F32)
        gsc = sbt(nc, es, "m_gsc", [128, 1], F32)
        pT = [sbt(nc, es, "m_pT%d" % i, [128, 512], BF16) for i in range(4)]
        r1 = sbt(nc, es, "m_r1", [128, 512], F32)
        r2 = sbt(nc, es, "m_r2", [128, 512], F32)
        t1 = sbt(nc, es, "m_t1", [128, 512], F32)
        t2 = sbt(nc, es, "m_t2", [128, 512], F32)
        sqb = sbt(nc, es, "m_sqb", [128, 512], BF16)
        ybf = sbt(nc, es, "m_ybf", [128, 512], BF16)
        ktd = sbt(nc, es, "m_ktd", [128, NB, 64], BF16)
        DT = sbt(nc, es, "m_DT", [128, 512], F32)
        qdec = sbt(nc, es, "m_qdec", [64, 512], F32)
        kdec = sbt(nc, es, "m_kdec", [128, 1], F32)
        Rst = sbt(nc, es, "m_Rst", [64, 2, 128], F32)
        Rbf = sbt(nc, es, "m_Rbf", [64, NB, 128], BF16)
        bank = [pst(nc, es, "m_b%d" % i, [128, 512], F32) for i in range(8)]

        kb.dma("sp", out=lamt.t[:, :], in_=io["ev_lambda"].rearrange("a b c -> (a b c)").partition_broadcast(128), writes=[lamt.g])
        kb.op("dve", lambda e: e.tensor_tensor(out=lprod.t[:, 0:64], in0=lamt.t[:, 0:64], in1=lamt.t[:, 64:128], op=ALU.mult),
              reads=[lamt.g], writes=[lprod.g])
        kb.op("dve", lambda e: e.tensor_tensor(out=lprod.t[:, 64:128], in0=lamt.t[:, 128:192], in1=lamt.t[:, 192:256], op=ALU.mult),
              reads=[lamt.g], writes=[lprod.g])
        kb.op("dve", lambda e: e.reduce_sum(out=lsum.t[:, 0:1], in_=lprod.t[:, 0:64], axis=AX.X), reads=[lprod.g], writes=[lsum.g])
        kb.op("dve", lambda e: e.reduce_sum(out=lsum.t[:, 1:2], in_=lprod.t[:, 64:128], axis=AX.X), reads=[lprod.g], writes=[lsum.g])
        kb.op("act", lambda e: e.activation(out=lexp.t[:, 0:2], in_=lsum.t[:, 0:2], func=AF.Exp), reads=[lsum.g], writes=[lexp.g])
        kb.op("dve", lambda e: e.tensor_tensor(out=neglam.t[:, 0:1], in0=lexp.t[:, 1:2], in1=lexp.t[:, 0:1], op=ALU.subtract),
              reads=[lexp.g], writes=[neglam.g])
        kb.op("dve", lambda e: e.tensor_scalar(out=neglam.t[:, 0:1], in0=neglam.t[:, 0:1], scalar1=-LAMBDA_INIT0, scalar2=None, op0=ALU.add),
              reads=[neglam.g], writes=[neglam.g])
        kb.dma("sp", out=gsc.t[:, :], in_=io["ev_subln_g"].rearrange("o v -> v o"), writes=[gsc.g], allow_slow_non_contiguous=True)
        kb.op("dve", lambda e: e.tensor_scalar(out=gsc.t[:, 0:1], in0=gsc.t[:, 0:1], scalar1=1.0 - LAMBDA_INIT0, scalar2=None, op0=ALU.mult),
              reads=[gsc.g], writes=[gsc.g])
        for m in range(2):
            kb.dma("sp", out=kT[m].t[64:66, :], in_=io["c_ones2"][:, :], writes=[kT[m].g])

        def proj_fm(dst, prow, co, ncol, evac_eng_i):
            for tt in range(8):
                bk = bank[tt % 2]
                for kc in range(8):
                    kb.op("pe", lambda e: e.matmul(out=bk.t[0:ncol, :], lhsT=wq.t[:, kc, co:co + ncol],
                                                   rhs=xT.t[:, kc, tt * 512:(tt + 1) * 512], start=(kc == 0), stop=(kc == 7)),
                          reads=[wq.g, xT.g], writes=[bk.g])
                if (tt + evac_eng_i) % 2 == 0:
                    kb.op("act", lambda e: e.activation(out=dst.t[prow:prow + ncol, tt * 512:(tt + 1) * 512], in_=bk.t[0:ncol, :], func=AF.Copy),
                          reads=[bk.g], writes=[dst.g])
                else:
                    kb.op("dve", lambda e: e.tensor_copy(out=dst.t[prow:prow + ncol, tt * 512:(tt + 1) * 512], in_=bk.t[0:ncol, :]),
                          reads=[bk.g], writes=[dst.g])

        for h in range(4):
            for i, co in enumerate((h * 128, 512 + h * 128, 1024 + h * 128)):
                kb.dma("pool", out=wq.t[:, :, i * 128:(i + 1) * 128], in_=W_in[:, :, co:co + 128], reads=[], writes=[wq.g])
            kb.dma("sp", out=abias.t[:, :], in_=io["c_abias"][h, :, :], writes=[abias.g])
            for m in range(2):
                kb.dma("sp", out=qT[m].t[64:66, :], in_=io["c_alibiq"][h, :, :], writes=[qT[m].g])
            for m in range(2):
                proj_fm(qT[m], 0, m * 64, 64, 0)
                proj_fm(kT[m], 0, 128 + m * 64, 64, 1)
            for g4 in range(8):
                bk = bank[2 + g4 % 2]
                for bb in range(4):
                    b = g4 * 4 + bb
                    for kc in range(8):
                        kb.op("pe", lambda e: e.matmul(out=bk.t[:, bb * 128:(bb + 1) * 128], lhsT=xT.t[:, kc, b * 128:(b + 1) * 128],
                                                       rhs=wq.t[:, kc, 256:384], start=(kc == 0), stop=(kc == 7)),
                              reads=[wq.g, xT.g], writes=[bk.g])
                kb.op("dve", lambda e: e.tensor_copy(out=vtok.t[:, g4 * 4:(g4 + 1) * 4, :],
                                                     in_=bk.t[:, :].rearrange("p (b v) -> p b v", b=4)), reads=[bk.g], writes=[vtok.g])
            if MIX_STOP == "h0proj":
                kb.barrier()
                kb.dma("sp", out=dbg[0:64, 0, :], in_=qT[0].t[0:64, :], reads=[qT[0].g], pool="st")
                kb.dma("sp", out=dbg[0:64, 1, :], in_=qT[1].t[0:64, :], reads=[qT[1].g], pool="st")
                kb.dma("sp", out=dbg[0:64, 2, :], in_=kT[0].t[0:64, :], reads=[kT[0].g], pool="st")
                kb.dma("sp", out=dbg[0:64, 3, :], in_=kT[1].t[0:64, :], reads=[kT[1].g], pool="st")
                kb.dma("sp", out=dbg[:, 4, :].rearrange("p (b v) -> p b v", b=NB), in_=vtok.t[:, :, :], reads=[vtok.g], pool="st")
                kb.barrier()
                return True
            O = [bank[4], bank[6]]
            Sm = [bank[5], bank[7]]
            for c in range(8):
                steps = [(kbi, m) for kbi in range(4 * c + 4) for m in range(2)]

                def geom(i):
                    kbi, m = steps[i]
                    j = kbi - 4 * c
                    lo = 128 * j if j > 0 else 0
                    return kbi, m, j, lo

                def emit_qk(i):
                    kbi, m, j, lo = geom(i)
                    sb = bank[i % 4]
                    KK = 64 if LOOPV == -1 else 66
                    kb.op("pe", lambda e: e.matmul(out=sb.t[:, lo:512], lhsT=kT[m].t[0:KK, kbi * 128:(kbi + 1) * 128],
                                                   rhs=qT[m].t[0:KK, c * 512 + lo:(c + 1) * 512], start=True, stop=True),
                          reads=[kT[m].g, qT[m].g], writes=[sb.g])

                def emit_pv(i):
                    kbi, m, j, lo = geom(i)
                    sb = bank[i % 4]
                    pt = pT[i % 4]
                    oi = (kbi - 4 * c) + 28
                    if LOOPV == -4:
                        return
                    kb.op("act", lambda e: e.activation(out=pt.t[:, lo:512], in_=sb.t[:, lo:512], func=(AF.Copy if LOOPV == -3 else AF.Exp),
                                                        bias=(0.0 if LOOPV in (-2, -3) else abias.t[:, oi:oi + 1]), scale=0.125),
                          reads=[sb.g, abias.g], writes=[pt.g])
                    if LOOPV < 1:
                        return
                    if j >= 0:
                        kb.op("dve", lambda e: e.tensor_tensor(out=pt.t[:, lo:lo + 128], in0=pt.t[:, lo:lo + 128], in1=tri.t[:, :], op=ALU.mult),
                              reads=[pt.g, tri.g], writes=[pt.g])
                    if LOOPV < 2:
                        return
                    first = kbi == 0
                    last = kbi == 4 * c + 3
                    kb.op("pe", lambda e: e.matmul(out=O[m].t[:, lo:512], lhsT=vtok.t[:, kbi, :], rhs=pt.t[:, lo:512], start=first, stop=last),
                          reads=[vtok.g, pt.g], writes=[O[m].g])
                    kb.op("pe", lambda e: e.matmul(out=Sm[m].t[:, lo:512], lhsT=ones.t[:, :], rhs=pt.t[:, lo:512], start=first, stop=last),
                          reads=[ones.g, pt.g], writes=[Sm[m].g])

                n = len(steps)
                emit_qk(0)
                emit_qk(1)
                emit_qk(2)
                for i in range(n):
                    if i + 3 < n:
                        emit_qk(i + 3)
                    emit_pv(i)
                if MIX_STOP == "c0loop":
                    kb.barrier()
                    kb.dma("sp", out=dbg[:, 4, :].rearrange("p (b v) -> p b v", b=NB), in_=vtok.t[:, :, :], reads=[vtok.g], pool="st")
                    kb.barrier()
                    return True
                kb.op("dve", lambda e: e.reciprocal(out=r1.t[:, :], in_=Sm[0].t[:, :]), reads=[Sm[0].g], writes=[r1.g])
                kb.op("dve", lambda e: e.reciprocal(out=r2.t[:, :], in_=Sm[1].t[:, :]), reads=[Sm[1].g], writes=[r2.g])
                kb.op("dve", lambda e: e.tensor_tensor(out=t1.t[:, :], in0=O[0].t[:, :], in1=r1.t[:, :], op=ALU.mult), reads=[O[0].g, r1.g], writes=[t1.g])
                kb.op("dve", lambda e: e.tensor_tensor(out=t2.t[:, :], in0=O[1].t[:, :], in1=r2.t[:, :], op=ALU.mult), reads=[O[1].g, r2.g], writes=[t2.g])
                kb.op("dve", lambda e: e.scalar_tensor_tensor(out=t1.t[:, :], in0=t2.t[:, :], scalar=neglam.t[:, 0:1], in1=t1.t[:, :],
                                                              op0=ALU.mult, op1=ALU.add), reads=[t1.g, t2.g, neglam.g], writes=[t1.g])
                kb.op("act", lambda e: e.activation(out=sqb.t[:, :], in_=t1.t[:, :], func=AF.Square), reads=[t1.g], writes=[sqb.g])
                ssb = bank[0]
                kb.op("pe", lambda e: e.matmul(out=ssb.t[:, :], lhsT=ones.t[:, :], rhs=sqb.t[:, :], start=True, stop=True),
                      reads=[ones.g, sqb.g], writes=[ssb.g])
                kb.op("dve", lambda e: e.tensor_scalar(out=r1.t[:, :], in0=ssb.t[:, :], scalar1=1.0 / 128, scalar2=LN_EPS, op0=ALU.mult, op1=ALU.add),
                      reads=[ssb.g], writes=[r1.g])
                kb.op("act", lambda e: e.activation(out=r1.t[:, :], in_=r1.t[:, :], func=AF.Sqrt), reads=[r1.g], writes=[r1.g])
                kb.op("dve", lambda e: e.reciprocal(out=r1.t[:, :], in_=r1.t[:, :]), reads=[r1.g], writes=[r1.g])
                kb.op("dve", lambda e: e.scalar_tensor_tensor(out=OT.t[:, h, c * 512:(c + 1) * 512], in0=t1.t[:, :], scalar=gsc.t[:, 0:1], in1=r1.t[:, :],
                                                              op0=ALU.mult, op1=ALU.mult), reads=[t1.g, gsc.g, r1.g], writes=[OT.g])

        if MIX_STOP == "diff":
            return dump_and_stop(kb, dbg[:, :, :], OT.t[:, :, :], OT.g)
        for h in range(4):
            gam = GAMMAS[h]
            kb.dma("pool", out=wq.t[:, :, 0:64], in_=W_in[:, :, 1536 + h * 64:1536 + (h + 1) * 64], writes=[wq.g])
            kb.dma("pool", out=wq.t[:, :, 64:128], in_=W_in[:, :, 1792 + h * 64:1792 + (h + 1) * 64], writes=[wq.g])
            kb.dma("pool", out=wq.t[:, :, 128:256], in_=W_in[:, :, 2048 + h * 128:2048 + (h + 1) * 128], writes=[wq.g])
            kb.dma("pool", out=wq.t[:, :, 256:384], in_=W_in[:, :, 2560 + h * 128:2560 + (h + 1) * 128], writes=[wq.g])
            kb.dma("sp", out=DT.t[:, :], in_=io["c_retDT"][h, :, :], writes=[DT.g])
            kb.dma("sp", out=qdec.t[:, :], in_=io["c_retqdec"][h, :, :], writes=[qdec.g])
            kb.dma("sp", out=kdec.t[:, :], in_=io["c_retkdec"][h, :, :], writes=[kdec.g])
            if MIX_STOP == "r_load":
                kb.barrier()
                kb.dma("sp", out=dbg[:, 4, :].rearrange("p (b v) -> p b v", b=NB), in_=vtok.t[:, :, :], reads=[vtok.g], pool="st")
                kb.barrier()
                return True
            for tt in range(8):
                bk = bank[tt % 2]
                for kc in range(8):
                    kb.op("pe", lambda e: e.matmul(out=bk.t[0:64, :], lhsT=wq.t[:, kc, 0:64], rhs=xT.t[:, kc, tt * 512:(tt + 1) * 512],
                                                   start=(kc == 0), stop=(kc == 7)), reads=[wq.g, xT.g], writes=[bk.g])
                kb.op("act", lambda e: e.activation(out=qT[0].t[0:64, tt * 512:(tt + 1) * 512], in_=bk.t[0:64, :], func=AF.Copy),
                      reads=[bk.g], writes=[qT[0].g])
                if LOOPV >= 1:
                    kb.op("dve", lambda e: e.tensor_tensor(out=qT[1].t[0:64, tt * 512:(tt + 1) * 512], in0=bk.t[0:64, :], in1=qdec.t[:, :], op=ALU.mult),
                          reads=[bk.g, qdec.g], writes=[qT[1].g])
            if LOOPV >= 2:
                proj_fm(kT[0], 0, 64, 64, 0)
            for b in range(NB if LOOPV >= 3 else 0):
                bk = bank[2 + b % 2]
                for kc in range(8):
                    kb.op("pe", lambda e: e.matmul(out=bk.t[:, 0:192], lhsT=xT.t[:, kc, b * 128:(b + 1) * 128], rhs=wq.t[:, kc, 64:256],
                                                   start=(kc == 0), stop=(kc == 7)), reads=[wq.g, xT.g], writes=[bk.g])
                kb.op("dve", lambda e: e.tensor_scalar(out=ktd.t[:, b, :], in0=bk.t[:, 0:64], scalar1=kdec.t[:, 0:1], scalar2=None, op0=ALU.mult),
                      reads=[bk.g, kdec.g], writes=[ktd.g])
                kb.op("act", lambda e: e.activation(out=vtok.t[:, b, :], in_=bk.t[:, 64:192], func=AF.Copy), reads=[bk.g], writes=[vtok.g])
            if MIX_STOP == "r_proj":
                kb.barrier()
                kb.dma("sp", out=dbg[:, 4, :].rearrange("p (b v) -> p b v", b=NB), in_=vtok.t[:, :, :], reads=[vtok.g], pool="st")
                kb.barrier()
                return True
            kb.op("dve", lambda e: e.memset(Rst.t[:, 0, :], 0.0), writes=[Rst.g])
            kb.op("dve", lambda e: e.memset(Rbf.t[:, 0, :], 0.0), writes=[Rbf.g])
            for g4 in range(8):
                bk = bank[4 + g4 % 2]
                for bb in range(4):
                    b = g4 * 4 + bb
                    kb.op("pe", lambda e: e.matmul(out=bk.t[0:64, bb * 128:(bb + 1) * 128], lhsT=ktd.t[:, b, :], rhs=vtok.t[:, b, :],
                                                   start=True, stop=True), reads=[ktd.g, vtok.g], writes=[bk.g])
                for bb in range(4):
                    b = g4 * 4 + bb
                    if b == NB - 1:
                        continue
                    kb.op("dve", lambda e: e.scalar_tensor_tensor(out=Rst.t[:, (b + 1) % 2, :], in0=Rst.t[:, b % 2, :], scalar=float(gam ** 128),
                                                                  in1=bk.t[0:64, bb * 128:(bb + 1) * 128], op0=ALU.mult, op1=ALU.add),
                          reads=[Rst.g, bk.g], writes=[Rst.g])
                    kb.op("act", lambda e: e.activation(out=Rbf.t[:, b + 1, :], in_=Rst.t[:, (b + 1) % 2, :], func=AF.Copy), reads=[Rst.g], writes=[Rbf.g])
            if MIX_STOP == "r_state":
                kb.barrier()
                kb.dma("sp", out=dbg[:, 4, :].rearrange("p (b v) -> p b v", b=NB), in_=vtok.t[:, :, :], reads=[vtok.g], pool="st")
                kb.barrier()
                return True
            for g4 in range(8):
                sc = bank[g4 % 2]
                yb = bank[2 + g4 % 2]
                gb = bank[6 + g4 % 2]
                pt = pT[g4 % 2]
                for bb in range(4):
                    b = g4 * 4 + bb
                    kb.op("pe", lambda e: e.matmul(out=sc.t[:, bb * 128:(bb + 1) * 128], lhsT=kT[0].t[0:64, b * 128:(b + 1) * 128],
                                                   rhs=qT[0].t[0:64, b * 128:(b + 1) * 128], start=True, stop=True),
                          reads=[kT[0].g, qT[0].g], writes=[sc.g])
                for kc in range(8):
                    kb.op("pe", lambda e: e.matmul(out=gb.t[:, :], lhsT=wq.t[:, kc, 256:384], rhs=xT.t[:, kc, g4 * 512:(g4 + 1) * 512],
                                                   start=(kc == 0), stop=(kc == 7)), reads=[wq.g, xT.g], writes=[gb.g])
                kb.op("dve", lambda e: e.tensor_tensor(out=pt.t[:, :], in0=sc.t[:, :], in1=DT.t[:, :], op=ALU.mult), reads=[sc.g, DT.g], writes=[pt.g])
                for bb in range(4):
                    b = g4 * 4 + bb
                    kb.op("pe", lambda e: e.matmul(out=yb.t[:, bb * 128:(bb + 1) * 128], lhsT=vtok.t[:, b, :], rhs=pt.t[:, bb * 128:(bb + 1) * 128],
                                                   start=True, stop=False), reads=[vtok.g, pt.g], writes=[yb.g])
                    kb.op("pe", lambda e: e.matmul(out=yb.t[:, bb * 128:(bb + 1) * 128], lhsT=Rbf.t[:, b, :], rhs=qT[1].t[0:64, b * 128:(b + 1) * 128],
                                                   start=False, stop=True), reads=[Rbf.g, qT[1].g], writes=[yb.g])
                kb.op("act", lambda e: e.activation(out=ybf.t[:, :], in_=yb.t[:, :], func=AF.Copy), reads=[yb.g], writes=[ybf.g])
                kb.op("act", lambda e: e.activation(out=sqb.t[:, :], in_=yb.t[:, :], func=AF.Square), reads=[yb.g], writes=[sqb.g])
                m1 = bank[4]
                m2 = bank[5]
                kb.op("pe", lambda e: e.matmul(out=m1.t[:, :], lhsT=ones.t[:, :], rhs=ybf.t[:, :], start=True, stop=True), reads=[ones.g, ybf.g], writes=[m1.g])
                kb.op("pe", lambda e: e.matmul(out=m2.t[:, :], lhsT=ones.t[:, :], rhs=sqb.t[:, :], start=True, stop=True), reads=[ones.g, sqb.g], writes=[m2.g])
                kb.op("dve", lambda e: e.tensor_scalar(out=r1.t[:, :], in0=m1.t[:, :], scalar1=1.0 / 128, scalar2=None, op0=ALU.mult), reads=[m1.g], writes=[r1.g])
                kb.op("dve", lambda e: e.tensor_tensor(out=t2.t[:, :], in0=r1.t[:, :], in1=r1.t[:, :], op=ALU.mult), reads=[r1.g], writes=[t2.g])
                kb.op("dve", lambda e: e.scalar_tensor_tensor(out=r2.t[:, :], in0=m2.t[:, :], scalar=1.0 / 128, in1=t2.t[:, :], op0=ALU.mult, op1=ALU.subtract),
                      reads=[m2.g, t2.g], writes=[r2.g])
                kb.op("dve", lambda e: e.tensor_scalar(out=r2.t[:, :], in0=r2.t[:, :], scalar1=LN_EPS, scalar2=None, op0=ALU.add),
                      reads=[r2.g], writes=[r2.g])
                kb.op("act", lambda e: e.activation(out=r2.t[:, :], in_=r2.t[:, :], func=AF.Sqrt), reads=[r2.g], writes=[r2.g])
                kb.op("dve", lambda e: e.reciprocal(out=r2.t[:, :], in_=r2.t[:, :]), reads=[r2.g], writes=[r2.g])
                sg = t2
                kb.op("act", lambda e: e.activation(out=sg.t[:, :], in_=gb.t[:, :], func=AF.Silu), reads=[gb.g], writes=[sg.g])
                kb.op("dve", lambda e: e.tensor_tensor(out=t1.t[:, :], in0=yb.t[:, :], in1=r1.t[:, :], op=ALU.subtract), reads=[yb.g, r1.g], writes=[t1.g])
                kb.op("dve", lambda e: e.tensor_tensor(out=t1.t[:, :], in0=t1.t[:, :], in1=r2.t[:, :], op=ALU.mult), reads=[t1.g, r2.g], writes=[t1.g])
                kb.op("dve", lambda e: e.tensor_tensor(out=OT.t[:, 4 + h, g4 * 512:(g4 + 1) * 512], in0=t1.t[:, :], in1=sg.t[:, :], op=ALU.mult),
                      reads=[t1.g, sg.g], writes=[OT.g])
        kb.barrier()
        if dbg is not None:
            kb.dma("sp", out=dbg[:, :, :], in_=OT.t[:, :, :], reads=[OT.g], pool="st")
            kb.barrier()


def phase_out_ln(kb, nc, io, OT, xT, ident, w_ap, xres_ap, gam_ap, bet_ap, xout_ap, xout_reg):
    with ExitStack() as es:
        wo = sbt(nc, es, "o_w", [128, 8, 1024], BF16)
        gam = sbt(nc, es, "o_gam", [128, 1024], F32)
        bet = sbt(nc, es, "o_bet", [128, 1024], F32)
        xin = [sbt(nc, es, "o_x%d" % i, [128, 1024], F32) for i in range(2)]
        z = [sbt(nc, es, "o_z%d" % i, [128, 1024], F32) for i in range(2)]
        xo = [sbt(nc, es, "o_xo%d" % i, [128, 1024], F32) for i in range(2)]
        xbf = [sbt(nc, es, "o_xb%d" % i, [128, 1024], BF16) for i in range(2)]
        stats = sbt(nc, es, "o_stats", [128, 12], F32)
        mv = sbt(nc, es, "o_mv", [128, 2], F32)
        rstd = sbt(nc, es, "o_rstd", [128, 1], F32)
        mm = [[pst(nc, es, "o_mm%d%d" % (i, j), [128, 512], F32) for j in range(2)] for i in range(2)]
        ptr = [pst(nc, es, "o_pt%d" % i, [128, 1024], BF16) for i in range(2)]
        kb.dma("pool", out=wo.t[:, :, :], in_=w_ap.rearrange("(kc p) n -> p kc n", p=128), writes=[wo.g])
        load_bcast(kb, gam, gam_ap)
        load_bcast(kb, bet, bet_ap)
        for b in range(NB):
            xi = xin[b % 2]
            kb.dma("sp", out=xi.t[:, :], in_=xres_ap[b * 128:(b + 1) * 128, :], writes=[xi.g])
            for hf in range(2):
                for fc in range(8):
                    kb.op("pe", lambda e: e.matmul(out=mm[b % 2][hf].t[:, :], lhsT=OT.t[:, fc, b * 128:(b + 1) * 128],
                                                   rhs=wo.t[:, fc, hf * 512:(hf + 1) * 512], start=(fc == 0), stop=(fc == 7)),
                          reads=[OT.g, wo.g], writes=[mm[b % 2][hf].g])
            zz = z[b % 2]
            for hf in range(2):
                kb.op("dve", lambda e: e.scalar_tensor_tensor(out=zz.t[:, hf * 512:(hf + 1) * 512], in0=xi.t[:, hf * 512:(hf + 1) * 512], scalar=DN_ALPHA,
                                                              in1=mm[b % 2][hf].t[:, :], op0=ALU.mult, op1=ALU.add),
                      reads=[xi.g, mm[b % 2][hf].g], writes=[zz.g])
            ln_block(kb, zz, gam, bet, xo[b % 2], stats, mv, rstd, LN_EPS)
            kb.dma("sp", out=xout_ap[b * 128:(b + 1) * 128, :], in_=xo[b % 2].t[:, :], reads=[xo[b % 2].g], writes=[xout_reg], pool="st")
            transpose_to_fm(kb, xo[b % 2], xbf[b % 2], xT, b, ident, ptr[b % 2])
        kb.barrier()


def phase_ffn(kb, nc, io, layer, xT, ident, xres_ap, xres_reg, xout_ap, xout_reg, G_ap, G_reg, want_T):
    Wup = io["ffn_w_up"][layer].rearrange("(kc p) n -> p kc n", p=128)
    Wdn = io["ffn_w_down"][layer].rearrange("(fc p) n -> p fc n", p=128)
    with ExitStack() as es:
        wu = [sbt(nc, es, "f_wu%d" % i, [128, 8, 256], BF16) for i in range(2)]
        cw = sbt(nc, es, "f_cw", [128, NFC, 3], F32)
        cb = sbt(nc, es, "f_cb", [128, NFC], F32)
        ubuf = [sbt(nc, es, "f_ub%d" % i, [128, 514], F32) for i in range(2)]
        cbuf = [sbt(nc, es, "f_c%d" % i, [128, 512], F32) for i in range(2)]
        gl = [sbt(nc, es, "f_gl%d" % i, [128, 512], F32) for i in range(2)]
        gt = [sbt(nc, es, "f_gt%d" % i, [128, 512], BF16) for i in range(3)]
        pu = [pst(nc, es, "f_pu%d" % i, [128, 512], F32) for i in range(3)]
        pv = [pst(nc, es, "f_pv%d" % i, [128, 512], F32) for i in range(3)]
        for j in range(3):
            kb.dma("sp", out=cw.t[:, :, j], in_=io["ffn_conv_w"][layer][j].rearrange("(fc p) -> p fc", p=128), writes=[cw.g],
                   allow_slow_non_contiguous=True)
        kb.dma("sp", out=cb.t[:, :], in_=io["ffn_conv_b"][layer].rearrange("(fc p) -> p fc", p=128), writes=[cb.g],
               allow_slow_non_contiguous=True)
        it = 0
        for fc in range(NFC):
            w = wu[fc % 2]
            kb.dma("pool", out=w.t[:, :, 0:128], in_=Wup[:, :, fc * 128:(fc + 1) * 128], writes=[w.g])
            kb.dma("pool", out=w.t[:, :, 128:256], in_=Wup[:, :, FF + fc * 128:FF + (fc + 1) * 128], writes=[w.g])
            for tt in range(8):
                u_ps = pu[it % 3]
                v_ps = pv[it % 3]
                ub = ubuf[it % 2]
                ubn = ubuf[(it + 1) % 2]
                c = cbuf[it % 2]
                g_ = gl[it % 2]
                go = gt[it % 3]
                for kc in range(8):
                    kb.op("pe", lambda e: e.matmul(out=u_ps.t[:, :], lhsT=w.t[:, kc, 0:128], rhs=xT.t[:, kc, tt * 512:(tt + 1) * 512],
                                                   start=(kc == 0), stop=(kc == 7)), reads=[w.g, xT.g], writes=[u_ps.g])
                for kc in range(8):
                    kb.op("pe", lambda e: e.matmul(out=v_ps.t[:, :], lhsT=w.t[:, kc, 128:256], rhs=xT.t[:, kc, tt * 512:(tt + 1) * 512],
                                                   start=(kc == 0), stop=(kc == 7)), reads=[w.g, xT.g], writes=[v_ps.g])
                if tt == 0:
                    kb.op("dve", lambda e: e.memset(ub.t[:, 0:2], 0.0), writes=[ub.g])
                kb.op("act", lambda e: e.activation(out=ub.t[:, 2:514], in_=u_ps.t[:, :], func=AF.Copy), reads=[u_ps.g], writes=[ub.g])
                if tt < 7:
                    kb.op("dve", lambda e: e.tensor_copy(out=ubn.t[:, 0:2], in_=ub.t[:, 512:514]), reads=[ub.g], writes=[ubn.g])
                kb.op("act", lambda e: e.activation(out=c.t[:, :], in_=u_ps.t[:, :], func=AF.Identity, scale=cw.t[:, fc, 2:3], bias=cb.t[:, fc:fc + 1]),
                      reads=[u_ps.g, cw.g, cb.g], writes=[c.g])
                kb.op("dve", lambda e: e.scalar_tensor_tensor(out=c.t[:, :], in0=ub.t[:, 1:513], scalar=cw.t[:, fc, 1:2], in1=c.t[:, :],
                                                              op0=ALU.mult, op1=ALU.add), reads=[ub.g, cw.g, c.g], writes=[c.g])
                kb.op("dve", lambda e: e.scalar_tensor_tensor(out=c.t[:, :], in0=ub.t[:, 0:512], scalar=cw.t[:, fc, 0:1], in1=c.t[:, :],
                                                              op0=ALU.mult, op1=ALU.add), reads=[ub.g, cw.g, c.g], writes=[c.g])
                kb.op("act", lambda e: e.activation(out=g_.t[:, :], in_=c.t[:, :], func=AF.Gelu), reads=[c.g], writes=[g_.g])
                kb.op("dve", lambda e: e.tensor_tensor(out=go.t[:, :], in0=v_ps.t[:, :], in1=g_.t[:, :], op=ALU.mult), reads=[v_ps.g, g_.g], writes=[go.g])
                kb.dma("sp", out=G_ap[fc * 128:(fc + 1) * 128, tt * 512:(tt + 1) * 512], in_=go.t[:, :], reads=[go.g], writes=[G_reg], pool="st")
                it += 1
        kb.barrier()
    Gv = G_ap.rearrange("(fc p) t -> p fc t", p=128)
    with ExitStack() as es:
        wd = sbt(nc, es, "g_wd", [128, NFC, 1024], BF16)
        gin = [sbt(nc, es, "g_gin%d" % i, [128, NFC, 512], BF16) for i in range(2)]
        gam = sbt(nc, es, "g_gam", [128, 1024], F32)
        bet = sbt(nc, es, "g_bet", [128, 1024], F32)
        xin = [sbt(nc, es, "g_x%d" % i, [128, 1024], F32) for i in range(2)]
        z = [sbt(nc, es, "g_z%d" % i, [128, 1024], F32) for i in range(2)]
        xo = [sbt(nc, es, "g_xo%d" % i, [128, 1024], F32) for i in range(2)]
        xbf = [sbt(nc, es, "g_xb%d" % i, [128, 1024], BF16) for i in range(2)]
        stats = sbt(nc, es, "g_stats", [128, 12], F32)
        mv = sbt(nc, es, "g_mv", [128, 2], F32)
        rstd = sbt(nc, es, "g_rstd", [128, 1], F32)
        mm = [[pst(nc, es, "g_mm%d%d" % (i, j), [128, 512], F32) for j in range(2)] for i in range(2)]
        ptr = [pst(nc, es, "g_pt%d" % i, [128, 1024], BF16) for i in range(2)]
        for q4 in range(2):
            kb.dma("pool", out=wd.t[:, q4 * 11:(q4 + 1) * 11, :], in_=Wdn[:, q4 * 11:(q4 + 1) * 11, :], writes=[wd.g])
        load_bcast(kb, gam, io["ln_ffn_g"][layer])
        load_bcast(kb, bet, io["ln_ffn_b"][layer])
        for tt in range(8):
            gi = gin[tt % 2]
            for q4 in range(2):
                kb.dma("sp", out=gi.t[:, q4 * 11:(q4 + 1) * 11, :], in_=Gv[:, q4 * 11:(q4 + 1) * 11, tt * 512:(tt + 1) * 512],
                       reads=[G_reg], writes=[gi.g])
            for bb in range(4):
                b = tt * 4 + bb
                xi = xin[b % 2]
                kb.dma("sp", out=xi.t[:, :], in_=xres_ap[b * 128:(b + 1) * 128, :], reads=[xres_reg], writes=[xi.g])
                for hf in range(2):
                    for fc in range(NFC):
                        kb.op("pe", lambda e: e.matmul(out=mm[b % 2][hf].t[:, :], lhsT=gi.t[:, fc, bb * 128:(bb + 1) * 128],
                                                       rhs=wd.t[:, fc, hf * 512:(hf + 1) * 512], start=(fc == 0), stop=(fc == NFC - 1)),
                              reads=[gi.g, wd.g], writes=[mm[b % 2][hf].g])
                zz = z[b % 2]
                for hf in range(2):
                    kb.op("dve", lambda e: e.scalar_tensor_tensor(out=zz.t[:, hf * 512:(hf + 1) * 512], in0=xi.t[:, hf * 512:(hf + 1) * 512], scalar=DN_ALPHA,
                                                                  in1=mm[b % 2][hf].t[:, :], op0=ALU.mult, op1=ALU.add),
                          reads=[xi.g, mm[b % 2][hf].g], writes=[zz.g])
                ln_block(kb, zz, gam, bet, xo[b % 2], stats, mv, rstd, LN_EPS)
                kb.dma("sp", out=xout_ap[b * 128:(b + 1) * 128, :], in_=xo[b % 2].t[:, :], reads=[xo[b % 2].g], writes=[xout_reg], pool="st")
                if want_T:
                    transpose_to_fm(kb, xo[b % 2], xbf[b % 2], xT, b, ident, ptr[b % 2])
        kb.barrier()


def phase_rwkv_a(kb, nc, io, xT, scr, scr_reg):
    Wrkv = io["od_w_rkv"][0].rearrange("n (kc p) e -> p n kc e", p=128)
    with ExitStack() as es:
        wr = sbt(nc, es, "ra_w", [128, 3, 8, 1024], BF16)
        l1 = sbt(nc, es, "ra_l1", [128, 8, 288], BF16)
        w2 = sbt(nc, es, "ra_w2", [64, 1024], BF16)
        a2 = sbt(nc, es, "ra_a2", [64, 1024], BF16)
        g2a = sbt(nc, es, "ra_g2a", [128, 1024], BF16)
        g2b = sbt(nc, es, "ra_g2b", [32, 1024], BF16)
        w0b = sbt(nc, es, "ra_w0b", [128, 1024], F32)
        a0b = sbt(nc, es, "ra_a0b", [128, 1024], F32)
        mu = sbt(nc, es, "ra_mu", [128, 6, 8], F32)
        xx = [sbt(nc, es, "ra_xx%d" % i, [128, 8, 128], F32) for i in range(2)]
        mixT = [[sbt(nc, es, "ra_mix%d_%d" % (n, i), [128, 8, 128], BF16) for i in range(2)] for n in range(6)]
        lo1 = [sbt(nc, es, "ra_lo%d" % i, [128, 128], BF16) for i in range(4)]
        lo2 = sbt(nc, es, "ra_l32", [32, 128], BF16)
        outF = [sbt(nc, es, "ra_o%d" % i, [128, 1024], F32) for i in range(4)]
        P = [pst(nc, es, "ra_p%d" % i, [128, 1024], F32) for i in range(3)]
        Q = [pst(nc, es, "ra_q%d" % i, [128, 512], F32) for i in range(2)]
        for n in range(3):
            kb.dma("pool", out=wr.t[:, n, :, :], in_=Wrkv[:, n, :, :], writes=[wr.g])
        kb.dma("pool", out=l1.t[:, :, 0:64], in_=io["od_w1"].rearrange("(kc p) e -> p kc e", p=128), writes=[l1.g])
        kb.dma("pool", out=l1.t[:, :, 64:128], in_=io["od_a1"].rearrange("(kc p) e -> p kc e", p=128), writes=[l1.g])
        kb.dma("pool", out=l1.t[:, :, 128:288], in_=io["od_g1"].rearrange("(kc p) e -> p kc e", p=128), writes=[l1.g])
        kb.dma("pool", out=w2.t[:, :], in_=io["od_w2"][:, :], writes=[w2.g])
        kb.dma("pool", out=a2.t[:, :], in_=io["od_a2"][:, :], writes=[a2.g])
        kb.dma("pool", out=g2a.t[:, :], in_=io["od_g2"][0:128, :], writes=[g2a.g])
        kb.dma("pool", out=g2b.t[:, :], in_=io["od_g2"][128:160, :], writes=[g2b.g])
        load_bcast(kb, w0b, io["od_w0"][0])
        load_bcast(kb, a0b, io["od_a0"][0])
        for n in range(6):
            kb.dma("sp", out=mu.t[:, n, :], in_=io["od_mu"][0, n].rearrange("(kc p) -> p kc", p=128), writes=[mu.g],
                   allow_slow_non_contiguous=True)
        oi = 0
        for b in range(NB):
            t0 = b * 128
            x_ = xx[b % 2]
            if b == 0:
                kb.op("dve", lambda e: e.tensor_tensor(out=x_.t[:, :, 1:128], in0=xT.t[:, :, 0:127], in1=xT.t[:, :, 1:128], op=ALU.subtract),
                      reads=[xT.g], writes=[x_.g])
                kb.op("dve", lambda e: e.tensor_scalar(out=x_.t[:, :, 0:1], in0=xT.t[:, :, 0:1], scalar1=-1.0, scalar2=None, op0=ALU.mult),
                      reads=[xT.g], writes=[x_.g])
            else:
                kb.op("dve", lambda e: e.tensor_tensor(out=x_.t[:, :, :], in0=xT.t[:, :, t0 - 1:t0 + 127], in1=xT.t[:, :, t0:t0 + 128], op=ALU.subtract),
                      reads=[xT.g], writes=[x_.g])
            mx = [mixT[n][b % 2] for n in range(6)]
            for n in range(6):
                for kc in range(8):
                    eng = "dve"
                    kb.op(eng, lambda e: e.scalar_tensor_tensor(out=mx[n].t[:, kc, :], in0=x_.t[:, kc, :], scalar=mu.t[:, n, kc:kc + 1],
                                                                in1=xT.t[:, kc, t0:t0 + 128], op0=ALU.mult, op1=ALU.add),
                          reads=[x_.g, mu.g, xT.g], writes=[mx[n].g])

            def store(idx, ps, pre=None, func=None):
                nonlocal oi
                o = outF[oi % 4]
                oi += 1
                if pre is not None:
                    kb.op("dve", lambda e: e.tensor_tensor(out=o.t[:, :], in0=ps.t[:, :], in1=pre.t[:, :], op=ALU.add), reads=[ps.g, pre.g], writes=[o.g])
                    kb.op("act", lambda e: e.activation(out=o.t[:, :], in_=o.t[:, :], func=func), reads=[o.g], writes=[o.g])
                else:
                    kb.op("act", lambda e: e.activation(out=o.t[:, :], in_=ps.t[:, :], func=AF.Copy), reads=[ps.g], writes=[o.g])
                kb.dma("sp", out=scr[idx][t0:t0 + 128, :], in_=o.t[:, :], reads=[o.g], writes=[scr_reg], pool="st")

            for n in range(3):
                ps = P[n]
                for hf in range(2):
                    for kc in range(8):
                        kb.op("pe", lambda e: e.matmul(out=ps.t[:, hf * 512:(hf + 1) * 512], lhsT=mx[n].t[:, kc, :], rhs=wr.t[:, n, kc, hf * 512:(hf + 1) * 512],
                                                       start=(kc == 0), stop=(kc == 7)), reads=[mx[n].g, wr.g], writes=[ps.g])
                store(n, ps)
            q = Q[0]
            for kc in range(8):
                kb.op("pe", lambda e: e.matmul(out=q.t[0:64, 0:128], lhsT=l1.t[:, kc, 0:64], rhs=mx[3].t[:, kc, :], start=(kc == 0), stop=(kc == 7)),
                      reads=[l1.g, mx[3].g], writes=[q.g])
            for kc in range(8):
                kb.op("pe", lambda e: e.matmul(out=q.t[0:64, 128:256], lhsT=l1.t[:, kc, 64:128], rhs=mx[4].t[:, kc, :], start=(kc == 0), stop=(kc == 7)),
                      reads=[l1.g, mx[4].g], writes=[q.g])
            for kc in range(8):
                kb.op("pe", lambda e: e.matmul(out=q.t[:, 256:384], lhsT=l1.t[:, kc, 128:256], rhs=mx[5].t[:, kc, :], start=(kc == 0), stop=(kc == 7)),
                      reads=[l1.g, mx[5].g], writes=[q.g])
            for kc in range(8):
                kb.op("pe", lambda e: e.matmul(out=q.t[0:32, 384:512], lhsT=l1.t[:, kc, 256:288], rhs=mx[5].t[:, kc, :], start=(kc == 0), stop=(kc == 7)),
                      reads=[l1.g, mx[5].g], writes=[q.g])
            tw, al, sg1 = lo1[0], lo1[1], lo1[2]
            kb.op("act", lambda e: e.activation(out=tw.t[0:64, :], in_=q.t[0:64, 0:128], func=AF.Tanh), reads=[q.g], writes=[tw.g])
            kb.op("act", lambda e: e.activation(out=al.t[0:64, :], in_=q.t[0:64, 128:256], func=AF.Copy), reads=[q.g], writes=[al.g])
            kb.op("act", lambda e: e.activation(out=sg1.t[:, :], in_=q.t[:, 256:384], func=AF.Sigmoid), reads=[q.g], writes=[sg1.g])
            kb.op("act", lambda e: e.activation(out=lo2.t[:, :], in_=q.t[0:32, 384:512], func=AF.Sigmoid), reads=[q.g], writes=[lo2.g])
            ps = P[0]
            for hf in range(2):
                kb.op("pe", lambda e: e.matmul(out=ps.t[:, hf * 512:(hf + 1) * 512], lhsT=tw.t[0:64, :], rhs=w2.t[:, hf * 512:(hf + 1) * 512], start=True, stop=True),
                      reads=[tw.g, w2.g], writes=[ps.g])
            store(3, ps, pre=w0b, func=AF.Sigmoid)
            ps = P[1]
            for hf in range(2):
                kb.op("pe", lambda e: e.matmul(out=ps.t[:, hf * 512:(hf + 1) * 512], lhsT=al.t[0:64, :], rhs=a2.t[:, hf * 512:(hf + 1) * 512], start=True, stop=True),
                      reads=[al.g, a2.g], writes=[ps.g])
            store(4, ps, pre=a0b, func=AF.Sigmoid)
            ps = P[2]
            for hf in range(2):
                kb.op("pe", lambda e: e.matmul(out=ps.t[:, hf * 512:(hf + 1) * 512], lhsT=sg1.t[:, :], rhs=g2a.t[:, hf * 512:(hf + 1) * 512], start=True, stop=False),
                      reads=[sg1.g, g2a.g], writes=[ps.g])
                kb.op("pe", lambda e: e.matmul(out=ps.t[:, hf * 512:(hf + 1) * 512], lhsT=lo2.t[:, :], rhs=g2b.t[:, hf * 512:(hf + 1) * 512], start=False, stop=True),
                      reads=[lo2.g, g2b.g], writes=[ps.g])
            store(5, ps)
        kb.barrier()


def phase_rwkv_b(kb, nc, io, XT3, xt3_reg, ident, ones, tri, scr, scr_reg, xres_ap, xres_reg, xout_ap, xout_reg):
    H3 = lambda ap: ap.rearrange("p (h d) -> p h d", h=16)
    with ExitStack() as es:
        def F(name):
            return sbt(nc, es, "rb_" + name, [128, 1024], F32)

        def B(name):
            return sbt(nc, es, "rb_" + name, [128, 1024], BF16)

        wo = sbt(nc, es, "rb_wo", [128, 8, 1024], BF16)
        vec = {}
        for nm, ap in (("k_k", io["od_k_k"][0]), ("k_a", io["od_k_a"][0]), ("r_k", io["od_r_k"][0]), ("lnx_g", io["od_lnx_g"][0]),
                       ("lnx_b", io["od_lnx_b"][0]), ("lng", io["ln_mix_g"][1]), ("lnb", io["ln_mix_b"][1])):
            vec[nm] = F("v_" + nm)
            load_bcast(kb, vec[nm], ap)
        kb.dma("pool", out=wo.t[:, :, :], in_=io["od_w_out"].rearrange("(kc p) n -> p kc n", p=128), writes=[wo.g])
        msk = {}
        for nm in ("c_su4", "c_sl4", "c_iu4", "c_id4"):
            msk[nm] = sbt(nc, es, "rb_" + nm, [128, 512], BF16)
            kb.dma("sp", out=msk[nm].t[:, :], in_=io[nm][:, :], writes=[msk[nm].g])
        inb = [[F("in%d_%d" % (i, j)) for i in range(6)] for j in range(2)]
        x3ts = [B("x3t0"), B("x3t1")]
        f1, f2, f3 = F("f1"), F("f2"), F("f3")
        vB, lhi, llo = B("vB"), B("lhi"), B("llo")
        rtB, atB, btB, ktB, bpB, kpB = B("rt"), B("at"), B("bt"), B("kt"), B("bp"), B("kp")
        rT, aT, bT, kTt = B("rT"), B("aT"), B("bT"), B("kT")
        Arb, Ark = [B("Arb0"), B("Arb1")], [B("Ark0"), B("Ark1")]
        Xall = [B("X0"), B("X1")]
        AhT = B("AhT")
        W1b, Ub, ygB = rtB, btB, ktB
        tmpg = [[sbt(nc, es, "rb_tg%d_%d" % (g, i), [128, 512], BF16) for i in range(10)] for g in range(4)]
        tmpb = tmpg[0]
        small = sbt(nc, es, "rb_small", [128, 96], F32)
        eLC = sbt(nc, es, "rb_eLC", [128, 8], F32)
        Hs = sbt(nc, es, "rb_H", [128, 8, 64], F32)
        Hb = sbt(nc, es, "rb_Hb", [128, 8, 64], BF16)
        xo = f3
        xbf = lhi
        stats = sbt(nc, es, "rb_stats", [128, 12], F32)
        mv = sbt(nc, es, "rb_mv", [128, 2], F32)
        rstd = sbt(nc, es, "rb_rstd", [128, 1], F32)
        P = [pst(nc, es, "rb_p%d" % i, [128, 1024], F32) for i in range(3)]
        QT = pst(nc, es, "rb_qt", [128, 1024], BF16)
        Q1 = pst(nc, es, "rb_q1", [128, 512], F32)
        kb.op("dve", lambda e: e.memset(Hs.t[:, :, :], 0.0), writes=[Hs.g])
        kb.op("dve", lambda e: e.memset(Hb.t[:, :, :], 0.0), writes=[Hb.g])

        def bc16(t, c0):
            return small.t[:, c0:c0 + 16].unsqueeze(2).to_broadcast([128, 16, 64])

        for b in range(NB):
            t0 = b * 128
            if b == 0:
                for idx, dst in enumerate(inb[0]):
                    kb.dma("sp", out=dst.t[:, :], in_=scr[idx][0:128, :], reads=[scr_reg], writes=[dst.g])
            rF, kF, vF, wF, aF, gF = inb[b % 2]
            xin = rF
            if b + 1 < NB:
                for idx, dst in enumerate(inb[(b + 1) % 2]):
                    kb.dma("sp", out=dst.t[:, :], in_=scr[idx][t0 + 128:t0 + 256, :], reads=[scr_reg], writes=[dst.g])
            kb.op("act", lambda e: e.activation(out=vB.t[:, :], in_=vF.t[:, :], func=AF.Copy), reads=[vF.g], writes=[vB.g])
            kb.op("dve", lambda e: e.tensor_scalar(out=wF.t[:, :], in0=wF.t[:, :], scalar1=-math.exp(-0.5), scalar2=None, op0=ALU.mult), reads=[wF.g], writes=[wF.g])
            kb.op("act", lambda e: e.activation(out=lhi.t[:, :], in_=wF.t[:, :], func=AF.Copy), reads=[wF.g], writes=[lhi.g])
            kb.op("dve", lambda e: e.tensor_tensor(out=llo.t[:, :], in0=wF.t[:, :], in1=lhi.t[:, :], op=ALU.subtract), reads=[wF.g, lhi.g], writes=[llo.g])
            kb.op("dve", lambda e: e.tensor_tensor(out=f1.t[:, :], in0=kF.t[:, :], in1=vec["k_k"].t[:, :], op=ALU.mult), reads=[kF.g, vec["k_k"].g], writes=[f1.g])
            kb.op("act", lambda e: e.activation(out=f2.t[:, :], in_=f1.t[:, :], func=AF.Square), reads=[f1.g], writes=[f2.g])
            kb.op("dve", lambda e: e.tensor_reduce(out=small.t[:, 0:16], in_=H3(f2.t[:, :]), axis=AX.X, op=ALU.add), reads=[f2.g], writes=[small.g])
            kb.op("act", lambda e: e.activation(out=small.t[:, 0:16], in_=small.t[:, 0:16], func=AF.Sqrt), reads=[small.g], writes=[small.g])
            kb.op("dve", lambda e: e.tensor_scalar(out=small.t[:, 0:16], in0=small.t[:, 0:16], scalar1=1e-12, scalar2=None, op0=ALU.max), reads=[small.g], writes=[small.g])
            kb.op("dve", lambda e: e.reciprocal(out=small.t[:, 0:16], in_=small.t[:, 0:16]), reads=[small.g], writes=[small.g])
            kb.op("dve", lambda e: e.tensor_tensor(out=H3(f1.t[:, :]), in0=H3(f1.t[:, :]), in1=bc16(small, 0), op=ALU.mult), reads=[f1.g, small.g], writes=[f1.g])
            kb.op("dve", lambda e: e.scalar_tensor_tensor(out=f2.t[:, :], in0=aF.t[:, :], scalar=-1.0, in1=vec["k_a"].t[:, :], op0=ALU.add, op1=ALU.mult),
                  reads=[aF.g, vec["k_a"].g], writes=[f2.g])
            kb.op("dve", lambda e: e.scalar_tensor_tensor(out=f2.t[:, :], in0=f2.t[:, :], scalar=1.0, in1=kF.t[:, :], op0=ALU.add, op1=ALU.mult),
                  reads=[f2.g, kF.g], writes=[f2.g])
            kb.op("dve", lambda e: e.tensor_tensor(out=kF.t[:, :], in0=f1.t[:, :], in1=aF.t[:, :], op=ALU.mult), reads=[f1.g, aF.g], writes=[kF.g])
            if RB_STOP == 1:
                kb.barrier()
                return
            for hf in range(2):
                sl = slice(hf * 512, (hf + 1) * 512)
                kb.op("pe", lambda e: e.matmul(out=P[0].t[:, sl], lhsT=tri.t[:, :], rhs=lhi.t[:, sl], start=True, stop=False), reads=[tri.g, lhi.g], writes=[P[0].g])
                kb.op("pe", lambda e: e.matmul(out=P[0].t[:, sl], lhsT=tri.t[:, :], rhs=llo.t[:, sl], start=False, stop=True), reads=[tri.g, llo.g], writes=[P[0].g])
                kb.op("pe", lambda e: e.matmul(out=P[1].t[:, sl], lhsT=ones.t[:, :], rhs=lhi.t[:, sl], start=True, stop=False), reads=[ones.g, lhi.g], writes=[P[1].g])
                kb.op("pe", lambda e: e.matmul(out=P[1].t[:, sl], lhsT=ones.t[:, :], rhs=llo.t[:, sl], start=False, stop=True), reads=[ones.g, llo.g], writes=[P[1].g])
            for hp in range(8):
                kb.op("pe", lambda e: e.matmul(out=Q1.t[:, hp:hp + 1], lhsT=lhi.t[:, hp * 128:(hp + 1) * 128], rhs=ones.t[:, 0:1], start=True, stop=False),
                      reads=[lhi.g, ones.g], writes=[Q1.g])
                kb.op("pe", lambda e: e.matmul(out=Q1.t[:, hp:hp + 1], lhsT=llo.t[:, hp * 128:(hp + 1) * 128], rhs=ones.t[:, 0:1], start=False, stop=True),
                      reads=[llo.g, ones.g], writes=[Q1.g])
            kb.op("act", lambda e: e.activation(out=eLC.t[:, :], in_=Q1.t[:, 0:8], func=AF.Exp), reads=[Q1.g], writes=[eLC.g])
            if RB_STOP == 2:
                kb.barrier()
                return
            kb.op("act", lambda e: e.activation(out=aF.t[:, :], in_=P[0].t[:, :], func=AF.Copy), reads=[P[0].g], writes=[aF.g])
            kb.op("act", lambda e: e.activation(out=f3.t[:, :], in_=P[0].t[:, :], func=AF.Exp), reads=[P[0].g], writes=[f3.g])
            kb.op("dve", lambda e: e.tensor_tensor(out=rtB.t[:, :], in0=rF.t[:, :], in1=f3.t[:, :], op=ALU.mult), reads=[rF.g, f3.g], writes=[rtB.g])
            kb.op("dve", lambda e: e.tensor_tensor(out=wF.t[:, :], in0=aF.t[:, :], in1=wF.t[:, :], op=ALU.subtract), reads=[aF.g, wF.g], writes=[wF.g])
            kb.op("act", lambda e: e.activation(out=wF.t[:, :], in_=wF.t[:, :], func=AF.Exp), reads=[wF.g], writes=[wF.g])
            kb.op("dve", lambda e: e.scalar_tensor_tensor(out=atB.t[:, :], in0=f1.t[:, :], scalar=-1.0, in1=wF.t[:, :], op0=ALU.mult, op1=ALU.mult),
                  reads=[f1.g, wF.g], writes=[atB.g])
            kb.op("act", lambda e: e.activation(out=f3.t[:, :], in_=aF.t[:, :], func=AF.Exp, scale=-1.0), reads=[aF.g], writes=[f3.g])
            kb.op("dve", lambda e: e.tensor_tensor(out=btB.t[:, :], in0=kF.t[:, :], in1=f3.t[:, :], op=ALU.mult), reads=[kF.g, f3.g], writes=[btB.g])
            kb.op("dve", lambda e: e.tensor_tensor(out=ktB.t[:, :], in0=f2.t[:, :], in1=f3.t[:, :], op=ALU.mult), reads=[f2.g, f3.g], writes=[ktB.g])
            kb.op("dve", lambda e: e.tensor_tensor(out=f3.t[:, :], in0=P[1].t[:, :], in1=aF.t[:, :], op=ALU.subtract), reads=[P[1].g, aF.g], writes=[f3.g])
            kb.op("act", lambda e: e.activation(out=f3.t[:, :], in_=f3.t[:, :], func=AF.Exp), reads=[f3.g], writes=[f3.g])
            kb.op("dve", lambda e: e.tensor_tensor(out=bpB.t[:, :], in0=kF.t[:, :], in1=f3.t[:, :], op=ALU.mult), reads=[kF.g, f3.g], writes=[bpB.g])
            kb.op("dve", lambda e: e.tensor_tensor(out=kpB.t[:, :], in0=f2.t[:, :], in1=f3.t[:, :], op=ALU.mult), reads=[f2.g, f3.g], writes=[kpB.g])
            kb.op("dve", lambda e: e.tensor_tensor(out=f3.t[:, :], in0=rF.t[:, :], in1=f2.t[:, :], op=ALU.mult), reads=[rF.g, f2.g], writes=[f3.g])
            kb.op("dve", lambda e: e.tensor_tensor(out=f3.t[:, :], in0=f3.t[:, :], in1=vec["r_k"].t[:, :], op=ALU.mult), reads=[f3.g, vec["r_k"].g], writes=[f3.g])
            kb.op("dve", lambda e: e.tensor_reduce(out=small.t[:, 16:32], in_=H3(f3.t[:, :]), axis=AX.X, op=ALU.add), reads=[f3.g], writes=[small.g])
            kb.op("dve", lambda e: e.tensor_tensor(out=H3(vF.t[:, :]), in0=H3(vF.t[:, :]), in1=bc16(small, 16), op=ALU.mult), reads=[vF.g, small.g], writes=[vF.g])
            if RB_STOP == 3:
                kb.barrier()
                return
            for src, dst in ((rtB, rT), (atB, aT), (btB, bT), (ktB, kTt)):
                for hp in range(8):
                    kb.op("pe", lambda e: e.transpose(out=QT.t[:, hp * 128:(hp + 1) * 128], in_=src.t[:, hp * 128:(hp + 1) * 128], identity=ident.t[:, :]),
                          reads=[src.g, ident.g], writes=[QT.g])
                kb.op("act", lambda e: e.activation(out=dst.t[:, :], in_=QT.t[:, :], func=AF.Copy), reads=[QT.g], writes=[dst.g])

            if RB_STOP == 4:
                kb.barrier()
                return
            def fm(t, h):
                r0 = 64 * (h % 2)
                return t.t[r0:r0 + 64, (h // 2) * 128:(h // 2 + 1) * 128]

            gst = []
            for g4 in range(4):
                heads = [g4 * 4 + i for i in range(4)]
                order = [(0, heads[0]), (2, heads[2]), (1, heads[1]), (3, heads[3])]
                tb = tmpg[g4]
                Nb, NTb = tb[0], tb[1]
                specs = ((bT, aT, Nb, "c_su4"), (aT, bT, NTb, "c_sl4"))
                ps = P[(2 * g4) % 3]
                for si, (la, rb_, dst, mk) in enumerate(specs):
                    off = si * 512
                    for i, h in order:
                        kb.op("pe", lambda e: e.matmul(out=ps.t[:, off + i * 128:off + (i + 1) * 128], lhsT=fm(la, h), rhs=fm(rb_, h), start=True, stop=True),
                              reads=[la.g, rb_.g], writes=[ps.g], rg=(64 * (h % 2), 64))
                    kb.op("dve", lambda e: e.tensor_tensor(out=dst.t[:, :], in0=ps.t[:, off:off + 512], in1=msk[mk].t[:, :], op=ALU.mult),
                          reads=[ps.g, msk[mk].g], writes=[dst.g])
                hi_ = g4 // 2
                co = (g4 % 2) * 512
                ps = P[(2 * g4 + 1) % 3]
                for si, (la, rb_, dst) in enumerate(((bT, rT, Arb[hi_]), (kTt, rT, Ark[hi_]))):
                    off = si * 512
                    for i, h in order:
                        kb.op("pe", lambda e: e.matmul(out=ps.t[:, off + i * 128:off + (i + 1) * 128], lhsT=fm(la, h), rhs=fm(rb_, h), start=True, stop=True),
                              reads=[la.g, rb_.g], writes=[ps.g], rg=(64 * (h % 2), 64))
                    kb.op("dve", lambda e: e.tensor_tensor(out=dst.t[:, co:co + 512], in0=ps.t[:, off:off + 512], in1=msk["c_iu4"].t[:, :], op=ALU.mult),
                          reads=[ps.g, msk["c_iu4"].g], writes=[dst.g])
                X, XT = tb[2], tb[3]
                kb.op("dve", lambda e: e.tensor_tensor(out=X.t[:, :], in0=Nb.t[:, :], in1=msk["c_id4"].t[:, :], op=ALU.add), reads=[Nb.g, msk["c_id4"].g], writes=[X.g])
                kb.op("dve", lambda e: e.tensor_tensor(out=XT.t[:, :], in0=NTb.t[:, :], in1=msk["c_id4"].t[:, :], op=ALU.add), reads=[NTb.g, msk["c_id4"].g], writes=[XT.g])
                gst.append({"X": X, "XT": XT, "P": Nb, "PT": NTb, "pp": 0})
            for it in range(6):
                last = it == 5
                for g4 in range(4):
                    st = gst[g4]
                    tb = tmpg[g4]
                    hi_ = g4 // 2
                    co = (g4 % 2) * 512
                    X, XT, Pm, PTm = st["X"], st["XT"], st["P"], st["PT"]
                    P2, P2T = tb[4 + st["pp"]], tb[6 + st["pp"]]
                    st["pp"] ^= 1
                    psa = P[(2 * g4 + 2 * it) % 3]
                    psb = P[(2 * g4 + 2 * it + 1) % 3]
                    for i in range(4):
                        sl = slice(i * 128, (i + 1) * 128)
                        kb.op("pe", lambda e: e.matmul(out=psa.t[:, sl], lhsT=PTm.t[:, sl], rhs=Pm.t[:, sl], start=True, stop=True), reads=[PTm.g, Pm.g], writes=[psa.g])
                    if not last:
                        for i in range(4):
                            sl = slice(i * 128, (i + 1) * 128)
                            sl2 = slice(512 + i * 128, 512 + (i + 1) * 128)
                            kb.op("pe", lambda e: e.matmul(out=psa.t[:, sl2], lhsT=Pm.t[:, sl], rhs=PTm.t[:, sl], start=True, stop=True), reads=[PTm.g, Pm.g], writes=[psa.g])
                    kb.op("act", lambda e: e.activation(out=P2.t[:, :], in_=psa.t[:, 0:512], func=AF.Copy), reads=[psa.g], writes=[P2.g])
                    if not last:
                        kb.op("act", lambda e: e.activation(out=P2T.t[:, :], in_=psa.t[:, 512:1024], func=AF.Copy), reads=[psa.g], writes=[P2T.g])
                    for i in range(4):
                        sl = slice(i * 128, (i + 1) * 128)
                        kb.op("pe", lambda e: e.matmul(out=psb.t[:, sl], lhsT=XT.t[:, sl], rhs=P2.t[:, sl], start=True, stop=True), reads=[XT.g, P2.g], writes=[psb.g])
                    if not last:
                        for i in range(4):
                            sl = slice(i * 128, (i + 1) * 128)
                            sl2 = slice(512 + i * 128, 512 + (i + 1) * 128)
                            kb.op("pe", lambda e: e.matmul(out=psb.t[:, sl2], lhsT=P2.t[:, sl], rhs=XT.t[:, sl], start=True, stop=True), reads=[XT.g, P2.g], writes=[psb.g])
                    if last:
                        kb.op("dve", lambda e: e.tensor_tensor(out=Xall[hi_].t[:, co:co + 512], in0=psb.t[:, 0:512], in1=X.t[:, :], op=ALU.add),
                              reads=[psb.g, X.g], writes=[Xall[hi_].g])
                    else:
                        Xn, XTn = (tb[8], tb[9]) if (it % 2 == 0) else (tb[2], tb[3])
                        kb.op("dve", lambda e: e.tensor_tensor(out=Xn.t[:, :], in0=psb.t[:, 0:512], in1=X.t[:, :], op=ALU.add), reads=[psb.g, X.g], writes=[Xn.g])
                        kb.op("dve", lambda e: e.tensor_tensor(out=XTn.t[:, :], in0=psb.t[:, 512:1024], in1=XT.t[:, :], op=ALU.add), reads=[psb.g, XT.g], writes=[XTn.g])
                        st["X"], st["XT"], st["P"], st["PT"] = Xn, XTn, P2, P2T
            if RB_STOP == 5:
                kb.barrier()
                return
            for g4 in range(4):
                heads = [g4 * 4 + i for i in range(4)]
                order = [(0, heads[0]), (2, heads[2]), (1, heads[1]), (3, heads[3])]
                Aak = tmpb[2]
                ps = P[2]
                for i, h in ((0, heads[0]), (2, heads[2]), (1, heads[1]), (3, heads[3])):
                    kb.op("pe", lambda e: e.matmul(out=ps.t[:, i * 128:(i + 1) * 128], lhsT=fm(kTt, h), rhs=fm(aT, h), start=True, stop=True),
                          reads=[kTt.g, aT.g], writes=[ps.g], rg=(64 * (h % 2), 64))
                kb.op("dve", lambda e: e.tensor_tensor(out=Aak.t[:, :], in0=ps.t[:, 0:512], in1=msk["c_su4"].t[:, :], op=ALU.mult),
                      reads=[ps.g, msk["c_su4"].g], writes=[Aak.g])
                for i, h in enumerate(heads):
                    kb.op("pe", lambda e: e.matmul(out=P[1].t[:, h * 64:(h + 1) * 64], lhsT=Aak.t[:, i * 128:(i + 1) * 128], rhs=vB.t[:, h * 64:(h + 1) * 64],
                                                   start=True, stop=True), reads=[Aak.g, vB.g], writes=[P[1].g])
                hi_ = g4 // 2
                co = (g4 % 2) * 512
                for i, h in enumerate(heads):
                    hp = h // 2
                    kb.op("pe", lambda e: e.matmul(out=ps.t[:, 512 + i * 128:512 + (i + 1) * 128], lhsT=atB.t[:, hp * 128:(hp + 1) * 128],
                                                   rhs=Xall[hi_].t[:, co + i * 128:co + (i + 1) * 128], start=True, stop=True),
                          reads=[atB.g, Xall[hi_].g], writes=[ps.g])
                v4 = ps.t[:, 512:1024].rearrange("p (j two t) -> p j two t", j=2, two=2)
                o4 = AhT.t[:, g4 * 256:(g4 + 1) * 256].rearrange("p (j t) -> p j t", j=2)
                kb.op("act", lambda e: e.activation(out=o4[0:64, :, :], in_=v4[0:64, :, 0, :], func=AF.Copy), reads=[ps.g], writes=[AhT.g])
                kb.op("act", lambda e: e.activation(out=o4[64:128, :, :], in_=v4[64:128, :, 1, :], func=AF.Copy), reads=[ps.g], writes=[AhT.g])
            kb.op("act", lambda e: e.activation(out=W1b.t[:, :], in_=P[1].t[:, :], func=AF.Copy), reads=[P[1].g], writes=[W1b.g])
            if RB_STOP == 6:
                kb.barrier()
                return
            Hb3 = Hb.t
            for h in range(16):
                hp, r0 = h // 2, 64 * (h % 2)
                hi_, co = h // 8, (h % 8) * 128
                kb.op("pe", lambda e: e.matmul(out=P[0].t[:, h * 64:(h + 1) * 64], lhsT=AhT.t[r0:r0 + 64, hp * 128:(hp + 1) * 128], rhs=Hb3[r0:r0 + 64, hp, :],
                                               start=True, stop=False), reads=[AhT.g, Hb.g], writes=[P[0].g])
                kb.op("pe", lambda e: e.matmul(out=P[0].t[:, h * 64:(h + 1) * 64], lhsT=Xall[hi_].t[:, co:co + 128], rhs=W1b.t[:, h * 64:(h + 1) * 64],
                                               start=False, stop=True), reads=[Xall[hi_].g, W1b.g], writes=[P[0].g])
            kb.op("act", lambda e: e.activation(out=Ub.t[:, :], in_=P[0].t[:, :], func=AF.Copy), reads=[P[0].g], writes=[Ub.g])
            for h in range(16):
                hp, r0 = h // 2, 64 * (h % 2)
                hi_, co = h // 8, (h % 8) * 128
                o = P[2].t[:, h * 64:(h + 1) * 64]
                kb.op("pe", lambda e: e.matmul(out=o, lhsT=fm(rT, h), rhs=Hb3[r0:r0 + 64, hp, :], start=True, stop=False), reads=[rT.g, Hb.g], writes=[P[2].g])
                kb.op("pe", lambda e: e.matmul(out=o, lhsT=Arb[hi_].t[:, co:co + 128], rhs=Ub.t[:, h * 64:(h + 1) * 64], start=False, stop=False),
                      reads=[Arb[hi_].g, Ub.g], writes=[P[2].g])
                kb.op("pe", lambda e: e.matmul(out=o, lhsT=Ark[hi_].t[:, co:co + 128], rhs=vB.t[:, h * 64:(h + 1) * 64], start=False, stop=True),
                      reads=[Ark[hi_].g, vB.g], writes=[P[2].g])
            for hp in range(8):
                sl = slice(hp * 128, (hp + 1) * 128)
                kb.op("pe", lambda e: e.matmul(out=P[1].t[:, sl], lhsT=bpB.t[:, sl], rhs=Ub.t[:, sl], start=True, stop=False), reads=[bpB.g, Ub.g], writes=[P[1].g])
                kb.op("pe", lambda e: e.matmul(out=P[1].t[:, sl], lhsT=kpB.t[:, sl], rhs=vB.t[:, sl], start=False, stop=True), reads=[kpB.g, vB.g], writes=[P[1].g])
            kb.op("dve", lambda e: e.tensor_tensor(out=Hs.t[:, :, :], in0=Hs.t[:, :, :], in1=eLC.t[:, 0:8].unsqueeze(2).to_broadcast([128, 8, 64]), op=ALU.mult),
                  reads=[Hs.g, eLC.g], writes=[Hs.g])
            hv = P[1].t[:, :].rearrange("p (hp two d) -> p hp two d", hp=8, two=2)
            kb.op("dve", lambda e: e.tensor_tensor(out=Hs.t[0:64, :, :], in0=Hs.t[0:64, :, :], in1=hv[0:64, :, 0, :], op=ALU.add), reads=[Hs.g, P[1].g], writes=[Hs.g])
            kb.op("dve", lambda e: e.tensor_tensor(out=Hs.t[64:128, :, :], in0=Hs.t[64:128, :, :], in1=hv[64:128, :, 1, :], op=ALU.add), reads=[Hs.g, P[1].g], writes=[Hs.g])
            kb.op("act", lambda e: e.activation(out=Hb.t[:, :, :], in_=Hs.t[:, :, :], func=AF.Copy), reads=[Hs.g], writes=[Hb.g])
            if RB_STOP == 7:
                kb.barrier()
                return
            Y = P[2]
            kb.op("act", lambda e: e.activation(out=f1.t[:, :], in_=Y.t[:, :], func=AF.Copy), reads=[Y.g], writes=[f1.g])
            kb.op("act", lambda e: e.activation(out=f2.t[:, :], in_=Y.t[:, :], func=AF.Square), reads=[Y.g], writes=[f2.g])
            kb.op("dve", lambda e: e.tensor_reduce(out=small.t[:, 32:48], in_=H3(f1.t[:, :]), axis=AX.X, op=ALU.add), reads=[f1.g], writes=[small.g])
            kb.op("dve", lambda e: e.tensor_reduce(out=small.t[:, 48:64], in_=H3(f2.t[:, :]), axis=AX.X, op=ALU.add), reads=[f2.g], writes=[small.g])
            kb.op("dve", lambda e: e.tensor_scalar(out=small.t[:, 32:64], in0=small.t[:, 32:64], scalar1=1.0 / 64, scalar2=None, op0=ALU.mult), reads=[small.g], writes=[small.g])
            kb.op("dve", lambda e: e.tensor_tensor(out=small.t[:, 64:80], in0=small.t[:, 32:48], in1=small.t[:, 32:48], op=ALU.mult), reads=[small.g], writes=[small.g])
            kb.op("dve", lambda e: e.tensor_tensor(out=small.t[:, 64:80], in0=small.t[:, 48:64], in1=small.t[:, 64:80], op=ALU.subtract), reads=[small.g], writes=[small.g])
            kb.op("dve", lambda e: e.tensor_scalar(out=small.t[:, 64:80], in0=small.t[:, 64:80], scalar1=64e-5, scalar2=None, op0=ALU.add), reads=[small.g], writes=[small.g])
            kb.op("act", lambda e: e.activation(out=small.t[:, 64:80], in_=small.t[:, 64:80], func=AF.Sqrt), reads=[small.g], writes=[small.g])
            kb.op("dve", lambda e: e.reciprocal(out=small.t[:, 64:80], in_=small.t[:, 64:80]), reads=[small.g], writes=[small.g])
            kb.op("dve", lambda e: e.tensor_tensor(out=H3(f1.t[:, :]), in0=H3(f1.t[:, :]), in1=bc16(small, 32), op=ALU.subtract), reads=[f1.g, small.g], writes=[f1.g])
            kb.op("dve", lambda e: e.tensor_tensor(out=H3(f1.t[:, :]), in0=H3(f1.t[:, :]), in1=bc16(small, 64), op=ALU.mult), reads=[f1.g, small.g], writes=[f1.g])
            kb.op("dve", lambda e: e.tensor_tensor(out=f1.t[:, :], in0=f1.t[:, :], in1=vec["lnx_g"].t[:, :], op=ALU.mult), reads=[f1.g, vec["lnx_g"].g], writes=[f1.g])
            kb.op("dve", lambda e: e.tensor_tensor(out=f1.t[:, :], in0=f1.t[:, :], in1=vec["lnx_b"].t[:, :], op=ALU.add), reads=[f1.g, vec["lnx_b"].g], writes=[f1.g])
            kb.op("dve", lambda e: e.tensor_tensor(out=f1.t[:, :], in0=f1.t[:, :], in1=vF.t[:, :], op=ALU.add), reads=[f1.g, vF.g], writes=[f1.g])
            kb.op("dve", lambda e: e.tensor_tensor(out=ygB.t[:, :], in0=f1.t[:, :], in1=gF.t[:, :], op=ALU.mult), reads=[f1.g, gF.g], writes=[ygB.g])
            if RB_STOP == 8:
                kb.barrier()
                return
            for kc in range(8):
                kb.op("pe", lambda e: e.transpose(out=QT.t[:, kc * 128:(kc + 1) * 128], in_=ygB.t[:, kc * 128:(kc + 1) * 128], identity=ident.t[:, :]),
                      reads=[ygB.g, ident.g], writes=[QT.g])
            ygT = aT
            kb.op("act", lambda e: e.activation(out=ygT.t[:, :], in_=QT.t[:, :], func=AF.Copy), reads=[QT.g], writes=[ygT.g])
            kb.dma("sp", out=xin.t[:, :], in_=xres_ap[t0:t0 + 128, :], reads=[xres_reg], writes=[xin.g])
            for hf in range(2):
                for fc in range(8):
                    kb.op("pe", lambda e: e.matmul(out=P[0].t[:, hf * 512:(hf + 1) * 512], lhsT=ygT.t[:, fc * 128:(fc + 1) * 128], rhs=wo.t[:, fc, hf * 512:(hf + 1) * 512],
                                                   start=(fc == 0), stop=(fc == 7)), reads=[ygT.g, wo.g], writes=[P[0].g])
            kb.op("dve", lambda e: e.scalar_tensor_tensor(out=f2.t[:, :], in0=xin.t[:, :], scalar=DN_ALPHA, in1=P[0].t[:, :], op0=ALU.mult, op1=ALU.add),
                  reads=[xin.g, P[0].g], writes=[f2.g])
            ln_block(kb, f2, vec["lng"], vec["lnb"], xo, stats, mv, rstd, LN_EPS)
            kb.dma("sp", out=xout_ap[t0:t0 + 128, :], in_=xo.t[:, :], reads=[xo.g], writes=[xout_reg], pool="st")
            kb.op("act", lambda e: e.activation(out=xbf.t[:, :], in_=xo.t[:, :], func=AF.Copy), reads=[xo.g], writes=[xbf.g])
            for kc in range(8):
                kb.op("pe", lambda e: e.transpose(out=QT.t[:, kc * 128:(kc + 1) * 128], in_=xbf.t[:, kc * 128:(kc + 1) * 128], identity=ident.t[:, :]),
                      reads=[xbf.g, ident.g], writes=[QT.g])
            x3t = x3ts[b % 2]
            kb.op("act", lambda e: e.activation(out=x3t.t[:, :], in_=QT.t[:, :], func=AF.Copy), reads=[QT.g], writes=[x3t.g])
            kb.dma("sp", out=XT3[:, :, t0:t0 + 128], in_=x3t.t[:, :].rearrange("p (k t) -> p k t", k=8), reads=[x3t.g], writes=[xt3_reg], pool="st")
            if RB_STOP >= 10 and b == RB_STOP - 10:
                kb.barrier()
                return
        kb.barrier()


CONST_SPECS = {
    "c_ident": ([128, 128], BF16),
    "c_ones": ([128, 128], BF16),
    "c_tri": ([128, 128], BF16),
    "c_ones2": ([2, S], BF16),
    "c_alibiq": ([4, 2, S], BF16),
    "c_abias": ([4, 128, 32], F32),
    "c_retDT": ([4, 128, 512], F32),
    "c_retqdec": ([4, 64, 512], F32),
    "c_retkdec": ([4, 128, 1], F32),
    "c_su4": ([128, 512], BF16),
    "c_sl4": ([128, 512], BF16),
    "c_iu4": ([128, 512], BF16),
    "c_id4": ([128, 512], BF16),
}


def make_consts():
    bf = ml_dtypes.bfloat16
    c = {}
    c["c_ident"] = np.eye(128, dtype=np.float32).astype(bf)
    c["c_ones"] = np.ones((128, 128), np.float32).astype(bf)
    p = np.arange(128)
    c["c_tri"] = (p[None, :] >= p[:, None]).astype(np.float32).astype(bf)
    c["c_ones2"] = np.ones((2, S), np.float32).astype(bf)
    t = np.arange(S) % 512
    hi = (t // 16) * 16
    lo = t % 16
    aq = np.zeros((4, 2, S), np.float64)
    ab = np.zeros((4, 128, 32), np.float64)
    for h in range(4):
        aq[h, 0] = -8.0 * SLOPES[h] * hi
        aq[h, 1] = -8.0 * SLOPES[h] * lo
        for oi in range(32):
            ab[h, :, oi] = SLOPES[h] * (p + 128.0 * (oi - 28))
    c["c_alibiq"] = aq.astype(np.float32).astype(bf)
    c["c_abias"] = ab.astype(np.float32)
    DTm = np.zeros((4, 128, 512), np.float64)
    qd = np.zeros((4, 64, 512), np.float64)
    kd = np.zeros((4, 128, 1), np.float64)
    i = np.arange(128)
    for h in range(4):
        g = GAMMAS[h]
        rel = i[None, :] - i[:, None]
        m = np.where(rel >= 0, 0.125 * g ** np.maximum(rel, 0), 0.0)
        DTm[h] = np.tile(m, (1, 4))
        qd[h] = np.tile(g ** (i + 1.0), (64, 4))
        kd[h, :, 0] = 0.125 * g ** (127.0 - i)
    c["c_retDT"] = DTm.astype(np.float32)
    c["c_retqdec"] = qd.astype(np.float32)
    c["c_retkdec"] = kd.astype(np.float32)
    c["c_su4"] = np.tile((p[None, :] > p[:, None]).astype(np.float32), (1, 4)).astype(bf)
    c["c_sl4"] = np.tile((p[None, :] < p[:, None]).astype(np.float32), (1, 4)).astype(bf)
    c["c_iu4"] = np.tile((p[None, :] >= p[:, None]).astype(np.float32), (1, 4)).astype(bf)
    c["c_id4"] = np.tile(np.eye(128, dtype=np.float32), (1, 4)).astype(bf)
    return c


INPUT_SHAPES = {
    "ev_w_in": [1, 1024, 3072], "ev_lambda": [1, 4, 64], "ev_subln_g": [1, 128], "ev_w_out": [1, 1024, 1024],
    "od_mu": [1, 6, 1024], "od_w_rkv": [1, 3, 1024, 1024], "od_w0": [1, 1024], "od_w1": [1, 1024, 64], "od_w2": [1, 64, 1024],
    "od_a0": [1, 1024], "od_a1": [1, 1024, 64], "od_a2": [1, 64, 1024], "od_g1": [1, 1024, 160], "od_g2": [1, 160, 1024],
    "od_k_k": [1, 1024], "od_k_a": [1, 1024], "od_r_k": [1, 1024], "od_lnx_g": [1, 1024], "od_lnx_b": [1, 1024],
    "od_w_out": [1, 1024, 1024], "ln_mix_g": [2, 1024], "ln_mix_b": [2, 1024], "ffn_w_up": [2, 1024, 5632],
    "ffn_conv_w": [2, 3, 2816], "ffn_conv_b": [2, 2816], "ffn_w_down": [2, 2816, 1024], "ln_ffn_g": [2, 1024], "ln_ffn_b": [2, 1024],
}


def build(stop_after=None, debug=False):
    nc = bass.Bass("TRN2", target_bir_lowering=False)
    io = {}
    io["x"] = nc.dram_tensor("x", [S, D], F32, kind="ExternalInput").ap()
    for k, shp in INPUT_SHAPES.items():
        io[k] = nc.dram_tensor(k, shp, F32, kind="ExternalInput").ap()
    for k, (shp, dt) in CONST_SPECS.items():
        io[k] = nc.dram_tensor(k, shp, dt, kind="ExternalInput").ap()
    y = nc.dram_tensor("y", [S, D], F32, kind="ExternalOutput").ap()
    XA = nc.dram_tensor("scr_xa", [S, D], F32, kind="Internal").ap()
    XB = nc.dram_tensor("scr_xb", [S, D], F32, kind="Internal").ap()
    G = nc.dram_tensor("scr_g", [FF, S], BF16, kind="Internal").ap()
    scr = [nc.dram_tensor("scr_r%d" % i, [S, D], F32, kind="Internal").ap() for i in range(6)]
    scr_reg = Reg(True)
    dbg = None
    if debug:
        dbg = nc.dram_tensor("dbg_ot", [128, 8, S], BF16, kind="ExternalOutput").ap()
    for k in ("ev_w_in", "ev_w_out", "od_w_out", "od_w1", "od_w2", "od_a1", "od_a2", "od_g1", "od_g2"):
        io[k] = io[k][0]
    xa_reg, xb_reg, g_reg, y_reg = Reg(True), Reg(True), Reg(True), Reg(True)
    with ExitStack() as es:
        kb = KB(nc, es)
        kb.dma_pool("sp_ld", 12)
        kb.dma_pool("sp_st", 8)
        kb.dma_pool("pool_ld", 6)
        ident = sbt(nc, es, "ident", [128, 128], BF16)
        ones = sbt(nc, es, "ones", [128, 128], BF16)
        kb.dma("sp", out=ident.t[:, :], in_=io["c_ident"][:, :], writes=[ident.g])
        kb.dma("sp", out=ones.t[:, :], in_=io["c_ones"][:, :], writes=[ones.g])
        tri = sbt(nc, es, "tri", [128, 128], BF16)
        kb.dma("sp", out=tri.t[:, :], in_=io["c_tri"][:, :], writes=[tri.g])
        XT3 = nc.dram_tensor("scr_xt3", [128, 8, S], BF16, kind="Internal").ap()
        xt3_reg = Reg(True)
        with ExitStack() as esx:
            xT = sbt(nc, esx, "xT", [128, 8, S], BF16)
            phase_prologue(kb, nc, io, xT, ident)
            if stop_after == "prologue":
                dump_and_stop(kb, dbg[:, :, :], xT.t[:, :, :], xT.g)
                return nc
            if stop_after in ("rwkvonly", "rwkvonly_a"):
                phase_rwkv_a(kb, nc, io, xT, scr, scr_reg)
                if stop_after == "rwkvonly_a":
                    return nc
            else:
                with ExitStack() as es2:
                    OT = sbt(nc, es2, "OT", [128, 8, S], BF16)
                    if phase_l0_mixer(kb, nc, io, xT, OT, ident, ones, tri, dbg):
                        return nc
                    if stop_after == "mixer":
                        kb.barrier()
                        return nc
                    phase_out_ln(kb, nc, io, OT, xT, ident, io["ev_w_out"], io["x"], io["ln_mix_g"][0], io["ln_mix_b"][0], XA, xa_reg)
                if stop_after == "outln":
                    return nc
                phase_ffn(kb, nc, io, 0, xT, ident, XA, xa_reg, y if stop_after == "ffn0" else XB, y_reg if stop_after == "ffn0" else xb_reg,
                          G, g_reg, want_T=True)
                if stop_after == "ffn0":
                    return nc
                phase_rwkv_a(kb, nc, io, xT, scr, scr_reg)
        if stop_after == "rwkvonly":
            phase_rwkv_b(kb, nc, io, XT3, xt3_reg, ident, ones, tri, scr, scr_reg, io["x"], Reg(True), y, y_reg)
            return nc
        last = stop_after == "rwkv"
        phase_rwkv_b(kb, nc, io, XT3, xt3_reg, ident, ones, tri, scr, scr_reg, XB, xb_reg, y if last else XA, y_reg if last else xa_reg)
        if last:
            return nc
        with ExitStack() as esx:
            xT = sbt(nc, esx, "xT", [128, 8, S], BF16)
            for kc in range(8):
                kb.dma("sp", out=xT.t[:, kc, :], in_=XT3[:, kc, :], reads=[xt3_reg], writes=[xT.g])
            phase_ffn(kb, nc, io, 1, xT, ident, XA, xa_reg, y, y_reg, G, g_reg, want_T=False)
    return nc


_NC_CACHE = {}


def kernel(**inputs):
    if "nc" not in _NC_CACHE:
        _NC_CACHE["nc"] = build()
        _NC_CACHE["consts"] = make_consts()
    nc = _NC_CACHE["nc"]
    consts = _NC_CACHE["consts"]
    x = np.ascontiguousarray(np.asarray(inputs["x"], dtype=np.float32))
    shared = {k: np.ascontiguousarray(np.asarray(inputs[k], dtype=np.float32)) for k in INPUT_SHAPES}
    in_maps = []
    for c in range(8):
        m = {"x": x[c]}
        m.update(shared)
        m.update(consts)
        in_maps.append(m)
    res = run_bass_kernel_spmd(nc, in_maps, core_ids=list(range(8)))
    return np.stack([np.asarray(res.results[c]["y"], dtype=np.float32) for c in range(8)], axis=0)
```

```python
import math
from contextlib import ExitStack

import numpy as np
import ml_dtypes

import concourse.bass as bass
import concourse.mybir as mybir
from concourse.bass_utils import run_bass_kernel_spmd

F32 = mybir.dt.float32
BF16 = mybir.dt.bfloat16
AF = mybir.ActivationFunctionType
ALU = mybir.AluOpType
AX = mybir.AxisListType

S = 4096
D = 1024
NB = S // 128
FF = 2816
NFC = FF // 128
DN_ALPHA = (2.0 * 2) ** 0.25
LN_EPS = 1e-5
LAMBDA_INIT0 = 0.8 - 0.6 * math.exp(-0.3 * 0)
SLOPES = [2.0 ** (-8.0 * (i + 1) / 4) for i in range(4)]
GAMMAS = [1.0 - 2.0 ** (-5.0 - h) for h in range(4)]


MIX_STOP = None
LOOPV = 9
RB_STOP = 0


class StopBuild(Exception):
    pass


def dump_and_stop(kb, dbg, tile_ap, reg):
    kb.barrier()
    kb.dma("sp", out=dbg, in_=tile_ap, reads=[reg], pool="st")
    kb.barrier()
    return True


class Reg:
    __slots__ = ("w", "r", "nowaw", "psum")

    def __init__(self, nowaw=False):
        self.w = {}
        self.r = {}
        self.nowaw = nowaw
        self.psum = False


class T:
    def __init__(self, t):
        self.t = t
        self.g = Reg()


class KB:
    def __init__(self, nc, es):
        self.nc = nc
        self.es = es
        self.E = {"pe": nc.tensor, "dve": nc.vector, "act": nc.scalar, "pool": nc.gpsimd, "sp": nc.sync}
        self.sems = {}
        self.cnt = {}
        for e in self.E:
            self.sems[e] = es.enter_context(nc.semaphore("s_" + e))
            self.cnt[e] = 0
        self.seen = {e: {} for e in self.E}
        self.dpool = {}
        self.dnext = {}

    def dma_pool(self, name, n):
        keys = []
        for i in range(n):
            k = "%s%d" % (name, i)
            self.sems[k] = self.es.enter_context(self.nc.semaphore("d_" + k))
            self.cnt[k] = 0
            keys.append(k)
        self.dpool[name] = keys
        self.dnext[name] = 0

    def _deps(self, e, reads, writes):
        need = {}

        def add(d, same_ok):
            for k, v in d.items():
                if k == e and same_ok:
                    continue
                if need.get(k, 0) < v:
                    need[k] = v

        for r in reads:
            add(r.w, e == "pe")
            if r.psum:
                add(r.r, True)
        for w in writes:
            if not w.nowaw:
                add(w.w, e == "pe")
            add(w.r, e == "pe")
        return need

    def _wait(self, e, need):
        sn = self.seen[e]
        for k, v in need.items():
            if sn.get(k, 0) < v:
                self.E[e].wait_ge(self.sems[k], v)
                sn[k] = v

    def op(self, e, fn, reads=(), writes=(), rg=(0, 128)):
        need = self._deps(e, reads, writes)
        if e == "pe":
            last = getattr(self, "_last_rg", (0, 128))
            if (rg[0] + rg[1] <= last[0] or last[0] + last[1] <= rg[0]) and self.cnt["pe"] > 0:
                need["pe"] = self.cnt["pe"]
            self._last_rg = rg
        self._wait(e, need)
        ins = fn(self.E[e])
        self.cnt[e] += 1
        ins.then_inc(self.sems[e], 1)
        tok = self.cnt[e]
        for r in reads:
            r.r[e] = tok
        for w in writes:
            w.w[e] = tok
            if not w.nowaw:
                w.r = {}
        return ins

    def dma(self, q, out, in_, reads=(), writes=(), pool="ld", **kw):
        pool = q + "_" + pool
        keys = self.dpool[pool]
        k = keys[self.dnext[pool] % len(keys)]
        self.dnext[pool] += 1
        need = self._deps(q, reads, writes)
        if self.cnt[k] > 0:
            need[k] = max(need.get(k, 0), self.cnt[k])
        self._wait(q, need)
        ins = self.E[q].dma_start(out=out, in_=in_, **kw)
        self.cnt[k] += 16
        ins.then_inc(self.sems[k], 16)
        tok = self.cnt[k]
        for r in reads:
            r.r[k] = tok
        for w in writes:
            w.w[k] = tok
            if not w.nowaw:
                w.r = {}
        return ins

    def barrier(self):
        for e in self.E:
            need = {k: v for k, v in self.cnt.items() if k != e and v > 0}
            self._wait(e, need)


_UNIQ = [0]


def _uniq(name):
    _UNIQ[0] += 1
    return "%s_%d" % (name, _UNIQ[0])


def sbt(nc, es, name, shape, dt):
    return T(es.enter_context(nc.sbuf_tensor(_uniq(name), list(shape), dt)))


def pst(nc, es, name, shape, dt):
    t = T(es.enter_context(nc.psum_tensor(_uniq(name), list(shape), dt)))
    t.g.psum = True
    return t


def ln_block(kb, z, gam, bet, outp, stats, mv, rstd, eps):
    kb.op("dve", lambda e: e.bn_stats(out=stats.t[:, 0:6], in_=z.t[:, 0:512]), reads=[z.g], writes=[stats.g])
    kb.op("dve", lambda e: e.bn_stats(out=stats.t[:, 6:12], in_=z.t[:, 512:1024]), reads=[z.g], writes=[stats.g])
    kb.op("dve", lambda e: e.bn_aggr(out=mv.t[:, 0:2], in_=stats.t[:, 0:12]), reads=[stats.g], writes=[mv.g])
    kb.op("dve", lambda e: e.tensor_scalar(out=rstd.t[:, 0:1], in0=mv.t[:, 1:2], scalar1=eps, scalar2=None, op0=ALU.add),
          reads=[mv.g], writes=[rstd.g])
    kb.op("act", lambda e: e.activation(out=rstd.t[:, 0:1], in_=rstd.t[:, 0:1], func=AF.Sqrt), reads=[rstd.g], writes=[rstd.g])
    kb.op("dve", lambda e: e.reciprocal(out=rstd.t[:, 0:1], in_=rstd.t[:, 0:1]), reads=[rstd.g], writes=[rstd.g])
    kb.op("dve", lambda e: e.tensor_scalar(out=z.t[:, :], in0=z.t[:, :], scalar1=mv.t[:, 0:1], scalar2=rstd.t[:, 0:1],
                                           op0=ALU.subtract, op1=ALU.mult), reads=[z.g, mv.g, rstd.g], writes=[z.g])
    kb.op("dve", lambda e: e.tensor_tensor(out=z.t[:, :], in0=z.t[:, :], in1=gam.t[:, :], op=ALU.mult),
          reads=[z.g, gam.g], writes=[z.g])
    kb.op("dve", lambda e: e.tensor_tensor(out=outp.t[:, :], in0=z.t[:, :], in1=bet.t[:, :], op=ALU.add),
          reads=[z.g, bet.g], writes=[outp.g])


def transpose_to_fm(kb, src32, srcbf, dstT, b, ident, ptr):
    kb.op("act", lambda e: e.activation(out=srcbf.t[:, :], in_=src32.t[:, :], func=AF.Copy), reads=[src32.g], writes=[srcbf.g])
    for kc in range(8):
        kb.op("pe", lambda e: e.transpose(out=ptr.t[:, kc * 128:(kc + 1) * 128], in_=srcbf.t[:, kc * 128:(kc + 1) * 128],
                                          identity=ident.t[:, :]), reads=[srcbf.g, ident.g], writes=[ptr.g])
    kb.op("dve", lambda e: e.tensor_copy(out=dstT.t[:, :, b * 128:(b + 1) * 128],
                                         in_=ptr.t[:, :].rearrange("p (k t) -> p k t", k=8)), reads=[ptr.g], writes=[dstT.g])


def load_bcast(kb, dst, vec_ap):
    kb.dma("sp", out=dst.t[:, :], in_=vec_ap.partition_broadcast(128), writes=[dst.g])


def phase_prologue(kb, nc, io, xT, ident):
    with ExitStack() as es:
        xin = [sbt(nc, es, "pr_x%d" % i, [128, 1024], F32) for i in range(2)]
        xbf = [sbt(nc, es, "pr_xb%d" % i, [128, 1024], BF16) for i in range(2)]
        ptr = [pst(nc, es, "pr_pt%d" % i, [128, 1024], BF16) for i in range(2)]
        for b in range(NB):
            xi = xin[b % 2]
            kb.dma("sp", out=xi.t[:, :], in_=io["x"][b * 128:(b + 1) * 128, :], writes=[xi.g])
            transpose_to_fm(kb, xi, xbf[b % 2], xT, b, ident, ptr[b % 2])
        kb.barrier()


def phase_l0_mixer(kb, nc, io, xT, OT, ident, ones, tri, dbg=None):
    W_in = io["ev_w_in"].rearrange("(kc p) n -> p kc n", p=128)
    with ExitStack() as es:
        wq = sbt(nc, es, "m_wq", [128, 8, 384], BF16)
        qT = [sbt(nc, es, "m_qT%d" % m, [128, S], BF16) for m in range(2)]
        kT = [sbt(nc, es, "m_kT%d" % m, [128, S], BF16) for m in range(2)]
        vtok = sbt(nc, es, "m_vtok", [128, NB, 128], BF16)
        abias = sbt(nc, es, "m_abias", [128, 32], F32)
        lamt = sbt(nc, es, "m_lam", [128, 256], F32)
        lprod = sbt(nc, es, "m_lprod", [128, 128], F32)
        lsum = sbt(nc, es, "m_lsum", [128, 2], F32)
        lexp = sbt(nc, es, "m_lexp", [128, 2], F32)
        neglam = sbt(nc, es, "m_neglam", [128, 1], F32)
        gsc = sbt(nc, es, "m_gsc", [128, 1], F32)
        pT = [sbt(nc, es, "m_pT%d" % i, [128, 512], BF16) for i in range(4)]
        r1 = sbt(nc, es, "m_r1", [128, 512], F32)
        r2 = sbt(nc, es, "m_r2", [128, 512], F32)
        t1 = sbt(nc, es, "m_t1", [128, 512], F32)
        t2 = sbt(nc, es, "m_t2", [128, 512], F32)
        sqb = sbt(nc, es, "m_sqb", [128, 512], BF16)
        ybf = sbt(nc, es, "m_ybf", [128, 512], BF16)
        ktd = sbt(nc, es, "m_ktd", [128, NB, 64], BF16)
        DT = sbt(nc, es, "m_DT", [128, 512], F32)
        qdec = sbt(nc, es, "m_qdec", [64, 512], F32)
        kdec = sbt(nc, es, "m_kdec", [128, 1], F32)
        Rst = sbt(nc, es, "m_Rst", [64, 2, 128], F32)
        Rbf = sbt(nc, es, "m_Rbf", [64, NB, 128], BF16)
        bank = [pst(nc, es, "m_b%d" % i, [128, 512], F32) for i in range(8)]

        kb.dma("sp", out=lamt.t[:, :], in_=io["ev_lambda"].rearrange("a b c -> (a b c)").partition_broadcast(128), writes=[lamt.g])
        kb.op("dve", lambda e: e.tensor_tensor(out=lprod.t[:, 0:64], in0=lamt.t[:, 0:64], in1=lamt.t[:, 64:128], op=ALU.mult),
              reads=[lamt.g], writes=[lprod.g])
        kb.op("dve", lambda e: e.tensor_tensor(out=lprod.t[:, 64:128], in0=lamt.t[:, 128:192], in1=lamt.t[:, 192:256], op=ALU.mult),
              reads=[lamt.g], writes=[lprod.g])
        kb.op("dve", lambda e: e.reduce_sum(out=lsum.t[:, 0:1], in_=lprod.t[:, 0:64], axis=AX.X), reads=[lprod.g], writes=[lsum.g])
        kb.op("dve", lambda e: e.reduce_sum(out=lsum.t[:, 1:2], in_=lprod.t[:, 64:128], axis=AX.X), reads=[lprod.g], writes=[lsum.g])
        kb.op("act", lambda e: e.activation(out=lexp.t[:, 0:2], in_=lsum.t[:, 0:2], func=AF.Exp), reads=[lsum.g], writes=[lexp.g])
        kb.op("dve", lambda e: e.tensor_tensor(out=neglam.t[:, 0:1], in0=lexp.t[:, 1:2], in1=lexp.t[:, 0:1], op=ALU.subtract),
              reads=[lexp.g], writes=[neglam.g])
        kb.op("dve", lambda e: e.tensor_scalar(out=neglam.t[:, 0:1], in0=neglam.t[:, 0:1], scalar1=-LAMBDA_INIT0, scalar2=None, op0=ALU.add),
              reads=[neglam.g], writes=[neglam.g])
        kb.dma("sp", out=gsc.t[:, :], in_=io["ev_subln_g"].rearrange("o v -> v o"), writes=[gsc.g], allow_slow_non_contiguous=True)
        kb.op("dve", lambda e: e.tensor_scalar(out=gsc.t[:, 0:1], in0=gsc.t[:, 0:1], scalar1=1.0 - LAMBDA_INIT0, scalar2=None, op0=ALU.mult),
              reads=[gsc.g], writes=[gsc.g])
        for m in range(2):
            kb.dma("sp", out=kT[m].t[64:66, :], in_=io["c_ones2"][:, :], writes=[kT[m].g])

        def proj_fm(dst, prow, co, ncol, evac_eng_i):
            for tt in range(8):
                bk = bank[tt % 2]
                for kc in range(8):
                    kb.op("pe", lambda e: e.matmul(out=bk.t[0:ncol, :], lhsT=wq.t[:, kc, co:co + ncol],
                                                   rhs=xT.t[:, kc, tt * 512:(tt + 1) * 512], start=(kc == 0), stop=(kc == 7)),
                          reads=[wq.g, xT.g], writes=[bk.g])
                if (tt + evac_eng_i) % 2 == 0:
                    kb.op("act", lambda e: e.activation(out=dst.t[prow:prow + ncol, tt * 512:(tt + 1) * 512], in_=bk.t[0:ncol, :], func=AF.Copy),
                          reads=[bk.g], writes=[dst.g])
                else:
                    kb.op("dve", lambda e: e.tensor_copy(out=dst.t[prow:prow + ncol, tt * 512:(tt + 1) * 512], in_=bk.t[0:ncol, :]),
                          reads=[bk.g], writes=[dst.g])

        for h in range(4):
            for i, co in enumerate((h * 128, 512 + h * 128, 1024 + h * 128)):
                kb.dma("pool", out=wq.t[:, :, i * 128:(i + 1) * 128], in_=W_in[:, :, co:co + 128], reads=[], writes=[wq.g])
            kb.dma("sp", out=abias.t[:, :], in_=io["c_abias"][h, :, :], writes=[abias.g])
            for m in range(2):
                kb.dma("sp", out=qT[m].t[64:66, :], in_=io["c_alibiq"][h, :, :], writes=[qT[m].g])
            for m in range(2):
                proj_fm(qT[m], 0, m * 64, 64, 0)
                proj_fm(kT[m], 0, 128 + m * 64, 64, 1)
            for g4 in range(8):
                bk = bank[2 + g4 % 2]
                for bb in range(4):
                    b = g4 * 4 + bb
                    for kc in range(8):
                        kb.op("pe", lambda e: e.matmul(out=bk.t[:, bb * 128:(bb + 1) * 128], lhsT=xT.t[:, kc, b * 128:(b + 1) * 128],
                                                       rhs=wq.t[:, kc, 256:384], start=(kc == 0), stop=(kc == 7)),
                              reads=[wq.g, xT.g], writes=[bk.g])
                kb.op("dve", lambda e: e.tensor_copy(out=vtok.t[:, g4 * 4:(g4 + 1) * 4, :],
                                                     in_=bk.t[:, :].rearrange("p (b v) -> p b v", b=4)), reads=[bk.g], writes=[vtok.g])
            if MIX_STOP == "h0proj":
                kb.barrier()
                kb.dma("sp", out=dbg[0:64, 0, :], in_=qT[0].t[0:64, :], reads=[qT[0].g], pool="st")
                kb.dma("sp", out=dbg[0:64, 1, :], in_=qT[1].t[0:64, :], reads=[qT[1].g], pool="st")
                kb.dma("sp", out=dbg[0:64, 2, :], in_=kT[0].t[0:64, :], reads=[kT[0].g], pool="st")
                kb.dma("sp", out=dbg[0:64, 3, :], in_=kT[1].t[0:64, :], reads=[kT[1].g], pool="st")
                kb.dma("sp", out=dbg[:, 4, :].rearrange("p (b v) -> p b v", b=NB), in_=vtok.t[:, :, :], reads=[vtok.g], pool="st")
                kb.barrier()
                return True
            O = [bank[4], bank[6]]
            Sm = [bank[5], bank[7]]
            for c in range(8):
                steps = [(kbi, m) for kbi in range(4 * c + 4) for m in range(2)]

                def geom(i):
                    kbi, m = steps[i]
                    j = kbi - 4 * c
                    lo = 128 * j if j > 0 else 0
                    return kbi, m, j, lo

                def emit_qk(i):
                    kbi, m, j, lo = geom(i)
                    sb = bank[i % 4]
                    KK = 64 if LOOPV == -1 else 66
                    kb.op("pe", lambda e: e.matmul(out=sb.t[:, lo:512], lhsT=kT[m].t[0:KK, kbi * 128:(kbi + 1) * 128],
                                                   rhs=qT[m].t[0:KK, c * 512 + lo:(c + 1) * 512], start=True, stop=True),
                          reads=[kT[m].g, qT[m].g], writes=[sb.g])

                def emit_pv(i):
                    kbi, m, j, lo = geom(i)
                    sb = bank[i % 4]
                    pt = pT[i % 4]
                    oi = (kbi - 4 * c) + 28
                    if LOOPV == -4:
                        return
                    kb.op("act", lambda e: e.activation(out=pt.t[:, lo:512], in_=sb.t[:, lo:512], func=(AF.Copy if LOOPV == -3 else AF.Exp),
                                                        bias=(0.0 if LOOPV in (-2, -3) else abias.t[:, oi:oi + 1]), scale=0.125),
                          reads=[sb.g, abias.g], writes=[pt.g])
                    if LOOPV < 1:
                        return
                    if j >= 0:
                        kb.op("dve", lambda e: e.tensor_tensor(out=pt.t[:, lo:lo + 128], in0=pt.t[:, lo:lo + 128], in1=tri.t[:, :], op=ALU.mult),
                              reads=[pt.g, tri.g], writes=[pt.g])
                    if LOOPV < 2:
                        return
                    first = kbi == 0
                    last = kbi == 4 * c + 3
                    kb.op("pe", lambda e: e.matmul(out=O[m].t[:, lo:512], lhsT=vtok.t[:, kbi, :], rhs=pt.t[:, lo:512], start=first, stop=last),
                          reads=[vtok.g, pt.g], writes=[O[m].g])
                    kb.op("pe", lambda e: e.matmul(out=Sm[m].t[:, lo:512], lhsT=ones.t[:, :], rhs=pt.t[:, lo:512], start=first, stop=last),
                          reads=[ones.g, pt.g], writes=[Sm[m].g])

                n = len(steps)
                emit_qk(0)
                emit_qk(1)
                emit_qk(2)
                for i in range(n):
                    if i + 3 < n:
                        emit_qk(i + 3)
                    emit_pv(i)
                if MIX_STOP == "c0loop":
                    kb.barrier()
                    kb.dma("sp", out=dbg[:, 4, :].rearrange("p (b v) -> p b v", b=NB), in_=vtok.t[:, :, :], reads=[vtok.g], pool="st")
                    kb.barrier()
                    return True
                kb.op("dve", lambda e: e.reciprocal(out=r1.t[:, :], in_=Sm[0].t[:, :]), reads=[Sm[0].g], writes=[r1.g])
                kb.op("dve", lambda e: e.reciprocal(out=r2.t[:, :], in_=Sm[1].t[:, :]), reads=[Sm[1].g], writes=[r2.g])
                kb.op("dve", lambda e: e.tensor_tensor(out=t1.t[:, :], in0=O[0].t[:, :], in1=r1.t[:, :], op=ALU.mult), reads=[O[0].g, r1.g], writes=[t1.g])
                kb.op("dve", lambda e: e.tensor_tensor(out=t2.t[:, :], in0=O[1].t[:, :], in1=r2.t[:, :], op=ALU.mult), reads=[O[1].g, r2.g], writes=[t2.g])
                kb.op("dve", lambda e: e.scalar_tensor_tensor(out=t1.t[:, :], in0=t2.t[:, :], scalar=neglam.t[:, 0:1], in1=t1.t[:, :],
                                                              op0=ALU.mult, op1=ALU.add), reads=[t1.g, t2.g, neglam.g], writes=[t1.g])
                kb.op("act", lambda e: e.activation(out=sqb.t[:, :], in_=t1.t[:, :], func=AF.Square), reads=[t1.g], writes=[sqb.g])
                ssb = bank[0]
                kb.op("pe", lambda e: e.matmul(out=ssb.t[:, :], lhsT=ones.t[:, :], rhs=sqb.t[:, :], start=True, stop=True),
                      reads=[ones.g, sqb.g], writes=[ssb.g])
                kb.op("dve", lambda e: e.tensor_scalar(out=r1.t[:, :], in0=ssb.t[:, :], scalar1=1.0 / 128, scalar2=LN_EPS, op0=ALU.mult, op1=ALU.add),
                      reads=[ssb.g], writes=[r1.g])
                kb.op("act", lambda e: e.activation(out=r1.t[:, :], in_=r1.t[:, :], func=AF.Sqrt), reads=[r1.g], writes=[r1.g])
                kb.op("dve", lambda e: e.reciprocal(out=r1.t[:, :], in_=r1.t[:, :]), reads=[r1.g], writes=[r1.g])
                kb.op("dve", lambda e: e.scalar_tensor_tensor(out=OT.t[:, h, c * 512:(c + 1) * 512], in0=t1.t[:, :], scalar=gsc.t[:, 0:1], in1=r1.t[:, :],
                                                              op0=ALU.mult, op1=ALU.mult), reads=[t1.g, gsc.g, r1.g], writes=[OT.g])

        if MIX_STOP == "diff":
            return dump_and_stop(kb, dbg[:, :, :], OT.t[:, :, :], OT.g)
        for h in range(4):
            gam = GAMMAS[h]
            kb.dma("pool", out=wq.t[:, :, 0:64], in_=W_in[:, :, 1536 + h * 64:1536 + (h + 1) * 64], writes=[wq.g])
            kb.dma("pool", out=wq.t[:, :, 64:128], in_=W_in[:, :, 1792 + h * 64:1792 + (h + 1) * 64], writes=[wq.g])
            kb.dma("pool", out=wq.t[:, :, 128:256], in_=W_in[:, :, 2048 + h * 128:2048 + (h + 1) * 128], writes=[wq.g])
            kb.dma("pool", out=wq.t[:, :, 256:384], in_=W_in[:, :, 2560 + h * 128:2560 + (h + 1) * 128], writes=[wq.g])
            kb.dma("sp", out=DT.t[:, :], in_=io["c_retDT"][h, :, :], writes=[DT.g])
            kb.dma("sp", out=qdec.t[:, :], in_=io["c_retqdec"][h, :, :], writes=[qdec.g])
            kb.dma("sp", out=kdec.t[:, :], in_=io["c_retkdec"][h, :, :], writes=[kdec.g])
            if MIX_STOP == "r_load":
                kb.barrier()
                kb.dma("sp", out=dbg[:, 4, :].rearrange("p (b v) -> p b v", b=NB), in_=vtok.t[:, :, :], reads=[vtok.g], pool="st")
                kb.barrier()
                return True
            for tt in range(8):
                bk = bank[tt % 2]
                for kc in range(8):
                    kb.op("pe", lambda e: e.matmul(out=bk.t[0:64, :], lhsT=wq.t[:, kc, 0:64], rhs=xT.t[:, kc, tt * 512:(tt + 1) * 512],
                                                   start=(kc == 0), stop=(kc == 7)), reads=[wq.g, xT.g], writes=[bk.g])
                kb.op("act", lambda e: e.activation(out=qT[0].t[0:64, tt * 512:(tt + 1) * 512], in_=bk.t[0:64, :], func=AF.Copy),
                      reads=[bk.g], writes=[qT[0].g])
                if LOOPV >= 1:
                    kb.op("dve", lambda e: e.tensor_tensor(out=qT[1].t[0:64, tt * 512:(tt + 1) * 512], in0=bk.t[0:64, :], in1=qdec.t[:, :], op=ALU.mult),
                          reads=[bk.g, qdec.g], writes=[qT[1].g])
            if LOOPV >= 2:
                proj_fm(kT[0], 0, 64, 64, 0)
            for b in range(NB if LOOPV >= 3 else 0):
                bk = bank[2 + b % 2]
                for kc in range(8):
                    kb.op("pe", lambda e: e.matmul(out=bk.t[:, 0:192], lhsT=xT.t[:, kc, b * 128:(b + 1) * 128], rhs=wq.t[:, kc, 64:256],
                                                   start=(kc == 0), stop=(kc == 7)), reads=[wq.g, xT.g], writes=[bk.g])
                kb.op("dve", lambda e: e.tensor_scalar(out=ktd.t[:, b, :], in0=bk.t[:, 0:64], scalar1=kdec.t[:, 0:1], scalar2=None, op0=ALU.mult),
                      reads=[bk.g, kdec.g], writes=[ktd.g])
                kb.op("act", lambda e: e.activation(out=vtok.t[:, b, :], in_=bk.t[:, 64:192], func=AF.Copy), reads=[bk.g], writes=[vtok.g])
            if MIX_STOP == "r_proj":
                kb.barrier()
                kb.dma("sp", out=dbg[:, 4, :].rearrange("p (b v) -> p b v", b=NB), in_=vtok.t[:, :, :], reads=[vtok.g], pool="st")
                kb.barrier()
                return True
            kb.op("dve", lambda e: e.memset(Rst.t[:, 0, :], 0.0), writes=[Rst.g])
            kb.op("dve", lambda e: e.memset(Rbf.t[:, 0, :], 0.0), writes=[Rbf.g])
            for g4 in range(8):
                bk = bank[4 + g4 % 2]
                for bb in range(4):
                    b = g4 * 4 + bb
                    kb.op("pe", lambda e: e.matmul(out=bk.t[0:64, bb * 128:(bb + 1) * 128], lhsT=ktd.t[:, b, :], rhs=vtok.t[:, b, :],
                                                   start=True, stop=True), reads=[ktd.g, vtok.g], writes=[bk.g])
                for bb in range(4):
                    b = g4 * 4 + bb
                    if b == NB - 1:
                        continue
                    kb.op("dve", lambda e: e.scalar_tensor_tensor(out=Rst.t[:, (b + 1) % 2, :], in0=Rst.t[:, b % 2, :], scalar=float(gam ** 128),
                                                                  in1=bk.t[0:64, bb * 128:(bb + 1) * 128], op0=ALU.mult, op1=ALU.add),
                          reads=[Rst.g, bk.g], writes=[Rst.g])
                    kb.op("act", lambda e: e.activation(out=Rbf.t[:, b + 1, :], in_=Rst.t[:, (b + 1) % 2, :], func=AF.Copy), reads=[Rst.g], writes=[Rbf.g])
            if MIX_STOP == "r_state":
                kb.barrier()
                kb.dma("sp", out=dbg[:, 4, :].rearrange("p (b v) -> p b v", b=NB), in_=vtok.t[:, :, :], reads=[vtok.g], pool="st")
                kb.barrier()
                return True
            for g4 in range(8):
                sc = bank[g4 % 2]
                yb = bank[2 + g4 % 2]
                gb = bank[6 + g4 % 2]
                pt = pT[g4 % 2]
                for bb in range(4):
                    b = g4 * 4 + bb
                    kb.op("pe", lambda e: e.matmul(out=sc.t[:, bb * 128:(bb + 1) * 128], lhsT=kT[0].t[0:64, b * 128:(b + 1) * 128],
                                                   rhs=qT[0].t[0:64, b * 128:(b + 1) * 128], start=True, stop=True),
                          reads=[kT[0].g, qT[0].g], writes=[sc.g])
                for kc in range(8):
                    kb.op("pe", lambda e: e.matmul(out=gb.t[:, :], lhsT=wq.t[:, kc, 256:384], rhs=xT.t[:, kc, g4 * 512:(g4 + 1) * 512],
                                                   start=(kc == 0), stop=(kc == 7)), reads=[wq.g, xT.g], writes=[gb.g])
                kb.op("dve", lambda e: e.tensor_tensor(out=pt.t[:, :], in0=sc.t[:, :], in1=DT.t[:, :], op=ALU.mult), reads=[sc.g, DT.g], writes=[pt.g])
                for bb in range(4):
                    b = g4 * 4 + bb
                    kb.op("pe", lambda e: e.matmul(out=yb.t[:, bb * 128:(bb + 1) * 128], lhsT=vtok.t[:, b, :], rhs=pt.t[:, bb * 128:(bb + 1) * 128],
                                                   start=True, stop=False), reads=[vtok.g, pt.g], writes=[yb.g])
                    kb.op("pe", lambda e: e.matmul(out=yb.t[:, bb * 128:(bb + 1) * 128], lhsT=Rbf.t[:, b, :], rhs=qT[1].t[0:64, b * 128:(b + 1) * 128],
                                                   start=False, stop=True), reads=[Rbf.g, qT[1].g], writes=[yb.g])
                kb.op("act", lambda e: e.activation(out=ybf.t[:, :], in_=yb.t[:, :], func=AF.Copy), reads=[yb.g], writes=[ybf.g])
                kb.op("act", lambda e: e.activation(out=sqb.t[:, :], in_=yb.t[:, :], func=AF.Square), reads=[yb.g], writes=[sqb.g])
                m1 = bank[4]
                m2 = bank[5]
                kb.op("pe", lambda e: e.matmul(out=m1.t[:, :], lhsT=ones.t[:, :], rhs=ybf.t[:, :], start=True, stop=True), reads=[ones.g, ybf.g], writes=[m1.g])
                kb.op("pe", lambda e: e.matmul(out=m2.t[:, :], lhsT=ones.t[:, :], rhs=sqb.t[:, :], start=True, stop=True), reads=[ones.g, sqb.g], writes=[m2.g])
                kb.op("dve", lambda e: e.tensor_scalar(out=r1.t[:, :], in0=m1.t[:, :], scalar1=1.0 / 128, scalar2=None, op0=ALU.mult), reads=[m1.g], writes=[r1.g])
                kb.op("dve", lambda e: e.tensor_tensor(out=t2.t[:, :], in0=r1.t[:, :], in1=r1.t[:, :], op=ALU.mult), reads=[r1.g], writes=[t2.g])
                kb.op("dve", lambda e: e.scalar_tensor_tensor(out=r2.t[:, :], in0=m2.t[:, :], scalar=1.0 / 128, in1=t2.t[:, :], op0=ALU.mult, op1=ALU.subtract),
                      reads=[m2.g, t2.g], writes=[r2.g])
                kb.op("dve", lambda e: e.tensor_scalar(out=r2.t[:, :], in0=r2.t[:, :], scalar1=LN_EPS, scalar2=None, op0=ALU.add),
                      reads=[r2.g], writes=[r2.g])
                kb.op("act", lambda e: e.activation(out=r2.t[:, :], in_=r2.t[:, :], func=AF.Sqrt), reads=[r2.g], writes=[r2.g])
                kb.op("dve", lambda e: e.reciprocal(out=r2.t[:, :], in_=r2.t[:, :]), reads=[r2.g], writes=[r2.g])
                sg = t2
                kb.op("act", lambda e: e.activation(out=sg.t[:, :], in_=gb.t[:, :], func=AF.Silu), reads=[gb.g], writes=[sg.g])
                kb.op("dve", lambda e: e.tensor_tensor(out=t1.t[:, :], in0=yb.t[:, :], in1=r1.t[:, :], op=ALU.subtract), reads=[yb.g, r1.g], writes=[t1.g])
                kb.op("dve", lambda e: e.tensor_tensor(out=t1.t[:, :], in0=t1.t[:, :], in1=r2.t[:, :], op=ALU.mult), reads=[t1.g, r2.g], writes=[t1.g])
                kb.op("dve", lambda e: e.tensor_tensor(out=OT.t[:, 4 + h, g4 * 512:(g4 + 1) * 512], in0=t1.t[:, :], in1=sg.t[:, :], op=ALU.mult),
                      reads=[t1.g, sg.g], writes=[OT.g])
        kb.barrier()
        if dbg is not None:
            kb.dma("sp", out=dbg[:, :, :], in_=OT.t[:, :, :], reads=[OT.g], pool="st")
            kb.barrier()


def phase_out_ln(kb, nc, io, OT, xT, ident, w_ap, xres_ap, gam_ap, bet_ap, xout_ap, xout_reg):
    with ExitStack() as es:
        wo = sbt(nc, es, "o_w", [128, 8, 1024], BF16)
        gam = sbt(nc, es, "o_gam", [128, 1024], F32)
        bet = sbt(nc, es, "o_bet", [128, 1024], F32)
        xin = [sbt(nc, es, "o_x%d" % i, [128, 1024], F32) for i in range(2)]
        z = [sbt(nc, es, "o_z%d" % i, [128, 1024], F32) for i in range(2)]
        xo = [sbt(nc, es, "o_xo%d" % i, [128, 1024], F32) for i in range(2)]
        xbf = [sbt(nc, es, "o_xb%d" % i, [128, 1024], BF16) for i in range(2)]
        stats = sbt(nc, es, "o_stats", [128, 12], F32)
        mv = sbt(nc, es, "o_mv", [128, 2], F32)
        rstd = sbt(nc, es, "o_rstd", [128, 1], F32)
        mm = [[pst(nc, es, "o_mm%d%d" % (i, j), [128, 512], F32) for j in range(2)] for i in range(2)]
        ptr = [pst(nc, es, "o_pt%d" % i, [128, 1024], BF16) for i in range(2)]
        kb.dma("pool", out=wo.t[:, :, :], in_=w_ap.rearrange("(kc p) n -> p kc n", p=128), writes=[wo.g])
        load_bcast(kb, gam, gam_ap)
        load_bcast(kb, bet, bet_ap)
        for b in range(NB):
            xi = xin[b % 2]
            kb.dma("sp", out=xi.t[:, :], in_=xres_ap[b * 128:(b + 1) * 128, :], writes=[xi.g])
            for hf in range(2):
                for fc in range(8):
                    kb.op("pe", lambda e: e.matmul(out=mm[b % 2][hf].t[:, :], lhsT=OT.t[:, fc, b * 128:(b + 1) * 128],
                                                   rhs=wo.t[:, fc, hf * 512:(hf + 1) * 512], start=(fc == 0), stop=(fc == 7)),
                          reads=[OT.g, wo.g], writes=[mm[b % 2][hf].g])
            zz = z[b % 2]
            for hf in range(2):
                kb.op("dve", lambda e: e.scalar_tensor_tensor(out=zz.t[:, hf * 512:(hf + 1) * 512], in0=xi.t[:, hf * 512:(hf + 1) * 512], scalar=DN_ALPHA,
                                                              in1=mm[b % 2][hf].t[:, :], op0=ALU.mult, op1=ALU.add),
                      reads=[xi.g, mm[b % 2][hf].g], writes=[zz.g])
            ln_block(kb, zz, gam, bet, xo[b % 2], stats, mv, rstd, LN_EPS)
            kb.dma("sp", out=xout_ap[b * 128:(b + 1) * 128, :], in_=xo[b % 2].t[:, :], reads=[xo[b % 2].g], writes=[xout_reg], pool="st")
            transpose_to_fm(kb, xo[b % 2], xbf[b % 2], xT, b, ident, ptr[b % 2])
        kb.barrier()


def phase_ffn(kb, nc, io, layer, xT, ident, xres_ap, xres_reg, xout_ap, xout_reg, G_ap, G_reg, want_T):
    Wup = io["ffn_w_up"][layer].rearrange("(kc p) n -> p kc n", p=128)
    Wdn = io["ffn_w_down"][layer].rearrange("(fc p) n -> p fc n", p=128)
    with ExitStack() as es:
        wu = [sbt(nc, es, "f_wu%d" % i, [128, 8, 256], BF16) for i in range(2)]
        cw = sbt(nc, es, "f_cw", [128, NFC, 3], F32)
        cb = sbt(nc, es, "f_cb", [128, NFC], F32)
        ubuf = [sbt(nc, es, "f_ub%d" % i, [128, 514], F32) for i in range(2)]
        cbuf = [sbt(nc, es, "f_c%d" % i, [128, 512], F32) for i in range(2)]
        gl = [sbt(nc, es, "f_gl%d" % i, [128, 512], F32) for i in range(2)]
        gt = [sbt(nc, es, "f_gt%d" % i, [128, 512], BF16) for i in range(3)]
        pu = [pst(nc, es, "f_pu%d" % i, [128, 512], F32) for i in range(3)]
        pv = [pst(nc, es, "f_pv%d" % i, [128, 512], F32) for i in range(3)]
        for j in range(3):
            kb.dma("sp", out=cw.t[:, :, j], in_=io["ffn_conv_w"][layer][j].rearrange("(fc p) -> p fc", p=128), writes=[cw.g],
                   allow_slow_non_contiguous=True)
        kb.dma("sp", out=cb.t[:, :], in_=io["ffn_conv_b"][layer].rearrange("(fc p) -> p fc", p=128), writes=[cb.g],
               allow_slow_non_contiguous=True)
        it = 0
        for fc in range(NFC):
            w = wu[fc % 2]
            kb.dma("pool", out=w.t[:, :, 0:128], in_=Wup[:, :, fc * 128:(fc + 1) * 128], writes=[w.g])
            kb.dma("pool", out=w.t[:, :, 128:256], in_=Wup[:, :, FF + fc * 128:FF + (fc + 1) * 128], writes=[w.g])
            for tt in range(8):
                u_ps = pu[it % 3]
                v_ps = pv[it % 3]
                ub = ubuf[it % 2]
                ubn = ubuf[(it + 1) % 2]
                c = cbuf[it % 2]
                g_ = gl[it % 2]
                go = gt[it % 3]
                for kc in range(8):
                    kb.op("pe", lambda e: e.matmul(out=u_ps.t[:, :], lhsT=w.t[:, kc, 0:128], rhs=xT.t[:, kc, tt * 512:(tt + 1) * 512],
                                                   start=(kc == 0), stop=(kc == 7)), reads=[w.g, xT.g], writes=[u_ps.g])
                for kc in range(8):
                    kb.op("pe", lambda e: e.matmul(out=v_ps.t[:, :], lhsT=w.t[:, kc, 128:256], rhs=xT.t[:, kc, tt * 512:(tt + 1) * 512],
                                                   start=(kc == 0), stop=(kc == 7)), reads=[w.g, xT.g], writes=[v_ps.g])
                if tt == 0:
                    kb.op("dve", lambda e: e.memset(ub.t[:, 0:2], 0.0), writes=[ub.g])
                kb.op("act", lambda e: e.activation(out=ub.t[:, 2:514], in_=u_ps.t[:, :], func=AF.Copy), reads=[u_ps.g], writes=[ub.g])
                if tt < 7:
                    kb.op("dve", lambda e: e.tensor_copy(out=ubn.t[:, 0:2], in_=ub.t[:, 512:514]), reads=[ub.g], writes=[ubn.g])
                kb.op("act", lambda e: e.activation(out=c.t[:, :], in_=u_ps.t[:, :], func=AF.Identity, scale=cw.t[:, fc, 2:3], bias=cb.t[:, fc:fc + 1]),
                      reads=[u_ps.g, cw.g, cb.g], writes=[c.g])
                kb.op("dve", lambda e: e.scalar_tensor_tensor(out=c.t[:, :], in0=ub.t[:, 1:513], scalar=cw.t[:, fc, 1:2], in1=c.t[:, :],
                                                              op0=ALU.mult, op1=ALU.add), reads=[ub.g, cw.g, c.g], writes=[c.g])
                kb.op("dve", lambda e: e.scalar_tensor_tensor(out=c.t[:, :], in0=ub.t[:, 0:512], scalar=cw.t[:, fc, 0:1], in1=c.t[:, :],
                                                              op0=ALU.mult, op1=ALU.add), reads=[ub.g, cw.g, c.g], writes=[c.g])
                kb.op("act", lambda e: e.activation(out=g_.t[:, :], in_=c.t[:, :], func=AF.Gelu), reads=[c.g], writes=[g_.g])
                kb.op("dve", lambda e: e.tensor_tensor(out=go.t[:, :], in0=v_ps.t[:, :], in1=g_.t[:, :], op=ALU.mult), reads=[v_ps.g, g_.g], writes=[go.g])
                kb.dma("sp", out=G_ap[fc * 128:(fc + 1) * 128, tt * 512:(tt + 1) * 512], in_=go.t[:, :], reads=[go.g], writes=[G_reg], pool="st")
                it += 1
        kb.barrier()
    Gv = G_ap.rearrange("(fc p) t -> p fc t", p=128)
    with ExitStack() as es:
        wd = sbt(nc, es, "g_wd", [128, NFC, 1024], BF16)
        gin = [sbt(nc, es, "g_gin%d" % i, [128, NFC, 512], BF16) for i in range(2)]
        gam = sbt(nc, es, "g_gam", [128, 1024], F32)
        bet = sbt(nc, es, "g_bet", [128, 1024], F32)
        xin = [sbt(nc, es, "g_x%d" % i, [128, 1024], F32) for i in range(2)]
        z = [sbt(nc, es, "g_z%d" % i, [128, 1024], F32) for i in range(2)]
        xo = [sbt(nc, es, "g_xo%d" % i, [128, 1024], F32) for i in range(2)]
        xbf = [sbt(nc, es, "g_xb%d" % i, [128, 1024], BF16) for i in range(2)]
        stats = sbt(nc, es, "g_stats", [128, 12], F32)
        mv = sbt(nc, es, "g_mv", [128, 2], F32)
        rstd = sbt(nc, es, "g_rstd", [128, 1], F32)
        mm = [[pst(nc, es, "g_mm%d%d" % (i, j), [128, 512], F32) for j in range(2)] for i in range(2)]
        ptr = [pst(nc, es, "g_pt%d" % i, [128, 1024], BF16) for i in range(2)]
        for q4 in range(2):
            kb.dma("pool", out=wd.t[:, q4 * 11:(q4 + 1) * 11, :], in_=Wdn[:, q4 * 11:(q4 + 1) * 11, :], writes=[wd.g])
        load_bcast(kb, gam, io["ln_ffn_g"][layer])
        load_bcast(kb, bet, io["ln_ffn_b"][layer])
        for tt in range(8):
            gi = gin[tt % 2]
            for q4 in range(2):
                kb.dma("sp", out=gi.t[:, q4 * 11:(q4 + 1) * 11, :], in_=Gv[:, q4 * 11:(q4 + 1) * 11, tt * 512:(tt + 1) * 512],
                       reads=[G_reg], writes=[gi.g])
            for bb in range(4):
                b = tt * 4 + bb
                xi = xin[b % 2]
                kb.dma("sp", out=xi.t[:, :], in_=xres_ap[b * 128:(b + 1) * 128, :], reads=[xres_reg], writes=[xi.g])
                for hf in range(2):
                    for fc in range(NFC):
                        kb.op("pe", lambda e: e.matmul(out=mm[b % 2][hf].t[:, :], lhsT=gi.t[:, fc, bb * 128:(bb + 1) * 128],
                                                       rhs=wd.t[:, fc, hf * 512:(hf + 1) * 512], start=(fc == 0), stop=(fc == NFC - 1)),
                              reads=[gi.g, wd.g], writes=[mm[b % 2][hf].g])
                zz = z[b % 2]
                for hf in range(2):
                    kb.op("dve", lambda e: e.scalar_tensor_tensor(out=zz.t[:, hf * 512:(hf + 1) * 512], in0=xi.t[:, hf * 512:(hf + 1) * 512], scalar=DN_ALPHA,
                                                                  in1=mm[b % 2][hf].t[:, :], op0=ALU.mult, op1=ALU.add),
                          reads=[xi.g, mm[b % 2][hf].g], writes=[zz.g])
                ln_block(kb, zz, gam, bet, xo[b % 2], stats, mv, rstd, LN_EPS)
                kb.dma("sp", out=xout_ap[b * 128:(b + 1) * 128, :], in_=xo[b % 2].t[:, :], reads=[xo[b % 2].g], writes=[xout_reg], pool="st")
                if want_T:
                    transpose_to_fm(kb, xo[b % 2], xbf[b % 2], xT, b, ident, ptr[b % 2])
        kb.barrier()


def phase_rwkv_a(kb, nc, io, xT, scr, scr_reg):
    Wrkv = io["od_w_rkv"][0].rearrange("n (kc p) e -> p n kc e", p=128)
    with ExitStack() as es:
        wr = sbt(nc, es, "ra_w", [128, 3, 8, 1024], BF16)
        l1 = sbt(nc, es, "ra_l1", [128, 8, 288], BF16)
        w2 = sbt(nc, es, "ra_w2", [64, 1024], BF16)
        a2 = sbt(nc, es, "ra_a2", [64, 1024], BF16)
        g2a = sbt(nc, es, "ra_g2a", [128, 1024], BF16)
        g2b = sbt(nc, es, "ra_g2b", [32, 1024], BF16)
        w0b = sbt(nc, es, "ra_w0b", [128, 1024], F32)
        a0b = sbt(nc, es, "ra_a0b", [128, 1024], F32)
        mu = sbt(nc, es, "ra_mu", [128, 6, 8], F32)
        xx = [sbt(nc, es, "ra_xx%d" % i, [128, 8, 128], F32) for i in range(2)]
        mixT = [[sbt(nc, es, "ra_mix%d_%d" % (n, i), [128, 8, 128], BF16) for i in range(2)] for n in range(6)]
        lo1 = [sbt(nc, es, "ra_lo%d" % i, [128, 128], BF16) for i in range(4)]
        lo2 = sbt(nc, es, "ra_l32", [32, 128], BF16)
        outF = [sbt(nc, es, "ra_o%d" % i, [128, 1024], F32) for i in range(4)]
        P = [pst(nc, es, "ra_p%d" % i, [128, 1024], F32) for i in range(3)]
        Q = [pst(nc, es, "ra_q%d" % i, [128, 512], F32) for i in range(2)]
        for n in range(3):
            kb.dma("pool", out=wr.t[:, n, :, :], in_=Wrkv[:, n, :, :], writes=[wr.g])
        kb.dma("pool", out=l1.t[:, :, 0:64], in_=io["od_w1"].rearrange("(kc p) e -> p kc e", p=128), writes=[l1.g])
        kb.dma("pool", out=l1.t[:, :, 64:128], in_=io["od_a1"].rearrange("(kc p) e -> p kc e", p=128), writes=[l1.g])
        kb.dma("pool", out=l1.t[:, :, 128:288], in_=io["od_g1"].rearrange("(kc p) e -> p kc e", p=128), writes=[l1.g])
        kb.dma("pool", out=w2.t[:, :], in_=io["od_w2"][:, :], writes=[w2.g])
        kb.dma("pool", out=a2.t[:, :], in_=io["od_a2"][:, :], writes=[a2.g])
        kb.dma("pool", out=g2a.t[:, :], in_=io["od_g2"][0:128, :], writes=[g2a.g])
        kb.dma("pool", out=g2b.t[:, :], in_=io["od_g2"][128:160, :], writes=[g2b.g])
        load_bcast(kb, w0b, io["od_w0"][0])
        load_bcast(kb, a0b, io["od_a0"][0])
        for n in range(6):
            kb.dma("sp", out=mu.t[:, n, :], in_=io["od_mu"][0, n].rearrange("(kc p) -> p kc", p=128), writes=[mu.g],
                   allow_slow_non_contiguous=True)
        oi = 0
        for b in range(NB):
            t0 = b * 128
            x_ = xx[b % 2]
            if b == 0:
                kb.op("dve", lambda e: e.tensor_tensor(out=x_.t[:, :, 1:128], in0=xT.t[:, :, 0:127], in1=xT.t[:, :, 1:128], op=ALU.subtract),
                      reads=[xT.g], writes=[x_.g])
                kb.op("dve", lambda e: e.tensor_scalar(out=x_.t[:, :, 0:1], in0=xT.t[:, :, 0:1], scalar1=-1.0, scalar2=None, op0=ALU.mult),
                      reads=[xT.g], writes=[x_.g])
            else:
                kb.op("dve", lambda e: e.tensor_tensor(out=x_.t[:, :, :], in0=xT.t[:, :, t0 - 1:t0 + 127], in1=xT.t[:, :, t0:t0 + 128], op=ALU.subtract),
                      reads=[xT.g], writes=[x_.g])
            mx = [mixT[n][b % 2] for n in range(6)]
            for n in range(6):
                for kc in range(8):
                    eng = "dve"
                    kb.op(eng, lambda e: e.scalar_tensor_tensor(out=mx[n].t[:, kc, :], in0=x_.t[:, kc, :], scalar=mu.t[:, n, kc:kc + 1],
                                                                in1=xT.t[:, kc, t0:t0 + 128], op0=ALU.mult, op1=ALU.add),
                          reads=[x_.g, mu.g, xT.g], writes=[mx[n].g])

            def store(idx, ps, pre=None, func=None):
                nonlocal oi
                o = outF[oi % 4]
                oi += 1
                if pre is not None:
                    kb.op("dve", lambda e: e.tensor_tensor(out=o.t[:, :], in0=ps.t[:, :], in1=pre.t[:, :], op=ALU.add), reads=[ps.g, pre.g], writes=[o.g])
                    kb.op("act", lambda e: e.activation(out=o.t[:, :], in_=o.t[:, :], func=func), reads=[o.g], writes=[o.g])
                else:
                    kb.op("act", lambda e: e.activation(out=o.t[:, :], in_=ps.t[:, :], func=AF.Copy), reads=[ps.g], writes=[o.g])
                kb.dma("sp", out=scr[idx][t0:t0 + 128, :], in_=o.t[:, :], reads=[o.g], writes=[scr_reg], pool="st")

            for n in range(3):
                ps = P[n]
                for hf in range(2):
                    for kc in range(8):
                        kb.op("pe", lambda e: e.matmul(out=ps.t[:, hf * 512:(hf + 1) * 512], lhsT=mx[n].t[:, kc, :], rhs=wr.t[:, n, kc, hf * 512:(hf + 1) * 512],
                                                       start=(kc == 0), stop=(kc == 7)), reads=[mx[n].g, wr.g], writes=[ps.g])
                store(n, ps)
            q = Q[0]
            for kc in range(8):
                kb.op("pe", lambda e: e.matmul(out=q.t[0:64, 0:128], lhsT=l1.t[:, kc, 0:64], rhs=mx[3].t[:, kc, :], start=(kc == 0), stop=(kc == 7)),
                      reads=[l1.g, mx[3].g], writes=[q.g])
            for kc in range(8):
                kb.op("pe", lambda e: e.matmul(out=q.t[0:64, 128:256], lhsT=l1.t[:, kc, 64:128], rhs=mx[4].t[:, kc, :], start=(kc == 0), stop=(kc == 7)),
                      reads=[l1.g, mx[4].g], writes=[q.g])
            for kc in range(8):
                kb.op("pe", lambda e: e.matmul(out=q.t[:, 256:384], lhsT=l1.t[:, kc, 128:256], rhs=mx[5].t[:, kc, :], start=(kc == 0), stop=(kc == 7)),
                      reads=[l1.g, mx[5].g], writes=[q.g])
            for kc in range(8):
                kb.op("pe", lambda e: e.matmul(out=q.t[0:32, 384:512], lhsT=l1.t[:, kc, 256:288], rhs=mx[5].t[:, kc, :], start=(kc == 0), stop=(kc == 7)),
                      reads=[l1.g, mx[5].g], writes=[q.g])
            tw, al, sg1 = lo1[0], lo1[1], lo1[2]
            kb.op("act", lambda e: e.activation(out=tw.t[0:64, :], in_=q.t[0:64, 0:128], func=AF.Tanh), reads=[q.g], writes=[tw.g])
            kb.op("act", lambda e: e.activation(out=al.t[0:64, :], in_=q.t[0:64, 128:256], func=AF.Copy), reads=[q.g], writes=[al.g])
            kb.op("act", lambda e: e.activation(out=sg1.t[:, :], in_=q.t[:, 256:384], func=AF.Sigmoid), reads=[q.g], writes=[sg1.g])
            kb.op("act", lambda e: e.activation(out=lo2.t[:, :], in_=q.t[0:32, 384:512], func=AF.Sigmoid), reads=[q.g], writes=[lo2.g])
            ps = P[0]
            for hf in range(2):
                kb.op("pe", lambda e: e.matmul(out=ps.t[:, hf * 512:(hf + 1) * 512], lhsT=tw.t[0:64, :], rhs=w2.t[:, hf * 512:(hf + 1) * 512], start=True, stop=True),
                      reads=[tw.g, w2.g], writes=[ps.g])
            store(3, ps, pre=w0b, func=AF.Sigmoid)
            ps = P[1]
            for hf in range(2):
                kb.op("pe", lambda e: e.matmul(out=ps.t[:, hf * 512:(hf + 1) * 512], lhsT=al.t[0:64, :], rhs=a2.t[:, hf * 512:(hf + 1) * 512], start=True, stop=True),
                      reads=[al.g, a2.g], writes=[ps.g])
            store(4, ps, pre=a0b, func=AF.Sigmoid)
            ps = P[2]
            for hf in range(2):
                kb.op("pe", lambda e: e.matmul(out=ps.t[:, hf * 512:(hf + 1) * 512], lhsT=sg1.t[:, :], rhs=g2a.t[:, hf * 512:(hf + 1) * 512], start=True, stop=False),
                      reads=[sg1.g, g2a.g], writes=[ps.g])
                kb.op("pe", lambda e: e.matmul(out=ps.t[:, hf * 512:(hf + 1) * 512], lhsT=lo2.t[:, :], rhs=g2b.t[:, hf * 512:(hf + 1) * 512], start=False, stop=True),
                      reads=[lo2.g, g2b.g], writes=[ps.g])
            store(5, ps)
        kb.barrier()


def phase_rwkv_b(kb, nc, io, XT3, xt3_reg, ident, ones, tri, scr, scr_reg, xres_ap, xres_reg, xout_ap, xout_reg):
    H3 = lambda ap: ap.rearrange("p (h d) -> p h d", h=16)
    with ExitStack() as es:
        def F(name):
            return sbt(nc, es, "rb_" + name, [128, 1024], F32)

        def B(name):
            return sbt(nc, es, "rb_" + name, [128, 1024], BF16)

        wo = sbt(nc, es, "rb_wo", [128, 8, 1024], BF16)
        vec = {}
        for nm, ap in (("k_k", io["od_k_k"][0]), ("k_a", io["od_k_a"][0]), ("r_k", io["od_r_k"][0]), ("lnx_g", io["od_lnx_g"][0]),
                       ("lnx_b", io["od_lnx_b"][0]), ("lng", io["ln_mix_g"][1]), ("lnb", io["ln_mix_b"][1])):
            vec[nm] = F("v_" + nm)
            load_bcast(kb, vec[nm], ap)
        kb.dma("pool", out=wo.t[:, :, :], in_=io["od_w_out"].rearrange("(kc p) n -> p kc n", p=128), writes=[wo.g])
        msk = {}
        for nm in ("c_su4", "c_sl4", "c_iu4", "c_id4"):
            msk[nm] = sbt(nc, es, "rb_" + nm, [128, 512], BF16)
            kb.dma("sp", out=msk[nm].t[:, :], in_=io[nm][:, :], writes=[msk[nm].g])
        inb = [[F("in%d_%d" % (i, j)) for i in range(6)] for j in range(2)]
        x3ts = [B("x3t0"), B("x3t1")]
        f1, f2, f3 = F("f1"), F("f2"), F("f3")
        vB, lhi, llo = B("vB"), B("lhi"), B("llo")
        rtB, atB, btB, ktB, bpB, kpB = B("rt"), B("at"), B("bt"), B("kt"), B("bp"), B("kp")
        rT, aT, bT, kTt = B("rT"), B("aT"), B("bT"), B("kT")
        Arb, Ark = [B("Arb0"), B("Arb1")], [B("Ark0"), B("Ark1")]
        Xall = [B("X0"), B("X1")]
        AhT = B("AhT")
        W1b, Ub, ygB = rtB, btB, ktB
        tmpg = [[sbt(nc, es, "rb_tg%d_%d" % (g, i), [128, 512], BF16) for i in range(10)] for g in range(4)]
        tmpb = tmpg[0]
        small = sbt(nc, es, "rb_small", [128, 96], F32)
        eLC = sbt(nc, es, "rb_eLC", [128, 8], F32)
        Hs = sbt(nc, es, "rb_H", [128, 8, 64], F32)
        Hb = sbt(nc, es, "rb_Hb", [128, 8, 64], BF16)
        xo = f3
        xbf = lhi
        stats = sbt(nc, es, "rb_stats", [128, 12], F32)
        mv = sbt(nc, es, "rb_mv", [128, 2], F32)
        rstd = sbt(nc, es, "rb_rstd", [128, 1], F32)
        P = [pst(nc, es, "rb_p%d" % i, [128, 1024], F32) for i in range(3)]
        QT = pst(nc, es, "rb_qt", [128, 1024], BF16)
        Q1 = pst(nc, es, "rb_q1", [128, 512], F32)
        kb.op("dve", lambda e: e.memset(Hs.t[:, :, :], 0.0), writes=[Hs.g])
        kb.op("dve", lambda e: e.memset(Hb.t[:, :, :], 0.0), writes=[Hb.g])

        def bc16(t, c0):
            return small.t[:, c0:c0 + 16].unsqueeze(2).to_broadcast([128, 16, 64])

        for b in range(NB):
            t0 = b * 128
            if b == 0:
                for idx, dst in enumerate(inb[0]):
                    kb.dma("sp", out=dst.t[:, :], in_=scr[idx][0:128, :], reads=[scr_reg], writes=[dst.g])
            rF, kF, vF, wF, aF, gF = inb[b % 2]
            xin = rF
            if b + 1 < NB:
                for idx, dst in enumerate(inb[(b + 1) % 2]):
                    kb.dma("sp", out=dst.t[:, :], in_=scr[idx][t0 + 128:t0 + 256, :], reads=[scr_reg], writes=[dst.g])
            kb.op("act", lambda e: e.activation(out=vB.t[:, :], in_=vF.t[:, :], func=AF.Copy), reads=[vF.g], writes=[vB.g])
            kb.op("dve", lambda e: e.tensor_scalar(out=wF.t[:, :], in0=wF.t[:, :], scalar1=-math.exp(-0.5), scalar2=None, op0=ALU.mult), reads=[wF.g], writes=[wF.g])
            kb.op("act", lambda e: e.activation(out=lhi.t[:, :], in_=wF.t[:, :], func=AF.Copy), reads=[wF.g], writes=[lhi.g])
            kb.op("dve", lambda e: e.tensor_tensor(out=llo.t[:, :], in0=wF.t[:, :], in1=lhi.t[:, :], op=ALU.subtract), reads=[wF.g, lhi.g], writes=[llo.g])
            kb.op("dve", lambda e: e.tensor_tensor(out=f1.t[:, :], in0=kF.t[:, :], in1=vec["k_k"].t[:, :], op=ALU.mult), reads=[kF.g, vec["k_k"].g], writes=[f1.g])
            kb.op("act", lambda e: e.activation(out=f2.t[:, :], in_=f1.t[:, :], func=AF.Square), reads=[f1.g], writes=[f2.g])
            kb.op("dve", lambda e: e.tensor_reduce(out=small.t[:, 0:16], in_=H3(f2.t[:, :]), axis=AX.X, op=ALU.add), reads=[f2.g], writes=[small.g])
            kb.op("act", lambda e: e.activation(out=small.t[:, 0:16], in_=small.t[:, 0:16], func=AF.Sqrt), reads=[small.g], writes=[small.g])
            kb.op("dve", lambda e: e.tensor_scalar(out=small.t[:, 0:16], in0=small.t[:, 0:16], scalar1=1e-12, scalar2=None, op0=ALU.max), reads=[small.g], writes=[small.g])
            kb.op("dve", lambda e: e.reciprocal(out=small.t[:, 0:16], in_=small.t[:, 0:16]), reads=[small.g], writes=[small.g])
            kb.op("dve", lambda e: e.tensor_tensor(out=H3(f1.t[:, :]), in0=H3(f1.t[:, :]), in1=bc16(small, 0), op=ALU.mult), reads=[f1.g, small.g], writes=[f1.g])
            kb.op("dve", lambda e: e.scalar_tensor_tensor(out=f2.t[:, :], in0=aF.t[:, :], scalar=-1.0, in1=vec["k_a"].t[:, :], op0=ALU.add, op1=ALU.mult),
                  reads=[aF.g, vec["k_a"].g], writes=[f2.g])
            kb.op("dve", lambda e: e.scalar_tensor_tensor(out=f2.t[:, :], in0=f2.t[:, :], scalar=1.0, in1=kF.t[:, :], op0=ALU.add, op1=ALU.mult),
                  reads=[f2.g, kF.g], writes=[f2.g])
            kb.op("dve", lambda e: e.tensor_tensor(out=kF.t[:, :], in0=f1.t[:, :], in1=aF.t[:, :], op=ALU.mult), reads=[f1.g, aF.g], writes=[kF.g])
            if RB_STOP == 1:
                kb.barrier()
                return
            for hf in range(2):
                sl = slice(hf * 512, (hf + 1) * 512)
                kb.op("pe", lambda e: e.matmul(out=P[0].t[:, sl], lhsT=tri.t[:, :], rhs=lhi.t[:, sl], start=True, stop=False), reads=[tri.g, lhi.g], writes=[P[0].g])
                kb.op("pe", lambda e: e.matmul(out=P[0].t[:, sl], lhsT=tri.t[:, :], rhs=llo.t[:, sl], start=False, stop=True), reads=[tri.g, llo.g], writes=[P[0].g])
                kb.op("pe", lambda e: e.matmul(out=P[1].t[:, sl], lhsT=ones.t[:, :], rhs=lhi.t[:, sl], start=True, stop=False), reads=[ones.g, lhi.g], writes=[P[1].g])
                kb.op("pe", lambda e: e.matmul(out=P[1].t[:, sl], lhsT=ones.t[:, :], rhs=llo.t[:, sl], start=False, stop=True), reads=[ones.g, llo.g], writes=[P[1].g])
            for hp in range(8):
                kb.op("pe", lambda e: e.matmul(out=Q1.t[:, hp:hp + 1], lhsT=lhi.t[:, hp * 128:(hp + 1) * 128], rhs=ones.t[:, 0:1], start=True, stop=False),
                      reads=[lhi.g, ones.g], writes=[Q1.g])
                kb.op("pe", lambda e: e.matmul(out=Q1.t[:, hp:hp + 1], lhsT=llo.t[:, hp * 128:(hp + 1) * 128], rhs=ones.t[:, 0:1], start=False, stop=True),
                      reads=[llo.g, ones.g], writes=[Q1.g])
            kb.op("act", lambda e: e.activation(out=eLC.t[:, :], in_=Q1.t[:, 0:8], func=AF.Exp), reads=[Q1.g], writes=[eLC.g])
            if RB_STOP == 2:
                kb.barrier()
                return
            kb.op("act", lambda e: e.activation(out=aF.t[:, :], in_=P[0].t[:, :], func=AF.Copy), reads=[P[0].g], writes=[aF.g])
            kb.op("act", lambda e: e.activation(out=f3.t[:, :], in_=P[0].t[:, :], func=AF.Exp), reads=[P[0].g], writes=[f3.g])
            kb.op("dve", lambda e: e.tensor_tensor(out=rtB.t[:, :], in0=rF.t[:, :], in1=f3.t[:, :], op=ALU.mult), reads=[rF.g, f3.g], writes=[rtB.g])
            kb.op("dve", lambda e: e.tensor_tensor(out=wF.t[:, :], in0=aF.t[:, :], in1=wF.t[:, :], op=ALU.subtract), reads=[aF.g, wF.g], writes=[wF.g])
            kb.op("act", lambda e: e.activation(out=wF.t[:, :], in_=wF.t[:, :], func=AF.Exp), reads=[wF.g], writes=[wF.g])
            kb.op("dve", lambda e: e.scalar_tensor_tensor(out=atB.t[:, :], in0=f1.t[:, :], scalar=-1.0, in1=wF.t[:, :], op0=ALU.mult, op1=ALU.mult),
                  reads=[f1.g, wF.g], writes=[atB.g])
            kb.op("act", lambda e: e.activation(out=f3.t[:, :], in_=aF.t[:, :], func=AF.Exp, scale=-1.0), reads=[aF.g], writes=[f3.g])
            kb.op("dve", lambda e: e.tensor_tensor(out=btB.t[:, :], in0=kF.t[:, :], in1=f3.t[:, :], op=ALU.mult), reads=[kF.g, f3.g], writes=[btB.g])
            kb.op("dve", lambda e: e.tensor_tensor(out=ktB.t[:, :], in0=f2.t[:, :], in1=f3.t[:, :], op=ALU.mult), reads=[f2.g, f3.g], writes=[ktB.g])
            kb.op("dve", lambda e: e.tensor_tensor(out=f3.t[:, :], in0=P[1].t[:, :], in1=aF.t[:, :], op=ALU.subtract), reads=[P[1].g, aF.g], writes=[f3.g])
            kb.op("act", lambda e: e.activation(out=f3.t[:, :], in_=f3.t[:, :], func=AF.Exp), reads=[f3.g], writes=[f3.g])
            kb.op("dve", lambda e: e.tensor_tensor(out=bpB.t[:, :], in0=kF.t[:, :], in1=f3.t[:, :], op=ALU.mult), reads=[kF.g, f3.g], writes=[bpB.g])
            kb.op("dve", lambda e: e.tensor_tensor(out=kpB.t[:, :], in0=f2.t[:, :], in1=f3.t[:, :], op=ALU.mult), reads=[f2.g, f3.g], writes=[kpB.g])
            kb.op("dve", lambda e: e.tensor_tensor(out=f3.t[:, :], in0=rF.t[:, :], in1=f2.t[:, :], op=ALU.mult), reads=[rF.g, f2.g], writes=[f3.g])
            kb.op("dve", lambda e: e.tensor_tensor(out=f3.t[:, :], in0=f3.t[:, :], in1=vec["r_k"].t[:, :], op=ALU.mult), reads=[f3.g, vec["r_k"].g], writes=[f3.g])
            kb.op("dve", lambda e: e.tensor_reduce(out=small.t[:, 16:32], in_=H3(f3.t[:, :]), axis=AX.X, op=ALU.add), reads=[f3.g], writes=[small.g])
            kb.op("dve", lambda e: e.tensor_tensor(out=H3(vF.t[:, :]), in0=H3(vF.t[:, :]), in1=bc16(small, 16), op=ALU.mult), reads=[vF.g, small.g], writes=[vF.g])
            if RB_STOP == 3:
                kb.barrier()
                return
            for src, dst in ((rtB, rT), (atB, aT), (btB, bT), (ktB, kTt)):
                for hp in range(8):
                    kb.op("pe", lambda e: e.transpose(out=QT.t[:, hp * 128:(hp + 1) * 128], in_=src.t[:, hp * 128:(hp + 1) * 128], identity=ident.t[:, :]),
                          reads=[src.g, ident.g], writes=[QT.g])
                kb.op("act", lambda e: e.activation(out=dst.t[:, :], in_=QT.t[:, :], func=AF.Copy), reads=[QT.g], writes=[dst.g])

            if RB_STOP == 4:
                kb.barrier()
                return
            def fm(t, h):
                r0 = 64 * (h % 2)
                return t.t[r0:r0 + 64, (h // 2) * 128:(h // 2 + 1) * 128]

            gst = []
            for g4 in range(4):
                heads = [g4 * 4 + i for i in range(4)]
                order = [(0, heads[0]), (2, heads[2]), (1, heads[1]), (3, heads[3])]
                tb = tmpg[g4]
                Nb, NTb = tb[0], tb[1]
                specs = ((bT, aT, Nb, "c_su4"), (aT, bT, NTb, "c_sl4"))
                ps = P[(2 * g4) % 3]
                for si, (la, rb_, dst, mk) in enumerate(specs):
                    off = si * 512
                    for i, h in order:
                        kb.op("pe", lambda e: e.matmul(out=ps.t[:, off + i * 128:off + (i + 1) * 128], lhsT=fm(la, h), rhs=fm(rb_, h), start=True, stop=True),
                              reads=[la.g, rb_.g], writes=[ps.g], rg=(64 * (h % 2), 64))
                    kb.op("dve", lambda e: e.tensor_tensor(out=dst.t[:, :], in0=ps.t[:, off:off + 512], in1=msk[mk].t[:, :], op=ALU.mult),
                          reads=[ps.g, msk[mk].g], writes=[dst.g])
                hi_ = g4 // 2
                co = (g4 % 2) * 512
                ps = P[(2 * g4 + 1) % 3]
                for si, (la, rb_, dst) in enumerate(((bT, rT, Arb[hi_]), (kTt, rT, Ark[hi_]))):
                    off = si * 512
                    for i, h in order:
                        kb.op("pe", lambda e: e.matmul(out=ps.t[:, off + i * 128:off + (i + 1) * 128], lhsT=fm(la, h), rhs=fm(rb_, h), start=True, stop=True),
                              reads=[la.g, rb_.g], writes=[ps.g], rg=(64 * (h % 2), 64))
                    kb.op("dve", lambda e: e.tensor_tensor(out=dst.t[:, co:co + 512], in0=ps.t[:, off:off + 512], in1=msk["c_iu4"].t[:, :], op=ALU.mult),
                          reads=[ps.g, msk["c_iu4"].g], writes=[dst.g])
                X, XT = tb[2], tb[3]
                kb.op("dve", lambda e: e.tensor_tensor(out=X.t[:, :], in0=Nb.t[:, :], in1=msk["c_id4"].t[:, :], op=ALU.add), reads=[Nb.g, msk["c_id4"].g], writes=[X.g])
                kb.op("dve", lambda e: e.tensor_tensor(out=XT.t[:, :], in0=NTb.t[:, :], in1=msk["c_id4"].t[:, :], op=ALU.add), reads=[NTb.g, msk["c_id4"].g], writes=[XT.g])
                gst.append({"X": X, "XT": XT, "P": Nb, "PT": NTb, "pp": 0})
            for it in range(6):
                last = it == 5
                cur = {}
                for g4 in range(4):
                    st = gst[g4]
                    tb = tmpg[g4]
                    Pm, PTm = st["P"], st["PT"]
                    P2, P2T = tb[4 + st["pp"]], tb[6 + st["pp"]]
                    st["pp"] ^= 1
                    psa = P[(2 * g4 + 2 * it) % 3]
                    for i in range(4):
                        sl = slice(i * 128, (i + 1) * 128)
                        kb.op("pe", lambda e: e.matmul(out=psa.t[:, sl], lhsT=PTm.t[:, sl], rhs=Pm.t[:, sl], start=True, stop=True), reads=[PTm.g, Pm.g], writes=[psa.g])
                    if not last:
                        for i in range(4):
                            sl = slice(i * 128, (i + 1) * 128)
                            sl2 = slice(512 + i * 128, 512 + (i + 1) * 128)
                            kb.op("pe", lambda e: e.matmul(out=psa.t[:, sl2], lhsT=Pm.t[:, sl], rhs=PTm.t[:, sl], start=True, stop=True), reads=[PTm.g, Pm.g], writes=[psa.g])
                    kb.op("act", lambda e: e.activation(out=P2.t[:, :], in_=psa.t[:, 0:512], func=AF.Copy), reads=[psa.g], writes=[P2.g])
                    if not last:
                        kb.op("act", lambda e: e.activation(out=P2T.t[:, :], in_=psa.t[:, 512:1024], func=AF.Copy), reads=[psa.g], writes=[P2T.g])
                    cur[g4] = (P2, P2T)
                for g4 in range(4):
                    st = gst[g4]
                    tb = tmpg[g4]
                    hi_ = g4 // 2
                    co = (g4 % 2) * 512
                    X, XT = st["X"], st["XT"]
                    P2, P2T = cur[g4]
                    psb = P[(2 * g4 + 2 * it + 1) % 3]
                    for i in range(4):
                        sl = slice(i * 128, (i + 1) * 128)
                        kb.op("pe", lambda e: e.matmul(out=psb.t[:, sl], lhsT=XT.t[:, sl], rhs=P2.t[:, sl], start=True, stop=True), reads=[XT.g, P2.g], writes=[psb.g])
                    if not last:
                        for i in range(4):
                            sl = slice(i * 128, (i + 1) * 128)
                            sl2 = slice(512 + i * 128, 512 + (i + 1) * 128)
                            kb.op("pe", lambda e: e.matmul(out=psb.t[:, sl2], lhsT=P2.t[:, sl], rhs=XT.t[:, sl], start=True, stop=True), reads=[XT.g, P2.g], writes=[psb.g])
                    if last:
                        kb.op("dve", lambda e: e.tensor_tensor(out=Xall[hi_].t[:, co:co + 512], in0=psb.t[:, 0:512], in1=X.t[:, :], op=ALU.add),
                              reads=[psb.g, X.g], writes=[Xall[hi_].g])
                    else:
                        Xn, XTn = (tb[8], tb[9]) if (it % 2 == 0) else (tb[2], tb[3])
                        kb.op("dve", lambda e: e.tensor_tensor(out=Xn.t[:, :], in0=psb.t[:, 0:512], in1=X.t[:, :], op=ALU.add), reads=[psb.g, X.g], writes=[Xn.g])
                        kb.op("dve", lambda e: e.tensor_tensor(out=XTn.t[:, :], in0=psb.t[:, 512:1024], in1=XT.t[:, :], op=ALU.add), reads=[psb.g, XT.g], writes=[XTn.g])
                        st["X"], st["XT"], st["P"], st["PT"] = Xn, XTn, P2, P2T
            if RB_STOP == 5:
                kb.barrier()
                return
            for g4 in range(4):
                heads = [g4 * 4 + i for i in range(4)]
                order = [(0, heads[0]), (2, heads[2]), (1, heads[1]), (3, heads[3])]
                Aak = tmpb[2]
                ps = P[2]
                for i, h in ((0, heads[0]), (2, heads[2]), (1, heads[1]), (3, heads[3])):
                    kb.op("pe", lambda e: e.matmul(out=ps.t[:, i * 128:(i + 1) * 128], lhsT=fm(kTt, h), rhs=fm(aT, h), start=True, stop=True),
                          reads=[kTt.g, aT.g], writes=[ps.g], rg=(64 * (h % 2), 64))
                kb.op("dve", lambda e: e.tensor_tensor(out=Aak.t[:, :], in0=ps.t[:, 0:512], in1=msk["c_su4"].t[:, :], op=ALU.mult),
                      reads=[ps.g, msk["c_su4"].g], writes=[Aak.g])
                for i, h in enumerate(heads):
                    kb.op("pe", lambda e: e.matmul(out=P[1].t[:, h * 64:(h + 1) * 64], lhsT=Aak.t[:, i * 128:(i + 1) * 128], rhs=vB.t[:, h * 64:(h + 1) * 64],
                                                   start=True, stop=True), reads=[Aak.g, vB.g], writes=[P[1].g])
                hi_ = g4 // 2
                co = (g4 % 2) * 512
                for i, h in enumerate(heads):
                    hp = h // 2
                    kb.op("pe", lambda e: e.matmul(out=ps.t[:, 512 + i * 128:512 + (i + 1) * 128], lhsT=atB.t[:, hp * 128:(hp + 1) * 128],
                                                   rhs=Xall[hi_].t[:, co + i * 128:co + (i + 1) * 128], start=True, stop=True),
                          reads=[atB.g, Xall[hi_].g], writes=[ps.g])
                v4 = ps.t[:, 512:1024].rearrange("p (j two t) -> p j two t", j=2, two=2)
                o4 = AhT.t[:, g4 * 256:(g4 + 1) * 256].rearrange("p (j t) -> p j t", j=2)
                kb.op("act", lambda e: e.activation(out=o4[0:64, :, :], in_=v4[0:64, :, 0, :], func=AF.Copy), reads=[ps.g], writes=[AhT.g])
                kb.op("act", lambda e: e.activation(out=o4[64:128, :, :], in_=v4[64:128, :, 1, :], func=AF.Copy), reads=[ps.g], writes=[AhT.g])
            kb.op("act", lambda e: e.activation(out=W1b.t[:, :], in_=P[1].t[:, :], func=AF.Copy), reads=[P[1].g], writes=[W1b.g])
            if RB_STOP == 6:
                kb.barrier()
                return
            Hb3 = Hb.t
            for h in range(16):
                hp, r0 = h // 2, 64 * (h % 2)
                hi_, co = h // 8, (h % 8) * 128
                kb.op("pe", lambda e: e.matmul(out=P[0].t[:, h * 64:(h + 1) * 64], lhsT=AhT.t[r0:r0 + 64, hp * 128:(hp + 1) * 128], rhs=Hb3[r0:r0 + 64, hp, :],
                                               start=True, stop=False), reads=[AhT.g, Hb.g], writes=[P[0].g])
                kb.op("pe", lambda e: e.matmul(out=P[0].t[:, h * 64:(h + 1) * 64], lhsT=Xall[hi_].t[:, co:co + 128], rhs=W1b.t[:, h * 64:(h + 1) * 64],
                                               start=False, stop=True), reads=[Xall[hi_].g, W1b.g], writes=[P[0].g])
            kb.op("act", lambda e: e.activation(out=Ub.t[:, :], in_=P[0].t[:, :], func=AF.Copy), reads=[P[0].g], writes=[Ub.g])
            for h in range(16):
                hp, r0 = h // 2, 64 * (h % 2)
                hi_, co = h // 8, (h % 8) * 128
                o = P[2].t[:, h * 64:(h + 1) * 64]
                kb.op("pe", lambda e: e.matmul(out=o, lhsT=fm(rT, h), rhs=Hb3[r0:r0 + 64, hp, :], start=True, stop=False), reads=[rT.g, Hb.g], writes=[P[2].g])
                kb.op("pe", lambda e: e.matmul(out=o, lhsT=Arb[hi_].t[:, co:co + 128], rhs=Ub.t[:, h * 64:(h + 1) * 64], start=False, stop=False),
                      reads=[Arb[hi_].g, Ub.g], writes=[P[2].g])
                kb.op("pe", lambda e: e.matmul(out=o, lhsT=Ark[hi_].t[:, co:co + 128], rhs=vB.t[:, h * 64:(h + 1) * 64], start=False, stop=True),
                      reads=[Ark[hi_].g, vB.g], writes=[P[2].g])
            for hp in range(8):
                sl = slice(hp * 128, (hp + 1) * 128)
                kb.op("pe", lambda e: e.matmul(out=P[1].t[:, sl], lhsT=bpB.t[:, sl], rhs=Ub.t[:, sl], start=True, stop=False), reads=[bpB.g, Ub.g], writes=[P[1].g])
                kb.op("pe", lambda e: e.matmul(out=P[1].t[:, sl], lhsT=kpB.t[:, sl], rhs=vB.t[:, sl], start=False, stop=True), reads=[kpB.g, vB.g], writes=[P[1].g])
            kb.op("dve", lambda e: e.tensor_tensor(out=Hs.t[:, :, :], in0=Hs.t[:, :, :], in1=eLC.t[:, 0:8].unsqueeze(2).to_broadcast([128, 8, 64]), op=ALU.mult),
                  reads=[Hs.g, eLC.g], writes=[Hs.g])
            hv = P[1].t[:, :].rearrange("p (hp two d) -> p hp two d", hp=8, two=2)
            kb.op("dve", lambda e: e.tensor_tensor(out=Hs.t[0:64, :, :], in0=Hs.t[0:64, :, :], in1=hv[0:64, :, 0, :], op=ALU.add), reads=[Hs.g, P[1].g], writes=[Hs.g])
            kb.op("dve", lambda e: e.tensor_tensor(out=Hs.t[64:128, :, :], in0=Hs.t[64:128, :, :], in1=hv[64:128, :, 1, :], op=ALU.add), reads=[Hs.g, P[1].g], writes=[Hs.g])
            kb.op("act", lambda e: e.activation(out=Hb.t[:, :, :], in_=Hs.t[:, :, :], func=AF.Copy), reads=[Hs.g], writes=[Hb.g])
            if RB_STOP == 7:
                kb.barrier()
                return
            Y = P[2]
            kb.op("act", lambda e: e.activation(out=f1.t[:, :], in_=Y.t[:, :], func=AF.Copy), reads=[Y.g], writes=[f1.g])
            kb.op("act", lambda e: e.activation(out=f2.t[:, :], in_=Y.t[:, :], func=AF.Square), reads=[Y.g], writes=[f2.g])
            kb.op("dve", lambda e: e.tensor_reduce(out=small.t[:, 32:48], in_=H3(f1.t[:, :]), axis=AX.X, op=ALU.add), reads=[f1.g], writes=[small.g])
            kb.op("dve", lambda e: e.tensor_reduce(out=small.t[:, 48:64], in_=H3(f2.t[:, :]), axis=AX.X, op=ALU.add), reads=[f2.g], writes=[small.g])
            kb.op("dve", lambda e: e.tensor_scalar(out=small.t[:, 32:64], in0=small.t[:, 32:64], scalar1=1.0 / 64, scalar2=None, op0=ALU.mult), reads=[small.g], writes=[small.g])
            kb.op("dve", lambda e: e.tensor_tensor(out=small.t[:, 64:80], in0=small.t[:, 32:48], in1=small.t[:, 32:48], op=ALU.mult), reads=[small.g], writes=[small.g])
            kb.op("dve", lambda e: e.tensor_tensor(out=small.t[:, 64:80], in0=small.t[:, 48:64], in1=small.t[:, 64:80], op=ALU.subtract), reads=[small.g], writes=[small.g])
            kb.op("dve", lambda e: e.tensor_scalar(out=small.t[:, 64:80], in0=small.t[:, 64:80], scalar1=64e-5, scalar2=None, op0=ALU.add), reads=[small.g], writes=[small.g])
            kb.op("act", lambda e: e.activation(out=small.t[:, 64:80], in_=small.t[:, 64:80], func=AF.Sqrt), reads=[small.g], writes=[small.g])
            kb.op("dve", lambda e: e.reciprocal(out=small.t[:, 64:80], in_=small.t[:, 64:80]), reads=[small.g], writes=[small.g])
            kb.op("dve", lambda e: e.tensor_tensor(out=H3(f1.t[:, :]), in0=H3(f1.t[:, :]), in1=bc16(small, 32), op=ALU.subtract), reads=[f1.g, small.g], writes=[f1.g])
            kb.op("dve", lambda e: e.tensor_tensor(out=H3(f1.t[:, :]), in0=H3(f1.t[:, :]), in1=bc16(small, 64), op=ALU.mult), reads=[f1.g, small.g], writes=[f1.g])
            kb.op("dve", lambda e: e.tensor_tensor(out=f1.t[:, :], in0=f1.t[:, :], in1=vec["lnx_g"].t[:, :], op=ALU.mult), reads=[f1.g, vec["lnx_g"].g], writes=[f1.g])
            kb.op("dve", lambda e: e.tensor_tensor(out=f1.t[:, :], in0=f1.t[:, :], in1=vec["lnx_b"].t[:, :], op=ALU.add), reads=[f1.g, vec["lnx_b"].g], writes=[f1.g])
            kb.op("dve", lambda e: e.tensor_tensor(out=f1.t[:, :], in0=f1.t[:, :], in1=vF.t[:, :], op=ALU.add), reads=[f1.g, vF.g], writes=[f1.g])
            kb.op("dve", lambda e: e.tensor_tensor(out=ygB.t[:, :], in0=f1.t[:, :], in1=gF.t[:, :], op=ALU.mult), reads=[f1.g, gF.g], writes=[ygB.g])
            if RB_STOP == 8:
                kb.barrier()
                return
            for kc in range(8):
                kb.op("pe", lambda e: e.transpose(out=QT.t[:, kc * 128:(kc + 1) * 128], in_=ygB.t[:, kc * 128:(kc + 1) * 128], identity=ident.t[:, :]),
                      reads=[ygB.g, ident.g], writes=[QT.g])
            ygT = aT
            kb.op("act", lambda e: e.activation(out=ygT.t[:, :], in_=QT.t[:, :], func=AF.Copy), reads=[QT.g], writes=[ygT.g])
            kb.dma("sp", out=xin.t[:, :], in_=xres_ap[t0:t0 + 128, :], reads=[xres_reg], writes=[xin.g])
            for hf in range(2):
                for fc in range(8):
                    kb.op("pe", lambda e: e.matmul(out=P[0].t[:, hf * 512:(hf + 1) * 512], lhsT=ygT.t[:, fc * 128:(fc + 1) * 128], rhs=wo.t[:, fc, hf * 512:(hf + 1) * 512],
                                                   start=(fc == 0), stop=(fc == 7)), reads=[ygT.g, wo.g], writes=[P[0].g])
            kb.op("dve", lambda e: e.scalar_tensor_tensor(out=f2.t[:, :], in0=xin.t[:, :], scalar=DN_ALPHA, in1=P[0].t[:, :], op0=ALU.mult, op1=ALU.add),
                  reads=[xin.g, P[0].g], writes=[f2.g])
            ln_block(kb, f2, vec["lng"], vec["lnb"], xo, stats, mv, rstd, LN_EPS)
            kb.dma("sp", out=xout_ap[t0:t0 + 128, :], in_=xo.t[:, :], reads=[xo.g], writes=[xout_reg], pool="st")
            kb.op("act", lambda e: e.activation(out=xbf.t[:, :], in_=xo.t[:, :], func=AF.Copy), reads=[xo.g], writes=[xbf.g])
            for kc in range(8):
                kb.op("pe", lambda e: e.transpose(out=QT.t[:, kc * 128:(kc + 1) * 128], in_=xbf.t[:, kc * 128:(kc + 1) * 128], identity=ident.t[:, :]),
                      reads=[xbf.g, ident.g], writes=[QT.g])
            x3t = x3ts[b % 2]
            kb.op("act", lambda e: e.activation(out=x3t.t[:, :], in_=QT.t[:, :], func=AF.Copy), reads=[QT.g], writes=[x3t.g])
            kb.dma("sp", out=XT3[:, :, t0:t0 + 128], in_=x3t.t[:, :].rearrange("p (k t) -> p k t", k=8), reads=[x3t.g], writes=[xt3_reg], pool="st")
            if RB_STOP >= 10 and b == RB_STOP - 10:
                kb.barrier()
                return
        kb.barrier()


CONST_SPECS = {
    "c_ident": ([128, 128], BF16),
    "c_ones": ([128, 128], BF16),
    "c_tri": ([128, 128], BF16),
    "c_ones2": ([2, S], BF16),
    "c_alibiq": ([4, 2, S], BF16),
    "c_abias": ([4, 128, 32], F32),
    "c_retDT": ([4, 128, 512], F32),
    "c_retqdec": ([4, 64, 512], F32),
    "c_retkdec": ([4, 128, 1], F32),
    "c_su4": ([128, 512], BF16),
    "c_sl4": ([128, 512], BF16),
    "c_iu4": ([128, 512], BF16),
    "c_id4": ([128, 512], BF16),
}


def make_consts():
    bf = ml_dtypes.bfloat16
    c = {}
    c["c_ident"] = np.eye(128, dtype=np.float32).astype(bf)
    c["c_ones"] = np.ones((128, 128), np.float32).astype(bf)
    p = np.arange(128)
    c["c_tri"] = (p[None, :] >= p[:, None]).astype(np.float32).astype(bf)
    c["c_ones2"] = np.ones((2, S), np.float32).astype(bf)
    t = np.arange(S) % 512
    hi = (t // 16) * 16
    lo = t % 16
    aq = np.zeros((4, 2, S), np.float64)
    ab = np.zeros((4, 128, 32), np.float64)
    for h in range(4):
        aq[h, 0] = -8.0 * SLOPES[h] * hi
        aq[h, 1] = -8.0 * SLOPES[h] * lo
        for oi in range(32):
            ab[h, :, oi] = SLOPES[h] * (p + 128.0 * (oi - 28))
    c["c_alibiq"] = aq.astype(np.float32).astype(bf)
    c["c_abias"] = ab.astype(np.float32)
    DTm = np.zeros((4, 128, 512), np.float64)
    qd = np.zeros((4, 64, 512), np.float64)
    kd = np.zeros((4, 128, 1), np.float64)
    i = np.arange(128)
    for h in range(4):
        g = GAMMAS[h]
        rel = i[None, :] - i[:, None]
        m = np.where(rel >= 0, 0.125 * g ** np.maximum(rel, 0), 0.0)
        DTm[h] = np.tile(m, (1, 4))
        qd[h] = np.tile(g ** (i + 1.0), (64, 4))
        kd[h, :, 0] = 0.125 * g ** (127.0 - i)
    c["c_retDT"] = DTm.astype(np.float32)
    c["c_retqdec"] = qd.astype(np.float32)
    c["c_retkdec"] = kd.astype(np.float32)
    c["c_su4"] = np.tile((p[None, :] > p[:, None]).astype(np.float32), (1, 4)).astype(bf)
    c["c_sl4"] = np.tile((p[None, :] < p[:, None]).astype(np.float32), (1, 4)).astype(bf)
    c["c_iu4"] = np.tile((p[None, :] >= p[:, None]).astype(np.float32), (1, 4)).astype(bf)
    c["c_id4"] = np.tile(np.eye(128, dtype=np.float32), (1, 4)).astype(bf)
    return c


INPUT_SHAPES = {
    "ev_w_in": [1, 1024, 3072], "ev_lambda": [1, 4, 64], "ev_subln_g": [1, 128], "ev_w_out": [1, 1024, 1024],
    "od_mu": [1, 6, 1024], "od_w_rkv": [1, 3, 1024, 1024], "od_w0": [1, 1024], "od_w1": [1, 1024, 64], "od_w2": [1, 64, 1024],
    "od_a0": [1, 1024], "od_a1": [1, 1024, 64], "od_a2": [1, 64, 1024], "od_g1": [1, 1024, 160], "od_g2": [1, 160, 1024],
    "od_k_k": [1, 1024], "od_k_a": [1, 1024], "od_r_k": [1, 1024], "od_lnx_g": [1, 1024], "od_lnx_b": [1, 1024],
    "od_w_out": [1, 1024, 1024], "ln_mix_g": [2, 1024], "ln_mix_b": [2, 1024], "ffn_w_up": [2, 1024, 5632],
    "ffn_conv_w": [2, 3, 2816], "ffn_conv_b": [2, 2816], "ffn_w_down": [2, 2816, 1024], "ln_ffn_g": [2, 1024], "ln_ffn_b": [2, 1024],
}


def build(stop_after=None, debug=False):
    nc = bass.Bass("TRN2", target_bir_lowering=False)
    io = {}
    io["x"] = nc.dram_tensor("x", [S, D], F32, kind="ExternalInput").ap()
    for k, shp in INPUT_SHAPES.items():
        io[k] = nc.dram_tensor(k, shp, F32, kind="ExternalInput").ap()
    for k, (shp, dt) in CONST_SPECS.items():
        io[k] = nc.dram_tensor(k, shp, dt, kind="ExternalInput").ap()
    y = nc.dram_tensor("y", [S, D], F32, kind="ExternalOutput").ap()
    XA = nc.dram_tensor("scr_xa", [S, D], F32, kind="Internal").ap()
    XB = nc.dram_tensor("scr_xb", [S, D], F32, kind="Internal").ap()
    G = nc.dram_tensor("scr_g", [FF, S], BF16, kind="Internal").ap()
    scr = [nc.dram_tensor("scr_r%d" % i, [S, D], F32, kind="Internal").ap() for i in range(6)]
    scr_reg = Reg(True)
    dbg = None
    if debug:
        dbg = nc.dram_tensor("dbg_ot", [128, 8, S], BF16, kind="ExternalOutput").ap()
    for k in ("ev_w_in", "ev_w_out", "od_w_out", "od_w1", "od_w2", "od_a1", "od_a2", "od_g1", "od_g2"):
        io[k] = io[k][0]
    xa_reg, xb_reg, g_reg, y_reg = Reg(True), Reg(True), Reg(True), Reg(True)
    with ExitStack() as es:
        kb = KB(nc, es)
        kb.dma_pool("sp_ld", 12)
        kb.dma_pool("sp_st", 8)
        kb.dma_pool("pool_ld", 6)
        ident = sbt(nc, es, "ident", [128, 128], BF16)
        ones = sbt(nc, es, "ones", [128, 128], BF16)
        kb.dma("sp", out=ident.t[:, :], in_=io["c_ident"][:, :], writes=[ident.g])
        kb.dma("sp", out=ones.t[:, :], in_=io["c_ones"][:, :], writes=[ones.g])
        tri = sbt(nc, es, "tri", [128, 128], BF16)
        kb.dma("sp", out=tri.t[:, :], in_=io["c_tri"][:, :], writes=[tri.g])
        XT3 = nc.dram_tensor("scr_xt3", [128, 8, S], BF16, kind="Internal").ap()
        xt3_reg = Reg(True)
        with ExitStack() as esx:
            xT = sbt(nc, esx, "xT", [128, 8, S], BF16)
            phase_prologue(kb, nc, io, xT, ident)
            if stop_after == "prologue":
                dump_and_stop(kb, dbg[:, :, :], xT.t[:, :, :], xT.g)
                return nc
            if stop_after in ("rwkvonly", "rwkvonly_a"):
                phase_rwkv_a(kb, nc, io, xT, scr, scr_reg)
                if stop_after == "rwkvonly_a":
                    return nc
            else:
                with ExitStack() as es2:
                    OT = sbt(nc, es2, "OT", [128, 8, S], BF16)
                    if phase_l0_mixer(kb, nc, io, xT, OT, ident, ones, tri, dbg):
                        return nc
                    if stop_after == "mixer":
                        kb.barrier()
                        return nc
                    phase_out_ln(kb, nc, io, OT, xT, ident, io["ev_w_out"], io["x"], io["ln_mix_g"][0], io["ln_mix_b"][0], XA, xa_reg)
                if stop_after == "outln":
                    return nc
                phase_ffn(kb, nc, io, 0, xT, ident, XA, xa_reg, y if stop_after == "ffn0" else XB, y_reg if stop_after == "ffn0" else xb_reg,
                          G, g_reg, want_T=True)
                if stop_after == "ffn0":
                    return nc
                phase_rwkv_a(kb, nc, io, xT, scr, scr_reg)
        if stop_after == "rwkvonly":
            phase_rwkv_b(kb, nc, io, XT3, xt3_reg, ident, ones, tri, scr, scr_reg, io["x"], Reg(True), y, y_reg)
            return nc
        last = stop_after == "rwkv"
        phase_rwkv_b(kb, nc, io, XT3, xt3_reg, ident, ones, tri, scr, scr_reg, XB, xb_reg, y if last else XA, y_reg if last else xa_reg)
        if last:
            return nc
        with ExitStack() as esx:
            xT = sbt(nc, esx, "xT", [128, 8, S], BF16)
            for kc in range(8):
                kb.dma("sp", out=xT.t[:, kc, :], in_=XT3[:, kc, :], reads=[xt3_reg], writes=[xT.g])
            phase_ffn(kb, nc, io, 1, xT, ident, XA, xa_reg, y, y_reg, G, g_reg, want_T=False)
    return nc


_NC_CACHE = {}


def kernel(**inputs):
    if "nc" not in _NC_CACHE:
        _NC_CACHE["nc"] = build()
        _NC_CACHE["consts"] = make_consts()
    nc = _NC_CACHE["nc"]
    consts = _NC_CACHE["consts"]
    x = np.ascontiguousarray(np.asarray(inputs["x"], dtype=np.float32))
    shared = {k: np.ascontiguousarray(np.asarray(inputs[k], dtype=np.float32)) for k in INPUT_SHAPES}
    in_maps = []
    for c in range(8):
        m = {"x": x[c]}
        m.update(shared)
        m.update(consts)
        in_maps.append(m)
    res = run_bass_kernel_spmd(nc, in_maps, core_ids=list(range(8)))
    return np.stack([np.asarray(res.results[c]["y"], dtype=np.float32) for c in range(8)], axis=0)
```

```python
import math
from contextlib import ExitStack

import numpy as np
import ml_dtypes

import concourse.bass as bass
import concourse.mybir as mybir
from concourse.bass_utils import run_bass_kernel_spmd

F32 = mybir.dt.float32
BF16 = mybir.dt.bfloat16
AF = mybir.ActivationFunctionType
ALU = mybir.AluOpType
AX = mybir.AxisListType

S = 4096
D = 1024
NB = S // 128
FF = 2816
NFC = FF // 128
DN_ALPHA = (2.0 * 2) ** 0.25
LN_EPS = 1e-5
LAMBDA_INIT0 = 0.8 - 0.6 * math.exp(-0.3 * 0)
SLOPES = [2.0 ** (-8.0 * (i + 1) / 4) for i in range(4)]
GAMMAS = [1.0 - 2.0 ** (-5.0 - h) for h in range(4)]


MIX_STOP = None
LOOPV = 9
RB_STOP = 0


class StopBuild(Exception):
    pass


def dump_and_stop(kb, dbg, tile_ap, reg):
    kb.barrier()
    kb.dma("sp", out=dbg, in_=tile_ap, reads=[reg], pool="st")
    kb.barrier()
    return True


class Reg:
    __slots__ = ("w", "r", "nowaw", "psum")

    def __init__(self, nowaw=False):
        self.w = {}
        self.r = {}
        self.nowaw = nowaw
        self.psum = False


class T:
    def __init__(self, t):
        self.t = t
        self.g = Reg()


class KB:
    def __init__(self, nc, es):
        self.nc = nc
        self.es = es
        self.E = {"pe": nc.tensor, "dve": nc.vector, "act": nc.scalar, "pool": nc.gpsimd, "sp": nc.sync}
        self.sems = {}
        self.cnt = {}
        for e in self.E:
            self.sems[e] = es.enter_context(nc.semaphore("s_" + e))
            self.cnt[e] = 0
        self.seen = {e: {} for e in self.E}
        self.dpool = {}
        self.dnext = {}

    def dma_pool(self, name, n):
        keys = []
        for i in range(n):
            k = "%s%d" % (name, i)
            self.sems[k] = self.es.enter_context(self.nc.semaphore("d_" + k))
            self.cnt[k] = 0
            keys.append(k)
        self.dpool[name] = keys
        self.dnext[name] = 0

    def _deps(self, e, reads, writes):
        need = {}

        def add(d, same_ok):
            for k, v in d.items():
                if k == e and same_ok:
                    continue
                if need.get(k, 0) < v:
                    need[k] = v

        for r in reads:
            add(r.w, e == "pe")
            if r.psum:
                add(r.r, True)
        for w in writes:
            if not w.nowaw:
                add(w.w, e == "pe")
            add(w.r, e == "pe")
        return need

    def _wait(self, e, need):
        sn = self.seen[e]
        for k, v in need.items():
            if sn.get(k, 0) < v:
                self.E[e].wait_ge(self.sems[k], v)
                sn[k] = v

    def op(self, e, fn, reads=(), writes=(), rg=(0, 128)):
        need = self._deps(e, reads, writes)
        if e == "pe":
            last = getattr(self, "_last_rg", (0, 128))
            if (rg[0] + rg[1] <= last[0] or last[0] + last[1] <= rg[0]) and self.cnt["pe"] > 0:
                need["pe"] = self.cnt["pe"]
            self._last_rg = rg
        self._wait(e, need)
        ins = fn(self.E[e])
        self.cnt[e] += 1
        ins.then_inc(self.sems[e], 1)
        tok = self.cnt[e]
        for r in reads:
            r.r[e] = tok
        for w in writes:
            w.w[e] = tok
            if not w.nowaw:
                w.r = {}
        return ins

    def dma(self, q, out, in_, reads=(), writes=(), pool="ld", **kw):
        pool = q + "_" + pool
        keys = self.dpool[pool]
        k = keys[self.dnext[pool] % len(keys)]
        self.dnext[pool] += 1
        need = self._deps(q, reads, writes)
        if self.cnt[k] > 0:
            need[k] = max(need.get(k, 0), self.cnt[k])
        self._wait(q, need)
        ins = self.E[q].dma_start(out=out, in_=in_, **kw)
        self.cnt[k] += 16
        ins.then_inc(self.sems[k], 16)
        tok = self.cnt[k]
        for r in reads:
            r.r[k] = tok
        for w in writes:
            w.w[k] = tok
            if not w.nowaw:
                w.r = {}
        return ins

    def barrier(self):
        for e in self.E:
            need = {k: v for k, v in self.cnt.items() if k != e and v > 0}
            self._wait(e, need)


_UNIQ = [0]


def _uniq(name):
    _UNIQ[0] += 1
    return "%s_%d" % (name, _UNIQ[0])


def sbt(nc, es, name, shape, dt):
    return T(es.enter_context(nc.sbuf_tensor(_uniq(name), list(shape), dt)))


def pst(nc, es, name, shape, dt):
    t = T(es.enter_context(nc.psum_tensor(_uniq(name), list(shape), dt)))
    t.g.psum = True
    return t


def ln_block(kb, z, gam, bet, outp, stats, mv, rstd, eps):
    kb.op("dve", lambda e: e.bn_stats(out=stats.t[:, 0:6], in_=z.t[:, 0:512]), reads=[z.g], writes=[stats.g])
    kb.op("dve", lambda e: e.bn_stats(out=stats.t[:, 6:12], in_=z.t[:, 512:1024]), reads=[z.g], writes=[stats.g])
    kb.op("dve", lambda e: e.bn_aggr(out=mv.t[:, 0:2], in_=stats.t[:, 0:12]), reads=[stats.g], writes=[mv.g])
    kb.op("dve", lambda e: e.tensor_scalar(out=rstd.t[:, 0:1], in0=mv.t[:, 1:2], scalar1=eps, scalar2=None, op0=ALU.add),
          reads=[mv.g], writes=[rstd.g])
    kb.op("act", lambda e: e.activation(out=rstd.t[:, 0:1], in_=rstd.t[:, 0:1], func=AF.Sqrt), reads=[rstd.g], writes=[rstd.g])
    kb.op("dve", lambda e: e.reciprocal(out=rstd.t[:, 0:1], in_=rstd.t[:, 0:1]), reads=[rstd.g], writes=[rstd.g])
    kb.op("dve", lambda e: e.scalar_tensor_tensor(out=z.t[:, :], in0=z.t[:, :], scalar=mv.t[:, 0:1], in1=gam.t[:, :], op0=ALU.subtract, op1=ALU.mult),
          reads=[z.g, mv.g, gam.g], writes=[z.g])
    kb.op("dve", lambda e: e.scalar_tensor_tensor(out=outp.t[:, :], in0=z.t[:, :], scalar=rstd.t[:, 0:1], in1=bet.t[:, :], op0=ALU.mult, op1=ALU.add),
          reads=[z.g, rstd.g, bet.g], writes=[outp.g])


def transpose_to_fm(kb, src32, srcbf, dstT, b, ident, ptr):
    kb.op("act", lambda e: e.activation(out=srcbf.t[:, :], in_=src32.t[:, :], func=AF.Copy), reads=[src32.g], writes=[srcbf.g])
    for kc in range(8):
        kb.op("pe", lambda e: e.transpose(out=ptr.t[:, kc * 128:(kc + 1) * 128], in_=srcbf.t[:, kc * 128:(kc + 1) * 128],
                                          identity=ident.t[:, :]), reads=[srcbf.g, ident.g], writes=[ptr.g])
    kb.op("dve", lambda e: e.tensor_copy(out=dstT.t[:, :, b * 128:(b + 1) * 128],
                                         in_=ptr.t[:, :].rearrange("p (k t) -> p k t", k=8)), reads=[ptr.g], writes=[dstT.g])


def load_bcast(kb, dst, vec_ap):
    kb.dma("sp", out=dst.t[:, :], in_=vec_ap.partition_broadcast(128), writes=[dst.g])


def phase_prologue(kb, nc, io, xT, ident):
    with ExitStack() as es:
        xin = [sbt(nc, es, "pr_x%d" % i, [128, 1024], F32) for i in range(2)]
        xbf = [sbt(nc, es, "pr_xb%d" % i, [128, 1024], BF16) for i in range(2)]
        ptr = [pst(nc, es, "pr_pt%d" % i, [128, 1024], BF16) for i in range(2)]
        for b in range(NB):
            xi = xin[b % 2]
            kb.dma("sp", out=xi.t[:, :], in_=io["x"][b * 128:(b + 1) * 128, :], writes=[xi.g])
            transpose_to_fm(kb, xi, xbf[b % 2], xT, b, ident, ptr[b % 2])
        kb.barrier()


def phase_l0_mixer(kb, nc, io, xT, OT, ident, ones, tri, dbg=None):
    W_in = io["ev_w_in"].rearrange("(kc p) n -> p kc n", p=128)
    with ExitStack() as es:
        wq = sbt(nc, es, "m_wq", [128, 8, 384], BF16)
        qT = [sbt(nc, es, "m_qT%d" % m, [128, S], BF16) for m in range(2)]
        kT = [sbt(nc, es, "m_kT%d" % m, [128, S], BF16) for m in range(2)]
        vtok = sbt(nc, es, "m_vtok", [128, NB, 128], BF16)
        abias = sbt(nc, es, "m_abias", [128, 32], F32)
        lamt = sbt(nc, es, "m_lam", [128, 256], F32)
        lprod = sbt(nc, es, "m_lprod", [128, 128], F32)
        lsum = sbt(nc, es, "m_lsum", [128, 2], F32)
        lexp = sbt(nc, es, "m_lexp", [128, 2], F32)
        neglam = sbt(nc, es, "m_neglam", [128, 1], F32)
        gsc = sbt(nc, es, "m_gsc", [128, 1], F32)
        pT = [sbt(nc, es, "m_pT%d" % i, [128, 512], BF16) for i in range(4)]
        r1 = sbt(nc, es, "m_r1", [128, 512], F32)
        r2 = sbt(nc, es, "m_r2", [128, 512], F32)
        t1 = sbt(nc, es, "m_t1", [128, 512], F32)
        t2 = sbt(nc, es, "m_t2", [128, 512], F32)
        sqb = sbt(nc, es, "m_sqb", [128, 512], BF16)
        ybf = sbt(nc, es, "m_ybf", [128, 512], BF16)
        ktd = sbt(nc, es, "m_ktd", [128, NB, 64], BF16)
        DT = sbt(nc, es, "m_DT", [128, 512], F32)
        qdec = sbt(nc, es, "m_qdec", [64, 512], F32)
        kdec = sbt(nc, es, "m_kdec", [128, 1], F32)
        Rst = sbt(nc, es, "m_Rst", [64, 2, 128], F32)
        Rbf = sbt(nc, es, "m_Rbf", [64, NB, 128], BF16)
        bank = [pst(nc, es, "m_b%d" % i, [128, 512], F32) for i in range(8)]

        kb.dma("sp", out=lamt.t[:, :], in_=io["ev_lambda"].rearrange("a b c -> (a b c)").partition_broadcast(128), writes=[lamt.g])
        kb.op("dve", lambda e: e.tensor_tensor(out=lprod.t[:, 0:64], in0=lamt.t[:, 0:64], in1=lamt.t[:, 64:128], op=ALU.mult),
              reads=[lamt.g], writes=[lprod.g])
        kb.op("dve", lambda e: e.tensor_tensor(out=lprod.t[:, 64:128], in0=lamt.t[:, 128:192], in1=lamt.t[:, 192:256], op=ALU.mult),
              reads=[lamt.g], writes=[lprod.g])
        kb.op("dve", lambda e: e.reduce_sum(out=lsum.t[:, 0:1], in_=lprod.t[:, 0:64], axis=AX.X), reads=[lprod.g], writes=[lsum.g])
        kb.op("dve", lambda e: e.reduce_sum(out=lsum.t[:, 1:2], in_=lprod.t[:, 64:128], axis=AX.X), reads=[lprod.g], writes=[lsum.g])
        kb.op("act", lambda e: e.activation(out=lexp.t[:, 0:2], in_=lsum.t[:, 0:2], func=AF.Exp), reads=[lsum.g], writes=[lexp.g])
        kb.op("dve", lambda e: e.tensor_tensor(out=neglam.t[:, 0:1], in0=lexp.t[:, 1:2], in1=lexp.t[:, 0:1], op=ALU.subtract),
              reads=[lexp.g], writes=[neglam.g])
        kb.op("dve", lambda e: e.tensor_scalar(out=neglam.t[:, 0:1], in0=neglam.t[:, 0:1], scalar1=-LAMBDA_INIT0, scalar2=None, op0=ALU.add),
              reads=[neglam.g], writes=[neglam.g])
        kb.dma("sp", out=gsc.t[:, :], in_=io["ev_subln_g"].rearrange("o v -> v o"), writes=[gsc.g], allow_slow_non_contiguous=True)
        kb.op("dve", lambda e: e.tensor_scalar(out=gsc.t[:, 0:1], in0=gsc.t[:, 0:1], scalar1=1.0 - LAMBDA_INIT0, scalar2=None, op0=ALU.mult),
              reads=[gsc.g], writes=[gsc.g])
        for m in range(2):
            kb.dma("sp", out=kT[m].t[64:66, :], in_=io["c_ones2"][:, :], writes=[kT[m].g])

        def proj_fm(dst, prow, co, ncol, evac_eng_i):
            for tt in range(8):
                bk = bank[tt % 2]
                for kc in range(8):
                    kb.op("pe", lambda e: e.matmul(out=bk.t[0:ncol, :], lhsT=wq.t[:, kc, co:co + ncol],
                                                   rhs=xT.t[:, kc, tt * 512:(tt + 1) * 512], start=(kc == 0), stop=(kc == 7)),
                          reads=[wq.g, xT.g], writes=[bk.g])
                if (tt + evac_eng_i) % 2 == 0:
                    kb.op("act", lambda e: e.activation(out=dst.t[prow:prow + ncol, tt * 512:(tt + 1) * 512], in_=bk.t[0:ncol, :], func=AF.Copy),
                          reads=[bk.g], writes=[dst.g])
                else:
                    kb.op("dve", lambda e: e.tensor_copy(out=dst.t[prow:prow + ncol, tt * 512:(tt + 1) * 512], in_=bk.t[0:ncol, :]),
                          reads=[bk.g], writes=[dst.g])

        for h in range(4):
            for i, co in enumerate((h * 128, 512 + h * 128, 1024 + h * 128)):
                kb.dma("pool", out=wq.t[:, :, i * 128:(i + 1) * 128], in_=W_in[:, :, co:co + 128], reads=[], writes=[wq.g])
            kb.dma("sp", out=abias.t[:, :], in_=io["c_abias"][h, :, :], writes=[abias.g])
            for m in range(2):
                kb.dma("sp", out=qT[m].t[64:66, :], in_=io["c_alibiq"][h, :, :], writes=[qT[m].g])
            for m in range(2):
                proj_fm(qT[m], 0, m * 64, 64, 0)
                proj_fm(kT[m], 0, 128 + m * 64, 64, 1)
            for g4 in range(8):
                bk = bank[2 + g4 % 2]
                for bb in range(4):
                    b = g4 * 4 + bb
                    for kc in range(8):
                        kb.op("pe", lambda e: e.matmul(out=bk.t[:, bb * 128:(bb + 1) * 128], lhsT=xT.t[:, kc, b * 128:(b + 1) * 128],
                                                       rhs=wq.t[:, kc, 256:384], start=(kc == 0), stop=(kc == 7)),
                              reads=[wq.g, xT.g], writes=[bk.g])
                kb.op("dve", lambda e: e.tensor_copy(out=vtok.t[:, g4 * 4:(g4 + 1) * 4, :],
                                                     in_=bk.t[:, :].rearrange("p (b v) -> p b v", b=4)), reads=[bk.g], writes=[vtok.g])
            if MIX_STOP == "h0proj":
                kb.barrier()
                kb.dma("sp", out=dbg[0:64, 0, :], in_=qT[0].t[0:64, :], reads=[qT[0].g], pool="st")
                kb.dma("sp", out=dbg[0:64, 1, :], in_=qT[1].t[0:64, :], reads=[qT[1].g], pool="st")
                kb.dma("sp", out=dbg[0:64, 2, :], in_=kT[0].t[0:64, :], reads=[kT[0].g], pool="st")
                kb.dma("sp", out=dbg[0:64, 3, :], in_=kT[1].t[0:64, :], reads=[kT[1].g], pool="st")
                kb.dma("sp", out=dbg[:, 4, :].rearrange("p (b v) -> p b v", b=NB), in_=vtok.t[:, :, :], reads=[vtok.g], pool="st")
                kb.barrier()
                return True
            O = [bank[4], bank[6]]
            Sm = [bank[5], bank[7]]
            for c in range(8):
                steps = [(kbi, m) for kbi in range(4 * c + 4) for m in range(2)]

                def geom(i):
                    kbi, m = steps[i]
                    j = kbi - 4 * c
                    lo = 128 * j if j > 0 else 0
                    return kbi, m, j, lo

                def emit_qk(i):
                    kbi, m, j, lo = geom(i)
                    sb = bank[i % 4]
                    KK = 64 if LOOPV == -1 else 66
                    kb.op("pe", lambda e: e.matmul(out=sb.t[:, lo:512], lhsT=kT[m].t[0:KK, kbi * 128:(kbi + 1) * 128],
                                                   rhs=qT[m].t[0:KK, c * 512 + lo:(c + 1) * 512], start=True, stop=True),
                          reads=[kT[m].g, qT[m].g], writes=[sb.g])

                def emit_pv(i):
                    kbi, m, j, lo = geom(i)
                    sb = bank[i % 4]
                    pt = pT[i % 4]
                    oi = (kbi - 4 * c) + 28
                    if LOOPV == -4:
                        return
                    kb.op("act", lambda e: e.activation(out=pt.t[:, lo:512], in_=sb.t[:, lo:512], func=(AF.Copy if LOOPV == -3 else AF.Exp),
                                                        bias=(0.0 if LOOPV in (-2, -3) else abias.t[:, oi:oi + 1]), scale=0.125),
                          reads=[sb.g, abias.g], writes=[pt.g])
                    if LOOPV < 1:
                        return
                    if j >= 0:
                        kb.op("dve", lambda e: e.tensor_tensor(out=pt.t[:, lo:lo + 128], in0=pt.t[:, lo:lo + 128], in1=tri.t[:, :], op=ALU.mult),
                              reads=[pt.g, tri.g], writes=[pt.g])
                    if LOOPV < 2:
                        return
                    first = kbi == 0
                    last = kbi == 4 * c + 3
                    kb.op("pe", lambda e: e.matmul(out=O[m].t[:, lo:512], lhsT=vtok.t[:, kbi, :], rhs=pt.t[:, lo:512], start=first, stop=last),
                          reads=[vtok.g, pt.g], writes=[O[m].g])
                    kb.op("pe", lambda e: e.matmul(out=Sm[m].t[:, lo:512], lhsT=ones.t[:, :], rhs=pt.t[:, lo:512], start=first, stop=last),
                          reads=[ones.g, pt.g], writes=[Sm[m].g])

                n = len(steps)
                emit_qk(0)
                emit_qk(1)
                emit_qk(2)
                for i in range(n):
                    if i + 3 < n:
                        emit_qk(i + 3)
                    emit_pv(i)
                if MIX_STOP == "c0loop":
                    kb.barrier()
                    kb.dma("sp", out=dbg[:, 4, :].rearrange("p (b v) -> p b v", b=NB), in_=vtok.t[:, :, :], reads=[vtok.g], pool="st")
                    kb.barrier()
                    return True
                kb.op("dve", lambda e: e.reciprocal(out=r1.t[:, :], in_=Sm[0].t[:, :]), reads=[Sm[0].g], writes=[r1.g])
                kb.op("dve", lambda e: e.reciprocal(out=r2.t[:, :], in_=Sm[1].t[:, :]), reads=[Sm[1].g], writes=[r2.g])
                kb.op("dve", lambda e: e.tensor_tensor(out=t1.t[:, :], in0=O[0].t[:, :], in1=r1.t[:, :], op=ALU.mult), reads=[O[0].g, r1.g], writes=[t1.g])
                kb.op("dve", lambda e: e.tensor_tensor(out=t2.t[:, :], in0=O[1].t[:, :], in1=r2.t[:, :], op=ALU.mult), reads=[O[1].g, r2.g], writes=[t2.g])
                kb.op("dve", lambda e: e.scalar_tensor_tensor(out=t1.t[:, :], in0=t2.t[:, :], scalar=neglam.t[:, 0:1], in1=t1.t[:, :],
                                                              op0=ALU.mult, op1=ALU.add), reads=[t1.g, t2.g, neglam.g], writes=[t1.g])
                kb.op("act", lambda e: e.activation(out=sqb.t[:, :], in_=t1.t[:, :], func=AF.Square), reads=[t1.g], writes=[sqb.g])
                ssb = bank[0]
                kb.op("pe", lambda e: e.matmul(out=ssb.t[:, :], lhsT=ones.t[:, :], rhs=sqb.t[:, :], start=True, stop=True),
                      reads=[ones.g, sqb.g], writes=[ssb.g])
                kb.op("dve", lambda e: e.tensor_scalar(out=r1.t[:, :], in0=ssb.t[:, :], scalar1=1.0 / 128, scalar2=LN_EPS, op0=ALU.mult, op1=ALU.add),
                      reads=[ssb.g], writes=[r1.g])
                kb.op("act", lambda e: e.activation(out=r1.t[:, :], in_=r1.t[:, :], func=AF.Sqrt), reads=[r1.g], writes=[r1.g])
                kb.op("dve", lambda e: e.reciprocal(out=r1.t[:, :], in_=r1.t[:, :]), reads=[r1.g], writes=[r1.g])
                kb.op("dve", lambda e: e.scalar_tensor_tensor(out=OT.t[:, h, c * 512:(c + 1) * 512], in0=t1.t[:, :], scalar=gsc.t[:, 0:1], in1=r1.t[:, :],
                                                              op0=ALU.mult, op1=ALU.mult), reads=[t1.g, gsc.g, r1.g], writes=[OT.g])

        if MIX_STOP == "diff":
            return dump_and_stop(kb, dbg[:, :, :], OT.t[:, :, :], OT.g)
        for h in range(4):
            gam = GAMMAS[h]
            kb.dma("pool", out=wq.t[:, :, 0:64], in_=W_in[:, :, 1536 + h * 64:1536 + (h + 1) * 64], writes=[wq.g])
            kb.dma("pool", out=wq.t[:, :, 64:128], in_=W_in[:, :, 1792 + h * 64:1792 + (h + 1) * 64], writes=[wq.g])
            kb.dma("pool", out=wq.t[:, :, 128:256], in_=W_in[:, :, 2048 + h * 128:2048 + (h + 1) * 128], writes=[wq.g])
            kb.dma("pool", out=wq.t[:, :, 256:384], in_=W_in[:, :, 2560 + h * 128:2560 + (h + 1) * 128], writes=[wq.g])
            kb.dma("sp", out=DT.t[:, :], in_=io["c_retDT"][h, :, :], writes=[DT.g])
            kb.dma("sp", out=qdec.t[:, :], in_=io["c_retqdec"][h, :, :], writes=[qdec.g])
            kb.dma("sp", out=kdec.t[:, :], in_=io["c_retkdec"][h, :, :], writes=[kdec.g])
            if MIX_STOP == "r_load":
                kb.barrier()
                kb.dma("sp", out=dbg[:, 4, :].rearrange("p (b v) -> p b v", b=NB), in_=vtok.t[:, :, :], reads=[vtok.g], pool="st")
                kb.barrier()
                return True
            for tt in range(8):
                bk = bank[tt % 2]
                for kc in range(8):
                    kb.op("pe", lambda e: e.matmul(out=bk.t[0:64, :], lhsT=wq.t[:, kc, 0:64], rhs=xT.t[:, kc, tt * 512:(tt + 1) * 512],
                                                   start=(kc == 0), stop=(kc == 7)), reads=[wq.g, xT.g], writes=[bk.g])
                kb.op("act", lambda e: e.activation(out=qT[0].t[0:64, tt * 512:(tt + 1) * 512], in_=bk.t[0:64, :], func=AF.Copy),
                      reads=[bk.g], writes=[qT[0].g])
                if LOOPV >= 1:
                    kb.op("dve", lambda e: e.tensor_tensor(out=qT[1].t[0:64, tt * 512:(tt + 1) * 512], in0=bk.t[0:64, :], in1=qdec.t[:, :], op=ALU.mult),
                          reads=[bk.g, qdec.g], writes=[qT[1].g])
            if LOOPV >= 2:
                proj_fm(kT[0], 0, 64, 64, 0)
            for b in range(NB if LOOPV >= 3 else 0):
                bk = bank[2 + b % 2]
                for kc in range(8):
                    kb.op("pe", lambda e: e.matmul(out=bk.t[:, 0:192], lhsT=xT.t[:, kc, b * 128:(b + 1) * 128], rhs=wq.t[:, kc, 64:256],
                                                   start=(kc == 0), stop=(kc == 7)), reads=[wq.g, xT.g], writes=[bk.g])
                kb.op("dve", lambda e: e.tensor_scalar(out=ktd.t[:, b, :], in0=bk.t[:, 0:64], scalar1=kdec.t[:, 0:1], scalar2=None, op0=ALU.mult),
                      reads=[bk.g, kdec.g], writes=[ktd.g])
                kb.op("act", lambda e: e.activation(out=vtok.t[:, b, :], in_=bk.t[:, 64:192], func=AF.Copy), reads=[bk.g], writes=[vtok.g])
            if MIX_STOP == "r_proj":
                kb.barrier()
                kb.dma("sp", out=dbg[:, 4, :].rearrange("p (b v) -> p b v", b=NB), in_=vtok.t[:, :, :], reads=[vtok.g], pool="st")
                kb.barrier()
                return True
            kb.op("dve", lambda e: e.memset(Rst.t[:, 0, :], 0.0), writes=[Rst.g])
            kb.op("dve", lambda e: e.memset(Rbf.t[:, 0, :], 0.0), writes=[Rbf.g])
            for g4 in range(8):
                bk = bank[4 + g4 % 2]
                for bb in range(4):
                    b = g4 * 4 + bb
                    kb.op("pe", lambda e: e.matmul(out=bk.t[0:64, bb * 128:(bb + 1) * 128], lhsT=ktd.t[:, b, :], rhs=vtok.t[:, b, :],
                                                   start=True, stop=True), reads=[ktd.g, vtok.g], writes=[bk.g])
                for bb in range(4):
                    b = g4 * 4 + bb
                    if b == NB - 1:
                        continue
                    kb.op("dve", lambda e: e.scalar_tensor_tensor(out=Rst.t[:, (b + 1) % 2, :], in0=Rst.t[:, b % 2, :], scalar=float(gam ** 128),
                                                                  in1=bk.t[0:64, bb * 128:(bb + 1) * 128], op0=ALU.mult, op1=ALU.add),
                          reads=[Rst.g, bk.g], writes=[Rst.g])
                    kb.op("act", lambda e: e.activation(out=Rbf.t[:, b + 1, :], in_=Rst.t[:, (b + 1) % 2, :], func=AF.Copy), reads=[Rst.g], writes=[Rbf.g])
            if MIX_STOP == "r_state":
                kb.barrier()
                kb.dma("sp", out=dbg[:, 4, :].rearrange("p (b v) -> p b v", b=NB), in_=vtok.t[:, :, :], reads=[vtok.g], pool="st")
                kb.barrier()
                return True
            for g4 in range(8):
                sc = bank[g4 % 2]
                yb = bank[2 + g4 % 2]
                gb = bank[6 + g4 % 2]
                pt = pT[g4 % 2]
                for bb in range(4):
                    b = g4 * 4 + bb
                    kb.op("pe", lambda e: e.matmul(out=sc.t[:, bb * 128:(bb + 1) * 128], lhsT=kT[0].t[0:64, b * 128:(b + 1) * 128],
                                                   rhs=qT[0].t[0:64, b * 128:(b + 1) * 128], start=True, stop=True),
                          reads=[kT[0].g, qT[0].g], writes=[sc.g])
                for kc in range(8):
                    kb.op("pe", lambda e: e.matmul(out=gb.t[:, :], lhsT=wq.t[:, kc, 256:384], rhs=xT.t[:, kc, g4 * 512:(g4 + 1) * 512],
                                                   start=(kc == 0), stop=(kc == 7)), reads=[wq.g, xT.g], writes=[gb.g])
                kb.op("dve", lambda e: e.tensor_tensor(out=pt.t[:, :], in0=sc.t[:, :], in1=DT.t[:, :], op=ALU.mult), reads=[sc.g, DT.g], writes=[pt.g])
                for bb in range(4):
                    b = g4 * 4 + bb
                    kb.op("pe", lambda e: e.matmul(out=yb.t[:, bb * 128:(bb + 1) * 128], lhsT=vtok.t[:, b, :], rhs=pt.t[:, bb * 128:(bb + 1) * 128],
                                                   start=True, stop=False), reads=[vtok.g, pt.g], writes=[yb.g])
                    kb.op("pe", lambda e: e.matmul(out=yb.t[:, bb * 128:(bb + 1) * 128], lhsT=Rbf.t[:, b, :], rhs=qT[1].t[0:64, b * 128:(b + 1) * 128],
                                                   start=False, stop=True), reads=[Rbf.g, qT[1].g], writes=[yb.g])
                kb.op("act", lambda e: e.activation(out=ybf.t[:, :], in_=yb.t[:, :], func=AF.Copy), reads=[yb.g], writes=[ybf.g])
                kb.op("act", lambda e: e.activation(out=sqb.t[:, :], in_=yb.t[:, :], func=AF.Square), reads=[yb.g], writes=[sqb.g])
                m1 = bank[4]
                m2 = bank[5]
                kb.op("pe", lambda e: e.matmul(out=m1.t[:, :], lhsT=ones.t[:, :], rhs=ybf.t[:, :], start=True, stop=True), reads=[ones.g, ybf.g], writes=[m1.g])
                kb.op("pe", lambda e: e.matmul(out=m2.t[:, :], lhsT=ones.t[:, :], rhs=sqb.t[:, :], start=True, stop=True), reads=[ones.g, sqb.g], writes=[m2.g])
                kb.op("dve", lambda e: e.tensor_scalar(out=r1.t[:, :], in0=m1.t[:, :], scalar1=1.0 / 128, scalar2=None, op0=ALU.mult), reads=[m1.g], writes=[r1.g])
                kb.op("dve", lambda e: e.tensor_tensor(out=t2.t[:, :], in0=r1.t[:, :], in1=r1.t[:, :], op=ALU.mult), reads=[r1.g], writes=[t2.g])
                kb.op("dve", lambda e: e.scalar_tensor_tensor(out=r2.t[:, :], in0=m2.t[:, :], scalar=1.0 / 128, in1=t2.t[:, :], op0=ALU.mult, op1=ALU.subtract),
                      reads=[m2.g, t2.g], writes=[r2.g])
                kb.op("dve", lambda e: e.tensor_scalar(out=r2.t[:, :], in0=r2.t[:, :], scalar1=LN_EPS, scalar2=None, op0=ALU.add),
                      reads=[r2.g], writes=[r2.g])
                kb.op("act", lambda e: e.activation(out=r2.t[:, :], in_=r2.t[:, :], func=AF.Sqrt), reads=[r2.g], writes=[r2.g])
                kb.op("dve", lambda e: e.reciprocal(out=r2.t[:, :], in_=r2.t[:, :]), reads=[r2.g], writes=[r2.g])
                sg = t2
                kb.op("act", lambda e: e.activation(out=sg.t[:, :], in_=gb.t[:, :], func=AF.Silu), reads=[gb.g], writes=[sg.g])
                kb.op("dve", lambda e: e.tensor_tensor(out=t1.t[:, :], in0=yb.t[:, :], in1=r1.t[:, :], op=ALU.subtract), reads=[yb.g, r1.g], writes=[t1.g])
                kb.op("dve", lambda e: e.tensor_tensor(out=t1.t[:, :], in0=t1.t[:, :], in1=r2.t[:, :], op=ALU.mult), reads=[t1.g, r2.g], writes=[t1.g])
                kb.op("dve", lambda e: e.tensor_tensor(out=OT.t[:, 4 + h, g4 * 512:(g4 + 1) * 512], in0=t1.t[:, :], in1=sg.t[:, :], op=ALU.mult),
                      reads=[t1.g, sg.g], writes=[OT.g])
        kb.barrier()
        if dbg is not None:
            kb.dma("sp", out=dbg[:, :, :], in_=OT.t[:, :, :], reads=[OT.g], pool="st")
            kb.barrier()


def phase_out_ln(kb, nc, io, OT, xT, ident, w_ap, xres_ap, gam_ap, bet_ap, xout_ap, xout_reg):
    with ExitStack() as es:
        wo = sbt(nc, es, "o_w", [128, 8, 1024], BF16)
        gam = sbt(nc, es, "o_gam", [128, 1024], F32)
        bet = sbt(nc, es, "o_bet", [128, 1024], F32)
        xin = [sbt(nc, es, "o_x%d" % i, [128, 1024], F32) for i in range(2)]
        z = [sbt(nc, es, "o_z%d" % i, [128, 1024], F32) for i in range(2)]
        xo = [sbt(nc, es, "o_xo%d" % i, [128, 1024], F32) for i in range(2)]
        xbf = [sbt(nc, es, "o_xb%d" % i, [128, 1024], BF16) for i in range(2)]
        stats = sbt(nc, es, "o_stats", [128, 12], F32)
        mv = sbt(nc, es, "o_mv", [128, 2], F32)
        rstd = sbt(nc, es, "o_rstd", [128, 1], F32)
        mm = [[pst(nc, es, "o_mm%d%d" % (i, j), [128, 512], F32) for j in range(2)] for i in range(2)]
        ptr = [pst(nc, es, "o_pt%d" % i, [128, 1024], BF16) for i in range(2)]
        kb.dma("pool", out=wo.t[:, :, :], in_=w_ap.rearrange("(kc p) n -> p kc n", p=128), writes=[wo.g])
        load_bcast(kb, gam, gam_ap)
        load_bcast(kb, bet, bet_ap)
        for b in range(NB):
            xi = xin[b % 2]
            kb.dma("sp", out=xi.t[:, :], in_=xres_ap[b * 128:(b + 1) * 128, :], writes=[xi.g])
            for hf in range(2):
                for fc in range(8):
                    kb.op("pe", lambda e: e.matmul(out=mm[b % 2][hf].t[:, :], lhsT=OT.t[:, fc, b * 128:(b + 1) * 128],
                                                   rhs=wo.t[:, fc, hf * 512:(hf + 1) * 512], start=(fc == 0), stop=(fc == 7)),
                          reads=[OT.g, wo.g], writes=[mm[b % 2][hf].g])
            zz = z[b % 2]
            for hf in range(2):
                kb.op("dve", lambda e: e.scalar_tensor_tensor(out=zz.t[:, hf * 512:(hf + 1) * 512], in0=xi.t[:, hf * 512:(hf + 1) * 512], scalar=DN_ALPHA,
                                                              in1=mm[b % 2][hf].t[:, :], op0=ALU.mult, op1=ALU.add),
                      reads=[xi.g, mm[b % 2][hf].g], writes=[zz.g])
            ln_block(kb, zz, gam, bet, xo[b % 2], stats, mv, rstd, LN_EPS)
            kb.dma("sp", out=xout_ap[b * 128:(b + 1) * 128, :], in_=xo[b % 2].t[:, :], reads=[xo[b % 2].g], writes=[xout_reg], pool="st")
            transpose_to_fm(kb, xo[b % 2], xbf[b % 2], xT, b, ident, ptr[b % 2])
        kb.barrier()


def phase_ffn(kb, nc, io, layer, xT, ident, xres_ap, xres_reg, xout_ap, xout_reg, G_ap, G_reg, want_T):
    Wup = io["ffn_w_up"][layer].rearrange("(kc p) n -> p kc n", p=128)
    Wdn = io["ffn_w_down"][layer].rearrange("(fc p) n -> p fc n", p=128)
    with ExitStack() as es:
        wu = [sbt(nc, es, "f_wu%d" % i, [128, 8, 256], BF16) for i in range(2)]
        cw = sbt(nc, es, "f_cw", [128, NFC, 3], F32)
        cb = sbt(nc, es, "f_cb", [128, NFC], F32)
        ubuf = [sbt(nc, es, "f_ub%d" % i, [128, 514], F32) for i in range(2)]
        cbuf = [sbt(nc, es, "f_c%d" % i, [128, 512], F32) for i in range(2)]
        gl = [sbt(nc, es, "f_gl%d" % i, [128, 512], F32) for i in range(2)]
        gt = [sbt(nc, es, "f_gt%d" % i, [128, 512], BF16) for i in range(3)]
        pu = [pst(nc, es, "f_pu%d" % i, [128, 512], F32) for i in range(3)]
        pv = [pst(nc, es, "f_pv%d" % i, [128, 512], F32) for i in range(3)]
        for j in range(3):
            kb.dma("sp", out=cw.t[:, :, j], in_=io["ffn_conv_w"][layer][j].rearrange("(fc p) -> p fc", p=128), writes=[cw.g],
                   allow_slow_non_contiguous=True)
        kb.dma("sp", out=cb.t[:, :], in_=io["ffn_conv_b"][layer].rearrange("(fc p) -> p fc", p=128), writes=[cb.g],
               allow_slow_non_contiguous=True)
        it = 0
        for fc in range(NFC):
            w = wu[fc % 2]
            kb.dma("pool", out=w.t[:, :, 0:128], in_=Wup[:, :, fc * 128:(fc + 1) * 128], writes=[w.g])
            kb.dma("pool", out=w.t[:, :, 128:256], in_=Wup[:, :, FF + fc * 128:FF + (fc + 1) * 128], writes=[w.g])
            for tt in range(8):
                u_ps = pu[it % 3]
                v_ps = pv[it % 3]
                ub = ubuf[it % 2]
                ubn = ubuf[(it + 1) % 2]
                c = cbuf[it % 2]
                g_ = gl[it % 2]
                go = gt[it % 3]
                for kc in range(8):
                    kb.op("pe", lambda e: e.matmul(out=u_ps.t[:, :], lhsT=w.t[:, kc, 0:128], rhs=xT.t[:, kc, tt * 512:(tt + 1) * 512],
                                                   start=(kc == 0), stop=(kc == 7)), reads=[w.g, xT.g], writes=[u_ps.g])
                for kc in range(8):
                    kb.op("pe", lambda e: e.matmul(out=v_ps.t[:, :], lhsT=w.t[:, kc, 128:256], rhs=xT.t[:, kc, tt * 512:(tt + 1) * 512],
                                                   start=(kc == 0), stop=(kc == 7)), reads=[w.g, xT.g], writes=[v_ps.g])
                if tt == 0:
                    kb.op("dve", lambda e: e.memset(ub.t[:, 0:2], 0.0), writes=[ub.g])
                kb.op("act", lambda e: e.activation(out=ub.t[:, 2:514], in_=u_ps.t[:, :], func=AF.Copy), reads=[u_ps.g], writes=[ub.g])
                if tt < 7:
                    kb.op("dve", lambda e: e.tensor_copy(out=ubn.t[:, 0:2], in_=ub.t[:, 512:514]), reads=[ub.g], writes=[ubn.g])
                kb.op("act", lambda e: e.activation(out=c.t[:, :], in_=u_ps.t[:, :], func=AF.Identity, scale=cw.t[:, fc, 2:3], bias=cb.t[:, fc:fc + 1]),
                      reads=[u_ps.g, cw.g, cb.g], writes=[c.g])
                kb.op("dve", lambda e: e.scalar_tensor_tensor(out=c.t[:, :], in0=ub.t[:, 1:513], scalar=cw.t[:, fc, 1:2], in1=c.t[:, :],
                                                              op0=ALU.mult, op1=ALU.add), reads=[ub.g, cw.g, c.g], writes=[c.g])
                kb.op("dve", lambda e: e.scalar_tensor_tensor(out=c.t[:, :], in0=ub.t[:, 0:512], scalar=cw.t[:, fc, 0:1], in1=c.t[:, :],
                                                              op0=ALU.mult, op1=ALU.add), reads=[ub.g, cw.g, c.g], writes=[c.g])
                kb.op("act", lambda e: e.activation(out=g_.t[:, :], in_=c.t[:, :], func=AF.Gelu), reads=[c.g], writes=[g_.g])
                kb.op("dve", lambda e: e.tensor_tensor(out=go.t[:, :], in0=v_ps.t[:, :], in1=g_.t[:, :], op=ALU.mult), reads=[v_ps.g, g_.g], writes=[go.g])
                kb.dma("sp", out=G_ap[fc * 128:(fc + 1) * 128, tt * 512:(tt + 1) * 512], in_=go.t[:, :], reads=[go.g], writes=[G_reg], pool="st")
                it += 1
        kb.barrier()
    Gv = G_ap.rearrange("(fc p) t -> p fc t", p=128)
    with ExitStack() as es:
        wd = sbt(nc, es, "g_wd", [128, NFC, 1024], BF16)
        gin = [sbt(nc, es, "g_gin%d" % i, [128, NFC, 512], BF16) for i in range(2)]
        gam = sbt(nc, es, "g_gam", [128, 1024], F32)
        bet = sbt(nc, es, "g_bet", [128, 1024], F32)
        xin = [sbt(nc, es, "g_x%d" % i, [128, 1024], F32) for i in range(2)]
        z = [sbt(nc, es, "g_z%d" % i, [128, 1024], F32) for i in range(2)]
        xo = [sbt(nc, es, "g_xo%d" % i, [128, 1024], F32) for i in range(2)]
        xbf = [sbt(nc, es, "g_xb%d" % i, [128, 1024], BF16) for i in range(2)]
        stats = sbt(nc, es, "g_stats", [128, 12], F32)
        mv = sbt(nc, es, "g_mv", [128, 2], F32)
        rstd = sbt(nc, es, "g_rstd", [128, 1], F32)
        mm = [[pst(nc, es, "g_mm%d%d" % (i, j), [128, 512], F32) for j in range(2)] for i in range(2)]
        ptr = [pst(nc, es, "g_pt%d" % i, [128, 1024], BF16) for i in range(2)]
        for q4 in range(2):
            kb.dma("pool", out=wd.t[:, q4 * 11:(q4 + 1) * 11, :], in_=Wdn[:, q4 * 11:(q4 + 1) * 11, :], writes=[wd.g])
        load_bcast(kb, gam, io["ln_ffn_g"][layer])
        load_bcast(kb, bet, io["ln_ffn_b"][layer])
        for tt in range(8):
            gi = gin[tt % 2]
            for q4 in range(2):
                kb.dma("sp", out=gi.t[:, q4 * 11:(q4 + 1) * 11, :], in_=Gv[:, q4 * 11:(q4 + 1) * 11, tt * 512:(tt + 1) * 512],
                       reads=[G_reg], writes=[gi.g])
            for bb in range(4):
                b = tt * 4 + bb
                xi = xin[b % 2]
                kb.dma("sp", out=xi.t[:, :], in_=xres_ap[b * 128:(b + 1) * 128, :], reads=[xres_reg], writes=[xi.g])
                for hf in range(2):
                    for fc in range(NFC):
                        kb.op("pe", lambda e: e.matmul(out=mm[b % 2][hf].t[:, :], lhsT=gi.t[:, fc, bb * 128:(bb + 1) * 128],
                                                       rhs=wd.t[:, fc, hf * 512:(hf + 1) * 512], start=(fc == 0), stop=(fc == NFC - 1)),
                              reads=[gi.g, wd.g], writes=[mm[b % 2][hf].g])
                zz = z[b % 2]
                for hf in range(2):
                    kb.op("dve", lambda e: e.scalar_tensor_tensor(out=zz.t[:, hf * 512:(hf + 1) * 512], in0=xi.t[:, hf * 512:(hf + 1) * 512], scalar=DN_ALPHA,
                                                                  in1=mm[b % 2][hf].t[:, :], op0=ALU.mult, op1=ALU.add),
                          reads=[xi.g, mm[b % 2][hf].g], writes=[zz.g])
                ln_block(kb, zz, gam, bet, xo[b % 2], stats, mv, rstd, LN_EPS)
                kb.dma("sp", out=xout_ap[b * 128:(b + 1) * 128, :], in_=xo[b % 2].t[:, :], reads=[xo[b % 2].g], writes=[xout_reg], pool="st")
                if want_T:
                    transpose_to_fm(kb, xo[b % 2], xbf[b % 2], xT, b, ident, ptr[b % 2])
        kb.barrier()


def phase_rwkv_a(kb, nc, io, xT, scr, scr_reg):
    Wrkv = io["od_w_rkv"][0].rearrange("n (kc p) e -> p n kc e", p=128)
    with ExitStack() as es:
        wr = sbt(nc, es, "ra_w", [128, 3, 8, 1024], BF16)
        l1 = sbt(nc, es, "ra_l1", [128, 8, 288], BF16)
        w2 = sbt(nc, es, "ra_w2", [64, 1024], BF16)
        a2 = sbt(nc, es, "ra_a2", [64, 1024], BF16)
        g2a = sbt(nc, es, "ra_g2a", [128, 1024], BF16)
        g2b = sbt(nc, es, "ra_g2b", [32, 1024], BF16)
        w0b = sbt(nc, es, "ra_w0b", [128, 1024], F32)
        a0b = sbt(nc, es, "ra_a0b", [128, 1024], F32)
        mu = sbt(nc, es, "ra_mu", [128, 6, 8], F32)
        xx = [sbt(nc, es, "ra_xx%d" % i, [128, 8, 128], F32) for i in range(2)]
        mixT = [[sbt(nc, es, "ra_mix%d_%d" % (n, i), [128, 8, 128], BF16) for i in range(2)] for n in range(6)]
        lo1 = [sbt(nc, es, "ra_lo%d" % i, [128, 128], BF16) for i in range(4)]
        lo2 = sbt(nc, es, "ra_l32", [32, 128], BF16)
        outF = [sbt(nc, es, "ra_o%d" % i, [128, 1024], F32) for i in range(4)]
        P = [pst(nc, es, "ra_p%d" % i, [128, 1024], F32) for i in range(3)]
        Q = [pst(nc, es, "ra_q%d" % i, [128, 512], F32) for i in range(2)]
        for n in range(3):
            kb.dma("pool", out=wr.t[:, n, :, :], in_=Wrkv[:, n, :, :], writes=[wr.g])
        kb.dma("pool", out=l1.t[:, :, 0:64], in_=io["od_w1"].rearrange("(kc p) e -> p kc e", p=128), writes=[l1.g])
        kb.dma("pool", out=l1.t[:, :, 64:128], in_=io["od_a1"].rearrange("(kc p) e -> p kc e", p=128), writes=[l1.g])
        kb.dma("pool", out=l1.t[:, :, 128:288], in_=io["od_g1"].rearrange("(kc p) e -> p kc e", p=128), writes=[l1.g])
        kb.dma("pool", out=w2.t[:, :], in_=io["od_w2"][:, :], writes=[w2.g])
        kb.dma("pool", out=a2.t[:, :], in_=io["od_a2"][:, :], writes=[a2.g])
        kb.dma("pool", out=g2a.t[:, :], in_=io["od_g2"][0:128, :], writes=[g2a.g])
        kb.dma("pool", out=g2b.t[:, :], in_=io["od_g2"][128:160, :], writes=[g2b.g])
        load_bcast(kb, w0b, io["od_w0"][0])
        load_bcast(kb, a0b, io["od_a0"][0])
        for n in range(6):
            kb.dma("sp", out=mu.t[:, n, :], in_=io["od_mu"][0, n].rearrange("(kc p) -> p kc", p=128), writes=[mu.g],
                   allow_slow_non_contiguous=True)
        oi = 0
        for b in range(NB):
            t0 = b * 128
            x_ = xx[b % 2]
            if b == 0:
                kb.op("dve", lambda e: e.tensor_tensor(out=x_.t[:, :, 1:128], in0=xT.t[:, :, 0:127], in1=xT.t[:, :, 1:128], op=ALU.subtract),
                      reads=[xT.g], writes=[x_.g])
                kb.op("dve", lambda e: e.tensor_scalar(out=x_.t[:, :, 0:1], in0=xT.t[:, :, 0:1], scalar1=-1.0, scalar2=None, op0=ALU.mult),
                      reads=[xT.g], writes=[x_.g])
            else:
                kb.op("dve", lambda e: e.tensor_tensor(out=x_.t[:, :, :], in0=xT.t[:, :, t0 - 1:t0 + 127], in1=xT.t[:, :, t0:t0 + 128], op=ALU.subtract),
                      reads=[xT.g], writes=[x_.g])
            mx = [mixT[n][b % 2] for n in range(6)]
            for n in range(6):
                for kc in range(8):
                    eng = "dve"
                    kb.op(eng, lambda e: e.scalar_tensor_tensor(out=mx[n].t[:, kc, :], in0=x_.t[:, kc, :], scalar=mu.t[:, n, kc:kc + 1],
                                                                in1=xT.t[:, kc, t0:t0 + 128], op0=ALU.mult, op1=ALU.add),
                          reads=[x_.g, mu.g, xT.g], writes=[mx[n].g])

            def store(idx, ps, pre=None, func=None):
                nonlocal oi
                o = outF[oi % 4]
                oi += 1
                if pre is not None:
                    kb.op("dve", lambda e: e.tensor_tensor(out=o.t[:, :], in0=ps.t[:, :], in1=pre.t[:, :], op=ALU.add), reads=[ps.g, pre.g], writes=[o.g])
                    kb.op("act", lambda e: e.activation(out=o.t[:, :], in_=o.t[:, :], func=func), reads=[o.g], writes=[o.g])
                else:
                    kb.op("act", lambda e: e.activation(out=o.t[:, :], in_=ps.t[:, :], func=AF.Copy), reads=[ps.g], writes=[o.g])
                kb.dma("sp", out=scr[idx][t0:t0 + 128, :], in_=o.t[:, :], reads=[o.g], writes=[scr_reg], pool="st")

            for n in range(3):
                ps = P[n]
                for hf in range(2):
                    for kc in range(8):
                        kb.op("pe", lambda e: e.matmul(out=ps.t[:, hf * 512:(hf + 1) * 512], lhsT=mx[n].t[:, kc, :], rhs=wr.t[:, n, kc, hf * 512:(hf + 1) * 512],
                                                       start=(kc == 0), stop=(kc == 7)), reads=[mx[n].g, wr.g], writes=[ps.g])
                store(n, ps)
            q = Q[0]
            for kc in range(8):
                kb.op("pe", lambda e: e.matmul(out=q.t[0:64, 0:128], lhsT=l1.t[:, kc, 0:64], rhs=mx[3].t[:, kc, :], start=(kc == 0), stop=(kc == 7)),
                      reads=[l1.g, mx[3].g], writes=[q.g])
            for kc in range(8):
                kb.op("pe", lambda e: e.matmul(out=q.t[0:64, 128:256], lhsT=l1.t[:, kc, 64:128], rhs=mx[4].t[:, kc, :], start=(kc == 0), stop=(kc == 7)),
                      reads=[l1.g, mx[4].g], writes=[q.g])
            for kc in range(8):
                kb.op("pe", lambda e: e.matmul(out=q.t[:, 256:384], lhsT=l1.t[:, kc, 128:256], rhs=mx[5].t[:, kc, :], start=(kc == 0), stop=(kc == 7)),
                      reads=[l1.g, mx[5].g], writes=[q.g])
            for kc in range(8):
                kb.op("pe", lambda e: e.matmul(out=q.t[0:32, 384:512], lhsT=l1.t[:, kc, 256:288], rhs=mx[5].t[:, kc, :], start=(kc == 0), stop=(kc == 7)),
                      reads=[l1.g, mx[5].g], writes=[q.g])
            tw, al, sg1 = lo1[0], lo1[1], lo1[2]
            kb.op("act", lambda e: e.activation(out=tw.t[0:64, :], in_=q.t[0:64, 0:128], func=AF.Tanh), reads=[q.g], writes=[tw.g])
            kb.op("act", lambda e: e.activation(out=al.t[0:64, :], in_=q.t[0:64, 128:256], func=AF.Copy), reads=[q.g], writes=[al.g])
            kb.op("act", lambda e: e.activation(out=sg1.t[:, :], in_=q.t[:, 256:384], func=AF.Sigmoid), reads=[q.g], writes=[sg1.g])
            kb.op("act", lambda e: e.activation(out=lo2.t[:, :], in_=q.t[0:32, 384:512], func=AF.Sigmoid), reads=[q.g], writes=[lo2.g])
            ps = P[0]
            for hf in range(2):
                kb.op("pe", lambda e: e.matmul(out=ps.t[:, hf * 512:(hf + 1) * 512], lhsT=tw.t[0:64, :], rhs=w2.t[:, hf * 512:(hf + 1) * 512], start=True, stop=True),
                      reads=[tw.g, w2.g], writes=[ps.g])
            store(3, ps, pre=w0b, func=AF.Sigmoid)
            ps = P[1]
            for hf in range(2):
                kb.op("pe", lambda e: e.matmul(out=ps.t[:, hf * 512:(hf + 1) * 512], lhsT=al.t[0:64, :], rhs=a2.t[:, hf * 512:(hf + 1) * 512], start=True, stop=True),
                      reads=[al.g, a2.g], writes=[ps.g])
            store(4, ps, pre=a0b, func=AF.Sigmoid)
            ps = P[2]
            for hf in range(2):
                kb.op("pe", lambda e: e.matmul(out=ps.t[:, hf * 512:(hf + 1) * 512], lhsT=sg1.t[:, :], rhs=g2a.t[:, hf * 512:(hf + 1) * 512], start=True, stop=False),
                      reads=[sg1.g, g2a.g], writes=[ps.g])
                kb.op("pe", lambda e: e.matmul(out=ps.t[:, hf * 512:(hf + 1) * 512], lhsT=lo2.t[:, :], rhs=g2b.t[:, hf * 512:(hf + 1) * 512], start=False, stop=True),
                      reads=[lo2.g, g2b.g], writes=[ps.g])
            store(5, ps)
        kb.barrier()


def phase_rwkv_b(kb, nc, io, XT3, xt3_reg, ident, ones, tri, scr, scr_reg, xres_ap, xres_reg, xout_ap, xout_reg):
    H3 = lambda ap: ap.rearrange("p (h d) -> p h d", h=16)
    with ExitStack() as es:
        def F(name):
            return sbt(nc, es, "rb_" + name, [128, 1024], F32)

        def B(name):
            return sbt(nc, es, "rb_" + name, [128, 1024], BF16)

        wo = sbt(nc, es, "rb_wo", [128, 8, 1024], BF16)
        vec = {}
        for nm, ap in (("k_k", io["od_k_k"][0]), ("k_a", io["od_k_a"][0]), ("r_k", io["od_r_k"][0]), ("lnx_g", io["od_lnx_g"][0]),
                       ("lnx_b", io["od_lnx_b"][0]), ("lng", io["ln_mix_g"][1]), ("lnb", io["ln_mix_b"][1])):
            vec[nm] = F("v_" + nm)
            load_bcast(kb, vec[nm], ap)
        kb.dma("pool", out=wo.t[:, :, :], in_=io["od_w_out"].rearrange("(kc p) n -> p kc n", p=128), writes=[wo.g])
        msk = {}
        for nm in ("c_su4", "c_sl4", "c_iu4", "c_id4"):
            msk[nm] = sbt(nc, es, "rb_" + nm, [128, 512], BF16)
            kb.dma("sp", out=msk[nm].t[:, :], in_=io[nm][:, :], writes=[msk[nm].g])
        inb = [[F("in%d_%d" % (i, j)) for i in range(6)] for j in range(2)]
        x3ts = [B("x3t0"), B("x3t1")]
        f1, f2, f3 = F("f1"), F("f2"), F("f3")
        vB, lhi, llo = B("vB"), B("lhi"), B("llo")
        rtB, atB, btB, ktB, bpB, kpB = B("rt"), B("at"), B("bt"), B("kt"), B("bp"), B("kp")
        rT, aT, bT, kTt = B("rT"), B("aT"), B("bT"), B("kT")
        Arb, Ark = [B("Arb0"), B("Arb1")], [B("Ark0"), B("Ark1")]
        Xall = [B("X0"), B("X1")]
        AhT = B("AhT")
        W1b, Ub, ygB = rtB, btB, ktB
        tmpg = [[sbt(nc, es, "rb_tg%d_%d" % (g, i), [128, 512], BF16) for i in range(10)] for g in range(4)]
        tmpb = tmpg[0]
        small = sbt(nc, es, "rb_small", [128, 96], F32)
        eLC = sbt(nc, es, "rb_eLC", [128, 8], F32)
        Hs = sbt(nc, es, "rb_H", [128, 8, 64], F32)
        Hb = sbt(nc, es, "rb_Hb", [128, 8, 64], BF16)
        xo = f3
        xbf = lhi
        stats = sbt(nc, es, "rb_stats", [128, 12], F32)
        mv = sbt(nc, es, "rb_mv", [128, 2], F32)
        rstd = sbt(nc, es, "rb_rstd", [128, 1], F32)
        P = [pst(nc, es, "rb_p%d" % i, [128, 1024], F32) for i in range(3)]
        QT = pst(nc, es, "rb_qt", [128, 1024], BF16)
        Q1 = pst(nc, es, "rb_q1", [128, 512], F32)
        kb.op("dve", lambda e: e.memset(Hs.t[:, :, :], 0.0), writes=[Hs.g])
        kb.op("dve", lambda e: e.memset(Hb.t[:, :, :], 0.0), writes=[Hb.g])

        def bc16(t, c0):
            return small.t[:, c0:c0 + 16].unsqueeze(2).to_broadcast([128, 16, 64])

        for b in range(NB):
            t0 = b * 128
            if b == 0:
                for idx, dst in enumerate(inb[0]):
                    kb.dma("sp", out=dst.t[:, :], in_=scr[idx][0:128, :], reads=[scr_reg], writes=[dst.g])
            rF, kF, vF, wF, aF, gF = inb[b % 2]
            xin = rF
            if b + 1 < NB:
                for idx, dst in enumerate(inb[(b + 1) % 2]):
                    kb.dma("sp", out=dst.t[:, :], in_=scr[idx][t0 + 128:t0 + 256, :], reads=[scr_reg], writes=[dst.g])
            kb.op("act", lambda e: e.activation(out=vB.t[:, :], in_=vF.t[:, :], func=AF.Copy), reads=[vF.g], writes=[vB.g])
            kb.op("dve", lambda e: e.tensor_scalar(out=wF.t[:, :], in0=wF.t[:, :], scalar1=-math.exp(-0.5), scalar2=None, op0=ALU.mult), reads=[wF.g], writes=[wF.g])
            kb.op("act", lambda e: e.activation(out=lhi.t[:, :], in_=wF.t[:, :], func=AF.Copy), reads=[wF.g], writes=[lhi.g])
            kb.op("dve", lambda e: e.tensor_tensor(out=llo.t[:, :], in0=wF.t[:, :], in1=lhi.t[:, :], op=ALU.subtract), reads=[wF.g, lhi.g], writes=[llo.g])
            kb.op("dve", lambda e: e.tensor_tensor(out=f1.t[:, :], in0=kF.t[:, :], in1=vec["k_k"].t[:, :], op=ALU.mult), reads=[kF.g, vec["k_k"].g], writes=[f1.g])
            kb.op("act", lambda e: e.activation(out=f2.t[:, :], in_=f1.t[:, :], func=AF.Square), reads=[f1.g], writes=[f2.g])
            kb.op("dve", lambda e: e.tensor_reduce(out=small.t[:, 0:16], in_=H3(f2.t[:, :]), axis=AX.X, op=ALU.add), reads=[f2.g], writes=[small.g])
            kb.op("act", lambda e: e.activation(out=small.t[:, 0:16], in_=small.t[:, 0:16], func=AF.Sqrt), reads=[small.g], writes=[small.g])
            kb.op("dve", lambda e: e.tensor_scalar(out=small.t[:, 0:16], in0=small.t[:, 0:16], scalar1=1e-12, scalar2=None, op0=ALU.max), reads=[small.g], writes=[small.g])
            kb.op("dve", lambda e: e.reciprocal(out=small.t[:, 0:16], in_=small.t[:, 0:16]), reads=[small.g], writes=[small.g])
            kb.op("dve", lambda e: e.tensor_tensor(out=H3(f1.t[:, :]), in0=H3(f1.t[:, :]), in1=bc16(small, 0), op=ALU.mult), reads=[f1.g, small.g], writes=[f1.g])
            kb.op("dve", lambda e: e.scalar_tensor_tensor(out=f2.t[:, :], in0=aF.t[:, :], scalar=-1.0, in1=vec["k_a"].t[:, :], op0=ALU.add, op1=ALU.mult),
                  reads=[aF.g, vec["k_a"].g], writes=[f2.g])
            kb.op("dve", lambda e: e.scalar_tensor_tensor(out=f2.t[:, :], in0=f2.t[:, :], scalar=1.0, in1=kF.t[:, :], op0=ALU.add, op1=ALU.mult),
                  reads=[f2.g, kF.g], writes=[f2.g])
            kb.op("dve", lambda e: e.tensor_tensor(out=kF.t[:, :], in0=f1.t[:, :], in1=aF.t[:, :], op=ALU.mult), reads=[f1.g, aF.g], writes=[kF.g])
            if RB_STOP == 1:
                kb.barrier()
                return
            for hf in range(2):
                sl = slice(hf * 512, (hf + 1) * 512)
                kb.op("pe", lambda e: e.matmul(out=P[0].t[:, sl], lhsT=tri.t[:, :], rhs=lhi.t[:, sl], start=True, stop=False), reads=[tri.g, lhi.g], writes=[P[0].g])
                kb.op("pe", lambda e: e.matmul(out=P[0].t[:, sl], lhsT=tri.t[:, :], rhs=llo.t[:, sl], start=False, stop=True), reads=[tri.g, llo.g], writes=[P[0].g])
                kb.op("pe", lambda e: e.matmul(out=P[1].t[:, sl], lhsT=ones.t[:, :], rhs=lhi.t[:, sl], start=True, stop=False), reads=[ones.g, lhi.g], writes=[P[1].g])
                kb.op("pe", lambda e: e.matmul(out=P[1].t[:, sl], lhsT=ones.t[:, :], rhs=llo.t[:, sl], start=False, stop=True), reads=[ones.g, llo.g], writes=[P[1].g])
            for hp in range(8):
                kb.op("pe", lambda e: e.matmul(out=Q1.t[:, hp:hp + 1], lhsT=lhi.t[:, hp * 128:(hp + 1) * 128], rhs=ones.t[:, 0:1], start=True, stop=False),
                      reads=[lhi.g, ones.g], writes=[Q1.g])
                kb.op("pe", lambda e: e.matmul(out=Q1.t[:, hp:hp + 1], lhsT=llo.t[:, hp * 128:(hp + 1) * 128], rhs=ones.t[:, 0:1], start=False, stop=True),
                      reads=[llo.g, ones.g], writes=[Q1.g])
            kb.op("act", lambda e: e.activation(out=eLC.t[:, :], in_=Q1.t[:, 0:8], func=AF.Exp), reads=[Q1.g], writes=[eLC.g])
            if RB_STOP == 2:
                kb.barrier()
                return
            kb.op("act", lambda e: e.activation(out=aF.t[:, :], in_=P[0].t[:, :], func=AF.Copy), reads=[P[0].g], writes=[aF.g])
            kb.op("act", lambda e: e.activation(out=f3.t[:, :], in_=P[0].t[:, :], func=AF.Exp), reads=[P[0].g], writes=[f3.g])
            kb.op("dve", lambda e: e.tensor_tensor(out=rtB.t[:, :], in0=rF.t[:, :], in1=f3.t[:, :], op=ALU.mult), reads=[rF.g, f3.g], writes=[rtB.g])
            kb.op("dve", lambda e: e.tensor_tensor(out=wF.t[:, :], in0=aF.t[:, :], in1=wF.t[:, :], op=ALU.subtract), reads=[aF.g, wF.g], writes=[wF.g])
            kb.op("act", lambda e: e.activation(out=wF.t[:, :], in_=wF.t[:, :], func=AF.Exp), reads=[wF.g], writes=[wF.g])
            kb.op("dve", lambda e: e.scalar_tensor_tensor(out=atB.t[:, :], in0=f1.t[:, :], scalar=-1.0, in1=wF.t[:, :], op0=ALU.mult, op1=ALU.mult),
                  reads=[f1.g, wF.g], writes=[atB.g])
            kb.op("act", lambda e: e.activation(out=f3.t[:, :], in_=aF.t[:, :], func=AF.Exp, scale=-1.0), reads=[aF.g], writes=[f3.g])
            kb.op("dve", lambda e: e.tensor_tensor(out=btB.t[:, :], in0=kF.t[:, :], in1=f3.t[:, :], op=ALU.mult), reads=[kF.g, f3.g], writes=[btB.g])
            kb.op("dve", lambda e: e.tensor_tensor(out=ktB.t[:, :], in0=f2.t[:, :], in1=f3.t[:, :], op=ALU.mult), reads=[f2.g, f3.g], writes=[ktB.g])
            kb.op("dve", lambda e: e.tensor_tensor(out=f3.t[:, :], in0=P[1].t[:, :], in1=aF.t[:, :], op=ALU.subtract), reads=[P[1].g, aF.g], writes=[f3.g])
            kb.op("act", lambda e: e.activation(out=f3.t[:, :], in_=f3.t[:, :], func=AF.Exp), reads=[f3.g], writes=[f3.g])
            kb.op("dve", lambda e: e.tensor_tensor(out=bpB.t[:, :], in0=kF.t[:, :], in1=f3.t[:, :], op=ALU.mult), reads=[kF.g, f3.g], writes=[bpB.g])
            kb.op("dve", lambda e: e.tensor_tensor(out=kpB.t[:, :], in0=f2.t[:, :], in1=f3.t[:, :], op=ALU.mult), reads=[f2.g, f3.g], writes=[kpB.g])
            kb.op("dve", lambda e: e.tensor_tensor(out=f3.t[:, :], in0=rF.t[:, :], in1=f2.t[:, :], op=ALU.mult), reads=[rF.g, f2.g], writes=[f3.g])
            kb.op("dve", lambda e: e.tensor_tensor(out=f3.t[:, :], in0=f3.t[:, :], in1=vec["r_k"].t[:, :], op=ALU.mult), reads=[f3.g, vec["r_k"].g], writes=[f3.g])
            kb.op("dve", lambda e: e.tensor_reduce(out=small.t[:, 16:32], in_=H3(f3.t[:, :]), axis=AX.X, op=ALU.add), reads=[f3.g], writes=[small.g])
            kb.op("dve", lambda e: e.tensor_tensor(out=H3(vF.t[:, :]), in0=H3(vF.t[:, :]), in1=bc16(small, 16), op=ALU.mult), reads=[vF.g, small.g], writes=[vF.g])
            if RB_STOP == 3:
                kb.barrier()
                return
            for src, dst in ((rtB, rT), (atB, aT), (btB, bT), (ktB, kTt)):
                for hp in range(8):
                    kb.op("pe", lambda e: e.transpose(out=QT.t[:, hp * 128:(hp + 1) * 128], in_=src.t[:, hp * 128:(hp + 1) * 128], identity=ident.t[:, :]),
                          reads=[src.g, ident.g], writes=[QT.g])
                kb.op("act", lambda e: e.activation(out=dst.t[:, :], in_=QT.t[:, :], func=AF.Copy), reads=[QT.g], writes=[dst.g])

            if RB_STOP == 4:
                kb.barrier()
                return
            def fm(t, h):
                r0 = 64 * (h % 2)
                return t.t[r0:r0 + 64, (h // 2) * 128:(h // 2 + 1) * 128]

            gst = []
            for g4 in range(4):
                heads = [g4 * 4 + i for i in range(4)]
                order = [(0, heads[0]), (2, heads[2]), (1, heads[1]), (3, heads[3])]
                tb = tmpg[g4]
                Nb, NTb = tb[0], tb[1]
                specs = ((bT, aT, Nb, "c_su4"), (aT, bT, NTb, "c_sl4"))
                ps = P[(2 * g4) % 3]
                for si, (la, rb_, dst, mk) in enumerate(specs):
                    off = si * 512
                    for i, h in order:
                        kb.op("pe", lambda e: e.matmul(out=ps.t[:, off + i * 128:off + (i + 1) * 128], lhsT=fm(la, h), rhs=fm(rb_, h), start=True, stop=True),
                              reads=[la.g, rb_.g], writes=[ps.g], rg=(64 * (h % 2), 64))
                    kb.op("dve", lambda e: e.tensor_tensor(out=dst.t[:, :], in0=ps.t[:, off:off + 512], in1=msk[mk].t[:, :], op=ALU.mult),
                          reads=[ps.g, msk[mk].g], writes=[dst.g])
                hi_ = g4 // 2
                co = (g4 % 2) * 512
                ps = P[(2 * g4 + 1) % 3]
                for si, (la, rb_, dst) in enumerate(((bT, rT, Arb[hi_]), (kTt, rT, Ark[hi_]))):
                    off = si * 512
                    for i, h in order:
                        kb.op("pe", lambda e: e.matmul(out=ps.t[:, off + i * 128:off + (i + 1) * 128], lhsT=fm(la, h), rhs=fm(rb_, h), start=True, stop=True),
                              reads=[la.g, rb_.g], writes=[ps.g], rg=(64 * (h % 2), 64))
                    kb.op("dve", lambda e: e.tensor_tensor(out=dst.t[:, co:co + 512], in0=ps.t[:, off:off + 512], in1=msk["c_iu4"].t[:, :], op=ALU.mult),
                          reads=[ps.g, msk["c_iu4"].g], writes=[dst.g])
                X, XT = tb[2], tb[3]
                kb.op("dve", lambda e: e.tensor_tensor(out=X.t[:, :], in0=Nb.t[:, :], in1=msk["c_id4"].t[:, :], op=ALU.add), reads=[Nb.g, msk["c_id4"].g], writes=[X.g])
                kb.op("dve", lambda e: e.tensor_tensor(out=XT.t[:, :], in0=NTb.t[:, :], in1=msk["c_id4"].t[:, :], op=ALU.add), reads=[NTb.g, msk["c_id4"].g], writes=[XT.g])
                gst.append({"X": X, "XT": XT, "P": Nb, "PT": NTb, "pp": 0})
            for it in range(6):
                last = it == 5
                cur = {}
                for g4 in range(4):
                    st = gst[g4]
                    tb = tmpg[g4]
                    Pm, PTm = st["P"], st["PT"]
                    P2, P2T = tb[4 + st["pp"]], tb[6 + st["pp"]]
                    st["pp"] ^= 1
                    psa = P[(2 * g4 + 2 * it) % 3]
                    for i in range(4):
                        sl = slice(i * 128, (i + 1) * 128)
                        kb.op("pe", lambda e: e.matmul(out=psa.t[:, sl], lhsT=PTm.t[:, sl], rhs=Pm.t[:, sl], start=True, stop=True), reads=[PTm.g, Pm.g], writes=[psa.g])
                    if not last:
                        for i in range(4):
                            sl = slice(i * 128, (i + 1) * 128)
                            sl2 = slice(512 + i * 128, 512 + (i + 1) * 128)
                            kb.op("pe", lambda e: e.matmul(out=psa.t[:, sl2], lhsT=Pm.t[:, sl], rhs=PTm.t[:, sl], start=True, stop=True), reads=[PTm.g, Pm.g], writes=[psa.g])
                    kb.op("act", lambda e: e.activation(out=P2.t[:, :], in_=psa.t[:, 0:512], func=AF.Copy), reads=[psa.g], writes=[P2.g])
                    if not last:
                        kb.op("act", lambda e: e.activation(out=P2T.t[:, :], in_=psa.t[:, 512:1024], func=AF.Copy), reads=[psa.g], writes=[P2T.g])
                    cur[g4] = (P2, P2T)
                for g4 in range(4):
                    st = gst[g4]
                    tb = tmpg[g4]
                    hi_ = g4 // 2
                    co = (g4 % 2) * 512
                    X, XT = st["X"], st["XT"]
                    P2, P2T = cur[g4]
                    psb = P[(2 * g4 + 2 * it + 1) % 3]
                    for i in range(4):
                        sl = slice(i * 128, (i + 1) * 128)
                        kb.op("pe", lambda e: e.matmul(out=psb.t[:, sl], lhsT=XT.t[:, sl], rhs=P2.t[:, sl], start=True, stop=True), reads=[XT.g, P2.g], writes=[psb.g])
                    if not last:
                        for i in range(4):
                            sl = slice(i * 128, (i + 1) * 128)
                            sl2 = slice(512 + i * 128, 512 + (i + 1) * 128)
                            kb.op("pe", lambda e: e.matmul(out=psb.t[:, sl2], lhsT=P2.t[:, sl], rhs=XT.t[:, sl], start=True, stop=True), reads=[XT.g, P2.g], writes=[psb.g])
                    if last:
                        kb.op("dve", lambda e: e.tensor_tensor(out=Xall[hi_].t[:, co:co + 512], in0=psb.t[:, 0:512], in1=X.t[:, :], op=ALU.add),
                              reads=[psb.g, X.g], writes=[Xall[hi_].g])
                    else:
                        Xn, XTn = (tb[8], tb[9]) if (it % 2 == 0) else (tb[2], tb[3])
                        kb.op("dve", lambda e: e.tensor_tensor(out=Xn.t[:, :], in0=psb.t[:, 0:512], in1=X.t[:, :], op=ALU.add), reads=[psb.g, X.g], writes=[Xn.g])
                        kb.op("dve", lambda e: e.tensor_tensor(out=XTn.t[:, :], in0=psb.t[:, 512:1024], in1=XT.t[:, :], op=ALU.add), reads=[psb.g, XT.g], writes=[XTn.g])
                        st["X"], st["XT"], st["P"], st["PT"] = Xn, XTn, P2, P2T
            if RB_STOP == 5:
                kb.barrier()
                return
            for g4 in range(4):
                heads = [g4 * 4 + i for i in range(4)]
                order = [(0, heads[0]), (2, heads[2]), (1, heads[1]), (3, heads[3])]
                Aak = tmpb[2]
                ps = P[2]
                for i, h in ((0, heads[0]), (2, heads[2]), (1, heads[1]), (3, heads[3])):
                    kb.op("pe", lambda e: e.matmul(out=ps.t[:, i * 128:(i + 1) * 128], lhsT=fm(kTt, h), rhs=fm(aT, h), start=True, stop=True),
                          reads=[kTt.g, aT.g], writes=[ps.g], rg=(64 * (h % 2), 64))
                kb.op("dve", lambda e: e.tensor_tensor(out=Aak.t[:, :], in0=ps.t[:, 0:512], in1=msk["c_su4"].t[:, :], op=ALU.mult),
                      reads=[ps.g, msk["c_su4"].g], writes=[Aak.g])
                for i, h in enumerate(heads):
                    kb.op("pe", lambda e: e.matmul(out=P[1].t[:, h * 64:(h + 1) * 64], lhsT=Aak.t[:, i * 128:(i + 1) * 128], rhs=vB.t[:, h * 64:(h + 1) * 64],
                                                   start=True, stop=True), reads=[Aak.g, vB.g], writes=[P[1].g])
                hi_ = g4 // 2
                co = (g4 % 2) * 512
                for i, h in enumerate(heads):
                    hp = h // 2
                    kb.op("pe", lambda e: e.matmul(out=ps.t[:, 512 + i * 128:512 + (i + 1) * 128], lhsT=atB.t[:, hp * 128:(hp + 1) * 128],
                                                   rhs=Xall[hi_].t[:, co + i * 128:co + (i + 1) * 128], start=True, stop=True),
                          reads=[atB.g, Xall[hi_].g], writes=[ps.g])
                v4 = ps.t[:, 512:1024].rearrange("p (j two t) -> p j two t", j=2, two=2)
                o4 = AhT.t[:, g4 * 256:(g4 + 1) * 256].rearrange("p (j t) -> p j t", j=2)
                kb.op("act", lambda e: e.activation(out=o4[0:64, :, :], in_=v4[0:64, :, 0, :], func=AF.Copy), reads=[ps.g], writes=[AhT.g])
                kb.op("act", lambda e: e.activation(out=o4[64:128, :, :], in_=v4[64:128, :, 1, :], func=AF.Copy), reads=[ps.g], writes=[AhT.g])
            kb.op("act", lambda e: e.activation(out=W1b.t[:, :], in_=P[1].t[:, :], func=AF.Copy), reads=[P[1].g], writes=[W1b.g])
            if RB_STOP == 6:
                kb.barrier()
                return
            Hb3 = Hb.t
            for h in range(16):
                hp, r0 = h // 2, 64 * (h % 2)
                hi_, co = h // 8, (h % 8) * 128
                kb.op("pe", lambda e: e.matmul(out=P[0].t[:, h * 64:(h + 1) * 64], lhsT=AhT.t[r0:r0 + 64, hp * 128:(hp + 1) * 128], rhs=Hb3[r0:r0 + 64, hp, :],
                                               start=True, stop=False), reads=[AhT.g, Hb.g], writes=[P[0].g])
                kb.op("pe", lambda e: e.matmul(out=P[0].t[:, h * 64:(h + 1) * 64], lhsT=Xall[hi_].t[:, co:co + 128], rhs=W1b.t[:, h * 64:(h + 1) * 64],
                                               start=False, stop=True), reads=[Xall[hi_].g, W1b.g], writes=[P[0].g])
            kb.op("act", lambda e: e.activation(out=Ub.t[:, :], in_=P[0].t[:, :], func=AF.Copy), reads=[P[0].g], writes=[Ub.g])
            for h in range(16):
                hp, r0 = h // 2, 64 * (h % 2)
                hi_, co = h // 8, (h % 8) * 128
                o = P[2].t[:, h * 64:(h + 1) * 64]
                kb.op("pe", lambda e: e.matmul(out=o, lhsT=fm(rT, h), rhs=Hb3[r0:r0 + 64, hp, :], start=True, stop=False), reads=[rT.g, Hb.g], writes=[P[2].g])
                kb.op("pe", lambda e: e.matmul(out=o, lhsT=Arb[hi_].t[:, co:co + 128], rhs=Ub.t[:, h * 64:(h + 1) * 64], start=False, stop=False),
                      reads=[Arb[hi_].g, Ub.g], writes=[P[2].g])
                kb.op("pe", lambda e: e.matmul(out=o, lhsT=Ark[hi_].t[:, co:co + 128], rhs=vB.t[:, h * 64:(h + 1) * 64], start=False, stop=True),
                      reads=[Ark[hi_].g, vB.g], writes=[P[2].g])
            for hp in range(8):
                sl = slice(hp * 128, (hp + 1) * 128)
                kb.op("pe", lambda e: e.matmul(out=P[1].t[:, sl], lhsT=bpB.t[:, sl], rhs=Ub.t[:, sl], start=True, stop=False), reads=[bpB.g, Ub.g], writes=[P[1].g])
                kb.op("pe", lambda e: e.matmul(out=P[1].t[:, sl], lhsT=kpB.t[:, sl], rhs=vB.t[:, sl], start=False, stop=True), reads=[kpB.g, vB.g], writes=[P[1].g])
            kb.op("dve", lambda e: e.tensor_tensor(out=Hs.t[:, :, :], in0=Hs.t[:, :, :], in1=eLC.t[:, 0:8].unsqueeze(2).to_broadcast([128, 8, 64]), op=ALU.mult),
                  reads=[Hs.g, eLC.g], writes=[Hs.g])
            hv = P[1].t[:, :].rearrange("p (hp two d) -> p hp two d", hp=8, two=2)
            kb.op("dve", lambda e: e.tensor_tensor(out=Hs.t[0:64, :, :], in0=Hs.t[0:64, :, :], in1=hv[0:64, :, 0, :], op=ALU.add), reads=[Hs.g, P[1].g], writes=[Hs.g])
            kb.op("dve", lambda e: e.tensor_tensor(out=Hs.t[64:128, :, :], in0=Hs.t[64:128, :, :], in1=hv[64:128, :, 1, :], op=ALU.add), reads=[Hs.g, P[1].g], writes=[Hs.g])
            kb.op("act", lambda e: e.activation(out=Hb.t[:, :, :], in_=Hs.t[:, :, :], func=AF.Copy), reads=[Hs.g], writes=[Hb.g])
            if RB_STOP == 7:
                kb.barrier()
                return
            Y = P[2]
            kb.op("act", lambda e: e.activation(out=f1.t[:, :], in_=Y.t[:, :], func=AF.Copy), reads=[Y.g], writes=[f1.g])
            kb.op("act", lambda e: e.activation(out=f2.t[:, :], in_=Y.t[:, :], func=AF.Square), reads=[Y.g], writes=[f2.g])
            kb.op("dve", lambda e: e.tensor_reduce(out=small.t[:, 32:48], in_=H3(f1.t[:, :]), axis=AX.X, op=ALU.add), reads=[f1.g], writes=[small.g])
            kb.op("dve", lambda e: e.tensor_reduce(out=small.t[:, 48:64], in_=H3(f2.t[:, :]), axis=AX.X, op=ALU.add), reads=[f2.g], writes=[small.g])
            kb.op("dve", lambda e: e.tensor_scalar(out=small.t[:, 32:64], in0=small.t[:, 32:64], scalar1=1.0 / 64, scalar2=None, op0=ALU.mult), reads=[small.g], writes=[small.g])
            kb.op("dve", lambda e: e.tensor_tensor(out=small.t[:, 64:80], in0=small.t[:, 32:48], in1=small.t[:, 32:48], op=ALU.mult), reads=[small.g], writes=[small.g])
            kb.op("dve", lambda e: e.tensor_tensor(out=small.t[:, 64:80], in0=small.t[:, 48:64], in1=small.t[:, 64:80], op=ALU.subtract), reads=[small.g], writes=[small.g])
            kb.op("dve", lambda e: e.tensor_scalar(out=small.t[:, 64:80], in0=small.t[:, 64:80], scalar1=64e-5, scalar2=None, op0=ALU.add), reads=[small.g], writes=[small.g])
            kb.op("act", lambda e: e.activation(out=small.t[:, 64:80], in_=small.t[:, 64:80], func=AF.Sqrt), reads=[small.g], writes=[small.g])
            kb.op("dve", lambda e: e.reciprocal(out=small.t[:, 64:80], in_=small.t[:, 64:80]), reads=[small.g], writes=[small.g])
            kb.op("dve", lambda e: e.tensor_tensor(out=H3(f1.t[:, :]), in0=H3(f1.t[:, :]), in1=bc16(small, 32), op=ALU.subtract), reads=[f1.g, small.g], writes=[f1.g])
            kb.op("dve", lambda e: e.tensor_tensor(out=H3(f1.t[:, :]), in0=H3(f1.t[:, :]), in1=bc16(small, 64), op=ALU.mult), reads=[f1.g, small.g], writes=[f1.g])
            kb.op("dve", lambda e: e.tensor_tensor(out=f1.t[:, :], in0=f1.t[:, :], in1=vec["lnx_g"].t[:, :], op=ALU.mult), reads=[f1.g, vec["lnx_g"].g], writes=[f1.g])
            kb.op("dve", lambda e: e.tensor_tensor(out=f1.t[:, :], in0=f1.t[:, :], in1=vec["lnx_b"].t[:, :], op=ALU.add), reads=[f1.g, vec["lnx_b"].g], writes=[f1.g])
            kb.op("dve", lambda e: e.tensor_tensor(out=f1.t[:, :], in0=f1.t[:, :], in1=vF.t[:, :], op=ALU.add), reads=[f1.g, vF.g], writes=[f1.g])
            kb.op("dve", lambda e: e.tensor_tensor(out=ygB.t[:, :], in0=f1.t[:, :], in1=gF.t[:, :], op=ALU.mult), reads=[f1.g, gF.g], writes=[ygB.g])
            if RB_STOP == 8:
                kb.barrier()
                return
            for kc in range(8):
                kb.op("pe", lambda e: e.transpose(out=QT.t[:, kc * 128:(kc + 1) * 128], in_=ygB.t[:, kc * 128:(kc + 1) * 128], identity=ident.t[:, :]),
                      reads=[ygB.g, ident.g], writes=[QT.g])
            ygT = aT
            kb.op("act", lambda e: e.activation(out=ygT.t[:, :], in_=QT.t[:, :], func=AF.Copy), reads=[QT.g], writes=[ygT.g])
            kb.dma("sp", out=xin.t[:, :], in_=xres_ap[t0:t0 + 128, :], reads=[xres_reg], writes=[xin.g])
            for hf in range(2):
                for fc in range(8):
                    kb.op("pe", lambda e: e.matmul(out=P[0].t[:, hf * 512:(hf + 1) * 512], lhsT=ygT.t[:, fc * 128:(fc + 1) * 128], rhs=wo.t[:, fc, hf * 512:(hf + 1) * 512],
                                                   start=(fc == 0), stop=(fc == 7)), reads=[ygT.g, wo.g], writes=[P[0].g])
            kb.op("dve", lambda e: e.scalar_tensor_tensor(out=f2.t[:, :], in0=xin.t[:, :], scalar=DN_ALPHA, in1=P[0].t[:, :], op0=ALU.mult, op1=ALU.add),
                  reads=[xin.g, P[0].g], writes=[f2.g])
            ln_block(kb, f2, vec["lng"], vec["lnb"], xo, stats, mv, rstd, LN_EPS)
            kb.dma("sp", out=xout_ap[t0:t0 + 128, :], in_=xo.t[:, :], reads=[xo.g], writes=[xout_reg], pool="st")
            kb.op("act", lambda e: e.activation(out=xbf.t[:, :], in_=xo.t[:, :], func=AF.Copy), reads=[xo.g], writes=[xbf.g])
            for kc in range(8):
                kb.op("pe", lambda e: e.transpose(out=QT.t[:, kc * 128:(kc + 1) * 128], in_=xbf.t[:, kc * 128:(kc + 1) * 128], identity=ident.t[:, :]),
                      reads=[xbf.g, ident.g], writes=[QT.g])
            x3t = x3ts[b % 2]
            kb.op("act", lambda e: e.activation(out=x3t.t[:, :], in_=QT.t[:, :], func=AF.Copy), reads=[QT.g], writes=[x3t.g])
            kb.dma("sp", out=XT3[:, :, t0:t0 + 128], in_=x3t.t[:, :].rearrange("p (k t) -> p k t", k=8), reads=[x3t.g], writes=[xt3_reg], pool="st")
            if RB_STOP >= 10 and b == RB_STOP - 10:
                kb.barrier()
                return
        kb.barrier()


CONST_SPECS = {
    "c_ident": ([128, 128], BF16),
    "c_ones": ([128, 128], BF16),
    "c_tri": ([128, 128], BF16),
    "c_ones2": ([2, S], BF16),
    "c_alibiq": ([4, 2, S], BF16),
    "c_abias": ([4, 128, 32], F32),
    "c_retDT": ([4, 128, 512], F32),
    "c_retqdec": ([4, 64, 512], F32),
    "c_retkdec": ([4, 128, 1], F32),
    "c_su4": ([128, 512], BF16),
    "c_sl4": ([128, 512], BF16),
    "c_iu4": ([128, 512], BF16),
    "c_id4": ([128, 512], BF16),
}


def make_consts():
    bf = ml_dtypes.bfloat16
    c = {}
    c["c_ident"] = np.eye(128, dtype=np.float32).astype(bf)
    c["c_ones"] = np.ones((128, 128), np.float32).astype(bf)
    p = np.arange(128)
    c["c_tri"] = (p[None, :] >= p[:, None]).astype(np.float32).astype(bf)
    c["c_ones2"] = np.ones((2, S), np.float32).astype(bf)
    t = np.arange(S) % 512
    hi = (t // 16) * 16
    lo = t % 16
    aq = np.zeros((4, 2, S), np.float64)
    ab = np.zeros((4, 128, 32), np.float64)
    for h in range(4):
        aq[h, 0] = -8.0 * SLOPES[h] * hi
        aq[h, 1] = -8.0 * SLOPES[h] * lo
        for oi in range(32):
            ab[h, :, oi] = SLOPES[h] * (p + 128.0 * (oi - 28))
    c["c_alibiq"] = aq.astype(np.float32).astype(bf)
    c["c_abias"] = ab.astype(np.float32)
    DTm = np.zeros((4, 128, 512), np.float64)
    qd = np.zeros((4, 64, 512), np.float64)
    kd = np.zeros((4, 128, 1), np.float64)
    i = np.arange(128)
    for h in range(4):
        g = GAMMAS[h]
        rel = i[None, :] - i[:, None]
        m = np.where(rel >= 0, 0.125 * g ** np.maximum(rel, 0), 0.0)
        DTm[h] = np.tile(m, (1, 4))
        qd[h] = np.tile(g ** (i + 1.0), (64, 4))
        kd[h, :, 0] = 0.125 * g ** (127.0 - i)
    c["c_retDT"] = DTm.astype(np.float32)
    c["c_retqdec"] = qd.astype(np.float32)
    c["c_retkdec"] = kd.astype(np.float32)
    c["c_su4"] = np.tile((p[None, :] > p[:, None]).astype(np.float32), (1, 4)).astype(bf)
    c["c_sl4"] = np.tile((p[None, :] < p[:, None]).astype(np.float32), (1, 4)).astype(bf)
    c["c_iu4"] = np.tile((p[None, :] >= p[:, None]).astype(np.float32), (1, 4)).astype(bf)
    c["c_id4"] = np.tile(np.eye(128, dtype=np.float32), (1, 4)).astype(bf)
    return c


INPUT_SHAPES = {
    "ev_w_in": [1, 1024, 3072], "ev_lambda": [1, 4, 64], "ev_subln_g": [1, 128], "ev_w_out": [1, 1024, 1024],
    "od_mu": [1, 6, 1024], "od_w_rkv": [1, 3, 1024, 1024], "od_w0": [1, 1024], "od_w1": [1, 1024, 64], "od_w2": [1, 64, 1024],
    "od_a0": [1, 1024], "od_a1": [1, 1024, 64], "od_a2": [1, 64, 1024], "od_g1": [1, 1024, 160], "od_g2": [1, 160, 1024],
    "od_k_k": [1, 1024], "od_k_a": [1, 1024], "od_r_k": [1, 1024], "od_lnx_g": [1, 1024], "od_lnx_b": [1, 1024],
    "od_w_out": [1, 1024, 1024], "ln_mix_g": [2, 1024], "ln_mix_b": [2, 1024], "ffn_w_up": [2, 1024, 5632],
    "ffn_conv_w": [2, 3, 2816], "ffn_conv_b": [2, 2816], "ffn_w_down": [2, 2816, 1024], "ln_ffn_g": [2, 1024], "ln_ffn_b": [2, 1024],
}


def build(stop_after=None, debug=False):
    nc = bass.Bass("TRN2", target_bir_lowering=False)
    io = {}
    io["x"] = nc.dram_tensor("x", [S, D], F32, kind="ExternalInput").ap()
    for k, shp in INPUT_SHAPES.items():
        io[k] = nc.dram_tensor(k, shp, F32, kind="ExternalInput").ap()
    for k, (shp, dt) in CONST_SPECS.items():
        io[k] = nc.dram_tensor(k, shp, dt, kind="ExternalInput").ap()
    y = nc.dram_tensor("y", [S, D], F32, kind="ExternalOutput").ap()
    XA = nc.dram_tensor("scr_xa", [S, D], F32, kind="Internal").ap()
    XB = nc.dram_tensor("scr_xb", [S, D], F32, kind="Internal").ap()
    G = nc.dram_tensor("scr_g", [FF, S], BF16, kind="Internal").ap()
    scr = [nc.dram_tensor("scr_r%d" % i, [S, D], F32, kind="Internal").ap() for i in range(6)]
    scr_reg = Reg(True)
    dbg = None
    if debug:
        dbg = nc.dram_tensor("dbg_ot", [128, 8, S], BF16, kind="ExternalOutput").ap()
    for k in ("ev_w_in", "ev_w_out", "od_w_out", "od_w1", "od_w2", "od_a1", "od_a2", "od_g1", "od_g2"):
        io[k] = io[k][0]
    xa_reg, xb_reg, g_reg, y_reg = Reg(True), Reg(True), Reg(True), Reg(True)
    with ExitStack() as es:
        kb = KB(nc, es)
        kb.dma_pool("sp_ld", 12)
        kb.dma_pool("sp_st", 8)
        kb.dma_pool("pool_ld", 6)
        ident = sbt(nc, es, "ident", [128, 128], BF16)
        ones = sbt(nc, es, "ones", [128, 128], BF16)
        kb.dma("sp", out=ident.t[:, :], in_=io["c_ident"][:, :], writes=[ident.g])
        kb.dma("sp", out=ones.t[:, :], in_=io["c_ones"][:, :], writes=[ones.g])
        tri = sbt(nc, es, "tri", [128, 128], BF16)
        kb.dma("sp", out=tri.t[:, :], in_=io["c_tri"][:, :], writes=[tri.g])
        XT3 = nc.dram_tensor("scr_xt3", [128, 8, S], BF16, kind="Internal").ap()
        xt3_reg = Reg(True)
        with ExitStack() as esx:
            xT = sbt(nc, esx, "xT", [128, 8, S], BF16)
            phase_prologue(kb, nc, io, xT, ident)
            if stop_after == "prologue":
                dump_and_stop(kb, dbg[:, :, :], xT.t[:, :, :], xT.g)
                return nc
            if stop_after in ("rwkvonly", "rwkvonly_a"):
                phase_rwkv_a(kb, nc, io, xT, scr, scr_reg)
                if stop_after == "rwkvonly_a":
                    return nc
            else:
                with ExitStack() as es2:
                    OT = sbt(nc, es2, "OT", [128, 8, S], BF16)
                    if phase_l0_mixer(kb, nc, io, xT, OT, ident, ones, tri, dbg):
                        return nc
                    if stop_after == "mixer":
                        kb.barrier()
                        return nc
                    phase_out_ln(kb, nc, io, OT, xT, ident, io["ev_w_out"], io["x"], io["ln_mix_g"][0], io["ln_mix_b"][0], XA, xa_reg)
                if stop_after == "outln":
                    return nc
                phase_ffn(kb, nc, io, 0, xT, ident, XA, xa_reg, y if stop_after == "ffn0" else XB, y_reg if stop_after == "ffn0" else xb_reg,
                          G, g_reg, want_T=True)
                if stop_after == "ffn0":
                    return nc
                phase_rwkv_a(kb, nc, io, xT, scr, scr_reg)
        if stop_after == "rwkvonly":
            phase_rwkv_b(kb, nc, io, XT3, xt3_reg, ident, ones, tri, scr, scr_reg, io["x"], Reg(True), y, y_reg)
            return nc
        last = stop_after == "rwkv"
        phase_rwkv_b(kb, nc, io, XT3, xt3_reg, ident, ones, tri, scr, scr_reg, XB, xb_reg, y if last else XA, y_reg if last else xa_reg)
        if last:
            return nc
        with ExitStack() as esx:
            xT = sbt(nc, esx, "xT", [128, 8, S], BF16)
            for kc in range(8):
                kb.dma("sp", out=xT.t[:, kc, :], in_=XT3[:, kc, :], reads=[xt3_reg], writes=[xT.g])
            phase_ffn(kb, nc, io, 1, xT, ident, XA, xa_reg, y, y_reg, G, g_reg, want_T=False)
    return nc


_NC_CACHE = {}


def kernel(**inputs):
    if "nc" not in _NC_CACHE:
        _NC_CACHE["nc"] = build()
        _NC_CACHE["consts"] = make_consts()
    nc = _NC_CACHE["nc"]
    consts = _NC_CACHE["consts"]
    x = np.ascontiguousarray(np.asarray(inputs["x"], dtype=np.float32))
    shared = {k: np.ascontiguousarray(np.asarray(inputs[k], dtype=np.float32)) for k in INPUT_SHAPES}
    in_maps = []
    for c in range(8):
        m = {"x": x[c]}
        m.update(shared)
        m.update(consts)
        in_maps.append(m)
    res = run_bass_kernel_spmd(nc, in_maps, core_ids=list(range(8)))
    return np.stack([np.asarray(res.results[c]["y"], dtype=np.float32) for c in range(8)], axis=0)
```
